# Optimizing a Trainium2 kernel written in Bass

```python
import math
import jax, jax.numpy as jnp
from jax import lax
import numpy as np

D_MODEL = 1024
BATCH = 4
SEQ = 4096
DEPTH = 2

GRID_W = 64
CTX_LEN = 256
HEAD_DIM = 64
EPS = 1e-6
NEG = -1e30
FOURIER_WIDTH = D_MODEL // 2
FOURIER_GROUPS = 4
FOURIER_GROUP_CH = FOURIER_WIDTH // FOURIER_GROUPS
WIN_Q_HEADS = (D_MODEL // 2) // HEAD_DIM
WIN_KV_HEADS = 2
WIN_RADIUS = 128
WIN_BLOCK = 128
EV_IN_WIDTH = FOURIER_WIDTH + (WIN_Q_HEADS + 2 * WIN_KV_HEADS) * HEAD_DIM
EV_OUT_WIDTH = FOURIER_WIDTH + WIN_Q_HEADS * HEAD_DIM
NA_HEADS = D_MODEL // HEAD_DIM
NA_KH = 8
NA_KW = 16
ROPE_THETA = 10000.0
ROPE_FREQS = HEAD_DIM // 4
D_FF = ((8 * D_MODEL // 3 + 255) // 256) * 256
N_EVEN = (DEPTH + 1) // 2
N_ODD = DEPTH // 2

kernel_name = "hybrid_fourier_window_natten_dit"


def _rms(x, g):
    x32 = x.astype(jnp.float32)
    y = x32 * lax.rsqrt(jnp.mean(x32 * x32, axis=-1, keepdims=True) + EPS)
    return (y * g.astype(jnp.float32)).astype(x.dtype)


def _axial_angles(L):
    t = jnp.arange(L, dtype=jnp.int32)
    row = (t // GRID_W).astype(jnp.float32)
    col = (t % GRID_W).astype(jnp.float32)
    inv = ROPE_THETA ** (-jnp.arange(ROPE_FREQS, dtype=jnp.float32) / ROPE_FREQS)
    return row[:, None] * inv[None, :], col[:, None] * inv[None, :]


def _rotate(xa, ang):
    cos = jnp.cos(ang)[:, None, :].astype(xa.dtype)
    sin = jnp.sin(ang)[:, None, :].astype(xa.dtype)
    x1, x2 = xa[..., :ROPE_FREQS], xa[..., ROPE_FREQS:]
    return jnp.concatenate([x1 * cos - x2 * sin, x2 * cos + x1 * sin], axis=-1)


def _rope_2d(x, ang_row, ang_col):
    h = HEAD_DIM // 2
    return jnp.concatenate([_rotate(x[..., :h], ang_row), _rotate(x[..., h:], ang_col)], axis=-1)


def _fourier_mix(f):
    B, L, _ = f.shape
    fg = f.astype(jnp.float32).reshape(B, L, FOURIER_GROUPS, FOURIER_GROUP_CH)
    mixed = jnp.fft.fft2(fg, axes=(1, 3), norm="ortho").real
    return mixed.reshape(B, L, FOURIER_WIDTH).astype(f.dtype)


def _ctx_attention(q, k, v, sink):
    B, C, H, d = q.shape
    KV = k.shape[2]
    G = H // KV
    qg = q.reshape(B, C, KV, G, d)
    s = jnp.einsum('bqkgd,bjkd->bkgqj', qg, k).astype(jnp.float32) * (1.0 / math.sqrt(d))
    if sink is not None:
        sk = jnp.broadcast_to(sink.astype(jnp.float32).reshape(1, KV, G, 1, 1), (B, KV, G, C, 1))
        p = jax.nn.softmax(jnp.concatenate([s, sk], axis=-1), axis=-1)[..., :C]
    else:
        p = jax.nn.softmax(s, axis=-1)
    o = jnp.einsum('bkgqj,bjkd->bqkgd', p.astype(v.dtype), v)
    return o.reshape(B, C, H * d)


def _window_attention(q, k, v, k_ctx, v_ctx, sink):
    B, L, H, d = q.shape
    KV = k.shape[2]
    G = H // KV
    C = k_ctx.shape[1]
    nb = L // WIN_BLOCK
    nw = 3 * WIN_BLOCK
    scale = 1.0 / math.sqrt(d)
    qb = q.reshape(B, nb, WIN_BLOCK, KV, G, d)
    pad = ((0, 0), (WIN_BLOCK, WIN_BLOCK), (0, 0), (0, 0))
    kp = jnp.pad(k, pad).reshape(B, nb + 2, WIN_BLOCK, KV, d)
    vp = jnp.pad(v, pad).reshape(B, nb + 2, WIN_BLOCK, KV, d)
    kw = jnp.concatenate([kp[:, :-2], kp[:, 1:-1], kp[:, 2:]], axis=2)
    vw = jnp.concatenate([vp[:, :-2], vp[:, 1:-1], vp[:, 2:]], axis=2)
    s_win = jnp.einsum('bnqkgd,bnjkd->bnkgqj', qb, kw).astype(jnp.float32) * scale
    s_ctx = jnp.einsum('bnqkgd,bjkd->bnkgqj', qb, k_ctx).astype(jnp.float32) * scale
    blk = jnp.arange(nb, dtype=jnp.int32)[:, None, None] * WIN_BLOCK
    qpos = blk + jnp.arange(WIN_BLOCK, dtype=jnp.int32)[None, :, None]
    kpos = blk - WIN_BLOCK + jnp.arange(nw, dtype=jnp.int32)[None, None, :]
    mask = (jnp.abs(kpos - qpos) <= WIN_RADIUS) & (kpos >= 0) & (kpos < L)
    s_win = jnp.where(mask[None, :, None, None], s_win, NEG)
    sk = jnp.broadcast_to(sink.astype(jnp.float32).reshape(1, 1, KV, G, 1, 1), (B, nb, KV, G, WIN_BLOCK, 1))
    p = jax.nn.softmax(jnp.concatenate([s_win, s_ctx, sk], axis=-1), axis=-1).astype(v.dtype)
    o = (jnp.einsum('bnkgqj,bnjkd->bnqkgd', p[..., :nw], vw)
         + jnp.einsum('bnkgqj,bjkd->bnqkgd', p[..., nw:nw + C], v_ctx))
    return o.reshape(B, L, H * d)


def _neighbourhood_attention(q, k, v, k_ctx, v_ctx, rel_bias):
    B, L, H, d = q.shape
    rows = L // GRID_W
    kh = min(NA_KH, rows)
    n = kh * GRID_W
    scale = 1.0 / math.sqrt(d)
    qg = q.reshape(B, rows, GRID_W, H, d)
    kg = k.reshape(B, rows, GRID_W, H, d)
    vg = v.reshape(B, rows, GRID_W, H, d)
    cq = jnp.arange(GRID_W, dtype=jnp.int32)
    c0 = jnp.clip(cq - NA_KW // 2, 0, GRID_W - NA_KW)
    col_ok = (cq[None, :] >= c0[:, None]) & (cq[None, :] < c0[:, None] + NA_KW)
    mask = jnp.tile(col_ok, (1, kh))
    dc_idx = jnp.clip(cq[None, :] - cq[:, None] + NA_KW - 1, 0, 2 * NA_KW - 2)

    def one_row(r):
        r0 = jnp.clip(r - kh // 2, 0, rows - kh)
        k_rows = lax.dynamic_slice_in_dim(kg, r0, kh, axis=1).reshape(B, n, H, d)
        v_rows = lax.dynamic_slice_in_dim(vg, r0, kh, axis=1).reshape(B, n, H, d)
        q_row = lax.dynamic_index_in_dim(qg, r, axis=1, keepdims=False)
        dr_idx = r0 + jnp.arange(kh, dtype=jnp.int32) - r + NA_KH - 1
        bias = rel_bias[:, dr_idx[None, :, None], dc_idx[:, None, :]]
        bias = bias.reshape(H, GRID_W, n).astype(jnp.float32)
        s_nb = jnp.einsum('bqhd,bjhd->bhqj', q_row, k_rows).astype(jnp.float32) * scale + bias[None]
        s_nb = jnp.where(mask[None, None], s_nb, NEG)
        s_cx = jnp.einsum('bqhd,bjhd->bhqj', q_row, k_ctx).astype(jnp.float32) * scale
        p = jax.nn.softmax(jnp.concatenate([s_nb, s_cx], axis=-1), axis=-1).astype(v.dtype)
        return (jnp.einsum('bhqj,bjhd->bqhd', p[..., :n], v_rows)
                + jnp.einsum('bhqj,bjhd->bqhd', p[..., n:], v_ctx))

    out = lax.map(one_row, jnp.arange(rows, dtype=jnp.int32))
    return out.transpose(1, 0, 2, 3, 4).reshape(B, L, H * d)


def _even_mixer(h, hc, w_in, w_out, q_g, k_g, sink, ang_row, ang_col, ctx_out):
    F = FOURIER_WIDTH
    QW = WIN_Q_HEADS * HEAD_DIM
    KW = WIN_KV_HEADS * HEAD_DIM

    def split(t):
        Bt, Lt = t.shape[0], t.shape[1]
        f = t[..., :F]
        q = _rms(t[..., F:F + QW].reshape(Bt, Lt, WIN_Q_HEADS, HEAD_DIM), q_g)
        k = _rms(t[..., F + QW:F + QW + KW].reshape(Bt, Lt, WIN_KV_HEADS, HEAD_DIM), k_g)
        v = t[..., F + QW + KW:].reshape(Bt, Lt, WIN_KV_HEADS, HEAD_DIM)
        return f, q, k, v

    f, q, k, v = split(h @ w_in)
    fc, qc, kc, vc = split(hc @ w_in)
    q = _rope_2d(q, ang_row, ang_col)
    k = _rope_2d(k, ang_row, ang_col)
    o = jnp.concatenate([_fourier_mix(f), _window_attention(q, k, v, kc, vc, sink)], axis=-1) @ w_out
    oc = None
    if ctx_out:
        oc = jnp.concatenate([_fourier_mix(fc), _ctx_attention(qc, kc, vc, sink)], axis=-1) @ w_out
    return o, oc


def _odd_mixer(h, hc, w_in, w_out, q_g, k_g, rel_bias, ctx_out):
    W = NA_HEADS * HEAD_DIM

    def split(t):
        Bt, Lt = t.shape[0], t.shape[1]
        q = _rms(t[..., :W].reshape(Bt, Lt, NA_HEADS, HEAD_DIM), q_g)
        k = _rms(t[..., W:2 * W].reshape(Bt, Lt, NA_HEADS, HEAD_DIM), k_g)
        v = t[..., 2 * W:].reshape(Bt, Lt, NA_HEADS, HEAD_DIM)
        return q, k, v

    q, k, v = split(h @ w_in)
    qc, kc, vc = split(hc @ w_in)
    o = _neighbourhood_attention(q, k, v, kc, vc, rel_bias) @ w_out
    oc = None
    if ctx_out:
        oc = _ctx_attention(qc, kc, vc, None) @ w_out
    return o, oc


def _swiglu(h, wg, wu, wd):
    return (jax.nn.silu(h @ wg) * (h @ wu)) @ wd


def setup_inputs(seed: int = 0) -> dict:
    key = jax.random.key(seed)
    ks = jax.random.split(key, 24)
    D = D_MODEL
    nrm = jax.random.normal
    f32 = jnp.float32
    return {
        "x": nrm(ks[0], (BATCH, SEQ, D), f32),
        "c": nrm(ks[1], (BATCH, D), f32),
        "ctx": nrm(ks[2], (BATCH, CTX_LEN, D), f32),
        "c_ctx": nrm(ks[3], (D,), f32),
        "ada_w": nrm(ks[4], (DEPTH, D, 6 * D), f32) * D ** -0.5,
        "ada_b": nrm(ks[5], (DEPTH, 6 * D), f32) * 0.01,
        "norm1_g": 1.0 + 0.01 * nrm(ks[6], (DEPTH, D), f32),
        "norm2_g": 1.0 + 0.01 * nrm(ks[7], (DEPTH, D), f32),
        "ffn_w_gate": nrm(ks[8], (DEPTH, D, D_FF), f32) * D ** -0.5,
        "ffn_w_up": nrm(ks[9], (DEPTH, D, D_FF), f32) * D ** -0.5,
        "ffn_w_down": nrm(ks[10], (DEPTH, D_FF, D), f32) * D_FF ** -0.5,
        "ev_w_in": nrm(ks[11], (N_EVEN, D, EV_IN_WIDTH), f32) * D ** -0.5,
        "ev_w_out": nrm(ks[12], (N_EVEN, EV_OUT_WIDTH, D), f32) * EV_OUT_WIDTH ** -0.5,
        "ev_q_norm": 1.0 + 0.01 * nrm(ks[13], (N_EVEN, HEAD_DIM), f32),
        "ev_k_norm": 1.0 + 0.01 * nrm(ks[14], (N_EVEN, HEAD_DIM), f32),
        "ev_sink": 0.5 * nrm(ks[15], (N_EVEN, WIN_Q_HEADS), f32),
        "od_w_in": nrm(ks[16], (N_ODD, D, 3 * NA_HEADS * HEAD_DIM), f32) * D ** -0.5,
        "od_w_out": nrm(ks[17], (N_ODD, NA_HEADS * HEAD_DIM, D), f32) * (NA_HEADS * HEAD_DIM) ** -0.5,
        "od_q_norm": 1.0 + 0.01 * nrm(ks[18], (N_ODD, HEAD_DIM), f32),
        "od_k_norm": 1.0 + 0.01 * nrm(ks[19], (N_ODD, HEAD_DIM), f32),
        "od_rel_bias": 0.1 * nrm(ks[20], (N_ODD, NA_HEADS, 2 * NA_KH - 1, 2 * NA_KW - 1), f32),
    }


def reference(x, c, ctx, c_ctx, ada_w, ada_b, norm1_g, norm2_g, ffn_w_gate, ffn_w_up, ffn_w_down,
              ev_w_in, ev_w_out, ev_q_norm, ev_k_norm, ev_sink,
              od_w_in, od_w_out, od_q_norm, od_k_norm, od_rel_bias):
    L = x.shape[1]
    ang_row, ang_col = _axial_angles(L)
    y = ctx
    for i in range(DEPTH):
        last = i == DEPTH - 1
        m_lat = (jax.nn.silu(c) @ ada_w[i] + ada_b[i])[:, None, :]
        m_ctx = jax.nn.silu(c_ctx) @ ada_w[i] + ada_b[i]
        sh1, sc1, g1, sh2, sc2, g2 = jnp.split(m_lat, 6, axis=-1)
        csh1, csc1, cg1, csh2, csc2, cg2 = jnp.split(m_ctx, 6, axis=-1)
        h = _rms(x, norm1_g[i]) * (1.0 + sc1) + sh1
        hc = _rms(y, norm1_g[i]) * (1.0 + csc1) + csh1
        if i % 2 == 0:
            j = i // 2
            o, oc = _even_mixer(h, hc, ev_w_in[j], ev_w_out[j], ev_q_norm[j], ev_k_norm[j],
                                ev_sink[j], ang_row, ang_col, not last)
        else:
            j = i // 2
            o, oc = _odd_mixer(h, hc, od_w_in[j], od_w_out[j], od_q_norm[j], od_k_norm[j],
                               od_rel_bias[j], not last)
        x = x + g1 * o
        h = _rms(x, norm2_g[i]) * (1.0 + sc2) + sh2
        x = x + g2 * _swiglu(h, ffn_w_gate[i], ffn_w_up[i], ffn_w_down[i])
        if not last:
            y = y + cg1 * oc
            hc = _rms(y, norm2_g[i]) * (1.0 + csc2) + csh2
            y = y + cg2 * _swiglu(hc, ffn_w_gate[i], ffn_w_up[i], ffn_w_down[i])
    return x
```

```python
import numpy as np
import ml_dtypes
import concourse.bass as bass
import concourse.mybir as mybir
from concourse.bass_utils import run_bass_kernel_spmd
from concourse.ap import AP

F32 = mybir.dt.float32
BF16 = mybir.dt.bfloat16
ALU = mybir.AluOpType
AF = mybir.ActivationFunctionType

NDMA_SEM = 48
D = 1024
DFF = 2816
NEG = -30000.0
EPS = 1e-6
T_ALL = 4096
T_X = 2304
T_A = 2560
T_OWN = 2048
NCTX = 256


class _Op:
    __slots__ = ("id", "eng", "fn", "deps", "is_dma", "sem_i", "sem_val", "signal", "count")


class Prog:
    ENGS = ("pe", "act", "dve", "pool", "sp")

    def __init__(self):
        self.ops = []
        self.lw = {}
        self.rd = {}
        self.dma_rr = 0
        self.dma_sem_uses = [0] * NDMA_SEM
        self.dma_sem_last = [None] * NDMA_SEM
        self.last_eng = {}
        self.bar = None
        self.dopen = set()
        self.saved = {}
        self.strict = True

    def add(self, eng, fn, reads=(), writes=(), dma=False):
        op = _Op()
        op.id = len(self.ops)
        op.eng = eng
        op.fn = fn
        op.is_dma = dma
        op.signal = False
        op.count = 0
        deps = {}
        if self.bar is not None:
            deps[self.bar] = True
        for k in reads:
            for w in self.lw.get(k, ()):
                deps[w] = True
            self.dopen.discard(k)
        cow = set()
        for k in writes:
            if dma and k in self.dopen:
                cow.add(k)
                for r in self.saved.get(k, ()):
                    deps.setdefault(r, False)
                continue
            for w in self.lw.get(k, ()):
                deps.setdefault(w, False)
            for r in self.rd.get(k, ()):
                deps.setdefault(r, False)
        if dma:
            i = self.dma_rr % NDMA_SEM
            self.dma_rr += 1
            prev = self.dma_sem_last[i]
            if prev is not None:
                deps.setdefault(prev, False)
            self.dma_sem_uses[i] += 1
            op.sem_i = i
            op.sem_val = 16 * self.dma_sem_uses[i]
            self.dma_sem_last[i] = op.id
        op.deps = deps
        for k in reads:
            self.rd.setdefault(k, []).append(op.id)
        for k in writes:
            if k in cow:
                self.lw[k].append(op.id)
                continue
            self.saved[k] = list(self.lw.get(k, ())) + list(self.rd.get(k, ()))
            self.lw[k] = [op.id]
            self.rd[k] = []
            if dma:
                self.dopen.add(k)
            else:
                self.dopen.discard(k)
        self.ops.append(op)
        if not dma:
            self.last_eng[eng] = op.id
        return op

    def barrier(self, fn):
        op = self.add("pool", fn)
        for e, i in self.last_eng.items():
            if i != op.id:
                op.deps[i] = True
        for i in self.dma_sem_last:
            if i is not None:
                op.deps[i] = True
        self.bar = op.id
        self.lw = {}
        self.rd = {}
        self.dopen = set()
        self.saved = {}

    def emit(self, block, eng_sems, dma_sems):
        ops = self.ops

        strict = self.strict

        def skip(op, dop, raw):
            if op.is_dma or dop.is_dma or dop.eng != op.eng:
                return False
            return op.eng == "pe" or (not raw and not strict)

        for op in ops:
            for d, raw in op.deps.items():
                dop = ops[d]
                if dop.is_dma or skip(op, dop, raw):
                    continue
                dop.signal = True
        cnt = {e: 0 for e in self.ENGS}
        for op in ops:
            if op.signal and not op.is_dma:
                cnt[op.eng] += 1
                op.count = cnt[op.eng]
        per_eng = {e: [o for o in ops if o.eng == e] for e in self.ENGS}

        def run(e, engine):
            known = {}
            for op in per_eng[e]:
                needs = {}
                for d, raw in op.deps.items():
                    dop = ops[d]
                    if dop.is_dma:
                        key = ("d", dop.sem_i)
                        val = dop.sem_val
                    else:
                        if skip(op, dop, raw):
                            continue
                        key = ("e", dop.eng)
                        val = dop.count
                    if needs.get(key, 0) < val:
                        needs[key] = val
                for key, val in needs.items():
                    if known.get(key, 0) >= val:
                        continue
                    known[key] = val
                    sem = dma_sems[key[1]] if key[0] == "d" else eng_sems[key[1]]
                    engine.wait_ge(sem, val)
                ins = op.fn(engine)
                if op.is_dma:
                    ins.then_inc(dma_sems[op.sem_i], 16)
                elif op.signal:
                    ins.then_inc(eng_sems[e], 1)
            if e == "sp":
                for i in range(NDMA_SEM):
                    if self.dma_sem_uses[i] > 0:
                        engine.wait_ge(dma_sems[i], 16 * self.dma_sem_uses[i])

        block.tensor(lambda eng: run("pe", eng))
        block.scalar(lambda eng: run("act", eng))
        block.vector(lambda eng: run("dve", eng))
        block.gpsimd(lambda eng: run("pool", eng))
        block.sync(lambda eng: run("sp", eng))


def groups(n, g=512):
    out = []
    s = 0
    while s < n:
        out.append((s, min(g, n - s)))
        s += g
    return out


class Builder:
    def __init__(self, stop_after=None, debug=False, only=None, flags=()):
        self.only = only
        self.flags = set(f for f in flags if not f.startswith('n='))
        self.nstep = ([int(f[2:]) for f in flags if f.startswith('n=')] + [8])[0]
        self.nc = bass.Bass("TRN2", target_bir_lowering=False)
        self.P = Prog()
        self.stop_after = stop_after
        self.debug = debug
        nc = self.nc
        self.lo = ((nc.sbuf_base + 63) // 64) * 64
        self.hi = nc.sbuf_top
        self.pers = self.lo
        self.cur = None
        self.uid = 0
        self.rr = 0
        self.evr = 0
        self.rot = [2, 3, 4, 5, 6]
        self.nmi = 0
        self.tfi = 0
        self.qni = 0

    def _alloc(self, ptr, name, shape, dt):
        esz = 4 if dt == F32 else 2
        n = 1
        for s in shape[1:]:
            n *= s
        nbytes = ((n * esz + 63) // 64) * 64
        self.uid += 1
        t = self.nc.alloc_sbuf_tensor_at("%s_%d" % (name, self.uid), list(shape), dt, offset=ptr)
        return t, ptr + nbytes

    def palloc(self, name, shape, dt):
        assert self.cur is None
        t, self.pers = self._alloc(self.pers, name, shape, dt)
        assert self.pers <= self.hi, "persistent overflow"
        return t

    def talloc(self, name, shape, dt):
        t, self.cur = self._alloc(self.cur, name, shape, dt)
        assert self.cur <= self.hi, "phase overflow %s %d" % (name, self.cur - self.hi)
        return t

    def phase(self):
        scr = self.scr
        self.P.barrier(lambda e: e.memset(scr[0:1, 0:8], 0.0))
        self.cur = self.pers

    def dma(self, eng, out, in_, reads=(), writes=()):
        self.P.add(eng, lambda e: e.dma_start(out=out, in_=in_), reads=reads, writes=writes, dma=True)

    def mm(self, out, lhsT, rhs, start, stop, reads, writes):
        self.P.add("pe", lambda e: e.matmul(out, lhsT=lhsT, rhs=rhs, start=start, stop=stop), reads=reads, writes=writes)

    def tr(self, out, in_, reads, writes):
        ident = self.identf
        k = in_.shape[0]
        self.P.add("pe", lambda e: e.transpose(out, in_, ident[0:k, 0:k]), reads=list(reads) + ["c_id"], writes=writes)

    def act(self, out, in_, func, reads, writes, bias=None, scale=None):
        kw = {}
        if bias is not None:
            kw["bias"] = bias
        if scale is not None:
            kw["scale"] = scale
        self.P.add("act", lambda e: e.activation(out=out, in_=in_, func=func, **kw), reads=reads, writes=writes)

    def tt(self, eng, out, in0, in1, op, reads, writes):
        self.P.add(eng, lambda e: e.tensor_tensor(out=out, in0=in0, in1=in1, op=op), reads=reads, writes=writes)

    def stt(self, eng, out, in0, scalar, in1, op0, op1, reads, writes):
        self.P.add(eng, lambda e: e.scalar_tensor_tensor(out=out, in0=in0, scalar=scalar, in1=in1, op0=op0, op1=op1),
                   reads=reads, writes=writes)

    def ts(self, eng, out, in0, s1, s2, op0, op1, reads, writes):
        if s2 is None:
            self.P.add(eng, lambda e: e.tensor_scalar(out=out, in0=in0, scalar1=s1, scalar2=None, op0=op0), reads=reads, writes=writes)
        else:
            self.P.add(eng, lambda e: e.tensor_scalar(out=out, in0=in0, scalar1=s1, scalar2=s2, op0=op0, op1=op1), reads=reads, writes=writes)

    def copy(self, eng, out, in_, reads, writes):
        if eng == "act":
            self.act(out, in_, AF.Copy, reads, writes)
        else:
            self.P.add(eng, lambda e: e.tensor_copy(out=out, in_=in_), reads=reads, writes=writes)

    def evac(self, out, in_, reads, writes):
        self.evr += 1
        self.copy("act" if self.evr % 2 else "dve", out, in_, reads, writes)

    def dump(self, name, t, key):
        if not self.debug:
            return
        shape = list(t.shape)
        d = self.nc.dram_tensor("dbg_" + name, shape, t.dtype, kind="ExternalOutput")
        self.dma("sp", d.ap(), t[:], reads=[key])

    def bank(self):
        self.rr += 1
        i = self.rot[self.rr % len(self.rot)]
        return self.pb[i], "pb%d" % i

    def build(self):
        nc = self.nc
        P = self.P
        dr = lambda name, shape, dt, kind="ExternalInput": nc.dram_tensor(name, list(shape), dt, kind=kind)
        self.x_d = dr("x_loc", [T_ALL, D], F32)
        self.ctx_d = dr("ctx_b", [NCTX, D], F32)
        self.cvec_d = dr("cvec", [2, D], F32)
        self.ada_w_d = dr("ada_w", [2, D, 6 * D], F32)
        self.ada_b_d = dr("ada_b", [2, 6 * D], F32)
        self.n1_d = dr("norm1_g", [2, D], F32)
        self.n2_d = dr("norm2_g", [2, D], F32)
        self.wg_d = dr("ffn_w_gate", [2, D, DFF], F32)
        self.wu_d = dr("ffn_w_up", [2, D, DFF], F32)
        self.wd_d = dr("ffn_w_down", [2, DFF, D], F32)
        self.evin_d = dr("ev_w_in", [D, 1280], F32)
        self.evout_d = dr("ev_w_out", [D, D], F32)
        self.evq_d = dr("ev_q_norm", [64], F32)
        self.evk_d = dr("ev_k_norm", [64], F32)
        self.sink_d = dr("ev_sink", [8], F32)
        self.odin_d = dr("od_w_in", [D, 3 * D], F32)
        self.odout_d = dr("od_w_out", [D, D], F32)
        self.odq_d = dr("od_q_norm", [64], F32)
        self.odk_d = dr("od_k_norm", [64], F32)
        self.rb_d = dr("rel_bias", [16 * 15 * 31 + 64], F32)
        self.cidf_d = dr("c_identf", [128, 128], F32)
        self.cpsw_d = dr("c_pswap", [128, 128], F32)
        self.cbf_d = dr("c_bf", [128, 10, 128], BF16)
        self.cctx_d = dr("c_ctxdft", [128, 2, 2, 256], BF16)
        self.cnam_d = dr("c_namask", [128, 3, 6, 128], BF16)
        self.crope_d = dr("c_rope", [2, 128, T_A], F32)
        self.cdft_d = [dr("c_dftc", [T_ALL, T_X], BF16), dr("c_dfts", [T_ALL, T_X], BF16)]
        self.out_d = dr("out_loc", [T_OWN, D], F32, kind="ExternalOutput")
        if self.debug:
            self.dbgx_d = dr("dbg_x", [128, 8, T_X], F32, kind="ExternalOutput")
            self.dbgy_d = dr("dbg_y", [128, 8, NCTX], F32, kind="ExternalOutput")
            self.dbga_d = dr("dbg_a", [128, 8, T_X], F32, kind="ExternalOutput")

        self.pb = [nc.alloc_psum_tensor("pb%d" % i, [128, 512], F32) for i in range(8)]
        self.eng_sems = {e: nc.alloc_semaphore("s_" + e) for e in Prog.ENGS}
        self.dma_sems = [nc.alloc_semaphore("d%d" % i) for i in range(NDMA_SEM)]

        self.setup()
        done = False
        for name, fn in [("A", self.layer0_A), ("B", self.layer0_B), ("C", self.layer0_C), ("D", lambda: self.ffn(0)),
                         ("E", self.layer1_E), ("F", lambda: self.ffn(1))]:
            if self.only is not None and name not in self.only:
                continue
            if not hasattr(self, "XT") and name in ("D", "E", "F"):
                self.ada_tick(100)
                self.rot = [2, 3, 4, 5, 6, 7]
                self.pers -= 2 * 4096
                self.XT = self.nc.alloc_sbuf_tensor_at("XT_res", [128, 8, T_X], F32, offset=self.hi - 8 * T_X * 4)
            fn()
            if self.stop_after == name:
                done = True
                break
        self.finish(final=not done)
        with nc.Block() as block:
            P.emit(block, self.eng_sems, self.dma_sems)
        return nc

    def setup(self):
        nc = self.nc
        pa = self.palloc
        self.scr = pa("scr", [128, 16], F32)
        self.identf = pa("identf", [128, 128], F32)
        self.pswap = pa("pswap", [128, 128], F32)
        self.cbf = pa("cbf", [128, 10, 128], BF16)
        self.cctx = pa("cctx", [128, 2, 2, 256], BF16)
        self.vecT = pa("vecT", [128, 48], F32)
        self.biasT = pa("biasT", [128, 96], F32)
        self.modT = pa("modT", [128, 2, 48, 2], F32)
        self.A1 = pa("A1", [128, 2, 2, 8], F32)
        self.A2 = pa("A2", [128, 2, 2, 8], F32)
        self.sT = pa("sT", [128, 2, 8], BF16)
        self.gq = pa("gq", [128, 4], F32)
        self.sinkbc = pa("sinkbc", [128, 4, 128], F32)
        self.skt = pa("skt", [128, 4], F32)
        self.zf = pa("zf", [128, 128], F32)
        self.YT = pa("YT", [128, 8, NCTX], F32)
        self.dma("sp", self.identf[:], self.cidf_d.ap(), writes=["c_id"])
        self.dma("sp", self.pswap[:], self.cpsw_d.ap(), writes=["c_ps"])
        self.dma("sp", self.cbf[:], self.cbf_d.ap(), writes=["c_bf"])
        self.dma("sp", self.cctx[:], self.cctx_d.ap(), writes=["c_cx"])
        self.P.add("pool", lambda e: e.memset(self.zf[:], 0.0), writes=["zf"])
        self.P.add("pool", lambda e: e.memset(self.scr[:], 0.0), writes=["scr"])
        self.ones1024 = self.cbf[:, 0, :]
        self.blk64 = self.cbf[:, 1, :]
        self.identb = self.cbf[:, 2, :]
        self.rblk = self.cbf[:, 3, :]
        self.zerob = self.cbf[:, 4, :]
        self.onesb = self.cbf[:, 5, :]
        self.ccos = self.cbf[:, 6, :]
        self.cnsin = self.cbf[:, 7, :]
        self.mprev = self.cbf[:, 8, :]
        self.mnext = self.cbf[:, 9, :]
        self.adabuf = None
        self.cur = self.pers + 2 * 4096
        ta = self.talloc
        rows1 = ta("rows1", [48, 128], F32)
        rows2 = ta("rows2", [96, 128], F32)
        self.dma("sp", rows1[0:16, :], self.n1_d.ap().rearrange("l (c p) -> (l c) p", p=128), writes=["rows1"])
        self.dma("sp", rows1[16:32, :], self.n2_d.ap().rearrange("l (c p) -> (l c) p", p=128), writes=["rows1"])
        self.dma("sp", rows1[32:48, :], self.cvec_d.ap().rearrange("s (c p) -> (s c) p", p=128), writes=["rows1"])
        self.dma("sp", rows2[:, :], self.ada_b_d.ap().rearrange("l (c p) -> (l c) p", p=128), writes=["rows2"])
        b, bk = self.bank()
        self.tr(b[:, 0:48], rows1[:, :], ["rows1"], [bk])
        self.copy("dve", self.vecT[:], b[:, 0:48], [bk], ["vecT"])
        b, bk = self.bank()
        self.tr(b[:, 0:96], rows2[:, :], ["rows2"], [bk])
        self.copy("dve", self.biasT[:], b[:, 0:96], [bk], ["biasT"])
        self.act(self.sT[:].rearrange("p s c -> p (s c)"), self.vecT[:, 32:48], AF.Silu, ["vecT"], ["sT"])
        for i, (d_, sc) in enumerate([(self.evq_d, 0.125), (self.evk_d, 1.0), (self.odq_d, 0.125), (self.odk_d, 1.0)]):
            src = d_.ap().rearrange("(d u) -> d u", u=1)
            self.dma("sp", self.gq[0:64, i:i + 1], src, writes=["gq"])
            self.dma("sp", self.gq[64:128, i:i + 1], src, writes=["gq"])
        self.ts("dve", self.gq[:, 0:1], self.gq[:, 0:1], 0.125, None, ALU.mult, None, ["gq"], ["gq"])
        self.ts("dve", self.gq[:, 2:3], self.gq[:, 2:3], 0.125, None, ALU.mult, None, ["gq"], ["gq"])
        self.dma("sp", self.skt[0:64, :], AP(self.sink_d, 0, [[0, 64], [1, 4]]), writes=["skt"])
        self.dma("sp", self.skt[64:128, :], AP(self.sink_d, 4, [[0, 64], [1, 4]]), writes=["skt"])
        self.act(self.skt[:], self.skt[:], AF.Exp, ["skt"], ["skt"])
        for j in range(4):
            self.ts("dve", self.sinkbc[:, j, :], self.zf[:], self.skt[:, j:j + 1], None, ALU.add, None, ["zf", "skt"], ["sinkbc"])
        self.adabuf = [self.palloc_late("adabuf%d" % i, [128, 8, 256], BF16) for i in range(2)]
        self.ada_list = [(l, sl) for l in range(2) for sl in range(24)]
        self.ada_pos = 0
        self.ada_issued = 0
        self.win = self.nc.alloc_sbuf_tensor_at("win_top", [128, 8, 1280], BF16, offset=self.hi - 20480)
        self.load_w(self.win[:], "win", self.evin_d.ap().rearrange("(c p) n -> p c n", p=128))
        self.ada_issue()
        self.ada_tick(8)

    def palloc_late(self, name, shape, dt):
        t, self.pers = self._alloc(self.pers, name, shape, dt)
        return t

    def ada_issue(self):
        if self.ada_issued >= len(self.ada_list):
            return
        l, sl = self.ada_list[self.ada_issued]
        i = self.ada_issued % 2
        src = self.ada_w_d.ap()[l].rearrange("(c p) n -> p c n", p=128)[:, :, sl * 256:(sl + 1) * 256]
        self.dma("pool", self.adabuf[i][:], src, writes=["adabuf%d" % i])
        self.ada_issued += 1

    def ada_tick(self, n=1):
        for _ in range(n):
            if self.ada_pos >= len(self.ada_list):
                return
            l, sl = self.ada_list[self.ada_pos]
            i = self.ada_pos % 2
            self.ada_pos += 1
            self.ada_issue()
            buf, bk = self.adabuf[i], "adabuf%d" % i
            pm, pmk = self.pb[7], "pb7"
            for o2 in range(2):
                oc = sl * 2 + o2
                for kc in range(8):
                    self.mm(pm[:, oc * 2:oc * 2 + 2], buf[:, kc, o2 * 128:(o2 + 1) * 128], self.sT[:, :, kc],
                            kc == 0, kc == 7, [bk, "sT"], [pmk])
            if sl == 7 or sl == 23:
                lo, hi = (0, 16) if sl == 7 else (16, 48)
                for s_ in range(2):
                    self.tt("dve", self.modT[:, l, lo:hi, s_], pm[:, 0:96].rearrange("p (o s) -> p o s", s=2)[:, lo:hi, s_],
                            self.biasT[:, l * 48 + lo:l * 48 + hi], ALU.add, [pmk, "biasT"], ["modT"])
                    if sl == 7:
                        self.stt("dve", self.A1[:, l, s_, :], self.modT[:, l, 8:16, s_], 1.0, self.vecT[:, l * 8:l * 8 + 8],
                                 ALU.add, ALU.mult, ["modT", "vecT"], ["A1"])
                    else:
                        self.stt("dve", self.A2[:, l, s_, :], self.modT[:, l, 32:40, s_], 1.0, self.vecT[:, 16 + l * 8:16 + l * 8 + 8],
                                 ALU.add, ALU.mult, ["modT", "vecT"], ["A2"])

    def mod(self, l, part, s, c):
        return self.modT[:, l, part * 8 + c, s:s + 1]

    def load_xT(self, src_rows, ntile, dst, dst_key, col0, xtoks):
        for t in range(ntile):
            xt, xk = xtoks[self.xti % len(xtoks)]
            self.xti += 1
            self.dma("sp", xt[:], src_rows[t * 128:(t + 1) * 128, :], writes=[xk])
            for half in range(2):
                b, bk = self.bank()
                for cc in range(4):
                    c = half * 4 + cc
                    self.tr(b[:, cc * 128:(cc + 1) * 128], xt[:, c * 128:(c + 1) * 128], [xk], [bk])
                o = dst[:, half * 4:half * 4 + 4, col0 + t * 128:col0 + (t + 1) * 128]
                self.evac(o, b[:, :].rearrange("p (c n) -> p c n", n=128), [bk], [dst_key])

    def norm_mod(self, xT, xkey, col0, N, A, Bf, hT, hkey, hcol0, tmp, part=None, state=None):
        sqbs, rss, tf = tmp
        if part in (None, 1):
            self.nmi += 1
            sqb, sqk = sqbs[self.nmi % len(sqbs)], "sqb%d" % (self.nmi % len(sqbs))
            rs, rk = rss[self.nmi % len(rss)], "rsn%d" % (self.nmi % len(rss))
            for c in range(8):
                xi = xT[:, c, col0:col0 + N]
                if c % 4 == 3:
                    self.tt("dve", sqb[:, c, 0:N], xi, xi, ALU.mult, [xkey], [sqk])
                else:
                    self.act(sqb[:, c, 0:N], xi, AF.Square, [xkey], [sqk])
            b, bk = self.bank()
            for c in range(8):
                self.mm(b[:, 0:N], self.ones1024, sqb[:, c, 0:N], c == 0, c == 7, [sqk, "c_bf"], [bk])
            self.act(rs[:, 0:N], b[:, 0:N], AF.Ln, [bk], [rk], bias=EPS)
            self.act(rs[:, 0:N], rs[:, 0:N], AF.Exp, [rk], [rk], scale=-0.5)
            state = (rs, rk)
            if part == 1:
                return state
        rs, rk = state
        for c in range(8):
            self.tfi += 1
            t_ = tf[self.tfi % len(tf)]
            tk = "tf%d" % (self.tfi % len(tf))
            self.stt("dve", t_[:, 0:N], xT[:, c, col0:col0 + N], A(c), rs[:, 0:N], ALU.mult, ALU.mult,
                     [xkey, rk, "A1", "A2"], [tk])
            if part == 2 and c % 3 == 2:
                self.ts("dve", hT[:, c, hcol0:hcol0 + N], t_[:, 0:N], Bf(c), None, ALU.add, None, [tk, "modT"], [hkey])
            else:
                self.act(hT[:, c, hcol0:hcol0 + N], t_[:, 0:N], AF.Identity, [tk, "modT"], [hkey], bias=Bf(c))
        return None

    def norm_pipeline(self, calls):
        st = [None] * len(calls)
        if calls:
            st[0] = self.norm_mod(part=1, **calls[0])
        for i in range(len(calls)):
            if i + 1 < len(calls):
                st[i + 1] = self.norm_mod(part=1, **calls[i + 1])
            self.norm_mod(part=2, state=st[i], **calls[i])

    def qk_norm(self, praw, pk, N, gcol, out, okey, tmp, rope=None):
        sq1s, rss, qns, r1s = tmp
        self.qni += 1
        i = self.qni
        sq1, sk = sq1s[i % len(sq1s)], "sq1_%d" % (i % len(sq1s))
        rs, rk = rss[i % len(rss)], "rsq%d" % (i % len(rss))
        self.act(sq1[:, 0:N], praw[:, 0:N], AF.Square, [pk], [sk])
        b, bk = self.bank()
        self.mm(b[:, 0:N], self.blk64, sq1[:, 0:N], True, True, [sk, "c_bf"], [bk])
        self.act(rs[:, 0:N], b[:, 0:N], AF.Ln, [bk], [rk], bias=EPS)
        self.act(rs[:, 0:N], rs[:, 0:N], AF.Exp, [rk], [rk], scale=-0.5)
        if rope is None:
            self.stt("dve", out, praw[:, 0:N], gcol, rs[:, 0:N], ALU.mult, ALU.mult, [pk, rk, "gq"], [okey])
            return
        qn, qk_ = qns[i % len(qns)], "qn%d" % (i % len(qns))
        r1, r1k = r1s[i % len(r1s)], "r1_%d" % (i % len(r1s))
        cosT, sinT, rpk = rope
        self.stt("dve", qn[:, 0:N], praw[:, 0:N], gcol, rs[:, 0:N], ALU.mult, ALU.mult, [pk, rk, "gq"], [qk_])
        b2, bk2 = self.bank()
        self.mm(b2[:, 0:N], self.pswap[:], qn[:, 0:N], True, True, [qk_, "c_bf"], [bk2])
        self.tt("pool", r1[:, 0:N], qn[:, 0:N], cosT, ALU.mult, [qk_] + rpk, [r1k])
        self.tt("dve", rs[:, 0:N], b2[:, 0:N], sinT, ALU.mult, [bk2] + rpk, [rk])
        self.tt("dve", out, rs[:, 0:N], r1[:, 0:N], ALU.add, [rk, r1k], [okey])

    def qk_pipeline(self, jobs, tmp):
        sq1s, rss, qns, r1s = tmp
        st = []
        for j, jb in enumerate(jobs):
            d = dict(jb)
            d["sq1"], d["sk"] = sq1s[j % len(sq1s)], "sq1_%d" % (j % len(sq1s))
            d["rs"], d["rk"] = rss[j % len(rss)], "rsq%d" % (j % len(rss))
            if jb["rope"] is not None:
                d["qn"], d["qk"] = qns[j % len(qns)], "qn%d" % (j % len(qns))
                d["r1"], d["r1k"] = r1s[j % len(r1s)], "r1_%d" % (j % len(r1s))
            st.append(d)

        def sa(d):
            d["b"], d["bk"] = d["proj"]()

        def sb(d):
            N = d["N"]
            praw, pk, sq1, sk, rs, rk = d["b"], d["bk"], d["sq1"], d["sk"], d["rs"], d["rk"]
            self.act(sq1[:, 0:N], praw[:, 0:N], AF.Square, [pk], [sk])
            b, bk = self.bank()
            self.mm(b[:, 0:N], self.blk64, sq1[:, 0:N], True, True, [sk, "c_bf"], [bk])
            self.act(rs[:, 0:N], b[:, 0:N], AF.Ln, [bk], [rk], bias=EPS)
            self.act(rs[:, 0:N], rs[:, 0:N], AF.Exp, [rk], [rk], scale=-0.5)
            if d["rope"] is None:
                self.stt("dve", d["out"], praw[:, 0:N], d["gcol"], rs[:, 0:N], ALU.mult, ALU.mult, [pk, rk, "gq"], [d["okey"]])
            else:
                self.stt("dve", d["qn"][:, 0:N], praw[:, 0:N], d["gcol"], rs[:, 0:N], ALU.mult, ALU.mult, [pk, rk, "gq"], [d["qk"]])

        def sc(d):
            if d["rope"] is None:
                return
            N = d["N"]
            cosT, sinT, rpk = d["rope"]
            qn, qk_, r1, r1k, rs, rk = d["qn"], d["qk"], d["r1"], d["r1k"], d["rs"], d["rk"]
            b2, bk2 = self.bank()
            self.mm(b2[:, 0:N], self.pswap[:], qn[:, 0:N], True, True, [qk_, "c_bf"], [bk2])
            self.tt("pool", r1[:, 0:N], qn[:, 0:N], cosT, ALU.mult, [qk_] + rpk, [r1k])
            self.tt("dve", rs[:, 0:N], b2[:, 0:N], sinT, ALU.mult, [bk2] + rpk, [rk])
            self.tt("dve", d["out"], rs[:, 0:N], r1[:, 0:N], ALU.add, [rk, r1k], [d["okey"]])

        n = len(st)
        for t in range(n + 2):
            if t < n:
                sa(st[t])
            if 0 <= t - 1 < n:
                sb(st[t - 1])
            if 0 <= t - 2 < n:
                sc(st[t - 2])

    def load_w(self, dst, dkey, src):
        self.dma("pool", dst, src, writes=[dkey])

    def layer0_A(self):
        self.phase()
        ta = self.talloc
        self.rot = [0, 1, 2, 3, 4, 5, 6]
        self.QT = ta("QT", [128, 4, T_A], BF16)
        self.KT = ta("KT", [128, T_A], BF16)
        self.V = ta("V", [128, T_A // 128, 128], BF16)
        self.QcT = ta("QcT", [128, 4, NCTX], BF16)
        self.KcT = ta("KcT", [128, NCTX], BF16)
        self.Vc = ta("Vc", [128, 2, 128], BF16)
        fo_off = self.cur
        self.FO = ta("FO", [128, 4, T_X], BF16)
        self.FOc = ta("FOc", [128, 4, NCTX], BF16)
        self.keepC = self.cur
        self.F_all = ta("F_all", [128, 32, 512], BF16)
        self.Fc = ta("Fc", [128, 2, 512], BF16)
        self.keepA = self.cur
        win = self.win
        xtoks = [(ta("xtok%d" % i, [128, 1024], F32), "xtok%d" % i) for i in range(3)]
        self.xti = 0
        self.uid += 1
        xTb2 = self.nc.alloc_sbuf_tensor_at("xTb2_%d" % self.uid, [128, 8, 512], F32, offset=fo_off)
        xTbs = [(ta("xTb", [128, 8, 512], F32), "xTb0"), (xTb2, "xTb1")]
        hTs = [(ta("hT%d" % i, [128, 8, 512], BF16), "hT%d" % i) for i in range(2)]
        sqbs = [ta("sqb", [128, 8, 512], BF16)]
        rss = [ta("rsn%d" % i, [128, 512], F32) for i in range(2)]
        tf = [ta("tf%d" % i, [128, 512], F32) for i in range(2)]
        sq1s = [ta("sq1_%d" % i, [128, 512], BF16) for i in range(2)]
        rsq = [ta("rsq%d" % i, [128, 512], F32) for i in range(2)]
        self.uid += 1
        qns = [self.nc.alloc_sbuf_tensor_at("qn0_%d" % self.uid, [128, 512], F32, offset=fo_off + 16384), ta("qn1", [128, 512], F32)]
        r1s = [self.nc.alloc_sbuf_tensor_at("r10_%d" % self.uid, [128, 512], F32, offset=fo_off + 16384 + 2048)]
        ropeb = [ta("rope%d" % i, [128, 2, 512], F32) for i in range(1)]
        tmpn = (sqbs, rss, tf)
        tmpq = (sq1s, rsq, qns, r1s)
        assert self.cur <= self.hi - 20480, "phase A overflow into win"
        l = 0

        def projA(hT, hk, N, qkv, s, tok0, tile0, ropek):
            for t in range(N // 128):
                b, bk = self.bank()
                for c in range(8):
                    self.mm(b[:, 0:512], hT[:, c, t * 128:(t + 1) * 128], win[:, c, 0:512], c == 0, c == 7, [hk, "win"], [bk])
                dst = self.F_all[:, tile0 + t, :] if s == 0 else self.Fc[:, t, :]
                self.evac(dst, b[:, 0:512], [bk], ["F_all" if s == 0 else "Fc"])
            if not qkv:
                return
            for t in range(N // 128):
                b, bk = self.bank()
                for c in range(8):
                    self.mm(b[:, 0:128], hT[:, c, t * 128:(t + 1) * 128], win[:, c, 1152:1280], c == 0, c == 7, [hk, "win"], [bk])
                dst = self.V[:, tile0 + t, :] if s == 0 else self.Vc[:, t, :]
                self.evac(dst, b[:, 0:128], [bk], ["V" if s == 0 else "Vc"])

        def projB(hT, hk, N, qkv, s, tok0, tile0, ropek):
            if not qkv:
                return
            jobs = []
            for j in range(5):
                def proj(j=j):
                    b, bk = self.bank()
                    for c in range(8):
                        self.mm(b[:, 0:N], win[:, c, 512 + j * 128:512 + (j + 1) * 128], hT[:, c, 0:N], c == 0, c == 7, [hk, "win"], [bk])
                    return b, bk
                if s == 0:
                    out = self.QT[:, j, tok0:tok0 + N] if j < 4 else self.KT[:, tok0:tok0 + N]
                    okey = "QT" if j < 4 else "KT"
                    rb_ = ropeb[0]
                    rope = (rb_[:, 0, 0:N], rb_[:, 1, 0:N], ["rope0c", "rope0s"])
                else:
                    out = self.QcT[:, j, 0:N] if j < 4 else self.KcT[:, 0:N]
                    okey = "QcT" if j < 4 else "KcT"
                    rope = None
                jobs.append(dict(proj=proj, N=N, gcol=self.gq[:, 0:1] if j < 4 else self.gq[:, 1:2], out=out, okey=okey, rope=rope))
            self.qk_pipeline(jobs, tmpq)

        items = [("ctx", 0)] + [("lat", g) for g in range(8)]

        def stage1(idx):
            kind, g = items[idx]
            hT, hk = hTs[idx % 2]
            if kind == "ctx":
                self.load_xT(self.ctx_d.ap(), 2, self.YT, "YT", 0, xtoks)
                return (hT, hk, NCTX, True, 1, 0, 0, 0, self.YT, "YT")
            tok0 = g * 512
            xTb, xk = xTbs[idx % 2]
            self.load_xT(self.x_d.ap()[tok0:tok0 + 512, :], 4, xTb, xk, 0, xtoks)
            return (hT, hk, 512, g < 5, 0, tok0, g * 4, g, xTb, xk)

        def stage2(st, part, state=None):
            hT, hk, N, qkv, s_, tok0, tile0, rk, xT, xk = st
            return self.norm_mod(xT, xk, 0, N, lambda c: self.A1[:, l, s_, c:c + 1], lambda c: self.mod(l, 0, s_, c),
                                 hT, hk, 0, tmpn, part=part, state=state)

        cur = stage1(0)
        stage2(cur, 2, stage2(cur, 1))
        for idx in range(len(items)):
            nxt = stage1(idx + 1) if idx + 1 < len(items) else None
            nst = stage2(nxt, 1) if nxt is not None else None
            projA(*cur[:8])
            self.ada_tick(1)
            if nxt is not None:
                stage2(nxt, 2, nst)
            projB(*cur[:8])
            if nxt is not None and nxt[3] and nxt[4] == 0:
                rb_ = ropeb[0]
                tk0 = nxt[5]
                self.dma("sp", rb_[:, 0, :], self.crope_d.ap()[0, :, tk0:tk0 + 512], writes=["rope0c"])
                self.dma("sp", rb_[:, 1, :], self.crope_d.ap()[1, :, tk0:tk0 + 512], writes=["rope0s"])
            self.ada_tick(1)
            cur = nxt
        if self.stop_after == "A":
            for nm_, t_, k_ in [("F_all", self.F_all, "F_all"), ("QT", self.QT, "QT"), ("KT", self.KT, "KT"), ("V", self.V, "V"),
                                ("Fc", self.Fc, "Fc"), ("QcT", self.QcT, "QcT"), ("KcT", self.KcT, "KcT"), ("Vc", self.Vc, "Vc"),
                                ("modT", self.modT, "modT")]:
                self.dump(nm_, t_, k_)

    def dft(self, Fsrc, fkey, ntile, tabs, K, FO, fokey, tabbuf, ABT):
        ti = 0
        for (k0, N) in groups(K):
            for cs in range(2):
                tb, tk = tabbuf[ti % 2], "tab%d" % (ti % 2)
                ti += 1
                src_t = tabs(cs, k0, N)
                for a0 in range(0, ntile, 8):
                    self.dma("sp", tb[:, a0:a0 + 8, 0:N], src_t[:, a0:a0 + 8, :], writes=[tk])
                for g in range(4):
                    b, bk = self.bank()
                    for a in range(ntile):
                        self.mm(b[:, 0:N], Fsrc[:, a, g * 128:(g + 1) * 128], tb[:, a, 0:N], a == 0, a == ntile - 1, [fkey, tk], [bk])
                    self.evac(ABT[:, cs, g, 0:N], b[:, 0:N], [bk], ["ABT"])
                    self.ada_tick(1)
            for g in range(4):
                b, bk = self.bank()
                self.mm(b[:, 0:N], self.ccos, ABT[:, 0, g, 0:N], True, False, ["ABT", "c_bf"], [bk])
                self.mm(b[:, 0:N], self.cnsin, ABT[:, 1, g, 0:N], False, True, ["ABT", "c_bf"], [bk])
                self.evac(FO[:, g, k0:k0 + N], b[:, 0:N], [bk], [fokey])

    def layer0_B(self):
        scr = self.scr
        self.P.barrier(lambda e: e.memset(scr[0:1, 0:8], 0.0))
        self.cur = self.keepA
        ta = self.talloc
        tabbuf = [ta("tab%d" % i, [128, 32, 512], BF16) for i in range(2)]
        ABT = ta("ABT", [128, 2, 4, 512], BF16)
        for cs in range(2):
            for g in range(4):
                b, bk = self.bank()
                for a in range(2):
                    self.mm(b[:, 0:NCTX], self.Fc[:, a, g * 128:(g + 1) * 128], self.cctx[:, cs, a, :], a == 0, a == 1, ["Fc", "c_bf"], [bk])
                self.evac(ABT[:, cs, g, 0:NCTX], b[:, 0:NCTX], [bk], ["ABT"])
        for g in range(4):
            b, bk = self.bank()
            self.mm(b[:, 0:NCTX], self.ccos, ABT[:, 0, g, 0:NCTX], True, False, ["ABT", "c_bf"], [bk])
            self.mm(b[:, 0:NCTX], self.cnsin, ABT[:, 1, g, 0:NCTX], False, True, ["ABT", "c_bf"], [bk])
            self.evac(self.FOc[:, g, :], b[:, 0:NCTX], [bk], ["FOc"])
        tabs = lambda cs, k0, N: self.cdft_d[cs].ap().rearrange("(a p) k -> p a k", p=128)[:, :, k0:k0 + N]
        self.dft(self.F_all, "F_all", 32, tabs, T_X, self.FO, "FO", tabbuf, ABT)
        self.ada_tick(100)
        self.rot = [2, 3, 4, 5, 6, 7]
        self.pers -= 2 * 4096
        if self.stop_after == "B":
            self.dump("FO", self.FO, "FO")
            self.dump("FOc", self.FOc, "FOc")

    def layer0_C(self):
        scr = self.scr
        self.P.barrier(lambda e: e.memset(scr[0:1, 0:8], 0.0))
        self.cur = self.keepC
        ta = self.talloc
        l = 0
        wout = ta("wout", [128, 8, D], BF16)
        self.load_w(wout[:], "wout", self.evout_d.ap().rearrange("(c p) n -> p c n", p=128))
        ATTs = [(ta("ATT%d" % i, [128, 4, 512], BF16), "ATT%d" % i) for i in range(2)]
        qlos = [(ta("qlo%d" % i, [128, 4, 128], BF16), "qlo%d" % i) for i in range(2)]
        qhis = [(ta("qhi%d" % i, [128, 4, 128], BF16), "qhi%d" % i) for i in range(2)]
        PT = [ta("PT%d" % i, [128, 512], BF16) for i in range(4)]
        dtots = [(ta("dtot%d" % i, [128, 512], F32), "dtot%d" % i) for i in range(2)]
        xtoks = [(ta("xtokC%d" % i, [128, 1024], F32), "xtokC%d" % i) for i in range(3)]
        self.xti = 0
        self.XT = self.nc.alloc_sbuf_tensor_at("XT_res", [128, 8, T_X], F32, offset=self.hi - 8 * T_X * 4)
        assert self.cur <= self.hi - 8 * T_X * 4, "phase C overflow"
        for (q_, k_) in qlos + qhis:
            self.P.add("pool", lambda e, q_=q_: e.memset(q_[:], 0.0), writes=[k_])
        self.rot = [4, 5, 6, 7]
        accs = [((self.pb[0], "pb0"), (self.pb[1], "pb1")), ((self.pb[2], "pb2"), (self.pb[3], "pb3"))]
        ctxtiles = [(self.KcT[:, t * 128:(t + 1) * 128], self.Vc[:, t, :], None, ["KcT", "Vc"]) for t in range(2)]

        jobs = []

        def outproj(N, FOsrc, fokey, focol, Xdst, xkey, xcol, s_, ATT, ak):
            for oc in range(8):
                b, bk = self.bank()
                for kc in range(8):
                    rhs = FOsrc[:, kc, focol:focol + N] if kc < 4 else ATT[:, kc - 4, 0:N]
                    self.mm(b[:, 0:N], wout[:, kc, oc * 128:(oc + 1) * 128], rhs, kc == 0, kc == 7, ["wout", fokey, ak], [bk])
                xo = Xdst[:, oc, xcol:xcol + N]
                self.stt("dve", xo, b[:, 0:N], self.mod(l, 2, s_, oc), xo, ALU.mult, ALU.add, [bk, "modT", xkey], [xkey])

        gi = 0
        for qt in range(2):
            post = (lambda ai=gi % 2: outproj(NCTX, self.FOc, "FOc", 0, self.YT, "YT", 0, 1, *ATTs[ai])) if qt == 1 else None
            jobs.append((self.QcT, "QcT", qt * 128, ctxtiles, gi % 2, qt * 128, None, post))
        gi += 1
        for (t0, N) in groups(T_X):
            for tq in range(N // 128):
                qt = t0 // 128 + tq
                kts = []
                for dlt, mask in ((-1, self.mprev), (0, None), (1, self.mnext)):
                    kt = qt + dlt
                    if kt < 0:
                        continue
                    kts.append((self.KT[:, kt * 128:(kt + 1) * 128], self.V[:, kt, :], mask, ["KT", "V"]))
                pre = (lambda t0=t0, N=N: self.load_xT(self.x_d.ap()[t0:t0 + N, :], N // 128, self.XT, "XT%d" % (t0 // 512), t0, xtoks)) if tq == 0 else None
                post = (lambda t0=t0, N=N, ai=gi % 2: outproj(N, self.FO, "FO", t0, self.XT, "XT%d" % (t0 // 512), t0, 0, *ATTs[ai])) \
                    if tq == N // 128 - 1 else None
                jobs.append((self.QT, "QT", qt * 128, kts + ctxtiles, gi % 2, tq * 128, pre, post))
            gi += 1

        steps = []
        for ji, job in enumerate(jobs):
            n = len(job[3])
            for kvh in range(2):
                for i in range(n):
                    steps.append((ji, kvh, i, n))
        pti = [0]

        def s_stage(st):
            ji, kvh, i, n = st
            QTsrc, qkey, qcol, keytiles, ai, dcol, pre, post = jobs[ji]
            (qlo, qlk), (qhi, qhk) = qlos[ji % 2], qhis[ji % 2]
            (pbO, ok_), (pbD, dk_) = accs[ji % 2]
            if kvh == 0 and i == 0:
                if pre is not None:
                    pre()
                self.copy("pool", qlo[0:64, :, :], QTsrc[0:64, :, qcol:qcol + 128], [qkey], [qlk])
                self.copy("pool", qhi[64:128, :, :], QTsrc[64:128, :, qcol:qcol + 128], [qkey], [qhk])
            qp, qk_ = (qlo, qlk) if kvh == 0 else (qhi, qhk)
            kap, vap, mask, kkeys = keytiles[i]
            b, bk = self.bank()
            self.mm(b[:, 0:512], kap, qp[:].rearrange("p j n -> p (j n)"), True, mask is None, [qk_] + kkeys, [bk])
            if mask is not None:
                self.mm(b[:, 0:512], self.identb, mask.unsqueeze(1).broadcast_to([128, 4, 128]), False, True, ["c_bf"], [bk])
            pt = PT[pti[0] % 4]
            pk = "PT%d" % (pti[0] % 4)
            pti[0] += 1
            self.act(pt[:], b[:, 0:512], AF.Exp, [bk], [pk])
            return (pt, pk)

        def p_stage(st, pt, pk):
            ji, kvh, i, n = st
            QTsrc, qkey, qcol, keytiles, ai, dcol, pre, post = jobs[ji]
            (pbO, ok_), (pbD, dk_) = accs[ji % 2]
            kap, vap, mask, kkeys = keytiles[i]
            rows = slice(kvh * 64, kvh * 64 + 64)
            self.mm(pbO[rows, 0:512], vap[:, kvh * 64:kvh * 64 + 64], pt[:], i == 0, i == n - 1, [pk] + kkeys, [ok_])
            self.mm(pbD[rows, 0:512], self.onesb[:, 0:64], pt[:], i == 0, i == n - 1, [pk, "c_bf"], [dk_])
            if kvh == 1 and i == n - 1:
                dtot, dtk = dtots[ji % 2]
                ATT, ak = ATTs[ai]
                self.tt("dve", dtot[:], pbD[:, 0:512], self.sinkbc[:].rearrange("p j n -> p (j n)"), ALU.add, [dk_, "sinkbc"], [dtk])
                self.act(dtot[:], dtot[:], AF.Ln, [dtk], [dtk])
                self.act(dtot[:], dtot[:], AF.Exp, [dtk], [dtk], scale=-1.0)
                self.tt("dve", ATT[:, :, dcol:dcol + 128], pbO[:, 0:512].rearrange("p (j n) -> p j n", n=128),
                        dtot[:].rearrange("p (j n) -> p j n", n=128), ALU.mult, [ok_, dtk], [ak])
                if post is not None:
                    post()

        prev = None
        for st in steps:
            cur = s_stage(st)
            if prev is not None:
                p_stage(*prev)
            prev = (st, cur[0], cur[1])
        p_stage(*prev)
        self.rot = [2, 3, 4, 5, 6, 7]

    def ffn(self, l):
        scr = self.scr
        self.P.barrier(lambda e: e.memset(scr[0:1, 0:8], 0.0))
        self.cur = self.pers
        ta = self.talloc
        T = T_X if l == 0 else T_OWN
        toks = [(t0, N, 0) for (t0, N) in groups(T)]
        HT = T + (NCTX if l == 0 else 0)
        h2 = ta("h2", [128, 8, HT], BF16)
        keepF = self.cur
        sqbs = [ta("sqb%d" % i, [128, 8, 512], BF16) for i in range(2)]
        rss = [ta("rsn%d" % i, [128, 512], F32) for i in range(2)]
        tf = [ta("tf%d" % i, [128, 512], F32) for i in range(4)]
        tmpn = (sqbs, rss, tf)
        xkey = lambda t0: "XT%d" % (t0 // 512)
        calls = [dict(xT=self.XT, xkey=xkey(t0), col0=t0, N=N, A=(lambda c: self.A2[:, l, 0, c:c + 1]),
                      Bf=(lambda c: self.mod(l, 3, 0, c)), hT=h2, hkey="h2_%d" % (t0 // 512), hcol0=t0, tmp=tmpn) for (t0, N, _) in toks]
        if l == 0:
            calls.append(dict(xT=self.YT, xkey="YT", col0=0, N=NCTX, A=(lambda c: self.A2[:, l, 1, c:c + 1]),
                              Bf=(lambda c: self.mod(l, 3, 1, c)), hT=h2, hkey="h2_c", hcol0=T, tmp=tmpn))
            toks = toks + [(T, NCTX, 1)]
        self.norm_pipeline(calls)
        self.P.barrier(lambda e: e.memset(scr[0:1, 0:8], 0.0))
        self.cur = keepF
        splits = [(0, 4), (4, 4), (8, 4), (12, 4), (16, 3), (19, 3)]
        wgb = [ta("wg%d" % i, [128, 8, 512], BF16) for i in range(2)]
        wub = [ta("wu%d" % i, [128, 8, 512], BF16) for i in range(2)]
        wdb = [ta("wd%d" % i, [128, 4, D], BF16) for i in range(2)]
        actb = [ta("act%d" % i, [128, 4, 512], BF16) for i in range(2)]
        sg = [ta("sg%d" % i, [128, 512], F32) for i in range(3)]
        obs = [ta("ob%d" % i, [128, D], F32) for i in range(2)] if l == 1 else None
        assert self.cur <= self.hi - 8 * T_X * 4, "ffn overflow"
        ai = 0
        si = 0
        for sp_i, (j0, nj) in enumerate(splits):
            wi = sp_i % 2
            wg, wu, wd = wgb[wi], wub[wi], wdb[wi]
            kg, ku, kd = "wg%d" % wi, "wu%d" % wi, "wd%d" % wi
            self.load_w(wg[:, :, 0:nj * 128], kg, self.wg_d.ap()[l].rearrange("(c p) n -> p c n", p=128)[:, :, j0 * 128:(j0 + nj) * 128])
            self.load_w(wu[:, :, 0:nj * 128], ku, self.wu_d.ap()[l].rearrange("(c p) n -> p c n", p=128)[:, :, j0 * 128:(j0 + nj) * 128])
            self.load_w(wd[:, 0:nj, :], kd, self.wd_d.ap()[l].rearrange("(j p) n -> p j n", p=128)[:, j0:j0 + nj, :])
            for (t0, N, s) in toks:
                ab = actb[ai % 2]
                ak = "act%d" % (ai % 2)
                ai += 1
                hk = "h2_c" if s == 1 else "h2_%d" % (t0 // 512)
                for jj in range(nj):
                    bg, bgk = self.bank()
                    for c in range(8):
                        self.mm(bg[:, 0:N], wg[:, c, jj * 128:(jj + 1) * 128], h2[:, c, t0:t0 + N], c == 0, c == 7, [kg, hk], [bgk])
                    bu, buk = self.bank()
                    for c in range(8):
                        self.mm(bu[:, 0:N], wu[:, c, jj * 128:(jj + 1) * 128], h2[:, c, t0:t0 + N], c == 0, c == 7, [ku, hk], [buk])
                    s_ = sg[si % 3]
                    sk = "sg%d" % (si % 3)
                    si += 1
                    self.act(s_[:, 0:N], bg[:, 0:N], AF.Silu, [bgk], [sk])
                    self.tt("dve", ab[:, jj, 0:N], bu[:, 0:N], s_[:, 0:N], ALU.mult, [buk, sk], [ak])
                X, xk, xc = (self.XT, xkey(t0), t0) if s == 0 else (self.YT, "YT", 0)
                for oc in range(8):
                    b, bk = self.bank()
                    for jj in range(nj):
                        self.mm(b[:, 0:N], wd[:, jj, oc * 128:(oc + 1) * 128], ab[:, jj, 0:N], jj == 0, jj == nj - 1, [kd, ak], [bk])
                    xo = X[:, oc, xc:xc + N]
                    self.stt("dve", xo, b[:, 0:N], self.mod(l, 5, s, oc), xo, ALU.mult, ALU.add, [bk, "modT", xk], [xk])
                if l == 1 and sp_i == len(splits) - 1:
                    self.emit_out(range(t0 // 128, (t0 + N) // 128), obs)
        if l == 1:
            self.out_done = True

    def layer1_E(self):
        scr = self.scr
        self.P.barrier(lambda e: e.memset(scr[0:1, 0:8], 0.0))
        self.cur = self.pers
        ta = self.talloc
        l = 1
        NK = T_X + NCTX
        hT = ta("hT1", [128, 8, NK], BF16)
        keepE = self.cur
        sqbs = [ta("sqb%d" % i, [128, 8, 512], BF16) for i in range(2)]
        rss = [ta("rsn%d" % i, [128, 512], F32) for i in range(2)]
        tf = [ta("tf%d" % i, [128, 512], F32) for i in range(4)]
        tmpn = (sqbs, rss, tf)
        calls = [dict(xT=self.XT, xkey="XT%d" % (t0 // 512), col0=t0, N=N, A=(lambda c: self.A1[:, l, 0, c:c + 1]),
                      Bf=(lambda c: self.mod(l, 0, 0, c)), hT=hT, hkey="hT1_%d" % (t0 // 512), hcol0=t0, tmp=tmpn) for (t0, N) in groups(T_X)]
        calls.append(dict(xT=self.YT, xkey="YT", col0=0, N=NCTX, A=(lambda c: self.A1[:, l, 1, c:c + 1]),
                          Bf=(lambda c: self.mod(l, 0, 1, c)), hT=hT, hkey="hT1_%d" % (T_X // 512), hcol0=T_X, tmp=tmpn))
        self.norm_pipeline(calls)
        self.P.barrier(lambda e: e.memset(scr[0:1, 0:8], 0.0))
        self.cur = keepE
        rsq = [ta("rsq%d" % i, [128, 512], F32) for i in range(2)]
        sq1s = [ta("sq1_%d" % i, [128, 512], BF16) for i in range(1)]
        tmpq = (sq1s, rsq, None, None)
        namask = ta("namask", [128, 3, 6, 128], BF16)
        self.dma("sp", namask[:], self.cnam_d.ap(), writes=["namask"])
        wq = ta("wq", [128, 8, 256], BF16)
        wk = ta("wk", [128, 8, 256], BF16)
        wv = ta("wv", [128, 8, 256], BF16)
        wo = ta("wo", [128, 2, D], BF16)
        Tt = ta("Tt", [128, 6, 4, 128], BF16)
        BMI = ta("BMI", [128, 5, 4, 128], BF16)
        bmtmp = rsq[0]
        KTh = ta("KTh", [128, 2, NK], BF16)
        Vh = ta("Vh", [128, NK // 128, 256], BF16)
        QTh = ta("QTh", [128, 2, T_OWN], BF16)
        ATTs = [(ta("ATT1_%d" % i, [128, 2, 512], BF16), "ATT1_%d" % i) for i in range(2)]
        qpads = [[(ta("qpad%d_%d" % (e, i), [128, 2, 128], BF16), "qpad%d_%d" % (e, i)) for e in range(2)] for i in range(2)]
        PT = [ta("PT1_%d" % i, [128, 512], BF16) for i in range(5)]
        dtots = [(ta("dtot1_%d" % i, [128, 512], F32), "dtot1_%d" % i) for i in range(1)]
        assert self.cur <= self.hi - 8 * T_X * 4, "layer1 overflow %d" % (self.cur - (self.hi - 8 * T_X * 4))
        for i in range(2):
            for e_ in range(2):
                q_, k_ = qpads[i][e_]
                self.P.add("pool", lambda e, q_=q_: e.memset(q_[:], 0.0), writes=[k_])
        self.rot = [4, 5, 6, 7]
        accs = [((self.pb[0], "pb0"), (self.pb[1], "pb1")), ((self.pb[2], "pb2"), (self.pb[3], "pb3"))]
        z256 = self.cbf[:, 4:6, :].rearrange("p a n -> p (a n)")
        z512 = self.cbf[:, 4:8, :].rearrange("p a n -> p (a n)")
        hkey = lambda t0: "hT1_%d" % (t0 // 512)
        pti = [0]
        tglob = [0]
        src = self.odin_d.ap().rearrange("(c p) n -> p c n", p=128)

        def load_qkv(hg_):
            self.load_w(wq[:], "wq", src[:, :, hg_ * 256:(hg_ + 1) * 256])
            self.load_w(wk[:], "wk", src[:, :, D + hg_ * 256:D + (hg_ + 1) * 256])
            self.load_w(wv[:], "wv", src[:, :, 2 * D + hg_ * 256:2 * D + (hg_ + 1) * 256])

        load_qkv(0)
        for hg in range(4):
            self.load_w(wo[:], "wo", self.odout_d.ap().rearrange("(c p) n -> p c n", p=128)[:, hg * 2:hg * 2 + 2, :])
            for dl in range(6):
                for kr in range(2):
                    for qr in range(2):
                        dr_idx = 2 * (dl - 2) + kr - qr + 7
                        srcb = AP(self.rb_d, hg * 4 * 465 + dr_idx * 31 - 48, [[1, 64], [465, 4], [1, 64]])
                        self.dma("pool", Tt[qr * 64:(qr + 1) * 64, dl, :, kr * 64:(kr + 1) * 64], srcb, writes=["Tt%d" % dl])
            for dl in range(5):
                b, bk = self.bank()
                for h4 in range(4):
                    self.mm(b[:, h4 * 128:(h4 + 1) * 128], Tt[:, dl, h4, :], self.rblk, True, True, ["Tt%d" % dl, "c_bf"], [bk])
                self.tt("dve", BMI[:, dl, :, :], b[:, 0:512].rearrange("p (h n) -> p h n", n=128),
                        namask[:, 2, dl, :].unsqueeze(1).broadcast_to([128, 4, 128]), ALU.add, [bk, "namask"], ["BMI"])
            self.rot = [0, 1, 2, 3, 4, 5, 6, 7]
            for (t0, N) in groups(NK):
                jobs = []
                for ch in range(2):
                    def projk(ch=ch, t0=t0, N=N):
                        b, bk = self.bank()
                        for c in range(8):
                            self.mm(b[:, 0:N], wk[:, c, ch * 128:(ch + 1) * 128], hT[:, c, t0:t0 + N], c == 0, c == 7, ["wk", hkey(t0)], [bk])
                        return b, bk
                    jobs.append(dict(proj=projk, N=N, gcol=self.gq[:, 3:4], out=KTh[:, ch, t0:t0 + N], okey="KTh", rope=None))
                    if t0 + N <= T_OWN:
                        def projq(ch=ch, t0=t0, N=N):
                            b, bk = self.bank()
                            for c in range(8):
                                self.mm(b[:, 0:N], wq[:, c, ch * 128:(ch + 1) * 128], hT[:, c, t0:t0 + N], c == 0, c == 7, ["wq", hkey(t0)], [bk])
                            return b, bk
                        jobs.append(dict(proj=projq, N=N, gcol=self.gq[:, 2:3], out=QTh[:, ch, t0:t0 + N], okey="QTh", rope=None))
                self.qk_pipeline(jobs, tmpq)
                for t in range(N // 128):
                    b, bk = self.bank()
                    for c in range(8):
                        self.mm(b[:, 0:256], hT[:, c, t0 + t * 128:t0 + (t + 1) * 128], wv[:, c, :], c == 0, c == 7, ["wv", hkey(t0)], [bk])
                    self.evac(Vh[:, t0 // 128 + t, :], b[:, 0:256], [bk], ["Vh"])
            self.rot = [4, 5, 6, 7]
            if hg + 1 < 4:
                load_qkv(hg + 1)
            steps = []
            for qt in range(T_OWN // 128):
                kts = [(qt + d_, d_ + 2) for d_ in range(-2, 4 if qt == 0 else 3) if qt + d_ >= 0] + \
                      [(T_X // 128, None), (T_X // 128 + 1, None)]
                for i, (kt, dl) in enumerate(kts):
                    steps.append((qt, i, len(kts), kt, dl))

            def s_stage(st):
                qt, i, n, kt, dl = st
                tq_ = tglob[0] + qt
                qp = qpads[tq_ % 2]
                (pbO, ok_), (pbD, dk_) = accs[tq_ % 2]
                if i == 0:
                    for e_ in range(2):
                        rows = slice(e_ * 64, e_ * 64 + 64)
                        self.copy("pool", qp[e_][0][rows, :, :], QTh[rows, :, qt * 128:(qt + 1) * 128], ["QTh"], [qp[e_][1]])
                    self.mm(pbO[:, 0:256], self.zerob, z256, True, False, ["c_bf"], [ok_])
                    self.mm(pbD[:, 0:512], self.zerob, z512, True, False, ["c_bf"], [dk_])
                b, bk = self.bank()
                first = True
                if dl is not None:
                    if qt >= 2:
                        self.mm(b[:, 0:512], self.identb, BMI[:, dl, :, :].rearrange("p h n -> p (h n)"), True, False, ["BMI", "c_bf"], [bk])
                        first = False
                    else:
                        self.mm(b[:, 0:512], self.identb, namask[:, qt, dl, :].unsqueeze(1).broadcast_to([128, 4, 128]), True, False,
                                ["namask", "c_bf"], [bk])
                        for h4 in range(4):
                            self.mm(b[:, h4 * 128:(h4 + 1) * 128], Tt[:, dl, h4, :], self.rblk, False, False, ["Tt%d" % dl, "c_bf"], [bk])
                        first = False
                for h4 in range(4):
                    ch, e_ = h4 // 2, h4 % 2
                    self.mm(b[:, h4 * 128:(h4 + 1) * 128], KTh[:, ch, kt * 128:(kt + 1) * 128], qp[e_][0][:, ch, :], first, True,
                            ["KTh", qp[e_][1]], [bk])
                pt = PT[pti[0] % 5]
                pk = "PT1_%d" % (pti[0] % 5)
                pti[0] += 1
                self.act(pt[:], b[:, 0:512], AF.Exp, [bk], [pk])
                return (pt, pk)

            def p_stage(st, pt, pk):
                qt, i, n, kt, dl = st
                tq_ = tglob[0] + qt
                (pbO, ok_), (pbD, dk_) = accs[tq_ % 2]
                for h4 in range(4):
                    ch, e_ = h4 // 2, h4 % 2
                    rows = slice(e_ * 64, e_ * 64 + 64)
                    self.mm(pbO[rows, ch * 128:(ch + 1) * 128], Vh[:, kt, h4 * 64:(h4 + 1) * 64], pt[:, h4 * 128:(h4 + 1) * 128],
                            False, True, [pk, "Vh"], [ok_])
                self.mm(pbD[:, 0:512], self.onesb, pt[:], False, True, [pk, "c_bf"], [dk_])
                if i == n - 1:
                    dtot, dtk = dtots[0]
                    ATT, ak = ATTs[(qt // 4) % 2]
                    tq = qt % 4
                    self.act(dtot[:], pbD[:, 0:512], AF.Ln, [dk_], [dtk])
                    self.act(dtot[:], dtot[:], AF.Exp, [dtk], [dtk], scale=-1.0)
                    dv = dtot[:].rearrange("p (c e n) -> p c e n", e=2, n=128)
                    for e_ in range(2):
                        rows = slice(e_ * 64, e_ * 64 + 64)
                        self.tt("dve", ATT[rows, :, tq * 128:(tq + 1) * 128], pbO[rows, 0:256].rearrange("p (j n) -> p j n", n=128),
                                dv[rows, :, e_, :], ALU.mult, [ok_, dtk], [ak])
                    if tq == 3:
                        t0 = (qt // 4) * 512
                        for oc in range(8):
                            b, bk = self.bank()
                            for ch in range(2):
                                self.mm(b[:, 0:512], wo[:, ch, oc * 128:(oc + 1) * 128], ATT[:, ch, 0:512], ch == 0, ch == 1, ["wo", ak], [bk])
                            xo = self.XT[:, oc, t0:t0 + 512]
                            xk = "XT%d" % (t0 // 512)
                            self.stt("dve", xo, b[:, 0:512], self.mod(l, 2, 0, oc), xo, ALU.mult, ALU.add, [bk, "modT", xk], [xk])

            if "noattn" in self.flags:
                steps = []
            if "few" in self.flags:
                steps = steps[:self.nstep]
            pend = []
            for st in steps:
                cur = s_stage(st)
                pend.append((st, cur[0], cur[1]))
                if len(pend) > 2:
                    p_stage(*pend.pop(0))
            for pp in pend:
                p_stage(*pp)
            tglob[0] += T_OWN // 128
            if "onehg" in self.flags:
                break
        self.rot = [2, 3, 4, 5, 6, 7]

    def finish(self, final=True):
        scr = self.scr
        self.P.barrier(lambda e: e.memset(scr[0:1, 0:8], 0.0))
        self.cur = self.pers
        ta = self.talloc
        if self.debug and hasattr(self, "XT"):
            self.dma("sp", self.dbgx_d.ap(), self.XT[:], reads=["XT%d" % i for i in range(5)])
            self.dma("sp", self.dbgy_d.ap(), self.YT[:], reads=["YT"])
        if not hasattr(self, "XT"):
            return
        if getattr(self, "out_done", False):
            return
        ob = [ta("ob%d" % i, [128, D], F32) for i in range(2)]
        self.emit_out(range(T_OWN // 128), ob)

    def emit_out(self, tiles, ob):
        for t in tiles:
            o_, ok = ob[t % 2], "ob%d" % (t % 2)
            for half in range(2):
                b, bk = self.bank()
                for cc in range(4):
                    c = half * 4 + cc
                    self.tr(b[:, cc * 128:(cc + 1) * 128], self.XT[:, c, t * 128:(t + 1) * 128], ["XT%d" % (t // 4)], [bk])
                self.evac(o_[:, half * 512:(half + 1) * 512], b[:, 0:512], [bk], [ok])
            self.dma("sp", self.out_d.ap()[t * 128:(t + 1) * 128, :], o_[:], reads=[ok])


_CONST_CACHE = {}


def _bf(a):
    return np.ascontiguousarray(a.astype(ml_dtypes.bfloat16))


def host_consts(par):
    if par in _CONST_CACHE:
        return _CONST_CACHE[par]
    c = {}
    c["c_identf"] = np.eye(128, dtype=np.float32)
    psw = np.zeros((128, 128), np.float32)
    for m in range(128):
        i = m % 32
        partner = m + 16 if i < 16 else m - 16
        psw[partner, m] = 1.0
    c["c_pswap"] = psw
    cb = np.zeros((128, 10, 128), np.float32)
    cb[:, 0, :] = 1.0 / 1024.0
    cb[0:64, 1, 0:64] = 1.0 / 64.0
    cb[64:128, 1, 64:128] = 1.0 / 64.0
    cb[:, 2, :] = np.eye(128)
    for blk in range(2):
        for u in range(64):
            cb[blk * 64 + u, 3, blk * 64 + 63 - u] = 1.0
    cb[:, 5, :] = 1.0
    cc = np.arange(128)
    ang = 2.0 * np.pi * ((cc[:, None] * cc[None, :]) % 128) / 128.0
    cb[:, 6, :] = np.cos(ang) / np.sqrt(128.0)
    cb[:, 7, :] = -np.sin(ang) / np.sqrt(128.0)
    j = np.arange(128)[:, None]
    q = np.arange(128)[None, :]
    cb[:, 8, :] = np.where(j >= q, 0.0, NEG)
    cb[:, 9, :] = np.where(j <= q, 0.0, NEG)
    c["c_bf"] = _bf(cb)
    n = np.arange(256)
    a2 = 2.0 * np.pi * ((n[:, None] * n[None, :]) % 256) / 256.0
    C2 = (np.cos(a2) / 16.0).reshape(2, 128, 256)
    S2 = (np.sin(a2) / 16.0).reshape(2, 128, 256)
    cx = np.stack([C2.transpose(1, 0, 2), S2.transpose(1, 0, 2)], axis=1)
    c["c_ctxdft"] = _bf(cx)
    loc = np.arange(T_ALL)
    glob = loc if par == 0 else (T_ALL - 1 - loc)
    inv = (np.float32(10000.0) ** (-np.arange(16, dtype=np.float32) / np.float32(16))).astype(np.float32)
    g_ = glob[:T_A]
    row = (g_ // 64).astype(np.float32)
    col = (g_ % 64).astype(np.float32)
    ar = (row[None, :] * inv[:, None]).astype(np.float32)
    ac = (col[None, :] * inv[:, None]).astype(np.float32)
    cos64 = np.concatenate([np.cos(ar), np.cos(ar), np.cos(ac), np.cos(ac)], axis=0)
    sin64 = np.concatenate([-np.sin(ar), np.sin(ar), -np.sin(ac), np.sin(ac)], axis=0)
    c["c_rope"] = np.ascontiguousarray(np.stack([np.concatenate([cos64, cos64], 0), np.concatenate([sin64, sin64], 0)], 0).astype(np.float32))
    gk = glob[:T_X].astype(np.int64)
    gn = glob.astype(np.int64)
    ph = (gn[:, None] * gk[None, :]) % T_ALL
    angL = (2.0 * np.pi / T_ALL) * ph
    c["c_dftc"] = _bf(np.cos(angL) / 64.0)
    c["c_dfts"] = _bf(np.sin(angL) / 64.0)
    nm = np.zeros((128, 3, 6, 128), np.float32)
    for cls, qt in enumerate((0, 1, 8)):
        for dl in range(6):
            kt = qt + dl - 2
            if kt < 0:
                nm[:, cls, dl, :] = NEG
                continue
            kg = glob[kt * 128 + np.arange(128)]
            qg = glob[qt * 128 + np.arange(128)]
            kr, kc = kg // 64, kg % 64
            qr, qc = qg // 64, qg % 64
            r0 = np.clip(qr - 4, 0, 56)
            c0 = np.clip(qc - 8, 0, 48)
            ok = (kr[:, None] >= r0[None, :]) & (kr[:, None] < r0[None, :] + 8) & (kc[:, None] >= c0[None, :]) & (kc[:, None] < c0[None, :] + 16)
            nm[:, cls, dl, :] = np.where(ok, 0.0, NEG)
    c["c_namask"] = _bf(nm)
    _CONST_CACHE[par] = c
    return c


_NC_CACHE = {}


def get_nc(stop_after=None, debug=False, only=None, flags=()):
    key = (stop_after, debug, only, tuple(flags))
    if key not in _NC_CACHE:
        bld = Builder(stop_after=stop_after, debug=debug, only=only, flags=flags)
        _NC_CACHE[key] = bld.build()
    return _NC_CACHE[key]


def make_in_maps(inputs):
    f32 = lambda a: np.ascontiguousarray(np.asarray(a, dtype=np.float32))
    x = f32(inputs["x"])
    c = f32(inputs["c"])
    ctx = f32(inputs["ctx"])
    c_ctx = f32(inputs["c_ctx"])
    ev_in = f32(inputs["ev_w_in"])[0]
    ev_out = f32(inputs["ev_w_out"])[0]
    hp = [0, 4, 1, 5, 2, 6, 3, 7]
    qcols = np.concatenate([512 + h * 64 + np.arange(64) for h in hp])
    cols = np.concatenate([np.arange(512), qcols, np.arange(1024, 1280)])
    ev_in_p = np.ascontiguousarray(ev_in[:, cols])
    rows = np.concatenate([np.arange(512), qcols])
    ev_out_p = np.ascontiguousarray(ev_out[rows, :])
    rb = f32(inputs["od_rel_bias"])[0]
    shared = {
        "ada_w": f32(inputs["ada_w"]), "ada_b": f32(inputs["ada_b"]),
        "norm1_g": f32(inputs["norm1_g"]), "norm2_g": f32(inputs["norm2_g"]),
        "ffn_w_gate": f32(inputs["ffn_w_gate"]), "ffn_w_up": f32(inputs["ffn_w_up"]), "ffn_w_down": f32(inputs["ffn_w_down"]),
        "ev_w_in": ev_in_p, "ev_w_out": ev_out_p,
        "ev_q_norm": f32(inputs["ev_q_norm"])[0], "ev_k_norm": f32(inputs["ev_k_norm"])[0], "ev_sink": f32(inputs["ev_sink"])[0],
        "od_w_in": f32(inputs["od_w_in"])[0], "od_w_out": f32(inputs["od_w_out"])[0],
        "od_q_norm": f32(inputs["od_q_norm"])[0], "od_k_norm": f32(inputs["od_k_norm"])[0],
    }
    pad = np.zeros(64, np.float32)
    rbs = [np.concatenate([rb.reshape(-1), pad]), np.concatenate([rb[:, ::-1, ::-1].reshape(-1), pad])]
    in_maps = []
    for cid in range(8):
        b, par = cid // 2, cid % 2
        m = dict(shared)
        m["x_loc"] = np.ascontiguousarray(x[b] if par == 0 else x[b][::-1])
        m["ctx_b"] = np.ascontiguousarray(ctx[b])
        m["cvec"] = np.ascontiguousarray(np.stack([c[b], c_ctx], 0))
        m["rel_bias"] = rbs[par]
        m.update(host_consts(par))
        in_maps.append(m)
    return in_maps


def assemble(results):
    out = np.zeros((4, T_ALL, D), np.float32)
    for cid in range(8):
        b, par = cid // 2, cid % 2
        o = np.asarray(results[cid]["out_loc"], dtype=np.float32)
        if par == 0:
            out[b, :T_OWN] = o
        else:
            out[b, T_OWN:] = o[::-1]
    return out


def kernel(**inputs):
    nc = get_nc()
    in_maps = make_in_maps(inputs)
    res = run_bass_kernel_spmd(nc, in_maps, core_ids=list(range(8)))
    return assemble(res.results)
```

```python
import numpy as np
import ml_dtypes
import concourse.bass as bass
import concourse.mybir as mybir
from concourse.bass_utils import run_bass_kernel_spmd
from concourse.ap import AP

F32 = mybir.dt.float32
BF16 = mybir.dt.bfloat16
ALU = mybir.AluOpType
AF = mybir.ActivationFunctionType

NDMA_SEM = 48
D = 1024
DFF = 2816
NEG = -30000.0
EPS = 1e-6
T_ALL = 4096
T_X = 2304
T_A = 2560
T_OWN = 2048
NCTX = 256


class _Op:
    __slots__ = ("id", "eng", "fn", "deps", "is_dma", "sem_i", "sem_val", "signal", "count")


class Prog:
    ENGS = ("pe", "act", "dve", "pool", "sp")

    def __init__(self):
        self.ops = []
        self.lw = {}
        self.rd = {}
        self.dma_rr = 0
        self.dma_sem_uses = [0] * NDMA_SEM
        self.dma_sem_last = [None] * NDMA_SEM
        self.last_eng = {}
        self.bar = None
        self.dopen = set()
        self.saved = {}
        self.strict = True

    def add(self, eng, fn, reads=(), writes=(), dma=False):
        op = _Op()
        op.id = len(self.ops)
        op.eng = eng
        op.fn = fn
        op.is_dma = dma
        op.signal = False
        op.count = 0
        deps = {}
        if self.bar is not None:
            deps[self.bar] = True
        for k in reads:
            for w in self.lw.get(k, ()):
                deps[w] = True
            self.dopen.discard(k)
        cow = set()
        for k in writes:
            if dma and k in self.dopen:
                cow.add(k)
                for r in self.saved.get(k, ()):
                    deps.setdefault(r, False)
                continue
            for w in self.lw.get(k, ()):
                deps.setdefault(w, False)
            for r in self.rd.get(k, ()):
                deps.setdefault(r, False)
        if dma:
            i = self.dma_rr % NDMA_SEM
            self.dma_rr += 1
            prev = self.dma_sem_last[i]
            if prev is not None:
                deps.setdefault(prev, False)
            self.dma_sem_uses[i] += 1
            op.sem_i = i
            op.sem_val = 16 * self.dma_sem_uses[i]
            self.dma_sem_last[i] = op.id
        op.deps = deps
        for k in reads:
            self.rd.setdefault(k, []).append(op.id)
        for k in writes:
            if k in cow:
                self.lw[k].append(op.id)
                continue
            self.saved[k] = list(self.lw.get(k, ())) + list(self.rd.get(k, ()))
            self.lw[k] = [op.id]
            self.rd[k] = []
            if dma:
                self.dopen.add(k)
            else:
                self.dopen.discard(k)
        self.ops.append(op)
        if not dma:
            self.last_eng[eng] = op.id
        return op

    def barrier(self, fn):
        op = self.add("pool", fn)
        for e, i in self.last_eng.items():
            if i != op.id:
                op.deps[i] = True
        for i in self.dma_sem_last:
            if i is not None:
                op.deps[i] = True
        self.bar = op.id
        self.lw = {}
        self.rd = {}
        self.dopen = set()
        self.saved = {}

    def emit(self, block, eng_sems, dma_sems):
        ops = self.ops

        strict = self.strict

        def skip(op, dop, raw):
            if op.is_dma or dop.is_dma or dop.eng != op.eng:
                return False
            return op.eng == "pe" or (not raw and not strict)

        for op in ops:
            for d, raw in op.deps.items():
                dop = ops[d]
                if dop.is_dma or skip(op, dop, raw):
                    continue
                dop.signal = True
        cnt = {e: 0 for e in self.ENGS}
        for op in ops:
            if op.signal and not op.is_dma:
                cnt[op.eng] += 1
                op.count = cnt[op.eng]
        per_eng = {e: [o for o in ops if o.eng == e] for e in self.ENGS}

        def run(e, engine):
            known = {}
            for op in per_eng[e]:
                needs = {}
                for d, raw in op.deps.items():
                    dop = ops[d]
                    if dop.is_dma:
                        key = ("d", dop.sem_i)
                        val = dop.sem_val
                    else:
                        if skip(op, dop, raw):
                            continue
                        key = ("e", dop.eng)
                        val = dop.count
                    if needs.get(key, 0) < val:
                        needs[key] = val
                for key, val in needs.items():
                    if known.get(key, 0) >= val:
                        continue
                    known[key] = val
                    sem = dma_sems[key[1]] if key[0] == "d" else eng_sems[key[1]]
                    engine.wait_ge(sem, val)
                ins = op.fn(engine)
                if op.is_dma:
                    ins.then_inc(dma_sems[op.sem_i], 16)
                elif op.signal:
                    ins.then_inc(eng_sems[e], 1)
            if e == "sp":
                for i in range(NDMA_SEM):
                    if self.dma_sem_uses[i] > 0:
                        engine.wait_ge(dma_sems[i], 16 * self.dma_sem_uses[i])

        block.tensor(lambda eng: run("pe", eng))
        block.scalar(lambda eng: run("act", eng))
        block.vector(lambda eng: run("dve", eng))
        block.gpsimd(lambda eng: run("pool", eng))
        block.sync(lambda eng: run("sp", eng))


def groups(n, g=512):
    out = []
    s = 0
    while s < n:
        out.append((s, min(g, n - s)))
        s += g
    return out


class Builder:
    def __init__(self, stop_after=None, debug=False, only=None, flags=()):
        self.only = only
        self.flags = set(f for f in flags if not f.startswith('n='))
        self.nstep = ([int(f[2:]) for f in flags if f.startswith('n=')] + [8])[0]
        self.nc = bass.Bass("TRN2", target_bir_lowering=False)
        self.P = Prog()
        self.stop_after = stop_after
        self.debug = debug
        nc = self.nc
        self.lo = ((nc.sbuf_base + 63) // 64) * 64
        self.hi = nc.sbuf_top
        self.pers = self.lo
        self.cur = None
        self.uid = 0
        self.rr = 0
        self.evr = 0
        self.rot = [2, 3, 4, 5, 6]
        self.nmi = 0
        self.tfi = 0
        self.qni = 0

    def _alloc(self, ptr, name, shape, dt):
        esz = 4 if dt == F32 else 2
        n = 1
        for s in shape[1:]:
            n *= s
        nbytes = ((n * esz + 63) // 64) * 64
        self.uid += 1
        t = self.nc.alloc_sbuf_tensor_at("%s_%d" % (name, self.uid), list(shape), dt, offset=ptr)
        return t, ptr + nbytes

    def palloc(self, name, shape, dt):
        assert self.cur is None
        t, self.pers = self._alloc(self.pers, name, shape, dt)
        assert self.pers <= self.hi, "persistent overflow"
        return t

    def talloc(self, name, shape, dt):
        t, self.cur = self._alloc(self.cur, name, shape, dt)
        assert self.cur <= self.hi, "phase overflow %s %d" % (name, self.cur - self.hi)
        return t

    def phase(self):
        scr = self.scr
        self.P.barrier(lambda e: e.memset(scr[0:1, 0:8], 0.0))
        self.cur = self.pers

    def dma(self, eng, out, in_, reads=(), writes=()):
        self.P.add(eng, lambda e: e.dma_start(out=out, in_=in_), reads=reads, writes=writes, dma=True)

    def mm(self, out, lhsT, rhs, start, stop, reads, writes):
        self.P.add("pe", lambda e: e.matmul(out, lhsT=lhsT, rhs=rhs, start=start, stop=stop), reads=reads, writes=writes)

    def tr(self, out, in_, reads, writes):
        ident = self.identf
        k = in_.shape[0]
        self.P.add("pe", lambda e: e.transpose(out, in_, ident[0:k, 0:k]), reads=list(reads) + ["c_id"], writes=writes)

    def act(self, out, in_, func, reads, writes, bias=None, scale=None):
        kw = {}
        if bias is not None:
            kw["bias"] = bias
        if scale is not None:
            kw["scale"] = scale
        self.P.add("act", lambda e: e.activation(out=out, in_=in_, func=func, **kw), reads=reads, writes=writes)

    def tt(self, eng, out, in0, in1, op, reads, writes):
        self.P.add(eng, lambda e: e.tensor_tensor(out=out, in0=in0, in1=in1, op=op), reads=reads, writes=writes)

    def stt(self, eng, out, in0, scalar, in1, op0, op1, reads, writes):
        self.P.add(eng, lambda e: e.scalar_tensor_tensor(out=out, in0=in0, scalar=scalar, in1=in1, op0=op0, op1=op1),
                   reads=reads, writes=writes)

    def ts(self, eng, out, in0, s1, s2, op0, op1, reads, writes):
        if s2 is None:
            self.P.add(eng, lambda e: e.tensor_scalar(out=out, in0=in0, scalar1=s1, scalar2=None, op0=op0), reads=reads, writes=writes)
        else:
            self.P.add(eng, lambda e: e.tensor_scalar(out=out, in0=in0, scalar1=s1, scalar2=s2, op0=op0, op1=op1), reads=reads, writes=writes)

    def copy(self, eng, out, in_, reads, writes):
        if eng == "act":
            self.act(out, in_, AF.Copy, reads, writes)
        else:
            self.P.add(eng, lambda e: e.tensor_copy(out=out, in_=in_), reads=reads, writes=writes)

    def evac(self, out, in_, reads, writes):
        self.evr += 1
        self.copy("act" if self.evr % 2 else "dve", out, in_, reads, writes)

    def dump(self, name, t, key):
        if not self.debug:
            return
        shape = list(t.shape)
        d = self.nc.dram_tensor("dbg_" + name, shape, t.dtype, kind="ExternalOutput")
        self.dma("sp", d.ap(), t[:], reads=[key])

    def bank(self):
        self.rr += 1
        i = self.rot[self.rr % len(self.rot)]
        return self.pb[i], "pb%d" % i

    def build(self):
        nc = self.nc
        P = self.P
        dr = lambda name, shape, dt, kind="ExternalInput": nc.dram_tensor(name, list(shape), dt, kind=kind)
        self.x_d = dr("x_loc", [T_ALL, D], F32)
        self.ctx_d = dr("ctx_b", [NCTX, D], F32)
        self.cvec_d = dr("cvec", [2, D], F32)
        self.ada_w_d = dr("ada_w", [2, D, 6 * D], F32)
        self.ada_b_d = dr("ada_b", [2, 6 * D], F32)
        self.n1_d = dr("norm1_g", [2, D], F32)
        self.n2_d = dr("norm2_g", [2, D], F32)
        self.wg_d = dr("ffn_w_gate", [2, D, DFF], F32)
        self.wu_d = dr("ffn_w_up", [2, D, DFF], F32)
        self.wd_d = dr("ffn_w_down", [2, DFF, D], F32)
        self.evin_d = dr("ev_w_in", [D, 1280], F32)
        self.evout_d = dr("ev_w_out", [D, D], F32)
        self.evq_d = dr("ev_q_norm", [64], F32)
        self.evk_d = dr("ev_k_norm", [64], F32)
        self.sink_d = dr("ev_sink", [8], F32)
        self.odin_d = dr("od_w_in", [D, 3 * D], F32)
        self.odout_d = dr("od_w_out", [D, D], F32)
        self.odq_d = dr("od_q_norm", [64], F32)
        self.odk_d = dr("od_k_norm", [64], F32)
        self.rb_d = dr("rel_bias", [16 * 15 * 31 + 64], F32)
        self.cidf_d = dr("c_identf", [128, 128], F32)
        self.cpsw_d = dr("c_pswap", [128, 128], F32)
        self.cbf_d = dr("c_bf", [128, 10, 128], BF16)
        self.cctx_d = dr("c_ctxdft", [128, 2, 2, 256], BF16)
        self.cnam_d = dr("c_namask", [128, 3, 6, 128], BF16)
        self.crope_d = dr("c_rope", [2, 128, T_A], F32)
        self.cdft_d = [dr("c_dftc", [T_ALL, T_X], BF16), dr("c_dfts", [T_ALL, T_X], BF16)]
        self.out_d = dr("out_loc", [T_OWN, D], F32, kind="ExternalOutput")
        if self.debug:
            self.dbgx_d = dr("dbg_x", [128, 8, T_X], F32, kind="ExternalOutput")
            self.dbgy_d = dr("dbg_y", [128, 8, NCTX], F32, kind="ExternalOutput")
            self.dbga_d = dr("dbg_a", [128, 8, T_X], F32, kind="ExternalOutput")

        self.pb = [nc.alloc_psum_tensor("pb%d" % i, [128, 512], F32) for i in range(8)]
        self.eng_sems = {e: nc.alloc_semaphore("s_" + e) for e in Prog.ENGS}
        self.dma_sems = [nc.alloc_semaphore("d%d" % i) for i in range(NDMA_SEM)]

        self.setup()
        done = False
        for name, fn in [("A", self.layer0_A), ("B", self.layer0_B), ("C", self.layer0_C), ("D", lambda: self.ffn(0)),
                         ("E", self.layer1_E), ("F", lambda: self.ffn(1))]:
            if self.only is not None and name not in self.only:
                continue
            if not hasattr(self, "XT") and name in ("D", "E", "F"):
                self.ada_tick(100)
                self.rot = [2, 3, 4, 5, 6, 7]
                self.pers -= 2 * 4096
                self.XT = self.nc.alloc_sbuf_tensor_at("XT_res", [128, 8, T_X], F32, offset=self.hi - 8 * T_X * 4)
            fn()
            if self.stop_after == name:
                done = True
                break
        self.finish(final=not done)
        with nc.Block() as block:
            P.emit(block, self.eng_sems, self.dma_sems)
        return nc

    def setup(self):
        nc = self.nc
        pa = self.palloc
        self.scr = pa("scr", [128, 16], F32)
        self.identf = pa("identf", [128, 128], F32)
        self.pswap = pa("pswap", [128, 128], F32)
        self.cbf = pa("cbf", [128, 10, 128], BF16)
        self.cctx = pa("cctx", [128, 2, 2, 256], BF16)
        self.vecT = pa("vecT", [128, 48], F32)
        self.biasT = pa("biasT", [128, 96], F32)
        self.modT = pa("modT", [128, 2, 48, 2], F32)
        self.A1 = pa("A1", [128, 2, 2, 8], F32)
        self.A2 = pa("A2", [128, 2, 2, 8], F32)
        self.sT = pa("sT", [128, 2, 8], BF16)
        self.gq = pa("gq", [128, 4], F32)
        self.sinkbc = pa("sinkbc", [128, 4, 128], F32)
        self.skt = pa("skt", [128, 4], F32)
        self.zf = pa("zf", [128, 128], F32)
        self.YT = pa("YT", [128, 8, NCTX], F32)
        self.dma("sp", self.identf[:], self.cidf_d.ap(), writes=["c_id"])
        self.dma("sp", self.pswap[:], self.cpsw_d.ap(), writes=["c_ps"])
        self.dma("sp", self.cbf[:], self.cbf_d.ap(), writes=["c_bf"])
        self.dma("sp", self.cctx[:], self.cctx_d.ap(), writes=["c_cx"])
        self.P.add("pool", lambda e: e.memset(self.zf[:], 0.0), writes=["zf"])
        self.P.add("pool", lambda e: e.memset(self.scr[:], 0.0), writes=["scr"])
        self.ones1024 = self.cbf[:, 0, :]
        self.blk64 = self.cbf[:, 1, :]
        self.identb = self.cbf[:, 2, :]
        self.rblk = self.cbf[:, 3, :]
        self.zerob = self.cbf[:, 4, :]
        self.onesb = self.cbf[:, 5, :]
        self.ccos = self.cbf[:, 6, :]
        self.cnsin = self.cbf[:, 7, :]
        self.mprev = self.cbf[:, 8, :]
        self.mnext = self.cbf[:, 9, :]
        self.adabuf = None
        self.cur = self.pers + 2 * 4096
        ta = self.talloc
        rows1 = ta("rows1", [48, 128], F32)
        rows2 = ta("rows2", [96, 128], F32)
        self.dma("sp", rows1[0:16, :], self.n1_d.ap().rearrange("l (c p) -> (l c) p", p=128), writes=["rows1"])
        self.dma("sp", rows1[16:32, :], self.n2_d.ap().rearrange("l (c p) -> (l c) p", p=128), writes=["rows1"])
        self.dma("sp", rows1[32:48, :], self.cvec_d.ap().rearrange("s (c p) -> (s c) p", p=128), writes=["rows1"])
        self.dma("sp", rows2[:, :], self.ada_b_d.ap().rearrange("l (c p) -> (l c) p", p=128), writes=["rows2"])
        b, bk = self.bank()
        self.tr(b[:, 0:48], rows1[:, :], ["rows1"], [bk])
        self.copy("dve", self.vecT[:], b[:, 0:48], [bk], ["vecT"])
        b, bk = self.bank()
        self.tr(b[:, 0:96], rows2[:, :], ["rows2"], [bk])
        self.copy("dve", self.biasT[:], b[:, 0:96], [bk], ["biasT"])
        self.act(self.sT[:].rearrange("p s c -> p (s c)"), self.vecT[:, 32:48], AF.Silu, ["vecT"], ["sT"])
        for i, (d_, sc) in enumerate([(self.evq_d, 0.125), (self.evk_d, 1.0), (self.odq_d, 0.125), (self.odk_d, 1.0)]):
            src = d_.ap().rearrange("(d u) -> d u", u=1)
            self.dma("sp", self.gq[0:64, i:i + 1], src, writes=["gq"])
            self.dma("sp", self.gq[64:128, i:i + 1], src, writes=["gq"])
        self.ts("dve", self.gq[:, 0:1], self.gq[:, 0:1], 0.125, None, ALU.mult, None, ["gq"], ["gq"])
        self.ts("dve", self.gq[:, 2:3], self.gq[:, 2:3], 0.125, None, ALU.mult, None, ["gq"], ["gq"])
        self.dma("sp", self.skt[0:64, :], AP(self.sink_d, 0, [[0, 64], [1, 4]]), writes=["skt"])
        self.dma("sp", self.skt[64:128, :], AP(self.sink_d, 4, [[0, 64], [1, 4]]), writes=["skt"])
        self.act(self.skt[:], self.skt[:], AF.Exp, ["skt"], ["skt"])
        for j in range(4):
            self.ts("dve", self.sinkbc[:, j, :], self.zf[:], self.skt[:, j:j + 1], None, ALU.add, None, ["zf", "skt"], ["sinkbc"])
        self.adabuf = [self.palloc_late("adabuf%d" % i, [128, 8, 256], BF16) for i in range(2)]
        self.ada_list = [(l, sl) for l in range(2) for sl in range(24)]
        self.ada_pos = 0
        self.ada_issued = 0
        self.win = self.nc.alloc_sbuf_tensor_at("win_top", [128, 8, 1280], BF16, offset=self.hi - 20480)
        self.load_w(self.win[:], "win", self.evin_d.ap().rearrange("(c p) n -> p c n", p=128))
        self.ada_issue()
        self.ada_tick(8)

    def palloc_late(self, name, shape, dt):
        t, self.pers = self._alloc(self.pers, name, shape, dt)
        return t

    def ada_issue(self):
        if self.ada_issued >= len(self.ada_list):
            return
        l, sl = self.ada_list[self.ada_issued]
        i = self.ada_issued % 2
        src = self.ada_w_d.ap()[l].rearrange("(c p) n -> p c n", p=128)[:, :, sl * 256:(sl + 1) * 256]
        self.dma("pool", self.adabuf[i][:], src, writes=["adabuf%d" % i])
        self.ada_issued += 1

    def ada_tick(self, n=1):
        for _ in range(n):
            if self.ada_pos >= len(self.ada_list):
                return
            l, sl = self.ada_list[self.ada_pos]
            i = self.ada_pos % 2
            self.ada_pos += 1
            self.ada_issue()
            buf, bk = self.adabuf[i], "adabuf%d" % i
            pm, pmk = self.pb[7], "pb7"
            for o2 in range(2):
                oc = sl * 2 + o2
                for kc in range(8):
                    self.mm(pm[:, oc * 2:oc * 2 + 2], buf[:, kc, o2 * 128:(o2 + 1) * 128], self.sT[:, :, kc],
                            kc == 0, kc == 7, [bk, "sT"], [pmk])
            if sl == 7 or sl == 23:
                lo, hi = (0, 16) if sl == 7 else (16, 48)
                for s_ in range(2):
                    self.tt("dve", self.modT[:, l, lo:hi, s_], pm[:, 0:96].rearrange("p (o s) -> p o s", s=2)[:, lo:hi, s_],
                            self.biasT[:, l * 48 + lo:l * 48 + hi], ALU.add, [pmk, "biasT"], ["modT"])
                    if sl == 7:
                        self.stt("dve", self.A1[:, l, s_, :], self.modT[:, l, 8:16, s_], 1.0, self.vecT[:, l * 8:l * 8 + 8],
                                 ALU.add, ALU.mult, ["modT", "vecT"], ["A1"])
                    else:
                        self.stt("dve", self.A2[:, l, s_, :], self.modT[:, l, 32:40, s_], 1.0, self.vecT[:, 16 + l * 8:16 + l * 8 + 8],
                                 ALU.add, ALU.mult, ["modT", "vecT"], ["A2"])

    def mod(self, l, part, s, c):
        return self.modT[:, l, part * 8 + c, s:s + 1]

    def load_xT(self, src_rows, ntile, dst, dst_key, col0, xtoks):
        for t in range(ntile):
            xt, xk = xtoks[self.xti % len(xtoks)]
            self.xti += 1
            self.dma("sp", xt[:], src_rows[t * 128:(t + 1) * 128, :], writes=[xk])
            for half in range(2):
                b, bk = self.bank()
                for cc in range(4):
                    c = half * 4 + cc
                    self.tr(b[:, cc * 128:(cc + 1) * 128], xt[:, c * 128:(c + 1) * 128], [xk], [bk])
                o = dst[:, half * 4:half * 4 + 4, col0 + t * 128:col0 + (t + 1) * 128]
                self.evac(o, b[:, :].rearrange("p (c n) -> p c n", n=128), [bk], [dst_key])

    def norm_mod(self, xT, xkey, col0, N, A, Bf, hT, hkey, hcol0, tmp, part=None, state=None):
        sqbs, rss, tf = tmp
        if part in (None, 1):
            self.nmi += 1
            sqb, sqk = sqbs[self.nmi % len(sqbs)], "sqb%d" % (self.nmi % len(sqbs))
            rs, rk = rss[self.nmi % len(rss)], "rsn%d" % (self.nmi % len(rss))
            nsq = sqb.shape[1]
            b, bk = self.bank()
            for c0 in range(0, 8, nsq):
                for c in range(c0, c0 + nsq):
                    xi = xT[:, c, col0:col0 + N]
                    if c % 4 == 3:
                        self.tt("dve", sqb[:, c % nsq, 0:N], xi, xi, ALU.mult, [xkey], [sqk])
                    else:
                        self.act(sqb[:, c % nsq, 0:N], xi, AF.Square, [xkey], [sqk])
                for c in range(c0, c0 + nsq):
                    self.mm(b[:, 0:N], self.ones1024, sqb[:, c % nsq, 0:N], c == 0, c == 7, [sqk, "c_bf"], [bk])
            self.act(rs[:, 0:N], b[:, 0:N], AF.Ln, [bk], [rk], bias=EPS)
            self.act(rs[:, 0:N], rs[:, 0:N], AF.Exp, [rk], [rk], scale=-0.5)
            state = (rs, rk)
            if part == 1:
                return state
        rs, rk = state
        for c in range(8):
            self.tfi += 1
            t_ = tf[self.tfi % len(tf)]
            tk = "tf%d" % (self.tfi % len(tf))
            self.stt("dve", t_[:, 0:N], xT[:, c, col0:col0 + N], A(c), rs[:, 0:N], ALU.mult, ALU.mult,
                     [xkey, rk, "A1", "A2"], [tk])
            if part == 2 and c % 3 == 2:
                self.ts("dve", hT[:, c, hcol0:hcol0 + N], t_[:, 0:N], Bf(c), None, ALU.add, None, [tk, "modT"], [hkey])
            else:
                self.act(hT[:, c, hcol0:hcol0 + N], t_[:, 0:N], AF.Identity, [tk, "modT"], [hkey], bias=Bf(c))
        return None

    def norm_pipeline(self, calls):
        st = [None] * len(calls)
        if calls:
            st[0] = self.norm_mod(part=1, **calls[0])
        for i in range(len(calls)):
            if i + 1 < len(calls):
                st[i + 1] = self.norm_mod(part=1, **calls[i + 1])
            self.norm_mod(part=2, state=st[i], **calls[i])

    def qk_norm(self, praw, pk, N, gcol, out, okey, tmp, rope=None):
        sq1s, rss, qns, r1s = tmp
        self.qni += 1
        i = self.qni
        sq1, sk = sq1s[i % len(sq1s)], "sq1_%d" % (i % len(sq1s))
        rs, rk = rss[i % len(rss)], "rsq%d" % (i % len(rss))
        self.act(sq1[:, 0:N], praw[:, 0:N], AF.Square, [pk], [sk])
        b, bk = self.bank()
        self.mm(b[:, 0:N], self.blk64, sq1[:, 0:N], True, True, [sk, "c_bf"], [bk])
        self.act(rs[:, 0:N], b[:, 0:N], AF.Ln, [bk], [rk], bias=EPS)
        self.act(rs[:, 0:N], rs[:, 0:N], AF.Exp, [rk], [rk], scale=-0.5)
        if rope is None:
            self.stt("dve", out, praw[:, 0:N], gcol, rs[:, 0:N], ALU.mult, ALU.mult, [pk, rk, "gq"], [okey])
            return
        qn, qk_ = qns[i % len(qns)], "qn%d" % (i % len(qns))
        r1, r1k = r1s[i % len(r1s)], "r1_%d" % (i % len(r1s))
        cosT, sinT, rpk = rope
        self.stt("dve", qn[:, 0:N], praw[:, 0:N], gcol, rs[:, 0:N], ALU.mult, ALU.mult, [pk, rk, "gq"], [qk_])
        b2, bk2 = self.bank()
        self.mm(b2[:, 0:N], self.pswap[:], qn[:, 0:N], True, True, [qk_, "c_bf"], [bk2])
        self.tt("pool", r1[:, 0:N], qn[:, 0:N], cosT, ALU.mult, [qk_] + rpk, [r1k])
        self.tt("dve", rs[:, 0:N], b2[:, 0:N], sinT, ALU.mult, [bk2] + rpk, [rk])
        self.tt("dve", out, rs[:, 0:N], r1[:, 0:N], ALU.add, [rk, r1k], [okey])

    def qk_pipeline(self, jobs, tmp):
        sq1s, rss, qns, r1s = tmp
        st = []
        for j, jb in enumerate(jobs):
            d = dict(jb)
            d["sq1"], d["sk"] = sq1s[j % len(sq1s)], "sq1_%d" % (j % len(sq1s))
            d["rs"], d["rk"] = rss[j % len(rss)], "rsq%d" % (j % len(rss))
            if jb["rope"] is not None:
                d["qn"], d["qk"] = qns[j % len(qns)], "qn%d" % (j % len(qns))
                d["r1"], d["r1k"] = r1s[j % len(r1s)], "r1_%d" % (j % len(r1s))
            st.append(d)

        def sa(d):
            d["b"], d["bk"] = d["proj"]()

        def sb(d):
            N = d["N"]
            praw, pk, sq1, sk, rs, rk = d["b"], d["bk"], d["sq1"], d["sk"], d["rs"], d["rk"]
            self.act(sq1[:, 0:N], praw[:, 0:N], AF.Square, [pk], [sk])
            b, bk = self.bank()
            self.mm(b[:, 0:N], self.blk64, sq1[:, 0:N], True, True, [sk, "c_bf"], [bk])
            self.act(rs[:, 0:N], b[:, 0:N], AF.Ln, [bk], [rk], bias=EPS)
            self.act(rs[:, 0:N], rs[:, 0:N], AF.Exp, [rk], [rk], scale=-0.5)
            if d["rope"] is None:
                self.stt("dve", d["out"], praw[:, 0:N], d["gcol"], rs[:, 0:N], ALU.mult, ALU.mult, [pk, rk, "gq"], [d["okey"]])
            else:
                self.stt("dve", d["qn"][:, 0:N], praw[:, 0:N], d["gcol"], rs[:, 0:N], ALU.mult, ALU.mult, [pk, rk, "gq"], [d["qk"]])

        def sc(d):
            if d["rope"] is None:
                return
            N = d["N"]
            cosT, sinT, rpk = d["rope"]
            qn, qk_, r1, r1k, rs, rk = d["qn"], d["qk"], d["r1"], d["r1k"], d["rs"], d["rk"]
            b2, bk2 = self.bank()
            self.mm(b2[:, 0:N], self.pswap[:], qn[:, 0:N], True, True, [qk_, "c_bf"], [bk2])
            self.tt("pool", r1[:, 0:N], qn[:, 0:N], cosT, ALU.mult, [qk_] + rpk, [r1k])
            self.tt("dve", rs[:, 0:N], b2[:, 0:N], sinT, ALU.mult, [bk2] + rpk, [rk])
            self.tt("dve", d["out"], rs[:, 0:N], r1[:, 0:N], ALU.add, [rk, r1k], [d["okey"]])

        n = len(st)
        for t in range(n + 2):
            if t < n:
                sa(st[t])
            if 0 <= t - 1 < n:
                sb(st[t - 1])
            if 0 <= t - 2 < n:
                sc(st[t - 2])

    def load_w(self, dst, dkey, src):
        self.dma("pool", dst, src, writes=[dkey])

    def layer0_A(self):
        self.phase()
        ta = self.talloc
        self.rot = [0, 1, 2, 3, 4, 5, 6]
        self.QT = ta("QT", [128, 4, T_A], BF16)
        self.KT = ta("KT", [128, T_A], BF16)
        self.V = ta("V", [128, T_A // 128, 128], BF16)
        self.QcT = ta("QcT", [128, 4, NCTX], BF16)
        self.KcT = ta("KcT", [128, NCTX], BF16)
        self.Vc = ta("Vc", [128, 2, 128], BF16)
        fo_off = self.cur
        self.FO = ta("FO", [128, 4, T_X], BF16)
        self.FOc = ta("FOc", [128, 4, NCTX], BF16)
        self.keepC = self.cur
        self.F_all = ta("F_all", [128, 32, 512], BF16)
        self.Fc = ta("Fc", [128, 2, 512], BF16)
        self.keepA = self.cur
        win = self.win
        xtoks = [(ta("xtok%d" % i, [128, 1024], F32), "xtok%d" % i) for i in range(4)]
        self.xti = 0
        self.uid += 1
        xTb2 = self.nc.alloc_sbuf_tensor_at("xTb2_%d" % self.uid, [128, 8, 512], F32, offset=fo_off)
        xTbs = [(ta("xTb", [128, 8, 512], F32), "xTb0"), (xTb2, "xTb1")]
        hTs = [(ta("hT%d" % i, [128, 8, 512], BF16), "hT%d" % i) for i in range(2)]
        sqbs = [ta("sqb", [128, 4, 512], BF16)]
        rss = [ta("rsn%d" % i, [128, 512], F32) for i in range(2)]
        tf = [ta("tf%d" % i, [128, 512], F32) for i in range(2)]
        sq1s = [ta("sq1_%d" % i, [128, 512], BF16) for i in range(2)]
        rsq = [ta("rsq%d" % i, [128, 512], F32) for i in range(2)]
        self.uid += 1
        qns = [self.nc.alloc_sbuf_tensor_at("qn0_%d" % self.uid, [128, 512], F32, offset=fo_off + 16384), ta("qn1", [128, 512], F32)]
        r1s = [self.nc.alloc_sbuf_tensor_at("r10_%d" % self.uid, [128, 512], F32, offset=fo_off + 16384 + 2048)]
        ropeb = [ta("rope%d" % i, [128, 2, 512], F32) for i in range(1)]
        tmpn = (sqbs, rss, tf)
        tmpq = (sq1s, rsq, qns, r1s)
        assert self.cur <= self.hi - 20480, "phase A overflow into win"
        l = 0

        def projA(hT, hk, N, qkv, s, tok0, tile0, ropek):
            for t in range(N // 128):
                b, bk = self.bank()
                for c in range(8):
                    self.mm(b[:, 0:512], hT[:, c, t * 128:(t + 1) * 128], win[:, c, 0:512], c == 0, c == 7, [hk, "win"], [bk])
                dst = self.F_all[:, tile0 + t, :] if s == 0 else self.Fc[:, t, :]
                self.evac(dst, b[:, 0:512], [bk], ["F_all" if s == 0 else "Fc"])
            if not qkv:
                return
            for t in range(N // 128):
                b, bk = self.bank()
                for c in range(8):
                    self.mm(b[:, 0:128], hT[:, c, t * 128:(t + 1) * 128], win[:, c, 1152:1280], c == 0, c == 7, [hk, "win"], [bk])
                dst = self.V[:, tile0 + t, :] if s == 0 else self.Vc[:, t, :]
                self.evac(dst, b[:, 0:128], [bk], ["V" if s == 0 else "Vc"])

        def projB(hT, hk, N, qkv, s, tok0, tile0, ropek):
            if not qkv:
                return
            jobs = []
            for j in range(5):
                def proj(j=j):
                    b, bk = self.bank()
                    for c in range(8):
                        self.mm(b[:, 0:N], win[:, c, 512 + j * 128:512 + (j + 1) * 128], hT[:, c, 0:N], c == 0, c == 7, [hk, "win"], [bk])
                    return b, bk
                if s == 0:
                    out = self.QT[:, j, tok0:tok0 + N] if j < 4 else self.KT[:, tok0:tok0 + N]
                    okey = "QT" if j < 4 else "KT"
                    rb_ = ropeb[0]
                    rope = (rb_[:, 0, 0:N], rb_[:, 1, 0:N], ["rope0c", "rope0s"])
                else:
                    out = self.QcT[:, j, 0:N] if j < 4 else self.KcT[:, 0:N]
                    okey = "QcT" if j < 4 else "KcT"
                    rope = None
                jobs.append(dict(proj=proj, N=N, gcol=self.gq[:, 0:1] if j < 4 else self.gq[:, 1:2], out=out, okey=okey, rope=rope))
            self.qk_pipeline(jobs, tmpq)

        items = [("ctx", 0)] + [("lat", g) for g in range(8)]

        def stage1(idx):
            kind, g = items[idx]
            hT, hk = hTs[idx % 2]
            if kind == "ctx":
                self.load_xT(self.ctx_d.ap(), 2, self.YT, "YT", 0, xtoks)
                return (hT, hk, NCTX, True, 1, 0, 0, 0, self.YT, "YT")
            tok0 = g * 512
            xTb, xk = xTbs[idx % 2]
            self.load_xT(self.x_d.ap()[tok0:tok0 + 512, :], 4, xTb, xk, 0, xtoks)
            return (hT, hk, 512, g < 5, 0, tok0, g * 4, g, xTb, xk)

        def stage2(st, part, state=None):
            hT, hk, N, qkv, s_, tok0, tile0, rk, xT, xk = st
            return self.norm_mod(xT, xk, 0, N, lambda c: self.A1[:, l, s_, c:c + 1], lambda c: self.mod(l, 0, s_, c),
                                 hT, hk, 0, tmpn, part=part, state=state)

        cur = stage1(0)
        stage2(cur, 2, stage2(cur, 1))
        for idx in range(len(items)):
            nxt = stage1(idx + 1) if idx + 1 < len(items) else None
            projA(*cur[:8])
            self.ada_tick(1)
            if nxt is not None:
                stage2(nxt, 2, stage2(nxt, 1))
            projB(*cur[:8])
            if nxt is not None and nxt[3] and nxt[4] == 0:
                rb_ = ropeb[0]
                tk0 = nxt[5]
                self.dma("sp", rb_[:, 0, :], self.crope_d.ap()[0, :, tk0:tk0 + 512], writes=["rope0c"])
                self.dma("sp", rb_[:, 1, :], self.crope_d.ap()[1, :, tk0:tk0 + 512], writes=["rope0s"])
            self.ada_tick(1)
            cur = nxt
        if self.stop_after == "A":
            for nm_, t_, k_ in [("F_all", self.F_all, "F_all"), ("QT", self.QT, "QT"), ("KT", self.KT, "KT"), ("V", self.V, "V"),
                                ("Fc", self.Fc, "Fc"), ("QcT", self.QcT, "QcT"), ("KcT", self.KcT, "KcT"), ("Vc", self.Vc, "Vc"),
                                ("modT", self.modT, "modT")]:
                self.dump(nm_, t_, k_)

    def dft(self, Fsrc, fkey, ntile, tabs, K, FO, fokey, tabbuf, ABT):
        ti = 0
        for (k0, N) in groups(K):
            for cs in range(2):
                tb, tk = tabbuf[ti % 2], "tab%d" % (ti % 2)
                ti += 1
                src_t = tabs(cs, k0, N)
                for a0 in range(0, ntile, 8):
                    self.dma("sp", tb[:, a0:a0 + 8, 0:N], src_t[:, a0:a0 + 8, :], writes=[tk])
                for g in range(4):
                    b, bk = self.bank()
                    for a in range(ntile):
                        self.mm(b[:, 0:N], Fsrc[:, a, g * 128:(g + 1) * 128], tb[:, a, 0:N], a == 0, a == ntile - 1, [fkey, tk], [bk])
                    self.evac(ABT[:, cs, g, 0:N], b[:, 0:N], [bk], ["ABT"])
                    self.ada_tick(1)
            for g in range(4):
                b, bk = self.bank()
                self.mm(b[:, 0:N], self.ccos, ABT[:, 0, g, 0:N], True, False, ["ABT", "c_bf"], [bk])
                self.mm(b[:, 0:N], self.cnsin, ABT[:, 1, g, 0:N], False, True, ["ABT", "c_bf"], [bk])
                self.evac(FO[:, g, k0:k0 + N], b[:, 0:N], [bk], [fokey])

    def layer0_B(self):
        scr = self.scr
        self.P.barrier(lambda e: e.memset(scr[0:1, 0:8], 0.0))
        self.cur = self.keepA
        ta = self.talloc
        tabbuf = [ta("tab%d" % i, [128, 32, 512], BF16) for i in range(2)]
        ABT = ta("ABT", [128, 2, 4, 512], BF16)
        for cs in range(2):
            for g in range(4):
                b, bk = self.bank()
                for a in range(2):
                    self.mm(b[:, 0:NCTX], self.Fc[:, a, g * 128:(g + 1) * 128], self.cctx[:, cs, a, :], a == 0, a == 1, ["Fc", "c_bf"], [bk])
                self.evac(ABT[:, cs, g, 0:NCTX], b[:, 0:NCTX], [bk], ["ABT"])
        for g in range(4):
            b, bk = self.bank()
            self.mm(b[:, 0:NCTX], self.ccos, ABT[:, 0, g, 0:NCTX], True, False, ["ABT", "c_bf"], [bk])
            self.mm(b[:, 0:NCTX], self.cnsin, ABT[:, 1, g, 0:NCTX], False, True, ["ABT", "c_bf"], [bk])
            self.evac(self.FOc[:, g, :], b[:, 0:NCTX], [bk], ["FOc"])
        tabs = lambda cs, k0, N: self.cdft_d[cs].ap().rearrange("(a p) k -> p a k", p=128)[:, :, k0:k0 + N]
        self.dft(self.F_all, "F_all", 32, tabs, T_X, self.FO, "FO", tabbuf, ABT)
        self.ada_tick(100)
        self.rot = [2, 3, 4, 5, 6, 7]
        self.pers -= 2 * 4096
        if self.stop_after == "B":
            self.dump("FO", self.FO, "FO")
            self.dump("FOc", self.FOc, "FOc")

    def layer0_C(self):
        scr = self.scr
        self.P.barrier(lambda e: e.memset(scr[0:1, 0:8], 0.0))
        self.cur = self.keepC
        ta = self.talloc
        l = 0
        wout = ta("wout", [128, 8, D], BF16)
        self.load_w(wout[:], "wout", self.evout_d.ap().rearrange("(c p) n -> p c n", p=128))
        ATTs = [(ta("ATT%d" % i, [128, 4, 512], BF16), "ATT%d" % i) for i in range(2)]
        qlos = [(ta("qlo%d" % i, [128, 4, 128], BF16), "qlo%d" % i) for i in range(2)]
        qhis = [(ta("qhi%d" % i, [128, 4, 128], BF16), "qhi%d" % i) for i in range(2)]
        PT = [ta("PT%d" % i, [128, 512], BF16) for i in range(4)]
        dtots = [(ta("dtot%d" % i, [128, 512], F32), "dtot%d" % i) for i in range(2)]
        xtoks = [(ta("xtokC%d" % i, [128, 1024], F32), "xtokC%d" % i) for i in range(3)]
        self.xti = 0
        self.XT = self.nc.alloc_sbuf_tensor_at("XT_res", [128, 8, T_X], F32, offset=self.hi - 8 * T_X * 4)
        assert self.cur <= self.hi - 8 * T_X * 4, "phase C overflow"
        for (q_, k_) in qlos + qhis:
            self.P.add("pool", lambda e, q_=q_: e.memset(q_[:], 0.0), writes=[k_])
        self.rot = [4, 5, 6, 7]
        accs = [((self.pb[0], "pb0"), (self.pb[1], "pb1")), ((self.pb[2], "pb2"), (self.pb[3], "pb3"))]
        ctxtiles = [(self.KcT[:, t * 128:(t + 1) * 128], self.Vc[:, t, :], None, ["KcT", "Vc"]) for t in range(2)]

        jobs = []

        def outproj(N, FOsrc, fokey, focol, Xdst, xkey, xcol, s_, ATT, ak):
            for oc in range(8):
                b, bk = self.bank()
                for kc in range(8):
                    rhs = FOsrc[:, kc, focol:focol + N] if kc < 4 else ATT[:, kc - 4, 0:N]
                    self.mm(b[:, 0:N], wout[:, kc, oc * 128:(oc + 1) * 128], rhs, kc == 0, kc == 7, ["wout", fokey, ak], [bk])
                xo = Xdst[:, oc, xcol:xcol + N]
                self.stt("dve", xo, b[:, 0:N], self.mod(l, 2, s_, oc), xo, ALU.mult, ALU.add, [bk, "modT", xkey], [xkey])

        gi = 0
        for qt in range(2):
            post = (lambda ai=gi % 2: outproj(NCTX, self.FOc, "FOc", 0, self.YT, "YT", 0, 1, *ATTs[ai])) if qt == 1 else None
            jobs.append((self.QcT, "QcT", qt * 128, ctxtiles, gi % 2, qt * 128, None, post))
        gi += 1
        for (t0, N) in groups(T_X):
            for tq in range(N // 128):
                qt = t0 // 128 + tq
                kts = []
                for dlt, mask in ((-1, self.mprev), (0, None), (1, self.mnext)):
                    kt = qt + dlt
                    if kt < 0:
                        continue
                    kts.append((self.KT[:, kt * 128:(kt + 1) * 128], self.V[:, kt, :], mask, ["KT", "V"]))
                pre = (lambda t0=t0, N=N: self.load_xT(self.x_d.ap()[t0:t0 + N, :], N // 128, self.XT, "XT%d" % (t0 // 512), t0, xtoks)) if tq == 0 else None
                post = (lambda t0=t0, N=N, ai=gi % 2: outproj(N, self.FO, "FO", t0, self.XT, "XT%d" % (t0 // 512), t0, 0, *ATTs[ai])) \
                    if tq == N // 128 - 1 else None
                jobs.append((self.QT, "QT", qt * 128, kts + ctxtiles, gi % 2, tq * 128, pre, post))
            gi += 1

        steps = []
        for ji, job in enumerate(jobs):
            n = len(job[3])
            for kvh in range(2):
                for i in range(n):
                    steps.append((ji, kvh, i, n))
        pti = [0]

        def s_stage(st):
            ji, kvh, i, n = st
            QTsrc, qkey, qcol, keytiles, ai, dcol, pre, post = jobs[ji]
            (qlo, qlk), (qhi, qhk) = qlos[ji % 2], qhis[ji % 2]
            (pbO, ok_), (pbD, dk_) = accs[ji % 2]
            if kvh == 0 and i == 0:
                if pre is not None:
                    pre()
                self.copy("pool", qlo[0:64, :, :], QTsrc[0:64, :, qcol:qcol + 128], [qkey], [qlk])
                self.copy("pool", qhi[64:128, :, :], QTsrc[64:128, :, qcol:qcol + 128], [qkey], [qhk])
            qp, qk_ = (qlo, qlk) if kvh == 0 else (qhi, qhk)
            kap, vap, mask, kkeys = keytiles[i]
            b, bk = self.bank()
            self.mm(b[:, 0:512], kap, qp[:].rearrange("p j n -> p (j n)"), True, mask is None, [qk_] + kkeys, [bk])
            if mask is not None:
                self.mm(b[:, 0:512], self.identb, mask.unsqueeze(1).broadcast_to([128, 4, 128]), False, True, ["c_bf"], [bk])
            pt = PT[pti[0] % 4]
            pk = "PT%d" % (pti[0] % 4)
            pti[0] += 1
            self.act(pt[:], b[:, 0:512], AF.Exp, [bk], [pk])
            return (pt, pk)

        def p_stage(st, pt, pk):
            ji, kvh, i, n = st
            QTsrc, qkey, qcol, keytiles, ai, dcol, pre, post = jobs[ji]
            (pbO, ok_), (pbD, dk_) = accs[ji % 2]
            kap, vap, mask, kkeys = keytiles[i]
            rows = slice(kvh * 64, kvh * 64 + 64)
            self.mm(pbO[rows, 0:512], vap[:, kvh * 64:kvh * 64 + 64], pt[:], i == 0, i == n - 1, [pk] + kkeys, [ok_])
            self.mm(pbD[rows, 0:512], self.onesb[:, 0:64], pt[:], i == 0, i == n - 1, [pk, "c_bf"], [dk_])
            if kvh == 1 and i == n - 1:
                dtot, dtk = dtots[ji % 2]
                ATT, ak = ATTs[ai]
                self.tt("dve", dtot[:], pbD[:, 0:512], self.sinkbc[:].rearrange("p j n -> p (j n)"), ALU.add, [dk_, "sinkbc"], [dtk])
                self.act(dtot[:], dtot[:], AF.Ln, [dtk], [dtk])
                self.act(dtot[:], dtot[:], AF.Exp, [dtk], [dtk], scale=-1.0)
                self.tt("dve", ATT[:, :, dcol:dcol + 128], pbO[:, 0:512].rearrange("p (j n) -> p j n", n=128),
                        dtot[:].rearrange("p (j n) -> p j n", n=128), ALU.mult, [ok_, dtk], [ak])
                if post is not None:
                    post()

        prev = None
        for st in steps:
            cur = s_stage(st)
            if prev is not None:
                p_stage(*prev)
            prev = (st, cur[0], cur[1])
        p_stage(*prev)
        self.rot = [2, 3, 4, 5, 6, 7]

    def ffn(self, l):
        scr = self.scr
        self.P.barrier(lambda e: e.memset(scr[0:1, 0:8], 0.0))
        self.cur = self.pers
        ta = self.talloc
        T = T_X if l == 0 else T_OWN
        toks = [(t0, N, 0) for (t0, N) in groups(T)]
        HT = T + (NCTX if l == 0 else 0)
        h2 = ta("h2", [128, 8, HT], BF16)
        keepF = self.cur
        sqbs = [ta("sqb%d" % i, [128, 8, 512], BF16) for i in range(2)]
        rss = [ta("rsn%d" % i, [128, 512], F32) for i in range(2)]
        tf = [ta("tf%d" % i, [128, 512], F32) for i in range(4)]
        tmpn = (sqbs, rss, tf)
        xkey = lambda t0: "XT%d" % (t0 // 512)
        calls = [dict(xT=self.XT, xkey=xkey(t0), col0=t0, N=N, A=(lambda c: self.A2[:, l, 0, c:c + 1]),
                      Bf=(lambda c: self.mod(l, 3, 0, c)), hT=h2, hkey="h2_%d" % (t0 // 512), hcol0=t0, tmp=tmpn) for (t0, N, _) in toks]
        if l == 0:
            calls.append(dict(xT=self.YT, xkey="YT", col0=0, N=NCTX, A=(lambda c: self.A2[:, l, 1, c:c + 1]),
                              Bf=(lambda c: self.mod(l, 3, 1, c)), hT=h2, hkey="h2_c", hcol0=T, tmp=tmpn))
            toks = toks + [(T, NCTX, 1)]
        self.norm_pipeline(calls)
        self.P.barrier(lambda e: e.memset(scr[0:1, 0:8], 0.0))
        self.cur = keepF
        splits = [(0, 4), (4, 4), (8, 4), (12, 4), (16, 3), (19, 3)]
        wgb = [ta("wg%d" % i, [128, 8, 512], BF16) for i in range(2)]
        wub = [ta("wu%d" % i, [128, 8, 512], BF16) for i in range(2)]
        wdb = [ta("wd%d" % i, [128, 4, D], BF16) for i in range(2)]
        actb = [ta("act%d" % i, [128, 4, 512], BF16) for i in range(2)]
        sg = [ta("sg%d" % i, [128, 512], F32) for i in range(3)]
        obs = [ta("ob%d" % i, [128, D], F32) for i in range(2)] if l == 1 else None
        assert self.cur <= self.hi - 8 * T_X * 4, "ffn overflow"
        ai = 0
        si = 0
        for sp_i, (j0, nj) in enumerate(splits):
            wi = sp_i % 2
            wg, wu, wd = wgb[wi], wub[wi], wdb[wi]
            kg, ku, kd = "wg%d" % wi, "wu%d" % wi, "wd%d" % wi
            self.load_w(wg[:, :, 0:nj * 128], kg, self.wg_d.ap()[l].rearrange("(c p) n -> p c n", p=128)[:, :, j0 * 128:(j0 + nj) * 128])
            self.load_w(wu[:, :, 0:nj * 128], ku, self.wu_d.ap()[l].rearrange("(c p) n -> p c n", p=128)[:, :, j0 * 128:(j0 + nj) * 128])
            self.load_w(wd[:, 0:nj, :], kd, self.wd_d.ap()[l].rearrange("(j p) n -> p j n", p=128)[:, j0:j0 + nj, :])
            for (t0, N, s) in toks:
                ab = actb[ai % 2]
                ak = "act%d" % (ai % 2)
                ai += 1
                hk = "h2_c" if s == 1 else "h2_%d" % (t0 // 512)
                for jj in range(nj):
                    bg, bgk = self.bank()
                    for c in range(8):
                        self.mm(bg[:, 0:N], wg[:, c, jj * 128:(jj + 1) * 128], h2[:, c, t0:t0 + N], c == 0, c == 7, [kg, hk], [bgk])
                    bu, buk = self.bank()
                    for c in range(8):
                        self.mm(bu[:, 0:N], wu[:, c, jj * 128:(jj + 1) * 128], h2[:, c, t0:t0 + N], c == 0, c == 7, [ku, hk], [buk])
                    s_ = sg[si % 3]
                    sk = "sg%d" % (si % 3)
                    si += 1
                    self.act(s_[:, 0:N], bg[:, 0:N], AF.Silu, [bgk], [sk])
                    self.tt("dve", ab[:, jj, 0:N], bu[:, 0:N], s_[:, 0:N], ALU.mult, [buk, sk], [ak])
                X, xk, xc = (self.XT, xkey(t0), t0) if s == 0 else (self.YT, "YT", 0)
                for oc in range(8):
                    b, bk = self.bank()
                    for jj in range(nj):
                        self.mm(b[:, 0:N], wd[:, jj, oc * 128:(oc + 1) * 128], ab[:, jj, 0:N], jj == 0, jj == nj - 1, [kd, ak], [bk])
                    xo = X[:, oc, xc:xc + N]
                    self.stt("dve", xo, b[:, 0:N], self.mod(l, 5, s, oc), xo, ALU.mult, ALU.add, [bk, "modT", xk], [xk])
                if l == 1 and sp_i == len(splits) - 1:
                    self.emit_out(range(t0 // 128, (t0 + N) // 128), obs)
        if l == 1:
            self.out_done = True

    def layer1_E(self):
        scr = self.scr
        self.P.barrier(lambda e: e.memset(scr[0:1, 0:8], 0.0))
        self.cur = self.pers
        ta = self.talloc
        l = 1
        NK = T_X + NCTX
        hT = ta("hT1", [128, 8, NK], BF16)
        keepE = self.cur
        sqbs = [ta("sqb%d" % i, [128, 8, 512], BF16) for i in range(2)]
        rss = [ta("rsn%d" % i, [128, 512], F32) for i in range(2)]
        tf = [ta("tf%d" % i, [128, 512], F32) for i in range(4)]
        tmpn = (sqbs, rss, tf)
        calls = [dict(xT=self.XT, xkey="XT%d" % (t0 // 512), col0=t0, N=N, A=(lambda c: self.A1[:, l, 0, c:c + 1]),
                      Bf=(lambda c: self.mod(l, 0, 0, c)), hT=hT, hkey="hT1_%d" % (t0 // 512), hcol0=t0, tmp=tmpn) for (t0, N) in groups(T_X)]
        calls.append(dict(xT=self.YT, xkey="YT", col0=0, N=NCTX, A=(lambda c: self.A1[:, l, 1, c:c + 1]),
                          Bf=(lambda c: self.mod(l, 0, 1, c)), hT=hT, hkey="hT1_%d" % (T_X // 512), hcol0=T_X, tmp=tmpn))
        self.norm_pipeline(calls)
        self.P.barrier(lambda e: e.memset(scr[0:1, 0:8], 0.0))
        self.cur = keepE
        rsq = [ta("rsq%d" % i, [128, 512], F32) for i in range(2)]
        sq1s = [ta("sq1_%d" % i, [128, 512], BF16) for i in range(1)]
        tmpq = (sq1s, rsq, None, None)
        namask = ta("namask", [128, 3, 6, 128], BF16)
        self.dma("sp", namask[:], self.cnam_d.ap(), writes=["namask"])
        wq = ta("wq", [128, 8, 256], BF16)
        wk = ta("wk", [128, 8, 256], BF16)
        wv = ta("wv", [128, 8, 256], BF16)
        wo = ta("wo", [128, 2, D], BF16)
        Tt = ta("Tt", [128, 6, 4, 128], BF16)
        BMI = ta("BMI", [128, 5, 4, 128], BF16)
        bmtmp = rsq[0]
        KTh = ta("KTh", [128, 2, NK], BF16)
        Vh = ta("Vh", [128, NK // 128, 256], BF16)
        QTh = ta("QTh", [128, 2, T_OWN], BF16)
        ATTs = [(ta("ATT1_%d" % i, [128, 2, 512], BF16), "ATT1_%d" % i) for i in range(2)]
        qpads = [[(ta("qpad%d_%d" % (e, i), [128, 2, 128], BF16), "qpad%d_%d" % (e, i)) for e in range(2)] for i in range(2)]
        PT = [ta("PT1_%d" % i, [128, 512], BF16) for i in range(5)]
        dtots = [(ta("dtot1_%d" % i, [128, 512], F32), "dtot1_%d" % i) for i in range(1)]
        assert self.cur <= self.hi - 8 * T_X * 4, "layer1 overflow %d" % (self.cur - (self.hi - 8 * T_X * 4))
        for i in range(2):
            for e_ in range(2):
                q_, k_ = qpads[i][e_]
                self.P.add("pool", lambda e, q_=q_: e.memset(q_[:], 0.0), writes=[k_])
        self.rot = [4, 5, 6, 7]
        accs = [((self.pb[0], "pb0"), (self.pb[1], "pb1")), ((self.pb[2], "pb2"), (self.pb[3], "pb3"))]
        z256 = self.cbf[:, 4:6, :].rearrange("p a n -> p (a n)")
        z512 = self.cbf[:, 4:8, :].rearrange("p a n -> p (a n)")
        hkey = lambda t0: "hT1_%d" % (t0 // 512)
        pti = [0]
        tglob = [0]
        src = self.odin_d.ap().rearrange("(c p) n -> p c n", p=128)

        def load_qkv(hg_):
            self.load_w(wq[:], "wq", src[:, :, hg_ * 256:(hg_ + 1) * 256])
            self.load_w(wk[:], "wk", src[:, :, D + hg_ * 256:D + (hg_ + 1) * 256])
            self.load_w(wv[:], "wv", src[:, :, 2 * D + hg_ * 256:2 * D + (hg_ + 1) * 256])

        load_qkv(0)
        for hg in range(4):
            self.load_w(wo[:], "wo", self.odout_d.ap().rearrange("(c p) n -> p c n", p=128)[:, hg * 2:hg * 2 + 2, :])
            for dl in range(6):
                for kr in range(2):
                    for qr in range(2):
                        dr_idx = 2 * (dl - 2) + kr - qr + 7
                        srcb = AP(self.rb_d, hg * 4 * 465 + dr_idx * 31 - 48, [[1, 64], [465, 4], [1, 64]])
                        self.dma("pool", Tt[qr * 64:(qr + 1) * 64, dl, :, kr * 64:(kr + 1) * 64], srcb, writes=["Tt%d" % dl])
            for dl in range(5):
                b, bk = self.bank()
                for h4 in range(4):
                    self.mm(b[:, h4 * 128:(h4 + 1) * 128], Tt[:, dl, h4, :], self.rblk, True, True, ["Tt%d" % dl, "c_bf"], [bk])
                self.tt("dve", BMI[:, dl, :, :], b[:, 0:512].rearrange("p (h n) -> p h n", n=128),
                        namask[:, 2, dl, :].unsqueeze(1).broadcast_to([128, 4, 128]), ALU.add, [bk, "namask"], ["BMI"])
            self.rot = [0, 1, 2, 3, 4, 5, 6, 7]
            for (t0, N) in groups(NK):
                jobs = []
                for ch in range(2):
                    def projk(ch=ch, t0=t0, N=N):
                        b, bk = self.bank()
                        for c in range(8):
                            self.mm(b[:, 0:N], wk[:, c, ch * 128:(ch + 1) * 128], hT[:, c, t0:t0 + N], c == 0, c == 7, ["wk", hkey(t0)], [bk])
                        return b, bk
                    jobs.append(dict(proj=projk, N=N, gcol=self.gq[:, 3:4], out=KTh[:, ch, t0:t0 + N], okey="KTh", rope=None))
                    if t0 + N <= T_OWN:
                        def projq(ch=ch, t0=t0, N=N):
                            b, bk = self.bank()
                            for c in range(8):
                                self.mm(b[:, 0:N], wq[:, c, ch * 128:(ch + 1) * 128], hT[:, c, t0:t0 + N], c == 0, c == 7, ["wq", hkey(t0)], [bk])
                            return b, bk
                        jobs.append(dict(proj=projq, N=N, gcol=self.gq[:, 2:3], out=QTh[:, ch, t0:t0 + N], okey="QTh", rope=None))
                self.qk_pipeline(jobs, tmpq)
                for t in range(N // 128):
                    b, bk = self.bank()
                    for c in range(8):
                        self.mm(b[:, 0:256], hT[:, c, t0 + t * 128:t0 + (t + 1) * 128], wv[:, c, :], c == 0, c == 7, ["wv", hkey(t0)], [bk])
                    self.evac(Vh[:, t0 // 128 + t, :], b[:, 0:256], [bk], ["Vh"])
            self.rot = [4, 5, 6, 7]
            if hg + 1 < 4:
                load_qkv(hg + 1)
            steps = []
            for qt in range(T_OWN // 128):
                kts = [(qt + d_, d_ + 2) for d_ in range(-2, 4 if qt == 0 else 3) if qt + d_ >= 0] + \
                      [(T_X // 128, None), (T_X // 128 + 1, None)]
                for i, (kt, dl) in enumerate(kts):
                    steps.append((qt, i, len(kts), kt, dl))

            def s_stage(st):
                qt, i, n, kt, dl = st
                tq_ = tglob[0] + qt
                qp = qpads[tq_ % 2]
                (pbO, ok_), (pbD, dk_) = accs[tq_ % 2]
                if i == 0:
                    for e_ in range(2):
                        rows = slice(e_ * 64, e_ * 64 + 64)
                        self.copy("pool", qp[e_][0][rows, :, :], QTh[rows, :, qt * 128:(qt + 1) * 128], ["QTh"], [qp[e_][1]])
                    self.mm(pbO[:, 0:256], self.zerob, z256, True, False, ["c_bf"], [ok_])
                    self.mm(pbD[:, 0:512], self.zerob, z512, True, False, ["c_bf"], [dk_])
                b, bk = self.bank()
                first = True
                if dl is not None:
                    if qt >= 2:
                        self.mm(b[:, 0:512], self.identb, BMI[:, dl, :, :].rearrange("p h n -> p (h n)"), True, False, ["BMI", "c_bf"], [bk])
                        first = False
                    else:
                        self.mm(b[:, 0:512], self.identb, namask[:, qt, dl, :].unsqueeze(1).broadcast_to([128, 4, 128]), True, False,
                                ["namask", "c_bf"], [bk])
                        for h4 in range(4):
                            self.mm(b[:, h4 * 128:(h4 + 1) * 128], Tt[:, dl, h4, :], self.rblk, False, False, ["Tt%d" % dl, "c_bf"], [bk])
                        first = False
                for h4 in range(4):
                    ch, e_ = h4 // 2, h4 % 2
                    self.mm(b[:, h4 * 128:(h4 + 1) * 128], KTh[:, ch, kt * 128:(kt + 1) * 128], qp[e_][0][:, ch, :], first, True,
                            ["KTh", qp[e_][1]], [bk])
                pt = PT[pti[0] % 5]
                pk = "PT1_%d" % (pti[0] % 5)
                pti[0] += 1
                self.act(pt[:], b[:, 0:512], AF.Exp, [bk], [pk])
                return (pt, pk)

            def p_stage(st, pt, pk):
                qt, i, n, kt, dl = st
                tq_ = tglob[0] + qt
                (pbO, ok_), (pbD, dk_) = accs[tq_ % 2]
                for h4 in range(4):
                    ch, e_ = h4 // 2, h4 % 2
                    rows = slice(e_ * 64, e_ * 64 + 64)
                    self.mm(pbO[rows, ch * 128:(ch + 1) * 128], Vh[:, kt, h4 * 64:(h4 + 1) * 64], pt[:, h4 * 128:(h4 + 1) * 128],
                            False, True, [pk, "Vh"], [ok_])
                self.mm(pbD[:, 0:512], self.onesb, pt[:], False, True, [pk, "c_bf"], [dk_])
                if i == n - 1:
                    dtot, dtk = dtots[0]
                    ATT, ak = ATTs[(qt // 4) % 2]
                    tq = qt % 4
                    self.act(dtot[:], pbD[:, 0:512], AF.Ln, [dk_], [dtk])
                    self.act(dtot[:], dtot[:], AF.Exp, [dtk], [dtk], scale=-1.0)
                    dv = dtot[:].rearrange("p (c e n) -> p c e n", e=2, n=128)
                    for e_ in range(2):
                        rows = slice(e_ * 64, e_ * 64 + 64)
                        self.tt("dve", ATT[rows, :, tq * 128:(tq + 1) * 128], pbO[rows, 0:256].rearrange("p (j n) -> p j n", n=128),
                                dv[rows, :, e_, :], ALU.mult, [ok_, dtk], [ak])
                    if tq == 3:
                        t0 = (qt // 4) * 512
                        for oc in range(8):
                            b, bk = self.bank()
                            for ch in range(2):
                                self.mm(b[:, 0:512], wo[:, ch, oc * 128:(oc + 1) * 128], ATT[:, ch, 0:512], ch == 0, ch == 1, ["wo", ak], [bk])
                            xo = self.XT[:, oc, t0:t0 + 512]
                            xk = "XT%d" % (t0 // 512)
                            self.stt("dve", xo, b[:, 0:512], self.mod(l, 2, 0, oc), xo, ALU.mult, ALU.add, [bk, "modT", xk], [xk])

            if "noattn" in self.flags:
                steps = []
            if "few" in self.flags:
                steps = steps[:self.nstep]
            pend = []
            for st in steps:
                cur = s_stage(st)
                pend.append((st, cur[0], cur[1]))
                if len(pend) > 2:
                    p_stage(*pend.pop(0))
            for pp in pend:
                p_stage(*pp)
            tglob[0] += T_OWN // 128
            if "onehg" in self.flags:
                break
        self.rot = [2, 3, 4, 5, 6, 7]

    def finish(self, final=True):
        scr = self.scr
        self.P.barrier(lambda e: e.memset(scr[0:1, 0:8], 0.0))
        self.cur = self.pers
        ta = self.talloc
        if self.debug and hasattr(self, "XT"):
            self.dma("sp", self.dbgx_d.ap(), self.XT[:], reads=["XT%d" % i for i in range(5)])
            self.dma("sp", self.dbgy_d.ap(), self.YT[:], reads=["YT"])
        if not hasattr(self, "XT"):
            return
        if getattr(self, "out_done", False):
            return
        ob = [ta("ob%d" % i, [128, D], F32) for i in range(2)]
        self.emit_out(range(T_OWN // 128), ob)

    def emit_out(self, tiles, ob):
        for t in tiles:
            o_, ok = ob[t % 2], "ob%d" % (t % 2)
            for half in range(2):
                b, bk = self.bank()
                for cc in range(4):
                    c = half * 4 + cc
                    self.tr(b[:, cc * 128:(cc + 1) * 128], self.XT[:, c, t * 128:(t + 1) * 128], ["XT%d" % (t // 4)], [bk])
                self.evac(o_[:, half * 512:(half + 1) * 512], b[:, 0:512], [bk], [ok])
            self.dma("sp", self.out_d.ap()[t * 128:(t + 1) * 128, :], o_[:], reads=[ok])


_CONST_CACHE = {}


def _bf(a):
    return np.ascontiguousarray(a.astype(ml_dtypes.bfloat16))


def host_consts(par):
    if par in _CONST_CACHE:
        return _CONST_CACHE[par]
    c = {}
    c["c_identf"] = np.eye(128, dtype=np.float32)
    psw = np.zeros((128, 128), np.float32)
    for m in range(128):
        i = m % 32
        partner = m + 16 if i < 16 else m - 16
        psw[partner, m] = 1.0
    c["c_pswap"] = psw
    cb = np.zeros((128, 10, 128), np.float32)
    cb[:, 0, :] = 1.0 / 1024.0
    cb[0:64, 1, 0:64] = 1.0 / 64.0
    cb[64:128, 1, 64:128] = 1.0 / 64.0
    cb[:, 2, :] = np.eye(128)
    for blk in range(2):
        for u in range(64):
            cb[blk * 64 + u, 3, blk * 64 + 63 - u] = 1.0
    cb[:, 5, :] = 1.0
    cc = np.arange(128)
    ang = 2.0 * np.pi * ((cc[:, None] * cc[None, :]) % 128) / 128.0
    cb[:, 6, :] = np.cos(ang) / np.sqrt(128.0)
    cb[:, 7, :] = -np.sin(ang) / np.sqrt(128.0)
    j = np.arange(128)[:, None]
    q = np.arange(128)[None, :]
    cb[:, 8, :] = np.where(j >= q, 0.0, NEG)
    cb[:, 9, :] = np.where(j <= q, 0.0, NEG)
    c["c_bf"] = _bf(cb)
    n = np.arange(256)
    a2 = 2.0 * np.pi * ((n[:, None] * n[None, :]) % 256) / 256.0
    C2 = (np.cos(a2) / 16.0).reshape(2, 128, 256)
    S2 = (np.sin(a2) / 16.0).reshape(2, 128, 256)
    cx = np.stack([C2.transpose(1, 0, 2), S2.transpose(1, 0, 2)], axis=1)
    c["c_ctxdft"] = _bf(cx)
    loc = np.arange(T_ALL)
    glob = loc if par == 0 else (T_ALL - 1 - loc)
    inv = (np.float32(10000.0) ** (-np.arange(16, dtype=np.float32) / np.float32(16))).astype(np.float32)
    g_ = glob[:T_A]
    row = (g_ // 64).astype(np.float32)
    col = (g_ % 64).astype(np.float32)
    ar = (row[None, :] * inv[:, None]).astype(np.float32)
    ac = (col[None, :] * inv[:, None]).astype(np.float32)
    cos64 = np.concatenate([np.cos(ar), np.cos(ar), np.cos(ac), np.cos(ac)], axis=0)
    sin64 = np.concatenate([-np.sin(ar), np.sin(ar), -np.sin(ac), np.sin(ac)], axis=0)
    c["c_rope"] = np.ascontiguousarray(np.stack([np.concatenate([cos64, cos64], 0), np.concatenate([sin64, sin64], 0)], 0).astype(np.float32))
    gk = glob[:T_X].astype(np.int64)
    gn = glob.astype(np.int64)
    ph = (gn[:, None] * gk[None, :]) % T_ALL
    angL = (2.0 * np.pi / T_ALL) * ph
    c["c_dftc"] = _bf(np.cos(angL) / 64.0)
    c["c_dfts"] = _bf(np.sin(angL) / 64.0)
    nm = np.zeros((128, 3, 6, 128), np.float32)
    for cls, qt in enumerate((0, 1, 8)):
        for dl in range(6):
            kt = qt + dl - 2
            if kt < 0:
                nm[:, cls, dl, :] = NEG
                continue
            kg = glob[kt * 128 + np.arange(128)]
            qg = glob[qt * 128 + np.arange(128)]
            kr, kc = kg // 64, kg % 64
            qr, qc = qg // 64, qg % 64
            r0 = np.clip(qr - 4, 0, 56)
            c0 = np.clip(qc - 8, 0, 48)
            ok = (kr[:, None] >= r0[None, :]) & (kr[:, None] < r0[None, :] + 8) & (kc[:, None] >= c0[None, :]) & (kc[:, None] < c0[None, :] + 16)
            nm[:, cls, dl, :] = np.where(ok, 0.0, NEG)
    c["c_namask"] = _bf(nm)
    _CONST_CACHE[par] = c
    return c


_NC_CACHE = {}


def get_nc(stop_after=None, debug=False, only=None, flags=()):
    key = (stop_after, debug, only, tuple(flags))
    if key not in _NC_CACHE:
        bld = Builder(stop_after=stop_after, debug=debug, only=only, flags=flags)
        _NC_CACHE[key] = bld.build()
    return _NC_CACHE[key]


def make_in_maps(inputs):
    f32 = lambda a: np.ascontiguousarray(np.asarray(a, dtype=np.float32))
    x = f32(inputs["x"])
    c = f32(inputs["c"])
    ctx = f32(inputs["ctx"])
    c_ctx = f32(inputs["c_ctx"])
    ev_in = f32(inputs["ev_w_in"])[0]
    ev_out = f32(inputs["ev_w_out"])[0]
    hp = [0, 4, 1, 5, 2, 6, 3, 7]
    qcols = np.concatenate([512 + h * 64 + np.arange(64) for h in hp])
    cols = np.concatenate([np.arange(512), qcols, np.arange(1024, 1280)])
    ev_in_p = np.ascontiguousarray(ev_in[:, cols])
    rows = np.concatenate([np.arange(512), qcols])
    ev_out_p = np.ascontiguousarray(ev_out[rows, :])
    rb = f32(inputs["od_rel_bias"])[0]
    shared = {
        "ada_w": f32(inputs["ada_w"]), "ada_b": f32(inputs["ada_b"]),
        "norm1_g": f32(inputs["norm1_g"]), "norm2_g": f32(inputs["norm2_g"]),
        "ffn_w_gate": f32(inputs["ffn_w_gate"]), "ffn_w_up": f32(inputs["ffn_w_up"]), "ffn_w_down": f32(inputs["ffn_w_down"]),
        "ev_w_in": ev_in_p, "ev_w_out": ev_out_p,
        "ev_q_norm": f32(inputs["ev_q_norm"])[0], "ev_k_norm": f32(inputs["ev_k_norm"])[0], "ev_sink": f32(inputs["ev_sink"])[0],
        "od_w_in": f32(inputs["od_w_in"])[0], "od_w_out": f32(inputs["od_w_out"])[0],
        "od_q_norm": f32(inputs["od_q_norm"])[0], "od_k_norm": f32(inputs["od_k_norm"])[0],
    }
    pad = np.zeros(64, np.float32)
    rbs = [np.concatenate([rb.reshape(-1), pad]), np.concatenate([rb[:, ::-1, ::-1].reshape(-1), pad])]
    in_maps = []
    for cid in range(8):
        b, par = cid // 2, cid % 2
        m = dict(shared)
        m["x_loc"] = np.ascontiguousarray(x[b] if par == 0 else x[b][::-1])
        m["ctx_b"] = np.ascontiguousarray(ctx[b])
        m["cvec"] = np.ascontiguousarray(np.stack([c[b], c_ctx], 0))
        m["rel_bias"] = rbs[par]
        m.update(host_consts(par))
        in_maps.append(m)
    return in_maps


def assemble(results):
    out = np.zeros((4, T_ALL, D), np.float32)
    for cid in range(8):
        b, par = cid // 2, cid % 2
        o = np.asarray(results[cid]["out_loc"], dtype=np.float32)
        if par == 0:
            out[b, :T_OWN] = o
        else:
            out[b, T_OWN:] = o[::-1]
    return out


def kernel(**inputs):
    nc = get_nc()
    in_maps = make_in_maps(inputs)
    res = run_bass_kernel_spmd(nc, in_maps, core_ids=list(range(8)))
    return assemble(res.results)
```

```python
import numpy as np
import ml_dtypes
import concourse.bass as bass
import concourse.mybir as mybir
from concourse.bass_utils import run_bass_kernel_spmd
from concourse.ap import AP

F32 = mybir.dt.float32
BF16 = mybir.dt.bfloat16
ALU = mybir.AluOpType
AF = mybir.ActivationFunctionType

NDMA_SEM = 48
D = 1024
DFF = 2816
NEG = -30000.0
EPS = 1e-6
T_ALL = 4096
T_X = 2304
T_A = 2560
T_OWN = 2048
NCTX = 256


class _Op:
    __slots__ = ("id", "eng", "fn", "deps", "is_dma", "sem_i", "sem_val", "signal", "count")


class Prog:
    ENGS = ("pe", "act", "dve", "pool", "sp")

    def __init__(self):
        self.ops = []
        self.lw = {}
        self.rd = {}
        self.dma_rr = 0
        self.dma_sem_uses = [0] * NDMA_SEM
        self.dma_sem_last = [None] * NDMA_SEM
        self.last_eng = {}
        self.bar = None
        self.dopen = set()
        self.saved = {}
        self.strict = True

    def add(self, eng, fn, reads=(), writes=(), dma=False):
        op = _Op()
        op.id = len(self.ops)
        op.eng = eng
        op.fn = fn
        op.is_dma = dma
        op.signal = False
        op.count = 0
        deps = {}
        if self.bar is not None:
            deps[self.bar] = True
        for k in reads:
            for w in self.lw.get(k, ()):
                deps[w] = True
            self.dopen.discard(k)
        cow = set()
        for k in writes:
            if dma and k in self.dopen:
                cow.add(k)
                for r in self.saved.get(k, ()):
                    deps.setdefault(r, False)
                continue
            for w in self.lw.get(k, ()):
                deps.setdefault(w, False)
            for r in self.rd.get(k, ()):
                deps.setdefault(r, False)
        if dma:
            i = self.dma_rr % NDMA_SEM
            self.dma_rr += 1
            prev = self.dma_sem_last[i]
            if prev is not None:
                deps.setdefault(prev, False)
            self.dma_sem_uses[i] += 1
            op.sem_i = i
            op.sem_val = 16 * self.dma_sem_uses[i]
            self.dma_sem_last[i] = op.id
        op.deps = deps
        for k in reads:
            self.rd.setdefault(k, []).append(op.id)
        for k in writes:
            if k in cow:
                self.lw[k].append(op.id)
                continue
            self.saved[k] = list(self.lw.get(k, ())) + list(self.rd.get(k, ()))
            self.lw[k] = [op.id]
            self.rd[k] = []
            if dma:
                self.dopen.add(k)
            else:
                self.dopen.discard(k)
        self.ops.append(op)
        if not dma:
            self.last_eng[eng] = op.id
        return op

    def barrier(self, fn):
        op = self.add("pool", fn)
        for e, i in self.last_eng.items():
            if i != op.id:
                op.deps[i] = True
        for i in self.dma_sem_last:
            if i is not None:
                op.deps[i] = True
        self.bar = op.id
        self.lw = {}
        self.rd = {}
        self.dopen = set()
        self.saved = {}

    def emit(self, block, eng_sems, dma_sems):
        ops = self.ops

        strict = self.strict

        def skip(op, dop, raw):
            if op.is_dma or dop.is_dma or dop.eng != op.eng:
                return False
            return op.eng == "pe" or (not raw and not strict)

        for op in ops:
            for d, raw in op.deps.items():
                dop = ops[d]
                if dop.is_dma or skip(op, dop, raw):
                    continue
                dop.signal = True
        cnt = {e: 0 for e in self.ENGS}
        for op in ops:
            if op.signal and not op.is_dma:
                cnt[op.eng] += 1
                op.count = cnt[op.eng]
        per_eng = {e: [o for o in ops if o.eng == e] for e in self.ENGS}

        def run(e, engine):
            known = {}
            for op in per_eng[e]:
                needs = {}
                for d, raw in op.deps.items():
                    dop = ops[d]
                    if dop.is_dma:
                        key = ("d", dop.sem_i)
                        val = dop.sem_val
                    else:
                        if skip(op, dop, raw):
                            continue
                        key = ("e", dop.eng)
                        val = dop.count
                    if needs.get(key, 0) < val:
                        needs[key] = val
                for key, val in needs.items():
                    if known.get(key, 0) >= val:
                        continue
                    known[key] = val
                    sem = dma_sems[key[1]] if key[0] == "d" else eng_sems[key[1]]
                    engine.wait_ge(sem, val)
                ins = op.fn(engine)
                if op.is_dma:
                    ins.then_inc(dma_sems[op.sem_i], 16)
                elif op.signal:
                    ins.then_inc(eng_sems[e], 1)
            if e == "sp":
                for i in range(NDMA_SEM):
                    if self.dma_sem_uses[i] > 0:
                        engine.wait_ge(dma_sems[i], 16 * self.dma_sem_uses[i])

        block.tensor(lambda eng: run("pe", eng))
        block.scalar(lambda eng: run("act", eng))
        block.vector(lambda eng: run("dve", eng))
        block.gpsimd(lambda eng: run("pool", eng))
        block.sync(lambda eng: run("sp", eng))


def groups(n, g=512):
    out = []
    s = 0
    while s < n:
        out.append((s, min(g, n - s)))
        s += g
    return out


class Builder:
    def __init__(self, stop_after=None, debug=False, only=None, flags=()):
        self.only = only
        self.flags = set(f for f in flags if not f.startswith('n='))
        self.nstep = ([int(f[2:]) for f in flags if f.startswith('n=')] + [8])[0]
        self.nc = bass.Bass("TRN2", target_bir_lowering=False)
        self.P = Prog()
        self.stop_after = stop_after
        self.debug = debug
        nc = self.nc
        self.lo = ((nc.sbuf_base + 63) // 64) * 64
        self.hi = nc.sbuf_top
        self.pers = self.lo
        self.cur = None
        self.uid = 0
        self.rr = 0
        self.evr = 0
        self.rot = [2, 3, 4, 5, 6]
        self.nmi = 0
        self.tfi = 0
        self.qni = 0

    def _alloc(self, ptr, name, shape, dt):
        esz = 4 if dt == F32 else 2
        n = 1
        for s in shape[1:]:
            n *= s
        nbytes = ((n * esz + 63) // 64) * 64
        self.uid += 1
        t = self.nc.alloc_sbuf_tensor_at("%s_%d" % (name, self.uid), list(shape), dt, offset=ptr)
        return t, ptr + nbytes

    def palloc(self, name, shape, dt):
        assert self.cur is None
        t, self.pers = self._alloc(self.pers, name, shape, dt)
        assert self.pers <= self.hi, "persistent overflow"
        return t

    def talloc(self, name, shape, dt):
        t, self.cur = self._alloc(self.cur, name, shape, dt)
        assert self.cur <= self.hi, "phase overflow %s %d" % (name, self.cur - self.hi)
        return t

    def phase(self):
        scr = self.scr
        self.P.barrier(lambda e: e.memset(scr[0:1, 0:8], 0.0))
        self.cur = self.pers

    def dma(self, eng, out, in_, reads=(), writes=()):
        self.P.add(eng, lambda e: e.dma_start(out=out, in_=in_), reads=reads, writes=writes, dma=True)

    def mm(self, out, lhsT, rhs, start, stop, reads, writes):
        self.P.add("pe", lambda e: e.matmul(out, lhsT=lhsT, rhs=rhs, start=start, stop=stop), reads=reads, writes=writes)

    def tr(self, out, in_, reads, writes):
        ident = self.identf
        k = in_.shape[0]
        self.P.add("pe", lambda e: e.transpose(out, in_, ident[0:k, 0:k]), reads=list(reads) + ["c_id"], writes=writes)

    def act(self, out, in_, func, reads, writes, bias=None, scale=None):
        kw = {}
        if bias is not None:
            kw["bias"] = bias
        if scale is not None:
            kw["scale"] = scale
        self.P.add("act", lambda e: e.activation(out=out, in_=in_, func=func, **kw), reads=reads, writes=writes)

    def tt(self, eng, out, in0, in1, op, reads, writes):
        self.P.add(eng, lambda e: e.tensor_tensor(out=out, in0=in0, in1=in1, op=op), reads=reads, writes=writes)

    def stt(self, eng, out, in0, scalar, in1, op0, op1, reads, writes):
        self.P.add(eng, lambda e: e.scalar_tensor_tensor(out=out, in0=in0, scalar=scalar, in1=in1, op0=op0, op1=op1),
                   reads=reads, writes=writes)

    def ts(self, eng, out, in0, s1, s2, op0, op1, reads, writes):
        if s2 is None:
            self.P.add(eng, lambda e: e.tensor_scalar(out=out, in0=in0, scalar1=s1, scalar2=None, op0=op0), reads=reads, writes=writes)
        else:
            self.P.add(eng, lambda e: e.tensor_scalar(out=out, in0=in0, scalar1=s1, scalar2=s2, op0=op0, op1=op1), reads=reads, writes=writes)

    def copy(self, eng, out, in_, reads, writes):
        if eng == "act":
            self.act(out, in_, AF.Copy, reads, writes)
        else:
            self.P.add(eng, lambda e: e.tensor_copy(out=out, in_=in_), reads=reads, writes=writes)

    def evac(self, out, in_, reads, writes):
        self.evr += 1
        self.copy("act" if self.evr % 2 else "dve", out, in_, reads, writes)

    def dump(self, name, t, key):
        if not self.debug:
            return
        shape = list(t.shape)
        d = self.nc.dram_tensor("dbg_" + name, shape, t.dtype, kind="ExternalOutput")
        self.dma("sp", d.ap(), t[:], reads=[key])

    def bank(self):
        self.rr += 1
        i = self.rot[self.rr % len(self.rot)]
        return self.pb[i], "pb%d" % i

    def build(self):
        nc = self.nc
        P = self.P
        dr = lambda name, shape, dt, kind="ExternalInput": nc.dram_tensor(name, list(shape), dt, kind=kind)
        self.x_d = dr("x_loc", [T_ALL, D], F32)
        self.ctx_d = dr("ctx_b", [NCTX, D], F32)
        self.cvec_d = dr("cvec", [2, D], F32)
        self.ada_w_d = dr("ada_w", [2, D, 6 * D], F32)
        self.ada_b_d = dr("ada_b", [2, 6 * D], F32)
        self.n1_d = dr("norm1_g", [2, D], F32)
        self.n2_d = dr("norm2_g", [2, D], F32)
        self.wg_d = dr("ffn_w_gate", [2, D, DFF], F32)
        self.wu_d = dr("ffn_w_up", [2, D, DFF], F32)
        self.wd_d = dr("ffn_w_down", [2, DFF, D], F32)
        self.evin_d = dr("ev_w_in", [D, 1280], F32)
        self.evout_d = dr("ev_w_out", [D, D], F32)
        self.evq_d = dr("ev_q_norm", [64], F32)
        self.evk_d = dr("ev_k_norm", [64], F32)
        self.sink_d = dr("ev_sink", [8], F32)
        self.odin_d = dr("od_w_in", [D, 3 * D], F32)
        self.odout_d = dr("od_w_out", [D, D], F32)
        self.odq_d = dr("od_q_norm", [64], F32)
        self.odk_d = dr("od_k_norm", [64], F32)
        self.rb_d = dr("rel_bias", [16 * 15 * 31 + 64], F32)
        self.cidf_d = dr("c_identf", [128, 128], F32)
        self.cpsw_d = dr("c_pswap", [128, 128], F32)
        self.cbf_d = dr("c_bf", [128, 10, 128], BF16)
        self.cctx_d = dr("c_ctxdft", [128, 2, 2, 256], BF16)
        self.cnam_d = dr("c_namask", [128, 3, 6, 128], BF16)
        self.crope_d = dr("c_rope", [2, 128, T_A], F32)
        self.cdft_d = [dr("c_dftc", [T_ALL, T_X], BF16), dr("c_dfts", [T_ALL, T_X], BF16)]
        self.out_d = dr("out_loc", [T_OWN, D], F32, kind="ExternalOutput")
        if self.debug:
            self.dbgx_d = dr("dbg_x", [128, 8, T_X], F32, kind="ExternalOutput")
            self.dbgy_d = dr("dbg_y", [128, 8, NCTX], F32, kind="ExternalOutput")
            self.dbga_d = dr("dbg_a", [128, 8, T_X], F32, kind="ExternalOutput")

        self.pb = [nc.alloc_psum_tensor("pb%d" % i, [128, 512], F32) for i in range(8)]
        self.eng_sems = {e: nc.alloc_semaphore("s_" + e) for e in Prog.ENGS}
        self.dma_sems = [nc.alloc_semaphore("d%d" % i) for i in range(NDMA_SEM)]

        self.setup()
        done = False
        for name, fn in [("A", self.layer0_A), ("B", self.layer0_B), ("C", self.layer0_C), ("D", lambda: self.ffn(0)),
                         ("E", self.layer1_E), ("F", lambda: self.ffn(1))]:
            if self.only is not None and name not in self.only:
                continue
            if not hasattr(self, "XT") and name in ("D", "E", "F"):
                self.ada_tick(100)
                self.rot = [2, 3, 4, 5, 6, 7]
                self.pers -= 2 * 4096
                self.XT = self.nc.alloc_sbuf_tensor_at("XT_res", [128, 8, T_X], F32, offset=self.hi - 8 * T_X * 4)
            fn()
            if self.stop_after == name:
                done = True
                break
        self.finish(final=not done)
        with nc.Block() as block:
            P.emit(block, self.eng_sems, self.dma_sems)
        return nc

    def setup(self):
        nc = self.nc
        pa = self.palloc
        self.scr = pa("scr", [128, 16], F32)
        self.identf = pa("identf", [128, 128], F32)
        self.pswap = pa("pswap", [128, 128], F32)
        self.cbf = pa("cbf", [128, 10, 128], BF16)
        self.cctx = pa("cctx", [128, 2, 2, 256], BF16)
        self.vecT = pa("vecT", [128, 48], F32)
        self.biasT = pa("biasT", [128, 96], F32)
        self.modT = pa("modT", [128, 2, 48, 2], F32)
        self.A1 = pa("A1", [128, 2, 2, 8], F32)
        self.A2 = pa("A2", [128, 2, 2, 8], F32)
        self.sT = pa("sT", [128, 2, 8], BF16)
        self.gq = pa("gq", [128, 4], F32)
        self.sinkbc = pa("sinkbc", [128, 4, 128], F32)
        self.skt = pa("skt", [128, 4], F32)
        self.zf = pa("zf", [128, 128], F32)
        self.YT = pa("YT", [128, 8, NCTX], F32)
        self.dma("sp", self.identf[:], self.cidf_d.ap(), writes=["c_id"])
        self.dma("sp", self.pswap[:], self.cpsw_d.ap(), writes=["c_ps"])
        self.dma("sp", self.cbf[:], self.cbf_d.ap(), writes=["c_bf"])
        self.dma("sp", self.cctx[:], self.cctx_d.ap(), writes=["c_cx"])
        self.P.add("pool", lambda e: e.memset(self.zf[:], 0.0), writes=["zf"])
        self.P.add("pool", lambda e: e.memset(self.scr[:], 0.0), writes=["scr"])
        self.ones1024 = self.cbf[:, 0, :]
        self.blk64 = self.cbf[:, 1, :]
        self.identb = self.cbf[:, 2, :]
        self.rblk = self.cbf[:, 3, :]
        self.zerob = self.cbf[:, 4, :]
        self.onesb = self.cbf[:, 5, :]
        self.ccos = self.cbf[:, 6, :]
        self.cnsin = self.cbf[:, 7, :]
        self.mprev = self.cbf[:, 8, :]
        self.mnext = self.cbf[:, 9, :]
        self.adabuf = None
        self.cur = self.pers + 2 * 4096
        ta = self.talloc
        rows1 = ta("rows1", [48, 128], F32)
        rows2 = ta("rows2", [96, 128], F32)
        self.dma("sp", rows1[0:16, :], self.n1_d.ap().rearrange("l (c p) -> (l c) p", p=128), writes=["rows1"])
        self.dma("sp", rows1[16:32, :], self.n2_d.ap().rearrange("l (c p) -> (l c) p", p=128), writes=["rows1"])
        self.dma("sp", rows1[32:48, :], self.cvec_d.ap().rearrange("s (c p) -> (s c) p", p=128), writes=["rows1"])
        self.dma("sp", rows2[:, :], self.ada_b_d.ap().rearrange("l (c p) -> (l c) p", p=128), writes=["rows2"])
        b, bk = self.bank()
        self.tr(b[:, 0:48], rows1[:, :], ["rows1"], [bk])
        self.copy("dve", self.vecT[:], b[:, 0:48], [bk], ["vecT"])
        b, bk = self.bank()
        self.tr(b[:, 0:96], rows2[:, :], ["rows2"], [bk])
        self.copy("dve", self.biasT[:], b[:, 0:96], [bk], ["biasT"])
        self.act(self.sT[:].rearrange("p s c -> p (s c)"), self.vecT[:, 32:48], AF.Silu, ["vecT"], ["sT"])
        for i, (d_, sc) in enumerate([(self.evq_d, 0.125), (self.evk_d, 1.0), (self.odq_d, 0.125), (self.odk_d, 1.0)]):
            src = d_.ap().rearrange("(d u) -> d u", u=1)
            self.dma("sp", self.gq[0:64, i:i + 1], src, writes=["gq"])
            self.dma("sp", self.gq[64:128, i:i + 1], src, writes=["gq"])
        self.ts("dve", self.gq[:, 0:1], self.gq[:, 0:1], 0.125, None, ALU.mult, None, ["gq"], ["gq"])
        self.ts("dve", self.gq[:, 2:3], self.gq[:, 2:3], 0.125, None, ALU.mult, None, ["gq"], ["gq"])
        self.dma("sp", self.skt[0:64, :], AP(self.sink_d, 0, [[0, 64], [1, 4]]), writes=["skt"])
        self.dma("sp", self.skt[64:128, :], AP(self.sink_d, 4, [[0, 64], [1, 4]]), writes=["skt"])
        self.act(self.skt[:], self.skt[:], AF.Exp, ["skt"], ["skt"])
        for j in range(4):
            self.ts("dve", self.sinkbc[:, j, :], self.zf[:], self.skt[:, j:j + 1], None, ALU.add, None, ["zf", "skt"], ["sinkbc"])
        self.adabuf = [self.palloc_late("adabuf%d" % i, [128, 8, 256], BF16) for i in range(2)]
        self.ada_list = [(l, sl) for l in range(2) for sl in range(24)]
        self.ada_pos = 0
        self.ada_issued = 0
        self.win = self.nc.alloc_sbuf_tensor_at("win_top", [128, 8, 1280], BF16, offset=self.hi - 20480)
        self.ada_issue()
        self.ada_tick(4)
        self.load_w(self.win[:], "win", self.evin_d.ap().rearrange("(c p) n -> p c n", p=128))
        self.ada_tick(4)

    def palloc_late(self, name, shape, dt):
        t, self.pers = self._alloc(self.pers, name, shape, dt)
        return t

    def ada_issue(self):
        if self.ada_issued >= len(self.ada_list):
            return
        l, sl = self.ada_list[self.ada_issued]
        i = self.ada_issued % 2
        src = self.ada_w_d.ap()[l].rearrange("(c p) n -> p c n", p=128)[:, :, sl * 256:(sl + 1) * 256]
        self.dma("pool", self.adabuf[i][:], src, writes=["adabuf%d" % i])
        self.ada_issued += 1

    def ada_tick(self, n=1):
        for _ in range(n):
            if self.ada_pos >= len(self.ada_list):
                return
            l, sl = self.ada_list[self.ada_pos]
            i = self.ada_pos % 2
            self.ada_pos += 1
            self.ada_issue()
            buf, bk = self.adabuf[i], "adabuf%d" % i
            pm, pmk = self.pb[7], "pb7"
            for o2 in range(2):
                oc = sl * 2 + o2
                for kc in range(8):
                    self.mm(pm[:, oc * 2:oc * 2 + 2], buf[:, kc, o2 * 128:(o2 + 1) * 128], self.sT[:, :, kc],
                            kc == 0, kc == 7, [bk, "sT"], [pmk])
            if sl == 7 or sl == 23:
                lo, hi = (0, 16) if sl == 7 else (16, 48)
                for s_ in range(2):
                    self.tt("dve", self.modT[:, l, lo:hi, s_], pm[:, 0:96].rearrange("p (o s) -> p o s", s=2)[:, lo:hi, s_],
                            self.biasT[:, l * 48 + lo:l * 48 + hi], ALU.add, [pmk, "biasT"], ["modT"])
                    if sl == 7:
                        self.stt("dve", self.A1[:, l, s_, :], self.modT[:, l, 8:16, s_], 1.0, self.vecT[:, l * 8:l * 8 + 8],
                                 ALU.add, ALU.mult, ["modT", "vecT"], ["A1"])
                    else:
                        self.stt("dve", self.A2[:, l, s_, :], self.modT[:, l, 32:40, s_], 1.0, self.vecT[:, 16 + l * 8:16 + l * 8 + 8],
                                 ALU.add, ALU.mult, ["modT", "vecT"], ["A2"])

    def mod(self, l, part, s, c):
        return self.modT[:, l, part * 8 + c, s:s + 1]

    def load_xT(self, src_rows, ntile, dst, dst_key, col0, xtoks):
        for t in range(ntile):
            xt, xk = xtoks[self.xti % len(xtoks)]
            self.xti += 1
            self.dma("sp", xt[:], src_rows[t * 128:(t + 1) * 128, :], writes=[xk])
            for half in range(2):
                b, bk = self.bank()
                for cc in range(4):
                    c = half * 4 + cc
                    self.tr(b[:, cc * 128:(cc + 1) * 128], xt[:, c * 128:(c + 1) * 128], [xk], [bk])
                o = dst[:, half * 4:half * 4 + 4, col0 + t * 128:col0 + (t + 1) * 128]
                self.evac(o, b[:, :].rearrange("p (c n) -> p c n", n=128), [bk], [dst_key])

    def norm_mod(self, xT, xkey, col0, N, A, Bf, hT, hkey, hcol0, tmp, part=None, state=None):
        sqbs, rss, tf = tmp
        if part in (None, 1):
            self.nmi += 1
            sqb, sqk = sqbs[self.nmi % len(sqbs)], "sqb%d" % (self.nmi % len(sqbs))
            rs, rk = rss[self.nmi % len(rss)], "rsn%d" % (self.nmi % len(rss))
            nsq = sqb.shape[1]
            b, bk = self.bank()
            for c0 in range(0, 8, nsq):
                for c in range(c0, c0 + nsq):
                    xi = xT[:, c, col0:col0 + N]
                    if c % 4 == 3:
                        self.tt("dve", sqb[:, c % nsq, 0:N], xi, xi, ALU.mult, [xkey], [sqk])
                    else:
                        self.act(sqb[:, c % nsq, 0:N], xi, AF.Square, [xkey], [sqk])
                for c in range(c0, c0 + nsq):
                    self.mm(b[:, 0:N], self.ones1024, sqb[:, c % nsq, 0:N], c == 0, c == 7, [sqk, "c_bf"], [bk])
            self.act(rs[:, 0:N], b[:, 0:N], AF.Ln, [bk], [rk], bias=EPS)
            self.act(rs[:, 0:N], rs[:, 0:N], AF.Exp, [rk], [rk], scale=-0.5)
            state = (rs, rk)
            if part == 1:
                return state
        rs, rk = state
        for c in range(8):
            self.tfi += 1
            t_ = tf[self.tfi % len(tf)]
            tk = "tf%d" % (self.tfi % len(tf))
            self.stt("dve", t_[:, 0:N], xT[:, c, col0:col0 + N], A(c), rs[:, 0:N], ALU.mult, ALU.mult,
                     [xkey, rk, "A1", "A2"], [tk])
            if part == 2 and c % 3 == 2:
                self.ts("dve", hT[:, c, hcol0:hcol0 + N], t_[:, 0:N], Bf(c), None, ALU.add, None, [tk, "modT"], [hkey])
            else:
                self.act(hT[:, c, hcol0:hcol0 + N], t_[:, 0:N], AF.Identity, [tk, "modT"], [hkey], bias=Bf(c))
        return None

    def norm_pipeline(self, calls):
        st = [None] * len(calls)
        if calls:
            st[0] = self.norm_mod(part=1, **calls[0])
        for i in range(len(calls)):
            if i + 1 < len(calls):
                st[i + 1] = self.norm_mod(part=1, **calls[i + 1])
            self.norm_mod(part=2, state=st[i], **calls[i])

    def qk_norm(self, praw, pk, N, gcol, out, okey, tmp, rope=None):
        sq1s, rss, qns, r1s = tmp
        self.qni += 1
        i = self.qni
        sq1, sk = sq1s[i % len(sq1s)], "sq1_%d" % (i % len(sq1s))
        rs, rk = rss[i % len(rss)], "rsq%d" % (i % len(rss))
        self.act(sq1[:, 0:N], praw[:, 0:N], AF.Square, [pk], [sk])
        b, bk = self.bank()
        self.mm(b[:, 0:N], self.blk64, sq1[:, 0:N], True, True, [sk, "c_bf"], [bk])
        self.act(rs[:, 0:N], b[:, 0:N], AF.Ln, [bk], [rk], bias=EPS)
        self.act(rs[:, 0:N], rs[:, 0:N], AF.Exp, [rk], [rk], scale=-0.5)
        if rope is None:
            self.stt("dve", out, praw[:, 0:N], gcol, rs[:, 0:N], ALU.mult, ALU.mult, [pk, rk, "gq"], [okey])
            return
        qn, qk_ = qns[i % len(qns)], "qn%d" % (i % len(qns))
        r1, r1k = r1s[i % len(r1s)], "r1_%d" % (i % len(r1s))
        cosT, sinT, rpk = rope
        self.stt("dve", qn[:, 0:N], praw[:, 0:N], gcol, rs[:, 0:N], ALU.mult, ALU.mult, [pk, rk, "gq"], [qk_])
        b2, bk2 = self.bank()
        self.mm(b2[:, 0:N], self.pswap[:], qn[:, 0:N], True, True, [qk_, "c_bf"], [bk2])
        self.tt("pool", r1[:, 0:N], qn[:, 0:N], cosT, ALU.mult, [qk_] + rpk, [r1k])
        self.tt("dve", rs[:, 0:N], b2[:, 0:N], sinT, ALU.mult, [bk2] + rpk, [rk])
        self.tt("dve", out, rs[:, 0:N], r1[:, 0:N], ALU.add, [rk, r1k], [okey])

    def qk_pipeline(self, jobs, tmp):
        sq1s, rss, qns, r1s = tmp
        st = []
        for j, jb in enumerate(jobs):
            d = dict(jb)
            d["sq1"], d["sk"] = sq1s[j % len(sq1s)], "sq1_%d" % (j % len(sq1s))
            d["rs"], d["rk"] = rss[j % len(rss)], "rsq%d" % (j % len(rss))
            if jb["rope"] is not None:
                d["qn"], d["qk"] = qns[j % len(qns)], "qn%d" % (j % len(qns))
                d["r1"], d["r1k"] = r1s[j % len(r1s)], "r1_%d" % (j % len(r1s))
            st.append(d)

        def sa(d):
            d["b"], d["bk"] = d["proj"]()

        def sb(d):
            N = d["N"]
            praw, pk, sq1, sk, rs, rk = d["b"], d["bk"], d["sq1"], d["sk"], d["rs"], d["rk"]
            self.act(sq1[:, 0:N], praw[:, 0:N], AF.Square, [pk], [sk])
            b, bk = self.bank()
            self.mm(b[:, 0:N], self.blk64, sq1[:, 0:N], True, True, [sk, "c_bf"], [bk])
            self.act(rs[:, 0:N], b[:, 0:N], AF.Ln, [bk], [rk], bias=EPS)
            self.act(rs[:, 0:N], rs[:, 0:N], AF.Exp, [rk], [rk], scale=-0.5)
            if d["rope"] is None:
                self.stt("dve", d["out"], praw[:, 0:N], d["gcol"], rs[:, 0:N], ALU.mult, ALU.mult, [pk, rk, "gq"], [d["okey"]])
            else:
                self.stt("dve", d["qn"][:, 0:N], praw[:, 0:N], d["gcol"], rs[:, 0:N], ALU.mult, ALU.mult, [pk, rk, "gq"], [d["qk"]])

        def sc(d):
            if d["rope"] is None:
                return
            N = d["N"]
            cosT, sinT, rpk = d["rope"]
            qn, qk_, r1, r1k, rs, rk = d["qn"], d["qk"], d["r1"], d["r1k"], d["rs"], d["rk"]
            b2, bk2 = self.bank()
            self.mm(b2[:, 0:N], self.pswap[:], qn[:, 0:N], True, True, [qk_, "c_bf"], [bk2])
            self.tt("pool", r1[:, 0:N], qn[:, 0:N], cosT, ALU.mult, [qk_] + rpk, [r1k])
            self.tt("dve", rs[:, 0:N], b2[:, 0:N], sinT, ALU.mult, [bk2] + rpk, [rk])
            self.tt("dve", d["out"], rs[:, 0:N], r1[:, 0:N], ALU.add, [rk, r1k], [d["okey"]])

        n = len(st)
        for t in range(n + 2):
            if t < n:
                sa(st[t])
            if 0 <= t - 1 < n:
                sb(st[t - 1])
            if 0 <= t - 2 < n:
                sc(st[t - 2])

    def load_w(self, dst, dkey, src):
        self.dma("pool", dst, src, writes=[dkey])

    def layer0_A(self):
        self.phase()
        ta = self.talloc
        self.rot = [0, 1, 2, 3, 4, 5, 6]
        self.QT = ta("QT", [128, 4, T_A], BF16)
        self.KT = ta("KT", [128, T_A], BF16)
        self.V = ta("V", [128, T_A // 128, 128], BF16)
        self.QcT = ta("QcT", [128, 4, NCTX], BF16)
        self.KcT = ta("KcT", [128, NCTX], BF16)
        self.Vc = ta("Vc", [128, 2, 128], BF16)
        fo_off = self.cur
        self.FO = ta("FO", [128, 4, T_X], BF16)
        self.FOc = ta("FOc", [128, 4, NCTX], BF16)
        self.keepC = self.cur
        self.F_all = ta("F_all", [128, 32, 512], BF16)
        self.Fc = ta("Fc", [128, 2, 512], BF16)
        self.keepA = self.cur
        win = self.win
        xtoks = [(ta("xtok%d" % i, [128, 1024], F32), "xtok%d" % i) for i in range(4)]
        self.xti = 0
        self.uid += 1
        xTb2 = self.nc.alloc_sbuf_tensor_at("xTb2_%d" % self.uid, [128, 8, 512], F32, offset=fo_off)
        xTbs = [(ta("xTb", [128, 8, 512], F32), "xTb0"), (xTb2, "xTb1")]
        hTs = [(ta("hT%d" % i, [128, 8, 512], BF16), "hT%d" % i) for i in range(2)]
        sqbs = [ta("sqb", [128, 4, 512], BF16)]
        rss = [ta("rsn%d" % i, [128, 512], F32) for i in range(2)]
        tf = [ta("tf%d" % i, [128, 512], F32) for i in range(2)]
        sq1s = [ta("sq1_%d" % i, [128, 512], BF16) for i in range(2)]
        rsq = [ta("rsq%d" % i, [128, 512], F32) for i in range(2)]
        self.uid += 1
        qns = [self.nc.alloc_sbuf_tensor_at("qn0_%d" % self.uid, [128, 512], F32, offset=fo_off + 16384), ta("qn1", [128, 512], F32)]
        r1s = [self.nc.alloc_sbuf_tensor_at("r10_%d" % self.uid, [128, 512], F32, offset=fo_off + 16384 + 2048)]
        ropeb = [ta("rope%d" % i, [128, 2, 512], F32) for i in range(1)]
        tmpn = (sqbs, rss, tf)
        tmpq = (sq1s, rsq, qns, r1s)
        assert self.cur <= self.hi - 20480, "phase A overflow into win"
        l = 0

        def projA(hT, hk, N, qkv, s, tok0, tile0, ropek):
            for t in range(N // 128):
                b, bk = self.bank()
                for c in range(8):
                    self.mm(b[:, 0:512], hT[:, c, t * 128:(t + 1) * 128], win[:, c, 0:512], c == 0, c == 7, [hk, "win"], [bk])
                dst = self.F_all[:, tile0 + t, :] if s == 0 else self.Fc[:, t, :]
                self.evac(dst, b[:, 0:512], [bk], ["F_all" if s == 0 else "Fc"])
            if not qkv:
                return
            for t in range(N // 128):
                b, bk = self.bank()
                for c in range(8):
                    self.mm(b[:, 0:128], hT[:, c, t * 128:(t + 1) * 128], win[:, c, 1152:1280], c == 0, c == 7, [hk, "win"], [bk])
                dst = self.V[:, tile0 + t, :] if s == 0 else self.Vc[:, t, :]
                self.evac(dst, b[:, 0:128], [bk], ["V" if s == 0 else "Vc"])

        def projB(hT, hk, N, qkv, s, tok0, tile0, ropek):
            if not qkv:
                return
            jobs = []
            for j in range(5):
                def proj(j=j):
                    b, bk = self.bank()
                    for c in range(8):
                        self.mm(b[:, 0:N], win[:, c, 512 + j * 128:512 + (j + 1) * 128], hT[:, c, 0:N], c == 0, c == 7, [hk, "win"], [bk])
                    return b, bk
                if s == 0:
                    out = self.QT[:, j, tok0:tok0 + N] if j < 4 else self.KT[:, tok0:tok0 + N]
                    okey = "QT" if j < 4 else "KT"
                    rb_ = ropeb[0]
                    rope = (rb_[:, 0, 0:N], rb_[:, 1, 0:N], ["rope0c", "rope0s"])
                else:
                    out = self.QcT[:, j, 0:N] if j < 4 else self.KcT[:, 0:N]
                    okey = "QcT" if j < 4 else "KcT"
                    rope = None
                jobs.append(dict(proj=proj, N=N, gcol=self.gq[:, 0:1] if j < 4 else self.gq[:, 1:2], out=out, okey=okey, rope=rope))
            self.qk_pipeline(jobs, tmpq)

        items = [("ctx", 0)] + [("lat", g) for g in range(8)]

        def stage1(idx):
            kind, g = items[idx]
            hT, hk = hTs[idx % 2]
            if kind == "ctx":
                self.load_xT(self.ctx_d.ap(), 2, self.YT, "YT", 0, xtoks)
                return (hT, hk, NCTX, True, 1, 0, 0, 0, self.YT, "YT")
            tok0 = g * 512
            xTb, xk = xTbs[idx % 2]
            self.load_xT(self.x_d.ap()[tok0:tok0 + 512, :], 4, xTb, xk, 0, xtoks)
            return (hT, hk, 512, g < 5, 0, tok0, g * 4, g, xTb, xk)

        def stage2(st, part, state=None):
            hT, hk, N, qkv, s_, tok0, tile0, rk, xT, xk = st
            return self.norm_mod(xT, xk, 0, N, lambda c: self.A1[:, l, s_, c:c + 1], lambda c: self.mod(l, 0, s_, c),
                                 hT, hk, 0, tmpn, part=part, state=state)

        cur = stage1(0)
        stage2(cur, 2, stage2(cur, 1))
        for idx in range(len(items)):
            nxt = stage1(idx + 1) if idx + 1 < len(items) else None
            projA(*cur[:8])
            self.ada_tick(1)
            if nxt is not None:
                stage2(nxt, 2, stage2(nxt, 1))
            projB(*cur[:8])
            if nxt is not None and nxt[3] and nxt[4] == 0:
                rb_ = ropeb[0]
                tk0 = nxt[5]
                self.dma("sp", rb_[:, 0, :], self.crope_d.ap()[0, :, tk0:tk0 + 512], writes=["rope0c"])
                self.dma("sp", rb_[:, 1, :], self.crope_d.ap()[1, :, tk0:tk0 + 512], writes=["rope0s"])
            self.ada_tick(1)
            cur = nxt
        if self.stop_after == "A":
            for nm_, t_, k_ in [("F_all", self.F_all, "F_all"), ("QT", self.QT, "QT"), ("KT", self.KT, "KT"), ("V", self.V, "V"),
                                ("Fc", self.Fc, "Fc"), ("QcT", self.QcT, "QcT"), ("KcT", self.KcT, "KcT"), ("Vc", self.Vc, "Vc"),
                                ("modT", self.modT, "modT")]:
                self.dump(nm_, t_, k_)

    def dft(self, Fsrc, fkey, ntile, tabs, K, FO, fokey, tabbuf, ABT):
        ti = 0
        for (k0, N) in groups(K):
            for cs in range(2):
                tb, tk = tabbuf[ti % 2], "tab%d" % (ti % 2)
                ti += 1
                src_t = tabs(cs, k0, N)
                for a0 in range(0, ntile, 8):
                    self.dma("sp", tb[:, a0:a0 + 8, 0:N], src_t[:, a0:a0 + 8, :], writes=[tk])
                for g in range(4):
                    b, bk = self.bank()
                    for a in range(ntile):
                        self.mm(b[:, 0:N], Fsrc[:, a, g * 128:(g + 1) * 128], tb[:, a, 0:N], a == 0, a == ntile - 1, [fkey, tk], [bk])
                    self.evac(ABT[:, cs, g, 0:N], b[:, 0:N], [bk], ["ABT"])
                    self.ada_tick(1)
            for g in range(4):
                b, bk = self.bank()
                self.mm(b[:, 0:N], self.ccos, ABT[:, 0, g, 0:N], True, False, ["ABT", "c_bf"], [bk])
                self.mm(b[:, 0:N], self.cnsin, ABT[:, 1, g, 0:N], False, True, ["ABT", "c_bf"], [bk])
                self.evac(FO[:, g, k0:k0 + N], b[:, 0:N], [bk], [fokey])

    def layer0_B(self):
        scr = self.scr
        self.P.barrier(lambda e: e.memset(scr[0:1, 0:8], 0.0))
        self.cur = self.keepA
        ta = self.talloc
        tabbuf = [ta("tab%d" % i, [128, 32, 512], BF16) for i in range(2)]
        ABT = ta("ABT", [128, 2, 4, 512], BF16)
        for cs in range(2):
            for g in range(4):
                b, bk = self.bank()
                for a in range(2):
                    self.mm(b[:, 0:NCTX], self.Fc[:, a, g * 128:(g + 1) * 128], self.cctx[:, cs, a, :], a == 0, a == 1, ["Fc", "c_bf"], [bk])
                self.evac(ABT[:, cs, g, 0:NCTX], b[:, 0:NCTX], [bk], ["ABT"])
        for g in range(4):
            b, bk = self.bank()
            self.mm(b[:, 0:NCTX], self.ccos, ABT[:, 0, g, 0:NCTX], True, False, ["ABT", "c_bf"], [bk])
            self.mm(b[:, 0:NCTX], self.cnsin, ABT[:, 1, g, 0:NCTX], False, True, ["ABT", "c_bf"], [bk])
            self.evac(self.FOc[:, g, :], b[:, 0:NCTX], [bk], ["FOc"])
        tabs = lambda cs, k0, N: self.cdft_d[cs].ap().rearrange("(a p) k -> p a k", p=128)[:, :, k0:k0 + N]
        self.dft(self.F_all, "F_all", 32, tabs, T_X, self.FO, "FO", tabbuf, ABT)
        self.ada_tick(100)
        self.rot = [2, 3, 4, 5, 6, 7]
        self.pers -= 2 * 4096
        if self.stop_after == "B":
            self.dump("FO", self.FO, "FO")
            self.dump("FOc", self.FOc, "FOc")

    def layer0_C(self):
        scr = self.scr
        self.P.barrier(lambda e: e.memset(scr[0:1, 0:8], 0.0))
        self.cur = self.keepC
        ta = self.talloc
        l = 0
        wout = ta("wout", [128, 8, D], BF16)
        self.load_w(wout[:], "wout", self.evout_d.ap().rearrange("(c p) n -> p c n", p=128))
        ATTs = [(ta("ATT%d" % i, [128, 4, 512], BF16), "ATT%d" % i) for i in range(2)]
        qlos = [(ta("qlo%d" % i, [128, 4, 128], BF16), "qlo%d" % i) for i in range(2)]
        qhis = [(ta("qhi%d" % i, [128, 4, 128], BF16), "qhi%d" % i) for i in range(2)]
        PT = [ta("PT%d" % i, [128, 512], BF16) for i in range(4)]
        paccs = [(ta("pacc%d" % i, [128, 512], BF16), "pacc%d" % i) for i in range(4)]
        dtots = [(ta("dtot%d" % i, [128, 512], F32), "dtot%d" % i) for i in range(2)]
        xtoks = [(ta("xtokC%d" % i, [128, 1024], F32), "xtokC%d" % i) for i in range(3)]
        self.xti = 0
        self.XT = self.nc.alloc_sbuf_tensor_at("XT_res", [128, 8, T_X], F32, offset=self.hi - 8 * T_X * 4)
        assert self.cur <= self.hi - 8 * T_X * 4, "phase C overflow"
        for (q_, k_) in qlos + qhis:
            self.P.add("pool", lambda e, q_=q_: e.memset(q_[:], 0.0), writes=[k_])
        self.rot = [4, 5, 6, 7]
        accs = [((self.pb[0], "pb0"), (self.pb[1], "pb1")), ((self.pb[2], "pb2"), (self.pb[3], "pb3"))]
        ctxtiles = [(self.KcT[:, t * 128:(t + 1) * 128], self.Vc[:, t, :], None, ["KcT", "Vc"]) for t in range(2)]

        jobs = []

        def outproj(N, FOsrc, fokey, focol, Xdst, xkey, xcol, s_, ATT, ak):
            for oc in range(8):
                b, bk = self.bank()
                for kc in range(8):
                    rhs = FOsrc[:, kc, focol:focol + N] if kc < 4 else ATT[:, kc - 4, 0:N]
                    self.mm(b[:, 0:N], wout[:, kc, oc * 128:(oc + 1) * 128], rhs, kc == 0, kc == 7, ["wout", fokey, ak], [bk])
                xo = Xdst[:, oc, xcol:xcol + N]
                self.stt("dve", xo, b[:, 0:N], self.mod(l, 2, s_, oc), xo, ALU.mult, ALU.add, [bk, "modT", xkey], [xkey])

        gi = 0
        for qt in range(2):
            post = (lambda ai=gi % 2: outproj(NCTX, self.FOc, "FOc", 0, self.YT, "YT", 0, 1, *ATTs[ai])) if qt == 1 else None
            jobs.append((self.QcT, "QcT", qt * 128, ctxtiles, gi % 2, qt * 128, None, post))
        gi += 1
        for (t0, N) in groups(T_X):
            for tq in range(N // 128):
                qt = t0 // 128 + tq
                kts = []
                for dlt, mask in ((-1, self.mprev), (0, None), (1, self.mnext)):
                    kt = qt + dlt
                    if kt < 0:
                        continue
                    kts.append((self.KT[:, kt * 128:(kt + 1) * 128], self.V[:, kt, :], mask, ["KT", "V"]))
                pre = (lambda t0=t0, N=N: self.load_xT(self.x_d.ap()[t0:t0 + N, :], N // 128, self.XT, "XT%d" % (t0 // 512), t0, xtoks)) if tq == 0 else None
                post = (lambda t0=t0, N=N, ai=gi % 2: outproj(N, self.FO, "FO", t0, self.XT, "XT%d" % (t0 // 512), t0, 0, *ATTs[ai])) \
                    if tq == N // 128 - 1 else None
                jobs.append((self.QT, "QT", qt * 128, kts + ctxtiles, gi % 2, tq * 128, pre, post))
            gi += 1

        steps = []
        for ji, job in enumerate(jobs):
            n = len(job[3])
            for kvh in range(2):
                for i in range(n):
                    steps.append((ji, kvh, i, n))
        pti = [0]

        def s_stage(st):
            ji, kvh, i, n = st
            QTsrc, qkey, qcol, keytiles, ai, dcol, pre, post = jobs[ji]
            (qlo, qlk), (qhi, qhk) = qlos[ji % 2], qhis[ji % 2]
            (pbO, ok_), (pbD, dk_) = accs[ji % 2]
            if kvh == 0 and i == 0:
                if pre is not None:
                    pre()
                self.copy("pool", qlo[0:64, :, :], QTsrc[0:64, :, qcol:qcol + 128], [qkey], [qlk])
                self.copy("pool", qhi[64:128, :, :], QTsrc[64:128, :, qcol:qcol + 128], [qkey], [qhk])
            qp, qk_ = (qlo, qlk) if kvh == 0 else (qhi, qhk)
            kap, vap, mask, kkeys = keytiles[i]
            b, bk = self.bank()
            self.mm(b[:, 0:512], kap, qp[:].rearrange("p j n -> p (j n)"), True, mask is None, [qk_] + kkeys, [bk])
            if mask is not None:
                self.mm(b[:, 0:512], self.identb, mask.unsqueeze(1).broadcast_to([128, 4, 128]), False, True, ["c_bf"], [bk])
            pt = PT[pti[0] % 4]
            pk = "PT%d" % (pti[0] % 4)
            pti[0] += 1
            self.act(pt[:], b[:, 0:512], AF.Exp, [bk], [pk])
            acc, ak_ = paccs[(ji % 2) * 2 + kvh]
            if i == 0:
                self.copy("dve", acc[:], pt[:], [pk], [ak_])
            else:
                self.tt("dve", acc[:], acc[:], pt[:], ALU.add, [ak_, pk], [ak_])
            return (pt, pk)

        def p_stage(st, pt, pk):
            ji, kvh, i, n = st
            QTsrc, qkey, qcol, keytiles, ai, dcol, pre, post = jobs[ji]
            (pbO, ok_), (pbD, dk_) = accs[ji % 2]
            kap, vap, mask, kkeys = keytiles[i]
            rows = slice(kvh * 64, kvh * 64 + 64)
            self.mm(pbO[rows, 0:512], vap[:, kvh * 64:kvh * 64 + 64], pt[:], i == 0, i == n - 1, [pk] + kkeys, [ok_])
            if i == n - 1:
                acc, ak_ = paccs[(ji % 2) * 2 + kvh]
                self.mm(pbD[rows, 0:512], self.onesb[:, 0:64], acc[:], True, True, [ak_, "c_bf"], [dk_])
            if kvh == 1 and i == n - 1:
                dtot, dtk = dtots[ji % 2]
                ATT, ak = ATTs[ai]
                self.tt("dve", dtot[:], pbD[:, 0:512], self.sinkbc[:].rearrange("p j n -> p (j n)"), ALU.add, [dk_, "sinkbc"], [dtk])
                self.act(dtot[:], dtot[:], AF.Ln, [dtk], [dtk])
                self.act(dtot[:], dtot[:], AF.Exp, [dtk], [dtk], scale=-1.0)
                self.tt("dve", ATT[:, :, dcol:dcol + 128], pbO[:, 0:512].rearrange("p (j n) -> p j n", n=128),
                        dtot[:].rearrange("p (j n) -> p j n", n=128), ALU.mult, [ok_, dtk], [ak])
                if post is not None:
                    post()

        prev = None
        for st in steps:
            cur = s_stage(st)
            if prev is not None:
                p_stage(*prev)
            prev = (st, cur[0], cur[1])
        p_stage(*prev)
        self.rot = [2, 3, 4, 5, 6, 7]

    def ffn(self, l):
        scr = self.scr
        self.P.barrier(lambda e: e.memset(scr[0:1, 0:8], 0.0))
        self.cur = self.pers
        ta = self.talloc
        T = T_X if l == 0 else T_OWN
        toks = [(t0, N, 0) for (t0, N) in groups(T)]
        HT = T + (NCTX if l == 0 else 0)
        h2 = ta("h2", [128, 8, HT], BF16)
        keepF = self.cur
        sqbs = [ta("sqb%d" % i, [128, 8, 512], BF16) for i in range(2)]
        rss = [ta("rsn%d" % i, [128, 512], F32) for i in range(2)]
        tf = [ta("tf%d" % i, [128, 512], F32) for i in range(4)]
        tmpn = (sqbs, rss, tf)
        xkey = lambda t0: "XT%d" % (t0 // 512)
        calls = [dict(xT=self.XT, xkey=xkey(t0), col0=t0, N=N, A=(lambda c: self.A2[:, l, 0, c:c + 1]),
                      Bf=(lambda c: self.mod(l, 3, 0, c)), hT=h2, hkey="h2_%d" % (t0 // 512), hcol0=t0, tmp=tmpn) for (t0, N, _) in toks]
        if l == 0:
            calls.append(dict(xT=self.YT, xkey="YT", col0=0, N=NCTX, A=(lambda c: self.A2[:, l, 1, c:c + 1]),
                              Bf=(lambda c: self.mod(l, 3, 1, c)), hT=h2, hkey="h2_c", hcol0=T, tmp=tmpn))
            toks = toks + [(T, NCTX, 1)]
        self.norm_pipeline(calls)
        self.P.barrier(lambda e: e.memset(scr[0:1, 0:8], 0.0))
        self.cur = keepF
        splits = [(0, 4), (4, 4), (8, 4), (12, 4), (16, 3), (19, 3)]
        wgb = [ta("wg%d" % i, [128, 8, 512], BF16) for i in range(2)]
        wub = [ta("wu%d" % i, [128, 8, 512], BF16) for i in range(2)]
        wdb = [ta("wd%d" % i, [128, 4, D], BF16) for i in range(2)]
        actb = [ta("act%d" % i, [128, 4, 512], BF16) for i in range(2)]
        sg = [ta("sg%d" % i, [128, 512], F32) for i in range(3)]
        obs = [ta("ob%d" % i, [128, D], F32) for i in range(2)] if l == 1 else None
        assert self.cur <= self.hi - 8 * T_X * 4, "ffn overflow"
        ai = 0
        si = 0
        for sp_i, (j0, nj) in enumerate(splits):
            wi = sp_i % 2
            wg, wu, wd = wgb[wi], wub[wi], wdb[wi]
            kg, ku, kd = "wg%d" % wi, "wu%d" % wi, "wd%d" % wi
            self.load_w(wg[:, :, 0:nj * 128], kg, self.wg_d.ap()[l].rearrange("(c p) n -> p c n", p=128)[:, :, j0 * 128:(j0 + nj) * 128])
            self.load_w(wu[:, :, 0:nj * 128], ku, self.wu_d.ap()[l].rearrange("(c p) n -> p c n", p=128)[:, :, j0 * 128:(j0 + nj) * 128])
            self.load_w(wd[:, 0:nj, :], kd, self.wd_d.ap()[l].rearrange("(j p) n -> p j n", p=128)[:, j0:j0 + nj, :])
            for (t0, N, s) in toks:
                ab = actb[ai % 2]
                ak = "act%d" % (ai % 2)
                ai += 1
                hk = "h2_c" if s == 1 else "h2_%d" % (t0 // 512)
                for jj in range(nj):
                    bg, bgk = self.bank()
                    for c in range(8):
                        self.mm(bg[:, 0:N], wg[:, c, jj * 128:(jj + 1) * 128], h2[:, c, t0:t0 + N], c == 0, c == 7, [kg, hk], [bgk])
                    bu, buk = self.bank()
                    for c in range(8):
                        self.mm(bu[:, 0:N], wu[:, c, jj * 128:(jj + 1) * 128], h2[:, c, t0:t0 + N], c == 0, c == 7, [ku, hk], [buk])
                    s_ = sg[si % 3]
                    sk = "sg%d" % (si % 3)
                    si += 1
                    self.act(s_[:, 0:N], bg[:, 0:N], AF.Silu, [bgk], [sk])
                    self.tt("dve", ab[:, jj, 0:N], bu[:, 0:N], s_[:, 0:N], ALU.mult, [buk, sk], [ak])
                X, xk, xc = (self.XT, xkey(t0), t0) if s == 0 else (self.YT, "YT", 0)
                for oc in range(8):
                    b, bk = self.bank()
                    for jj in range(nj):
                        self.mm(b[:, 0:N], wd[:, jj, oc * 128:(oc + 1) * 128], ab[:, jj, 0:N], jj == 0, jj == nj - 1, [kd, ak], [bk])
                    xo = X[:, oc, xc:xc + N]
                    self.stt("dve", xo, b[:, 0:N], self.mod(l, 5, s, oc), xo, ALU.mult, ALU.add, [bk, "modT", xk], [xk])
                if l == 1 and sp_i == len(splits) - 1:
                    self.emit_out(range(t0 // 128, (t0 + N) // 128), obs)
        if l == 1:
            self.out_done = True

    def layer1_E(self):
        scr = self.scr
        self.P.barrier(lambda e: e.memset(scr[0:1, 0:8], 0.0))
        self.cur = self.pers
        ta = self.talloc
        l = 1
        NK = T_X + NCTX
        hT = ta("hT1", [128, 8, NK], BF16)
        keepE = self.cur
        sqbs = [ta("sqb%d" % i, [128, 8, 512], BF16) for i in range(2)]
        rss = [ta("rsn%d" % i, [128, 512], F32) for i in range(2)]
        tf = [ta("tf%d" % i, [128, 512], F32) for i in range(4)]
        tmpn = (sqbs, rss, tf)
        calls = [dict(xT=self.XT, xkey="XT%d" % (t0 // 512), col0=t0, N=N, A=(lambda c: self.A1[:, l, 0, c:c + 1]),
                      Bf=(lambda c: self.mod(l, 0, 0, c)), hT=hT, hkey="hT1_%d" % (t0 // 512), hcol0=t0, tmp=tmpn) for (t0, N) in groups(T_X)]
        calls.append(dict(xT=self.YT, xkey="YT", col0=0, N=NCTX, A=(lambda c: self.A1[:, l, 1, c:c + 1]),
                          Bf=(lambda c: self.mod(l, 0, 1, c)), hT=hT, hkey="hT1_%d" % (T_X // 512), hcol0=T_X, tmp=tmpn))
        self.norm_pipeline(calls)
        self.P.barrier(lambda e: e.memset(scr[0:1, 0:8], 0.0))
        self.cur = keepE
        off_rsq = self.cur
        rsq = [ta("rsq%d" % i, [128, 512], F32) for i in range(2)]
        self.uid += 1
        paccs = [(self.nc.alloc_sbuf_tensor_at("pacc%d_%d" % (i, self.uid), [128, 512], BF16, offset=off_rsq + i * 1024), "acc%d" % i) for i in range(4)]
        acck = ["acc%d" % i for i in range(4)]
        sq1s = [ta("sq1_%d" % i, [128, 512], BF16) for i in range(1)]
        tmpq = (sq1s, rsq, None, None)
        namask = ta("namask", [128, 3, 6, 128], BF16)
        self.dma("sp", namask[:], self.cnam_d.ap(), writes=["namask"])
        wq = ta("wq", [128, 8, 256], BF16)
        wk = ta("wk", [128, 8, 256], BF16)
        wv = ta("wv", [128, 8, 256], BF16)
        wo = ta("wo", [128, 2, D], BF16)
        Tt = ta("Tt", [128, 6, 4, 128], BF16)
        BMI = ta("BMI", [128, 5, 4, 128], BF16)
        bmtmp = rsq[0]
        KTh = ta("KTh", [128, 2, NK], BF16)
        Vh = ta("Vh", [128, NK // 128, 256], BF16)
        QTh = ta("QTh", [128, 2, T_OWN], BF16)
        ATTs = [(ta("ATT1_%d" % i, [128, 2, 512], BF16), "ATT1_%d" % i) for i in range(2)]
        qpads = [[(ta("qpad%d_%d" % (e, i), [128, 2, 128], BF16), "qpad%d_%d" % (e, i)) for e in range(2)] for i in range(2)]
        PT = [ta("PT1_%d" % i, [128, 512], BF16) for i in range(5)]
        dtots = [(ta("dtot1_%d" % i, [128, 512], F32), "dtot1_%d" % i) for i in range(1)]
        assert self.cur <= self.hi - 8 * T_X * 4, "layer1 overflow %d" % (self.cur - (self.hi - 8 * T_X * 4))
        for i in range(2):
            for e_ in range(2):
                q_, k_ = qpads[i][e_]
                self.P.add("pool", lambda e, q_=q_: e.memset(q_[:], 0.0), writes=[k_])
        self.rot = [4, 5, 6, 7]
        accs = [((self.pb[0], "pb0"), (self.pb[1], "pb1")), ((self.pb[2], "pb2"), (self.pb[3], "pb3"))]
        z256 = self.cbf[:, 4:6, :].rearrange("p a n -> p (a n)")
        z512 = self.cbf[:, 4:8, :].rearrange("p a n -> p (a n)")
        hkey = lambda t0: "hT1_%d" % (t0 // 512)
        pti = [0]
        tglob = [0]
        src = self.odin_d.ap().rearrange("(c p) n -> p c n", p=128)

        def load_qkv(hg_):
            self.load_w(wq[:], "wq", src[:, :, hg_ * 256:(hg_ + 1) * 256])
            self.load_w(wk[:], "wk", src[:, :, D + hg_ * 256:D + (hg_ + 1) * 256])
            self.load_w(wv[:], "wv", src[:, :, 2 * D + hg_ * 256:2 * D + (hg_ + 1) * 256])

        load_qkv(0)
        for hg in range(4):
            self.load_w(wo[:], "wo", self.odout_d.ap().rearrange("(c p) n -> p c n", p=128)[:, hg * 2:hg * 2 + 2, :])
            for dl in range(6):
                for kr in range(2):
                    for qr in range(2):
                        dr_idx = 2 * (dl - 2) + kr - qr + 7
                        srcb = AP(self.rb_d, hg * 4 * 465 + dr_idx * 31 - 48, [[1, 64], [465, 4], [1, 64]])
                        self.dma("pool", Tt[qr * 64:(qr + 1) * 64, dl, :, kr * 64:(kr + 1) * 64], srcb, writes=["Tt%d" % dl])
            for dl in range(5):
                b, bk = self.bank()
                for h4 in range(4):
                    self.mm(b[:, h4 * 128:(h4 + 1) * 128], Tt[:, dl, h4, :], self.rblk, True, True, ["Tt%d" % dl, "c_bf"], [bk])
                self.tt("dve", BMI[:, dl, :, :], b[:, 0:512].rearrange("p (h n) -> p h n", n=128),
                        namask[:, 2, dl, :].unsqueeze(1).broadcast_to([128, 4, 128]), ALU.add, [bk, "namask"], ["BMI"])
            self.rot = [0, 1, 2, 3, 4, 5, 6, 7]
            self.P.add("dve", lambda e: e.memset(scr[0:1, 8:12], 0.0), reads=acck, writes=["rsq0", "rsq1"])
            for (t0, N) in groups(NK):
                jobs = []
                for ch in range(2):
                    def projk(ch=ch, t0=t0, N=N):
                        b, bk = self.bank()
                        for c in range(8):
                            self.mm(b[:, 0:N], wk[:, c, ch * 128:(ch + 1) * 128], hT[:, c, t0:t0 + N], c == 0, c == 7, ["wk", hkey(t0)], [bk])
                        return b, bk
                    jobs.append(dict(proj=projk, N=N, gcol=self.gq[:, 3:4], out=KTh[:, ch, t0:t0 + N], okey="KTh", rope=None))
                    if t0 + N <= T_OWN:
                        def projq(ch=ch, t0=t0, N=N):
                            b, bk = self.bank()
                            for c in range(8):
                                self.mm(b[:, 0:N], wq[:, c, ch * 128:(ch + 1) * 128], hT[:, c, t0:t0 + N], c == 0, c == 7, ["wq", hkey(t0)], [bk])
                            return b, bk
                        jobs.append(dict(proj=projq, N=N, gcol=self.gq[:, 2:3], out=QTh[:, ch, t0:t0 + N], okey="QTh", rope=None))
                self.qk_pipeline(jobs, tmpq)
                for t in range(N // 128):
                    b, bk = self.bank()
                    for c in range(8):
                        self.mm(b[:, 0:256], hT[:, c, t0 + t * 128:t0 + (t + 1) * 128], wv[:, c, :], c == 0, c == 7, ["wv", hkey(t0)], [bk])
                    self.evac(Vh[:, t0 // 128 + t, :], b[:, 0:256], [bk], ["Vh"])
            self.rot = [4, 5, 6, 7]
            self.P.add("dve", lambda e: e.memset(scr[0:1, 12:16], 0.0), reads=["rsq0", "rsq1"], writes=acck)
            if hg + 1 < 4:
                load_qkv(hg + 1)
            steps = []
            for qt in range(T_OWN // 128):
                kts = [(qt + d_, d_ + 2) for d_ in range(-2, 4 if qt == 0 else 3) if qt + d_ >= 0] + \
                      [(T_X // 128, None), (T_X // 128 + 1, None)]
                for i, (kt, dl) in enumerate(kts):
                    steps.append((qt, i, len(kts), kt, dl))

            def s_stage(st):
                qt, i, n, kt, dl = st
                tq_ = tglob[0] + qt
                qp = qpads[tq_ % 2]
                (pbO, ok_), (pbD, dk_) = accs[tq_ % 2]
                if i == 0:
                    for e_ in range(2):
                        rows = slice(e_ * 64, e_ * 64 + 64)
                        self.copy("pool", qp[e_][0][rows, :, :], QTh[rows, :, qt * 128:(qt + 1) * 128], ["QTh"], [qp[e_][1]])
                    self.mm(pbO[:, 0:256], self.zerob, z256, True, False, ["c_bf"], [ok_])
                b, bk = self.bank()
                first = True
                if dl is not None:
                    if qt >= 2:
                        self.mm(b[:, 0:512], self.identb, BMI[:, dl, :, :].rearrange("p h n -> p (h n)"), True, False, ["BMI", "c_bf"], [bk])
                        first = False
                    else:
                        self.mm(b[:, 0:512], self.identb, namask[:, qt, dl, :].unsqueeze(1).broadcast_to([128, 4, 128]), True, False,
                                ["namask", "c_bf"], [bk])
                        for h4 in range(4):
                            self.mm(b[:, h4 * 128:(h4 + 1) * 128], Tt[:, dl, h4, :], self.rblk, False, False, ["Tt%d" % dl, "c_bf"], [bk])
                        first = False
                for h4 in range(4):
                    ch, e_ = h4 // 2, h4 % 2
                    self.mm(b[:, h4 * 128:(h4 + 1) * 128], KTh[:, ch, kt * 128:(kt + 1) * 128], qp[e_][0][:, ch, :], first, True,
                            ["KTh", qp[e_][1]], [bk])
                pt = PT[pti[0] % 5]
                pk = "PT1_%d" % (pti[0] % 5)
                pti[0] += 1
                self.act(pt[:], b[:, 0:512], AF.Exp, [bk], [pk])
                acc, ak_ = paccs[(tq_ % 2) * 2 + (i % 2)]
                if i < 2:
                    self.copy("dve", acc[:], pt[:], [pk], [ak_])
                else:
                    self.tt("dve", acc[:], acc[:], pt[:], ALU.add, [ak_, pk], [ak_])
                return (pt, pk)

            def p_stage(st, pt, pk):
                qt, i, n, kt, dl = st
                tq_ = tglob[0] + qt
                (pbO, ok_), (pbD, dk_) = accs[tq_ % 2]
                for h4 in range(4):
                    ch, e_ = h4 // 2, h4 % 2
                    rows = slice(e_ * 64, e_ * 64 + 64)
                    self.mm(pbO[rows, ch * 128:(ch + 1) * 128], Vh[:, kt, h4 * 64:(h4 + 1) * 64], pt[:, h4 * 128:(h4 + 1) * 128],
                            False, True, [pk, "Vh"], [ok_])
                if i == n - 1:
                    accA, akA = paccs[(tq_ % 2) * 2]
                    accB, akB = paccs[(tq_ % 2) * 2 + 1]
                    self.mm(pbD[:, 0:512], self.onesb, accA[:], True, False, [akA, "c_bf"], [dk_])
                    self.mm(pbD[:, 0:512], self.onesb, accB[:], False, True, [akB, "c_bf"], [dk_])
                    dtot, dtk = dtots[0]
                    ATT, ak = ATTs[(qt // 4) % 2]
                    tq = qt % 4
                    self.act(dtot[:], pbD[:, 0:512], AF.Ln, [dk_], [dtk])
                    self.act(dtot[:], dtot[:], AF.Exp, [dtk], [dtk], scale=-1.0)
                    dv = dtot[:].rearrange("p (c e n) -> p c e n", e=2, n=128)
                    for e_ in range(2):
                        rows = slice(e_ * 64, e_ * 64 + 64)
                        self.tt("dve", ATT[rows, :, tq * 128:(tq + 1) * 128], pbO[rows, 0:256].rearrange("p (j n) -> p j n", n=128),
                                dv[rows, :, e_, :], ALU.mult, [ok_, dtk], [ak])
                    if tq == 3:
                        t0 = (qt // 4) * 512
                        for oc in range(8):
                            b, bk = self.bank()
                            for ch in range(2):
                                self.mm(b[:, 0:512], wo[:, ch, oc * 128:(oc + 1) * 128], ATT[:, ch, 0:512], ch == 0, ch == 1, ["wo", ak], [bk])
                            xo = self.XT[:, oc, t0:t0 + 512]
                            xk = "XT%d" % (t0 // 512)
                            self.stt("dve", xo, b[:, 0:512], self.mod(l, 2, 0, oc), xo, ALU.mult, ALU.add, [bk, "modT", xk], [xk])

            if "noattn" in self.flags:
                steps = []
            if "few" in self.flags:
                steps = steps[:self.nstep]
            pend = []
            for st in steps:
                cur = s_stage(st)
                pend.append((st, cur[0], cur[1]))
                if len(pend) > 2:
                    p_stage(*pend.pop(0))
            for pp in pend:
                p_stage(*pp)
            tglob[0] += T_OWN // 128
            if "onehg" in self.flags:
                break
        self.rot = [2, 3, 4, 5, 6, 7]

    def finish(self, final=True):
        scr = self.scr
        self.P.barrier(lambda e: e.memset(scr[0:1, 0:8], 0.0))
        self.cur = self.pers
        ta = self.talloc
        if self.debug and hasattr(self, "XT"):
            self.dma("sp", self.dbgx_d.ap(), self.XT[:], reads=["XT%d" % i for i in range(5)])
            self.dma("sp", self.dbgy_d.ap(), self.YT[:], reads=["YT"])
        if not hasattr(self, "XT"):
            return
        if getattr(self, "out_done", False):
            return
        ob = [ta("ob%d" % i, [128, D], F32) for i in range(2)]
        self.emit_out(range(T_OWN // 128), ob)

    def emit_out(self, tiles, ob):
        for t in tiles:
            o_, ok = ob[t % 2], "ob%d" % (t % 2)
            for half in range(2):
                b, bk = self.bank()
                for cc in range(4):
                    c = half * 4 + cc
                    self.tr(b[:, cc * 128:(cc + 1) * 128], self.XT[:, c, t * 128:(t + 1) * 128], ["XT%d" % (t // 4)], [bk])
                self.evac(o_[:, half * 512:(half + 1) * 512], b[:, 0:512], [bk], [ok])
            self.dma("sp", self.out_d.ap()[t * 128:(t + 1) * 128, :], o_[:], reads=[ok])


_CONST_CACHE = {}


def _bf(a):
    return np.ascontiguousarray(a.astype(ml_dtypes.bfloat16))


def host_consts(par):
    if par in _CONST_CACHE:
        return _CONST_CACHE[par]
    c = {}
    c["c_identf"] = np.eye(128, dtype=np.float32)
    psw = np.zeros((128, 128), np.float32)
    for m in range(128):
        i = m % 32
        partner = m + 16 if i < 16 else m - 16
        psw[partner, m] = 1.0
    c["c_pswap"] = psw
    cb = np.zeros((128, 10, 128), np.float32)
    cb[:, 0, :] = 1.0 / 1024.0
    cb[0:64, 1, 0:64] = 1.0 / 64.0
    cb[64:128, 1, 64:128] = 1.0 / 64.0
    cb[:, 2, :] = np.eye(128)
    for blk in range(2):
        for u in range(64):
            cb[blk * 64 + u, 3, blk * 64 + 63 - u] = 1.0
    cb[:, 5, :] = 1.0
    cc = np.arange(128)
    ang = 2.0 * np.pi * ((cc[:, None] * cc[None, :]) % 128) / 128.0
    cb[:, 6, :] = np.cos(ang) / np.sqrt(128.0)
    cb[:, 7, :] = -np.sin(ang) / np.sqrt(128.0)
    j = np.arange(128)[:, None]
    q = np.arange(128)[None, :]
    cb[:, 8, :] = np.where(j >= q, 0.0, NEG)
    cb[:, 9, :] = np.where(j <= q, 0.0, NEG)
    c["c_bf"] = _bf(cb)
    n = np.arange(256)
    a2 = 2.0 * np.pi * ((n[:, None] * n[None, :]) % 256) / 256.0
    C2 = (np.cos(a2) / 16.0).reshape(2, 128, 256)
    S2 = (np.sin(a2) / 16.0).reshape(2, 128, 256)
    cx = np.stack([C2.transpose(1, 0, 2), S2.transpose(1, 0, 2)], axis=1)
    c["c_ctxdft"] = _bf(cx)
    loc = np.arange(T_ALL)
    glob = loc if par == 0 else (T_ALL - 1 - loc)
    inv = (np.float32(10000.0) ** (-np.arange(16, dtype=np.float32) / np.float32(16))).astype(np.float32)
    g_ = glob[:T_A]
    row = (g_ // 64).astype(np.float32)
    col = (g_ % 64).astype(np.float32)
    ar = (row[None, :] * inv[:, None]).astype(np.float32)
    ac = (col[None, :] * inv[:, None]).astype(np.float32)
    cos64 = np.concatenate([np.cos(ar), np.cos(ar), np.cos(ac), np.cos(ac)], axis=0)
    sin64 = np.concatenate([-np.sin(ar), np.sin(ar), -np.sin(ac), np.sin(ac)], axis=0)
    c["c_rope"] = np.ascontiguousarray(np.stack([np.concatenate([cos64, cos64], 0), np.concatenate([sin64, sin64], 0)], 0).astype(np.float32))
    gk = glob[:T_X].astype(np.int64)
    gn = glob.astype(np.int64)
    ph = (gn[:, None] * gk[None, :]) % T_ALL
    angL = (2.0 * np.pi / T_ALL) * ph
    c["c_dftc"] = _bf(np.cos(angL) / 64.0)
    c["c_dfts"] = _bf(np.sin(angL) / 64.0)
    nm = np.zeros((128, 3, 6, 128), np.float32)
    for cls, qt in enumerate((0, 1, 8)):
        for dl in range(6):
            kt = qt + dl - 2
            if kt < 0:
                nm[:, cls, dl, :] = NEG
                continue
            kg = glob[kt * 128 + np.arange(128)]
            qg = glob[qt * 128 + np.arange(128)]
            kr, kc = kg // 64, kg % 64
            qr, qc = qg // 64, qg % 64
            r0 = np.clip(qr - 4, 0, 56)
            c0 = np.clip(qc - 8, 0, 48)
            ok = (kr[:, None] >= r0[None, :]) & (kr[:, None] < r0[None, :] + 8) & (kc[:, None] >= c0[None, :]) & (kc[:, None] < c0[None, :] + 16)
            nm[:, cls, dl, :] = np.where(ok, 0.0, NEG)
    c["c_namask"] = _bf(nm)
    _CONST_CACHE[par] = c
    return c


_NC_CACHE = {}


def get_nc(stop_after=None, debug=False, only=None, flags=()):
    key = (stop_after, debug, only, tuple(flags))
    if key not in _NC_CACHE:
        bld = Builder(stop_after=stop_after, debug=debug, only=only, flags=flags)
        _NC_CACHE[key] = bld.build()
    return _NC_CACHE[key]


def make_in_maps(inputs):
    f32 = lambda a: np.ascontiguousarray(np.asarray(a, dtype=np.float32))
    x = f32(inputs["x"])
    c = f32(inputs["c"])
    ctx = f32(inputs["ctx"])
    c_ctx = f32(inputs["c_ctx"])
    ev_in = f32(inputs["ev_w_in"])[0]
    ev_out = f32(inputs["ev_w_out"])[0]
    hp = [0, 4, 1, 5, 2, 6, 3, 7]
    qcols = np.concatenate([512 + h * 64 + np.arange(64) for h in hp])
    cols = np.concatenate([np.arange(512), qcols, np.arange(1024, 1280)])
    ev_in_p = np.ascontiguousarray(ev_in[:, cols])
    rows = np.concatenate([np.arange(512), qcols])
    ev_out_p = np.ascontiguousarray(ev_out[rows, :])
    rb = f32(inputs["od_rel_bias"])[0]
    shared = {
        "ada_w": f32(inputs["ada_w"]), "ada_b": f32(inputs["ada_b"]),
        "norm1_g": f32(inputs["norm1_g"]), "norm2_g": f32(inputs["norm2_g"]),
        "ffn_w_gate": f32(inputs["ffn_w_gate"]), "ffn_w_up": f32(inputs["ffn_w_up"]), "ffn_w_down": f32(inputs["ffn_w_down"]),
        "ev_w_in": ev_in_p, "ev_w_out": ev_out_p,
        "ev_q_norm": f32(inputs["ev_q_norm"])[0], "ev_k_norm": f32(inputs["ev_k_norm"])[0], "ev_sink": f32(inputs["ev_sink"])[0],
        "od_w_in": f32(inputs["od_w_in"])[0], "od_w_out": f32(inputs["od_w_out"])[0],
        "od_q_norm": f32(inputs["od_q_norm"])[0], "od_k_norm": f32(inputs["od_k_norm"])[0],
    }
    pad = np.zeros(64, np.float32)
    rbs = [np.concatenate([rb.reshape(-1), pad]), np.concatenate([rb[:, ::-1, ::-1].reshape(-1), pad])]
    in_maps = []
    for cid in range(8):
        b, par = cid // 2, cid % 2
        m = dict(shared)
        m["x_loc"] = np.ascontiguousarray(x[b] if par == 0 else x[b][::-1])
        m["ctx_b"] = np.ascontiguousarray(ctx[b])
        m["cvec"] = np.ascontiguousarray(np.stack([c[b], c_ctx], 0))
        m["rel_bias"] = rbs[par]
        m.update(host_consts(par))
        in_maps.append(m)
    return in_maps


def assemble(results):
    out = np.zeros((4, T_ALL, D), np.float32)
    for cid in range(8):
        b, par = cid // 2, cid % 2
        o = np.asarray(results[cid]["out_loc"], dtype=np.float32)
        if par == 0:
            out[b, :T_OWN] = o
        else:
            out[b, T_OWN:] = o[::-1]
    return out


def kernel(**inputs):
    nc = get_nc()
    in_maps = make_in_maps(inputs)
    res = run_bass_kernel_spmd(nc, in_maps, core_ids=list(range(8)))
    return assemble(res.results)
```

```python
import numpy as np
import ml_dtypes
import concourse.bass as bass
import concourse.mybir as mybir
from concourse.bass_utils import run_bass_kernel_spmd
from concourse.ap import AP

F32 = mybir.dt.float32
BF16 = mybir.dt.bfloat16
ALU = mybir.AluOpType
AF = mybir.ActivationFunctionType

NDMA_SEM = 48
D = 1024
DFF = 2816
NEG = -30000.0
EPS = 1e-6
T_ALL = 4096
T_X = 2304
T_A = 2560
T_OWN = 2048
NCTX = 256


class _Op:
    __slots__ = ("id", "eng", "fn", "deps", "is_dma", "sem_i", "sem_val", "signal", "count")


class Prog:
    ENGS = ("pe", "act", "dve", "pool", "sp")

    def __init__(self):
        self.ops = []
        self.lw = {}
        self.rd = {}
        self.dma_rr = 0
        self.dma_sem_uses = [0] * NDMA_SEM
        self.dma_sem_last = [None] * NDMA_SEM
        self.last_eng = {}
        self.bar = None
        self.dopen = set()
        self.saved = {}
        self.strict = True

    def add(self, eng, fn, reads=(), writes=(), dma=False):
        op = _Op()
        op.id = len(self.ops)
        op.eng = eng
        op.fn = fn
        op.is_dma = dma
        op.signal = False
        op.count = 0
        deps = {}
        if self.bar is not None:
            deps[self.bar] = True
        for k in reads:
            for w in self.lw.get(k, ()):
                deps[w] = True
            self.dopen.discard(k)
        cow = set()
        for k in writes:
            if dma and k in self.dopen:
                cow.add(k)
                for r in self.saved.get(k, ()):
                    deps.setdefault(r, False)
                continue
            for w in self.lw.get(k, ()):
                deps.setdefault(w, False)
            for r in self.rd.get(k, ()):
                deps.setdefault(r, False)
        if dma:
            i = self.dma_rr % NDMA_SEM
            self.dma_rr += 1
            prev = self.dma_sem_last[i]
            if prev is not None:
                deps.setdefault(prev, False)
            self.dma_sem_uses[i] += 1
            op.sem_i = i
            op.sem_val = 16 * self.dma_sem_uses[i]
            self.dma_sem_last[i] = op.id
        op.deps = deps
        for k in reads:
            self.rd.setdefault(k, []).append(op.id)
        for k in writes:
            if k in cow:
                self.lw[k].append(op.id)
                continue
            self.saved[k] = list(self.lw.get(k, ())) + list(self.rd.get(k, ()))
            self.lw[k] = [op.id]
            self.rd[k] = []
            if dma:
                self.dopen.add(k)
            else:
                self.dopen.discard(k)
        self.ops.append(op)
        if not dma:
            self.last_eng[eng] = op.id
        return op

    def barrier(self, fn):
        op = self.add("pool", fn)
        for e, i in self.last_eng.items():
            if i != op.id:
                op.deps[i] = True
        for i in self.dma_sem_last:
            if i is not None:
                op.deps[i] = True
        self.bar = op.id
        self.lw = {}
        self.rd = {}
        self.dopen = set()
        self.saved = {}

    def emit(self, block, eng_sems, dma_sems):
        ops = self.ops

        strict = self.strict

        def skip(op, dop, raw):
            if op.is_dma or dop.is_dma or dop.eng != op.eng:
                return False
            return op.eng == "pe" or (not raw and not strict)

        for op in ops:
            for d, raw in op.deps.items():
                dop = ops[d]
                if dop.is_dma or skip(op, dop, raw):
                    continue
                dop.signal = True
        cnt = {e: 0 for e in self.ENGS}
        for op in ops:
            if op.signal and not op.is_dma:
                cnt[op.eng] += 1
                op.count = cnt[op.eng]
        per_eng = {e: [o for o in ops if o.eng == e] for e in self.ENGS}

        def run(e, engine):
            known = {}
            for op in per_eng[e]:
                needs = {}
                for d, raw in op.deps.items():
                    dop = ops[d]
                    if dop.is_dma:
                        key = ("d", dop.sem_i)
                        val = dop.sem_val
                    else:
                        if skip(op, dop, raw):
                            continue
                        key = ("e", dop.eng)
                        val = dop.count
                    if needs.get(key, 0) < val:
                        needs[key] = val
                for key, val in needs.items():
                    if known.get(key, 0) >= val:
                        continue
                    known[key] = val
                    sem = dma_sems[key[1]] if key[0] == "d" else eng_sems[key[1]]
                    engine.wait_ge(sem, val)
                ins = op.fn(engine)
                if op.is_dma:
                    ins.then_inc(dma_sems[op.sem_i], 16)
                elif op.signal:
                    ins.then_inc(eng_sems[e], 1)
            if e == "sp":
                for i in range(NDMA_SEM):
                    if self.dma_sem_uses[i] > 0:
                        engine.wait_ge(dma_sems[i], 16 * self.dma_sem_uses[i])

        block.tensor(lambda eng: run("pe", eng))
        block.scalar(lambda eng: run("act", eng))
        block.vector(lambda eng: run("dve", eng))
        block.gpsimd(lambda eng: run("pool", eng))
        block.sync(lambda eng: run("sp", eng))


def groups(n, g=512):
    out = []
    s = 0
    while s < n:
        out.append((s, min(g, n - s)))
        s += g
    return out


class Builder:
    def __init__(self, stop_after=None, debug=False, only=None, flags=()):
        self.only = only
        self.flags = set(f for f in flags if not f.startswith('n='))
        self.nstep = ([int(f[2:]) for f in flags if f.startswith('n=')] + [8])[0]
        self.nc = bass.Bass("TRN2", target_bir_lowering=False)
        self.P = Prog()
        self.stop_after = stop_after
        self.debug = debug
        nc = self.nc
        self.lo = ((nc.sbuf_base + 63) // 64) * 64
        self.hi = nc.sbuf_top
        self.pers = self.lo
        self.cur = None
        self.uid = 0
        self.rr = 0
        self.evr = 0
        self.rot = [2, 3, 4, 5, 6]
        self.nmi = 0
        self.tfi = 0
        self.qni = 0

    def _alloc(self, ptr, name, shape, dt):
        esz = 4 if dt == F32 else 2
        n = 1
        for s in shape[1:]:
            n *= s
        nbytes = ((n * esz + 63) // 64) * 64
        self.uid += 1
        t = self.nc.alloc_sbuf_tensor_at("%s_%d" % (name, self.uid), list(shape), dt, offset=ptr)
        return t, ptr + nbytes

    def palloc(self, name, shape, dt):
        assert self.cur is None
        t, self.pers = self._alloc(self.pers, name, shape, dt)
        assert self.pers <= self.hi, "persistent overflow"
        return t

    def talloc(self, name, shape, dt):
        t, self.cur = self._alloc(self.cur, name, shape, dt)
        assert self.cur <= self.hi, "phase overflow %s %d" % (name, self.cur - self.hi)
        return t

    def phase(self):
        scr = self.scr
        self.P.barrier(lambda e: e.memset(scr[0:1, 0:8], 0.0))
        self.cur = self.pers

    def dma(self, eng, out, in_, reads=(), writes=()):
        self.P.add(eng, lambda e: e.dma_start(out=out, in_=in_), reads=reads, writes=writes, dma=True)

    def mm(self, out, lhsT, rhs, start, stop, reads, writes):
        self.P.add("pe", lambda e: e.matmul(out, lhsT=lhsT, rhs=rhs, start=start, stop=stop), reads=reads, writes=writes)

    def tr(self, out, in_, reads, writes):
        ident = self.identf
        k = in_.shape[0]
        self.P.add("pe", lambda e: e.transpose(out, in_, ident[0:k, 0:k]), reads=list(reads) + ["c_id"], writes=writes)

    def act(self, out, in_, func, reads, writes, bias=None, scale=None):
        kw = {}
        if bias is not None:
            kw["bias"] = bias
        if scale is not None:
            kw["scale"] = scale
        self.P.add("act", lambda e: e.activation(out=out, in_=in_, func=func, **kw), reads=reads, writes=writes)

    def tt(self, eng, out, in0, in1, op, reads, writes):
        self.P.add(eng, lambda e: e.tensor_tensor(out=out, in0=in0, in1=in1, op=op), reads=reads, writes=writes)

    def stt(self, eng, out, in0, scalar, in1, op0, op1, reads, writes):
        self.P.add(eng, lambda e: e.scalar_tensor_tensor(out=out, in0=in0, scalar=scalar, in1=in1, op0=op0, op1=op1),
                   reads=reads, writes=writes)

    def ts(self, eng, out, in0, s1, s2, op0, op1, reads, writes):
        if s2 is None:
            self.P.add(eng, lambda e: e.tensor_scalar(out=out, in0=in0, scalar1=s1, scalar2=None, op0=op0), reads=reads, writes=writes)
        else:
            self.P.add(eng, lambda e: e.tensor_scalar(out=out, in0=in0, scalar1=s1, scalar2=s2, op0=op0, op1=op1), reads=reads, writes=writes)

    def copy(self, eng, out, in_, reads, writes):
        if eng == "act":
            self.act(out, in_, AF.Copy, reads, writes)
        else:
            self.P.add(eng, lambda e: e.tensor_copy(out=out, in_=in_), reads=reads, writes=writes)

    def evac(self, out, in_, reads, writes):
        self.evr += 1
        self.copy("act" if self.evr % 2 else "dve", out, in_, reads, writes)

    def dump(self, name, t, key):
        if not self.debug:
            return
        shape = list(t.shape)
        d = self.nc.dram_tensor("dbg_" + name, shape, t.dtype, kind="ExternalOutput")
        self.dma("sp", d.ap(), t[:], reads=[key])

    def bank(self):
        self.rr += 1
        i = self.rot[self.rr % len(self.rot)]
        return self.pb[i], "pb%d" % i

    def build(self):
        nc = self.nc
        P = self.P
        dr = lambda name, shape, dt, kind="ExternalInput": nc.dram_tensor(name, list(shape), dt, kind=kind)
        self.x_d = dr("x_loc", [T_ALL, D], F32)
        self.ctx_d = dr("ctx_b", [NCTX, D], F32)
        self.cvec_d = dr("cvec", [2, D], F32)
        self.ada_w_d = dr("ada_w", [2, D, 6 * D], F32)
        self.ada_b_d = dr("ada_b", [2, 6 * D], F32)
        self.n1_d = dr("norm1_g", [2, D], F32)
        self.n2_d = dr("norm2_g", [2, D], F32)
        self.wg_d = dr("ffn_w_gate", [2, D, DFF], F32)
        self.wu_d = dr("ffn_w_up", [2, D, DFF], F32)
        self.wd_d = dr("ffn_w_down", [2, DFF, D], F32)
        self.evin_d = dr("ev_w_in", [D, 1280], F32)
        self.evout_d = dr("ev_w_out", [D, D], F32)
        self.evq_d = dr("ev_q_norm", [64], F32)
        self.evk_d = dr("ev_k_norm", [64], F32)
        self.sink_d = dr("ev_sink", [8], F32)
        self.odin_d = dr("od_w_in", [D, 3 * D], F32)
        self.odout_d = dr("od_w_out", [D, D], F32)
        self.odq_d = dr("od_q_norm", [64], F32)
        self.odk_d = dr("od_k_norm", [64], F32)
        self.rb_d = dr("rel_bias", [16 * 15 * 31 + 64], F32)
        self.cidf_d = dr("c_identf", [128, 128], F32)
        self.cpsw_d = dr("c_pswap", [128, 128], F32)
        self.cbf_d = dr("c_bf", [128, 10, 128], BF16)
        self.cctx_d = dr("c_ctxdft", [128, 2, 2, 256], BF16)
        self.cnam_d = dr("c_namask", [128, 3, 6, 128], BF16)
        self.crope_d = dr("c_rope", [2, 128, T_A], F32)
        self.cdft_d = [dr("c_dftc", [T_ALL, T_X], BF16), dr("c_dfts", [T_ALL, T_X], BF16)]
        self.out_d = dr("out_loc", [T_OWN, D], F32, kind="ExternalOutput")
        if self.debug:
            self.dbgx_d = dr("dbg_x", [128, 8, T_X], F32, kind="ExternalOutput")
            self.dbgy_d = dr("dbg_y", [128, 8, NCTX], F32, kind="ExternalOutput")
            self.dbga_d = dr("dbg_a", [128, 8, T_X], F32, kind="ExternalOutput")

        self.pb = [nc.alloc_psum_tensor("pb%d" % i, [128, 512], F32) for i in range(8)]
        self.eng_sems = {e: nc.alloc_semaphore("s_" + e) for e in Prog.ENGS}
        self.dma_sems = [nc.alloc_semaphore("d%d" % i) for i in range(NDMA_SEM)]

        self.setup()
        done = False
        for name, fn in [("A", self.layer0_A), ("B", self.layer0_B), ("C", self.layer0_C), ("D", lambda: self.ffn(0)),
                         ("E", self.layer1_E), ("F", lambda: self.ffn(1))]:
            if self.only is not None and name not in self.only:
                continue
            if not hasattr(self, "XT") and name in ("D", "E", "F"):
                self.ada_tick(100)
                self.rot = [2, 3, 4, 5, 6, 7]
                self.pers -= 2 * 4096
                self.XT = self.nc.alloc_sbuf_tensor_at("XT_res", [128, 8, T_X], F32, offset=self.hi - 8 * T_X * 4)
            fn()
            if self.stop_after == name:
                done = True
                break
        self.finish(final=not done)
        with nc.Block() as block:
            P.emit(block, self.eng_sems, self.dma_sems)
        return nc

    def setup(self):
        nc = self.nc
        pa = self.palloc
        self.scr = pa("scr", [128, 16], F32)
        self.identf = pa("identf", [128, 128], F32)
        self.pswap = pa("pswap", [128, 128], F32)
        self.cbf = pa("cbf", [128, 10, 128], BF16)
        self.cctx = pa("cctx", [128, 2, 2, 256], BF16)
        self.vecT = pa("vecT", [128, 48], F32)
        self.biasT = pa("biasT", [128, 96], F32)
        self.modT = pa("modT", [128, 2, 48, 2], F32)
        self.A1 = pa("A1", [128, 2, 2, 8], F32)
        self.A2 = pa("A2", [128, 2, 2, 8], F32)
        self.sT = pa("sT", [128, 2, 8], BF16)
        self.gq = pa("gq", [128, 4], F32)
        self.sinkbc = pa("sinkbc", [128, 4, 128], F32)
        self.skt = pa("skt", [128, 4], F32)
        self.zf = pa("zf", [128, 128], F32)
        self.YT = pa("YT", [128, 8, NCTX], F32)
        self.dma("sp", self.identf[:], self.cidf_d.ap(), writes=["c_id"])
        self.dma("sp", self.pswap[:], self.cpsw_d.ap(), writes=["c_ps"])
        self.dma("sp", self.cbf[:], self.cbf_d.ap(), writes=["c_bf"])
        self.dma("sp", self.cctx[:], self.cctx_d.ap(), writes=["c_cx"])
        self.P.add("pool", lambda e: e.memset(self.zf[:], 0.0), writes=["zf"])
        self.P.add("pool", lambda e: e.memset(self.scr[:], 0.0), writes=["scr"])
        self.ones1024 = self.cbf[:, 0, :]
        self.blk64 = self.cbf[:, 1, :]
        self.identb = self.cbf[:, 2, :]
        self.rblk = self.cbf[:, 3, :]
        self.zerob = self.cbf[:, 4, :]
        self.onesb = self.cbf[:, 5, :]
        self.ccos = self.cbf[:, 6, :]
        self.cnsin = self.cbf[:, 7, :]
        self.mprev = self.cbf[:, 8, :]
        self.mnext = self.cbf[:, 9, :]
        self.adabuf = None
        self.cur = self.pers + 2 * 4096
        ta = self.talloc
        rows1 = ta("rows1", [48, 128], F32)
        rows2 = ta("rows2", [96, 128], F32)
        self.dma("sp", rows1[0:16, :], self.n1_d.ap().rearrange("l (c p) -> (l c) p", p=128), writes=["rows1"])
        self.dma("sp", rows1[16:32, :], self.n2_d.ap().rearrange("l (c p) -> (l c) p", p=128), writes=["rows1"])
        self.dma("sp", rows1[32:48, :], self.cvec_d.ap().rearrange("s (c p) -> (s c) p", p=128), writes=["rows1"])
        self.dma("sp", rows2[:, :], self.ada_b_d.ap().rearrange("l (c p) -> (l c) p", p=128), writes=["rows2"])
        b, bk = self.bank()
        self.tr(b[:, 0:48], rows1[:, :], ["rows1"], [bk])
        self.copy("dve", self.vecT[:], b[:, 0:48], [bk], ["vecT"])
        b, bk = self.bank()
        self.tr(b[:, 0:96], rows2[:, :], ["rows2"], [bk])
        self.copy("dve", self.biasT[:], b[:, 0:96], [bk], ["biasT"])
        self.act(self.sT[:].rearrange("p s c -> p (s c)"), self.vecT[:, 32:48], AF.Silu, ["vecT"], ["sT"])
        for i, (d_, sc) in enumerate([(self.evq_d, 0.125), (self.evk_d, 1.0), (self.odq_d, 0.125), (self.odk_d, 1.0)]):
            src = d_.ap().rearrange("(d u) -> d u", u=1)
            self.dma("sp", self.gq[0:64, i:i + 1], src, writes=["gq"])
            self.dma("sp", self.gq[64:128, i:i + 1], src, writes=["gq"])
        self.ts("dve", self.gq[:, 0:1], self.gq[:, 0:1], 0.125, None, ALU.mult, None, ["gq"], ["gq"])
        self.ts("dve", self.gq[:, 2:3], self.gq[:, 2:3], 0.125, None, ALU.mult, None, ["gq"], ["gq"])
        self.dma("sp", self.skt[0:64, :], AP(self.sink_d, 0, [[0, 64], [1, 4]]), writes=["skt"])
        self.dma("sp", self.skt[64:128, :], AP(self.sink_d, 4, [[0, 64], [1, 4]]), writes=["skt"])
        self.act(self.skt[:], self.skt[:], AF.Exp, ["skt"], ["skt"])
        for j in range(4):
            self.ts("dve", self.sinkbc[:, j, :], self.zf[:], self.skt[:, j:j + 1], None, ALU.add, None, ["zf", "skt"], ["sinkbc"])
        self.adabuf = [self.palloc_late("adabuf%d" % i, [128, 8, 256], BF16) for i in range(2)]
        self.ada_list = [(l, sl) for l in range(2) for sl in range(24)]
        self.ada_pos = 0
        self.ada_issued = 0
        self.win = self.nc.alloc_sbuf_tensor_at("win_top", [128, 8, 1280], BF16, offset=self.hi - 20480)
        self.ada_issue()
        self.ada_tick(4)
        self.load_w(self.win[:], "win", self.evin_d.ap().rearrange("(c p) n -> p c n", p=128))
        self.ada_tick(4)

    def palloc_late(self, name, shape, dt):
        t, self.pers = self._alloc(self.pers, name, shape, dt)
        return t

    def ada_issue(self):
        if self.ada_issued >= len(self.ada_list):
            return
        l, sl = self.ada_list[self.ada_issued]
        i = self.ada_issued % 2
        src = self.ada_w_d.ap()[l].rearrange("(c p) n -> p c n", p=128)[:, :, sl * 256:(sl + 1) * 256]
        self.dma("pool", self.adabuf[i][:], src, writes=["adabuf%d" % i])
        self.ada_issued += 1

    def ada_tick(self, n=1):
        for _ in range(n):
            if self.ada_pos >= len(self.ada_list):
                return
            l, sl = self.ada_list[self.ada_pos]
            i = self.ada_pos % 2
            self.ada_pos += 1
            self.ada_issue()
            buf, bk = self.adabuf[i], "adabuf%d" % i
            pm, pmk = self.pb[7], "pb7"
            for o2 in range(2):
                oc = sl * 2 + o2
                for kc in range(8):
                    self.mm(pm[:, oc * 2:oc * 2 + 2], buf[:, kc, o2 * 128:(o2 + 1) * 128], self.sT[:, :, kc],
                            kc == 0, kc == 7, [bk, "sT"], [pmk])
            if sl == 7 or sl == 23:
                lo, hi = (0, 16) if sl == 7 else (16, 48)
                for s_ in range(2):
                    self.tt("dve", self.modT[:, l, lo:hi, s_], pm[:, 0:96].rearrange("p (o s) -> p o s", s=2)[:, lo:hi, s_],
                            self.biasT[:, l * 48 + lo:l * 48 + hi], ALU.add, [pmk, "biasT"], ["modT"])
                    if sl == 7:
                        self.stt("dve", self.A1[:, l, s_, :], self.modT[:, l, 8:16, s_], 1.0, self.vecT[:, l * 8:l * 8 + 8],
                                 ALU.add, ALU.mult, ["modT", "vecT"], ["A1"])
                    else:
                        self.stt("dve", self.A2[:, l, s_, :], self.modT[:, l, 32:40, s_], 1.0, self.vecT[:, 16 + l * 8:16 + l * 8 + 8],
                                 ALU.add, ALU.mult, ["modT", "vecT"], ["A2"])

    def mod(self, l, part, s, c):
        return self.modT[:, l, part * 8 + c, s:s + 1]

    def load_xT(self, src_rows, ntile, dst, dst_key, col0, xtoks):
        for t in range(ntile):
            xt, xk = xtoks[self.xti % len(xtoks)]
            self.xti += 1
            self.dma("sp", xt[:], src_rows[t * 128:(t + 1) * 128, :], writes=[xk])
            for half in range(2):
                b, bk = self.bank()
                for cc in range(4):
                    c = half * 4 + cc
                    self.tr(b[:, cc * 128:(cc + 1) * 128], xt[:, c * 128:(c + 1) * 128], [xk], [bk])
                o = dst[:, half * 4:half * 4 + 4, col0 + t * 128:col0 + (t + 1) * 128]
                self.evac(o, b[:, :].rearrange("p (c n) -> p c n", n=128), [bk], [dst_key])

    def norm_mod(self, xT, xkey, col0, N, A, Bf, hT, hkey, hcol0, tmp, part=None, state=None):
        sqbs, rss, tf = tmp
        if part in (None, 1):
            self.nmi += 1
            sqb, sqk = sqbs[self.nmi % len(sqbs)], "sqb%d" % (self.nmi % len(sqbs))
            rs, rk = rss[self.nmi % len(rss)], "rsn%d" % (self.nmi % len(rss))
            nsq = sqb.shape[1]
            b, bk = self.bank()
            for c0 in range(0, 8, nsq):
                for c in range(c0, c0 + nsq):
                    xi = xT[:, c, col0:col0 + N]
                    if c % 4 == 3:
                        self.tt("dve", sqb[:, c % nsq, 0:N], xi, xi, ALU.mult, [xkey], [sqk])
                    else:
                        self.act(sqb[:, c % nsq, 0:N], xi, AF.Square, [xkey], [sqk])
                for c in range(c0, c0 + nsq):
                    self.mm(b[:, 0:N], self.ones1024, sqb[:, c % nsq, 0:N], c == 0, c == 7, [sqk, "c_bf"], [bk])
            self.act(rs[:, 0:N], b[:, 0:N], AF.Ln, [bk], [rk], bias=EPS)
            self.act(rs[:, 0:N], rs[:, 0:N], AF.Exp, [rk], [rk], scale=-0.5)
            state = (rs, rk)
            if part == 1:
                return state
        rs, rk = state
        for c in range(8):
            self.tfi += 1
            t_ = tf[self.tfi % len(tf)]
            tk = "tf%d" % (self.tfi % len(tf))
            self.stt("dve", t_[:, 0:N], xT[:, c, col0:col0 + N], A(c), rs[:, 0:N], ALU.mult, ALU.mult,
                     [xkey, rk, "A1", "A2"], [tk])
            if part == 2 and c % 2 == 1:
                self.ts("dve", hT[:, c, hcol0:hcol0 + N], t_[:, 0:N], Bf(c), None, ALU.add, None, [tk, "modT"], [hkey])
            else:
                self.act(hT[:, c, hcol0:hcol0 + N], t_[:, 0:N], AF.Identity, [tk, "modT"], [hkey], bias=Bf(c))
        return None

    def norm_pipeline(self, calls):
        st = [None] * len(calls)
        if calls:
            st[0] = self.norm_mod(part=1, **calls[0])
        for i in range(len(calls)):
            if i + 1 < len(calls):
                st[i + 1] = self.norm_mod(part=1, **calls[i + 1])
            self.norm_mod(part=2, state=st[i], **calls[i])

    def qk_norm(self, praw, pk, N, gcol, out, okey, tmp, rope=None):
        sq1s, rss, qns, r1s = tmp
        self.qni += 1
        i = self.qni
        sq1, sk = sq1s[i % len(sq1s)], "sq1_%d" % (i % len(sq1s))
        rs, rk = rss[i % len(rss)], "rsq%d" % (i % len(rss))
        self.act(sq1[:, 0:N], praw[:, 0:N], AF.Square, [pk], [sk])
        b, bk = self.bank()
        self.mm(b[:, 0:N], self.blk64, sq1[:, 0:N], True, True, [sk, "c_bf"], [bk])
        self.act(rs[:, 0:N], b[:, 0:N], AF.Ln, [bk], [rk], bias=EPS)
        self.act(rs[:, 0:N], rs[:, 0:N], AF.Exp, [rk], [rk], scale=-0.5)
        if rope is None:
            self.stt("dve", out, praw[:, 0:N], gcol, rs[:, 0:N], ALU.mult, ALU.mult, [pk, rk, "gq"], [okey])
            return
        qn, qk_ = qns[i % len(qns)], "qn%d" % (i % len(qns))
        r1, r1k = r1s[i % len(r1s)], "r1_%d" % (i % len(r1s))
        cosT, sinT, rpk = rope
        self.stt("dve", qn[:, 0:N], praw[:, 0:N], gcol, rs[:, 0:N], ALU.mult, ALU.mult, [pk, rk, "gq"], [qk_])
        b2, bk2 = self.bank()
        self.mm(b2[:, 0:N], self.pswap[:], qn[:, 0:N], True, True, [qk_, "c_bf"], [bk2])
        self.tt("pool", r1[:, 0:N], qn[:, 0:N], cosT, ALU.mult, [qk_] + rpk, [r1k])
        self.tt("dve", rs[:, 0:N], b2[:, 0:N], sinT, ALU.mult, [bk2] + rpk, [rk])
        self.tt("dve", out, rs[:, 0:N], r1[:, 0:N], ALU.add, [rk, r1k], [okey])

    def qk_pipeline(self, jobs, tmp):
        sq1s, rss, qns, r1s = tmp
        st = []
        for j, jb in enumerate(jobs):
            d = dict(jb)
            d["sq1"], d["sk"] = sq1s[j % len(sq1s)], "sq1_%d" % (j % len(sq1s))
            d["rs"], d["rk"] = rss[j % len(rss)], "rsq%d" % (j % len(rss))
            if jb["rope"] is not None:
                d["qn"], d["qk"] = qns[j % len(qns)], "qn%d" % (j % len(qns))
                d["r1"], d["r1k"] = r1s[j % len(r1s)], "r1_%d" % (j % len(r1s))
            st.append(d)

        def sa(d):
            d["b"], d["bk"] = d["proj"]()

        def sb(d):
            N = d["N"]
            praw, pk, sq1, sk, rs, rk = d["b"], d["bk"], d["sq1"], d["sk"], d["rs"], d["rk"]
            self.act(sq1[:, 0:N], praw[:, 0:N], AF.Square, [pk], [sk])
            b, bk = self.bank()
            self.mm(b[:, 0:N], self.blk64, sq1[:, 0:N], True, True, [sk, "c_bf"], [bk])
            self.act(rs[:, 0:N], b[:, 0:N], AF.Ln, [bk], [rk], bias=EPS)
            self.act(rs[:, 0:N], rs[:, 0:N], AF.Exp, [rk], [rk], scale=-0.5)
            if d["rope"] is None:
                self.stt("dve", d["out"], praw[:, 0:N], d["gcol"], rs[:, 0:N], ALU.mult, ALU.mult, [pk, rk, "gq"], [d["okey"]])
            else:
                self.stt("dve", d["qn"][:, 0:N], praw[:, 0:N], d["gcol"], rs[:, 0:N], ALU.mult, ALU.mult, [pk, rk, "gq"], [d["qk"]])

        def sc(d):
            if d["rope"] is None:
                return
            N = d["N"]
            cosT, sinT, rpk = d["rope"]
            qn, qk_, r1, r1k, rs, rk = d["qn"], d["qk"], d["r1"], d["r1k"], d["rs"], d["rk"]
            b2, bk2 = self.bank()
            self.mm(b2[:, 0:N], self.pswap[:], qn[:, 0:N], True, True, [qk_, "c_bf"], [bk2])
            self.tt("pool", r1[:, 0:N], qn[:, 0:N], cosT, ALU.mult, [qk_] + rpk, [r1k])
            self.tt("dve", rs[:, 0:N], b2[:, 0:N], sinT, ALU.mult, [bk2] + rpk, [rk])
            self.tt("dve", d["out"], rs[:, 0:N], r1[:, 0:N], ALU.add, [rk, r1k], [d["okey"]])

        n = len(st)
        for t in range(n + 2):
            if t < n:
                sa(st[t])
            if 0 <= t - 1 < n:
                sb(st[t - 1])
            if 0 <= t - 2 < n:
                sc(st[t - 2])

    def load_w(self, dst, dkey, src):
        self.dma("pool", dst, src, writes=[dkey])

    def layer0_A(self):
        self.phase()
        ta = self.talloc
        self.rot = [0, 1, 2, 3, 4, 5, 6]
        self.QT = ta("QT", [128, 4, T_A], BF16)
        self.KT = ta("KT", [128, T_A], BF16)
        self.V = ta("V", [128, T_A // 128, 128], BF16)
        self.QcT = ta("QcT", [128, 4, NCTX], BF16)
        self.KcT = ta("KcT", [128, NCTX], BF16)
        self.Vc = ta("Vc", [128, 2, 128], BF16)
        fo_off = self.cur
        self.FO = ta("FO", [128, 4, T_X], BF16)
        self.FOc = ta("FOc", [128, 4, NCTX], BF16)
        self.keepC = self.cur
        self.F_all = ta("F_all", [128, 32, 512], BF16)
        self.Fc = ta("Fc", [128, 2, 512], BF16)
        self.keepA = self.cur
        win = self.win
        xtoks = [(ta("xtok%d" % i, [128, 1024], F32), "xtok%d" % i) for i in range(4)]
        self.xti = 0
        self.uid += 1
        xTb2 = self.nc.alloc_sbuf_tensor_at("xTb2_%d" % self.uid, [128, 8, 512], F32, offset=fo_off)
        xTbs = [(ta("xTb", [128, 8, 512], F32), "xTb0"), (xTb2, "xTb1")]
        hTs = [(ta("hT%d" % i, [128, 8, 512], BF16), "hT%d" % i) for i in range(2)]
        sqbs = [ta("sqb", [128, 4, 512], BF16)]
        rss = [ta("rsn%d" % i, [128, 512], F32) for i in range(2)]
        tf = [ta("tf%d" % i, [128, 512], F32) for i in range(2)]
        sq1s = [ta("sq1_%d" % i, [128, 512], BF16) for i in range(2)]
        rsq = [ta("rsq%d" % i, [128, 512], F32) for i in range(2)]
        self.uid += 1
        qns = [self.nc.alloc_sbuf_tensor_at("qn0_%d" % self.uid, [128, 512], F32, offset=fo_off + 16384), ta("qn1", [128, 512], F32)]
        r1s = [self.nc.alloc_sbuf_tensor_at("r10_%d" % self.uid, [128, 512], F32, offset=fo_off + 16384 + 2048)]
        ropeb = [ta("rope%d" % i, [128, 2, 512], F32) for i in range(1)]
        tmpn = (sqbs, rss, tf)
        tmpq = (sq1s, rsq, qns, r1s)
        assert self.cur <= self.hi - 20480, "phase A overflow into win"
        l = 0

        def projA(hT, hk, N, qkv, s, tok0, tile0, ropek):
            for t in range(N // 128):
                b, bk = self.bank()
                for c in range(8):
                    self.mm(b[:, 0:512], hT[:, c, t * 128:(t + 1) * 128], win[:, c, 0:512], c == 0, c == 7, [hk, "win"], [bk])
                dst = self.F_all[:, tile0 + t, :] if s == 0 else self.Fc[:, t, :]
                self.evac(dst, b[:, 0:512], [bk], ["F_all" if s == 0 else "Fc"])
            if not qkv:
                return
            for t in range(N // 128):
                b, bk = self.bank()
                for c in range(8):
                    self.mm(b[:, 0:128], hT[:, c, t * 128:(t + 1) * 128], win[:, c, 1152:1280], c == 0, c == 7, [hk, "win"], [bk])
                dst = self.V[:, tile0 + t, :] if s == 0 else self.Vc[:, t, :]
                self.evac(dst, b[:, 0:128], [bk], ["V" if s == 0 else "Vc"])

        def projB(hT, hk, N, qkv, s, tok0, tile0, ropek):
            if not qkv:
                return
            jobs = []
            for j in range(5):
                def proj(j=j):
                    b, bk = self.bank()
                    for c in range(8):
                        self.mm(b[:, 0:N], win[:, c, 512 + j * 128:512 + (j + 1) * 128], hT[:, c, 0:N], c == 0, c == 7, [hk, "win"], [bk])
                    return b, bk
                if s == 0:
                    out = self.QT[:, j, tok0:tok0 + N] if j < 4 else self.KT[:, tok0:tok0 + N]
                    okey = "QT" if j < 4 else "KT"
                    rb_ = ropeb[0]
                    rope = (rb_[:, 0, 0:N], rb_[:, 1, 0:N], ["rope0c", "rope0s"])
                else:
                    out = self.QcT[:, j, 0:N] if j < 4 else self.KcT[:, 0:N]
                    okey = "QcT" if j < 4 else "KcT"
                    rope = None
                jobs.append(dict(proj=proj, N=N, gcol=self.gq[:, 0:1] if j < 4 else self.gq[:, 1:2], out=out, okey=okey, rope=rope))
            self.qk_pipeline(jobs, tmpq)

        items = [("ctx", 0)] + [("lat", g) for g in range(8)]

        def stage1(idx):
            kind, g = items[idx]
            hT, hk = hTs[idx % 2]
            if kind == "ctx":
                self.load_xT(self.ctx_d.ap(), 2, self.YT, "YT", 0, xtoks)
                return (hT, hk, NCTX, True, 1, 0, 0, 0, self.YT, "YT")
            tok0 = g * 512
            xTb, xk = xTbs[idx % 2]
            self.load_xT(self.x_d.ap()[tok0:tok0 + 512, :], 4, xTb, xk, 0, xtoks)
            return (hT, hk, 512, g < 5, 0, tok0, g * 4, g, xTb, xk)

        def stage2(st, part, state=None):
            hT, hk, N, qkv, s_, tok0, tile0, rk, xT, xk = st
            return self.norm_mod(xT, xk, 0, N, lambda c: self.A1[:, l, s_, c:c + 1], lambda c: self.mod(l, 0, s_, c),
                                 hT, hk, 0, tmpn, part=part, state=state)

        cur = stage1(0)
        stage2(cur, 2, stage2(cur, 1))
        for idx in range(len(items)):
            nxt = stage1(idx + 1) if idx + 1 < len(items) else None
            projA(*cur[:8])
            self.ada_tick(1)
            if nxt is not None:
                stage2(nxt, 2, stage2(nxt, 1))
            projB(*cur[:8])
            if nxt is not None and nxt[3] and nxt[4] == 0:
                rb_ = ropeb[0]
                tk0 = nxt[5]
                self.dma("sp", rb_[:, 0, :], self.crope_d.ap()[0, :, tk0:tk0 + 512], writes=["rope0c"])
                self.dma("sp", rb_[:, 1, :], self.crope_d.ap()[1, :, tk0:tk0 + 512], writes=["rope0s"])
            self.ada_tick(1)
            cur = nxt
        if self.stop_after == "A":
            for nm_, t_, k_ in [("F_all", self.F_all, "F_all"), ("QT", self.QT, "QT"), ("KT", self.KT, "KT"), ("V", self.V, "V"),
                                ("Fc", self.Fc, "Fc"), ("QcT", self.QcT, "QcT"), ("KcT", self.KcT, "KcT"), ("Vc", self.Vc, "Vc"),
                                ("modT", self.modT, "modT")]:
                self.dump(nm_, t_, k_)

    def dft(self, Fsrc, fkey, ntile, tabs, K, FO, fokey, tabbuf, ABT):
        ti = 0
        for (k0, N) in groups(K):
            for cs in range(2):
                tb, tk = tabbuf[ti % 2], "tab%d" % (ti % 2)
                ti += 1
                src_t = tabs(cs, k0, N)
                for a0 in range(0, ntile, 8):
                    self.dma("sp", tb[:, a0:a0 + 8, 0:N], src_t[:, a0:a0 + 8, :], writes=[tk])
                for g in range(4):
                    b, bk = self.bank()
                    for a in range(ntile):
                        self.mm(b[:, 0:N], Fsrc[:, a, g * 128:(g + 1) * 128], tb[:, a, 0:N], a == 0, a == ntile - 1, [fkey, tk], [bk])
                    self.evac(ABT[:, cs, g, 0:N], b[:, 0:N], [bk], ["ABT"])
                    self.ada_tick(1)
            for g in range(4):
                b, bk = self.bank()
                self.mm(b[:, 0:N], self.ccos, ABT[:, 0, g, 0:N], True, False, ["ABT", "c_bf"], [bk])
                self.mm(b[:, 0:N], self.cnsin, ABT[:, 1, g, 0:N], False, True, ["ABT", "c_bf"], [bk])
                self.evac(FO[:, g, k0:k0 + N], b[:, 0:N], [bk], [fokey])

    def layer0_B(self):
        scr = self.scr
        self.P.barrier(lambda e: e.memset(scr[0:1, 0:8], 0.0))
        self.cur = self.keepA
        ta = self.talloc
        tabbuf = [ta("tab%d" % i, [128, 32, 512], BF16) for i in range(2)]
        ABT = ta("ABT", [128, 2, 4, 512], BF16)
        for cs in range(2):
            for g in range(4):
                b, bk = self.bank()
                for a in range(2):
                    self.mm(b[:, 0:NCTX], self.Fc[:, a, g * 128:(g + 1) * 128], self.cctx[:, cs, a, :], a == 0, a == 1, ["Fc", "c_bf"], [bk])
                self.evac(ABT[:, cs, g, 0:NCTX], b[:, 0:NCTX], [bk], ["ABT"])
        for g in range(4):
            b, bk = self.bank()
            self.mm(b[:, 0:NCTX], self.ccos, ABT[:, 0, g, 0:NCTX], True, False, ["ABT", "c_bf"], [bk])
            self.mm(b[:, 0:NCTX], self.cnsin, ABT[:, 1, g, 0:NCTX], False, True, ["ABT", "c_bf"], [bk])
            self.evac(self.FOc[:, g, :], b[:, 0:NCTX], [bk], ["FOc"])
        tabs = lambda cs, k0, N: self.cdft_d[cs].ap().rearrange("(a p) k -> p a k", p=128)[:, :, k0:k0 + N]
        self.dft(self.F_all, "F_all", 32, tabs, T_X, self.FO, "FO", tabbuf, ABT)
        self.ada_tick(100)
        self.rot = [2, 3, 4, 5, 6, 7]
        self.pers -= 2 * 4096
        if self.stop_after == "B":
            self.dump("FO", self.FO, "FO")
            self.dump("FOc", self.FOc, "FOc")

    def layer0_C(self):
        scr = self.scr
        self.P.barrier(lambda e: e.memset(scr[0:1, 0:8], 0.0))
        self.cur = self.keepC
        ta = self.talloc
        l = 0
        wout = ta("wout", [128, 8, D], BF16)
        self.load_w(wout[:], "wout", self.evout_d.ap().rearrange("(c p) n -> p c n", p=128))
        ATTs = [(ta("ATT%d" % i, [128, 4, 512], BF16), "ATT%d" % i) for i in range(2)]
        qlos = [(ta("qlo%d" % i, [128, 4, 128], BF16), "qlo%d" % i) for i in range(2)]
        qhis = [(ta("qhi%d" % i, [128, 4, 128], BF16), "qhi%d" % i) for i in range(2)]
        PT = [ta("PT%d" % i, [128, 512], BF16) for i in range(4)]
        dtots = [(ta("dtot%d" % i, [128, 512], F32), "dtot%d" % i) for i in range(2)]
        xtoks = [(ta("xtokC%d" % i, [128, 1024], F32), "xtokC%d" % i) for i in range(3)]
        self.xti = 0
        self.XT = self.nc.alloc_sbuf_tensor_at("XT_res", [128, 8, T_X], F32, offset=self.hi - 8 * T_X * 4)
        assert self.cur <= self.hi - 8 * T_X * 4, "phase C overflow"
        for (q_, k_) in qlos + qhis:
            self.P.add("pool", lambda e, q_=q_: e.memset(q_[:], 0.0), writes=[k_])
        self.rot = [4, 5, 6, 7]
        accs = [((self.pb[0], "pb0"), (self.pb[1], "pb1")), ((self.pb[2], "pb2"), (self.pb[3], "pb3"))]
        ctxtiles = [(self.KcT[:, t * 128:(t + 1) * 128], self.Vc[:, t, :], None, ["KcT", "Vc"]) for t in range(2)]

        jobs = []

        def outproj(N, FOsrc, fokey, focol, Xdst, xkey, xcol, s_, ATT, ak):
            for oc in range(8):
                b, bk = self.bank()
                for kc in range(8):
                    rhs = FOsrc[:, kc, focol:focol + N] if kc < 4 else ATT[:, kc - 4, 0:N]
                    self.mm(b[:, 0:N], wout[:, kc, oc * 128:(oc + 1) * 128], rhs, kc == 0, kc == 7, ["wout", fokey, ak], [bk])
                xo = Xdst[:, oc, xcol:xcol + N]
                self.stt("dve", xo, b[:, 0:N], self.mod(l, 2, s_, oc), xo, ALU.mult, ALU.add, [bk, "modT", xkey], [xkey])

        gi = 0
        for qt in range(2):
            post = (lambda ai=gi % 2: outproj(NCTX, self.FOc, "FOc", 0, self.YT, "YT", 0, 1, *ATTs[ai])) if qt == 1 else None
            jobs.append((self.QcT, "QcT", qt * 128, ctxtiles, gi % 2, qt * 128, None, post))
        gi += 1
        for (t0, N) in groups(T_X):
            for tq in range(N // 128):
                qt = t0 // 128 + tq
                kts = []
                for dlt, mask in ((-1, self.mprev), (0, None), (1, self.mnext)):
                    kt = qt + dlt
                    if kt < 0:
                        continue
                    kts.append((self.KT[:, kt * 128:(kt + 1) * 128], self.V[:, kt, :], mask, ["KT", "V"]))
                pre = (lambda t0=t0, N=N: self.load_xT(self.x_d.ap()[t0:t0 + N, :], N // 128, self.XT, "XT%d" % (t0 // 512), t0, xtoks)) if tq == 0 else None
                post = (lambda t0=t0, N=N, ai=gi % 2: outproj(N, self.FO, "FO", t0, self.XT, "XT%d" % (t0 // 512), t0, 0, *ATTs[ai])) \
                    if tq == N // 128 - 1 else None
                jobs.append((self.QT, "QT", qt * 128, kts + ctxtiles, gi % 2, tq * 128, pre, post))
            gi += 1

        steps = []
        for ji, job in enumerate(jobs):
            n = len(job[3])
            for kvh in range(2):
                for i in range(n):
                    steps.append((ji, kvh, i, n))
        pti = [0]

        def s_stage(st):
            ji, kvh, i, n = st
            QTsrc, qkey, qcol, keytiles, ai, dcol, pre, post = jobs[ji]
            (qlo, qlk), (qhi, qhk) = qlos[ji % 2], qhis[ji % 2]
            (pbO, ok_), (pbD, dk_) = accs[ji % 2]
            if kvh == 0 and i == 0:
                if pre is not None:
                    pre()
                self.copy("pool", qlo[0:64, :, :], QTsrc[0:64, :, qcol:qcol + 128], [qkey], [qlk])
                self.copy("pool", qhi[64:128, :, :], QTsrc[64:128, :, qcol:qcol + 128], [qkey], [qhk])
            qp, qk_ = (qlo, qlk) if kvh == 0 else (qhi, qhk)
            kap, vap, mask, kkeys = keytiles[i]
            b, bk = self.bank()
            self.mm(b[:, 0:512], kap, qp[:].rearrange("p j n -> p (j n)"), True, mask is None, [qk_] + kkeys, [bk])
            if mask is not None:
                self.mm(b[:, 0:512], self.identb, mask.unsqueeze(1).broadcast_to([128, 4, 128]), False, True, ["c_bf"], [bk])
            pt = PT[pti[0] % 4]
            pk = "PT%d" % (pti[0] % 4)
            pti[0] += 1
            self.act(pt[:], b[:, 0:512], AF.Exp, [bk], [pk])
            return (pt, pk)

        def p_stage(st, pt, pk):
            ji, kvh, i, n = st
            QTsrc, qkey, qcol, keytiles, ai, dcol, pre, post = jobs[ji]
            (pbO, ok_), (pbD, dk_) = accs[ji % 2]
            kap, vap, mask, kkeys = keytiles[i]
            rows = slice(kvh * 64, kvh * 64 + 64)
            self.mm(pbO[rows, 0:512], vap[:, kvh * 64:kvh * 64 + 64], pt[:], i == 0, i == n - 1, [pk] + kkeys, [ok_])
            self.mm(pbD[rows, 0:512], self.onesb[:, 0:64], pt[:], i == 0, i == n - 1, [pk, "c_bf"], [dk_])
            if kvh == 1 and i == n - 1:
                dtot, dtk = dtots[ji % 2]
                ATT, ak = ATTs[ai]
                self.tt("dve", dtot[:], pbD[:, 0:512], self.sinkbc[:].rearrange("p j n -> p (j n)"), ALU.add, [dk_, "sinkbc"], [dtk])
                self.act(dtot[:], dtot[:], AF.Ln, [dtk], [dtk])
                self.act(dtot[:], dtot[:], AF.Exp, [dtk], [dtk], scale=-1.0)
                self.tt("dve", ATT[:, :, dcol:dcol + 128], pbO[:, 0:512].rearrange("p (j n) -> p j n", n=128),
                        dtot[:].rearrange("p (j n) -> p j n", n=128), ALU.mult, [ok_, dtk], [ak])
                if post is not None:
                    post()

        prev = None
        for st in steps:
            cur = s_stage(st)
            if prev is not None:
                p_stage(*prev)
            prev = (st, cur[0], cur[1])
        p_stage(*prev)
        self.rot = [2, 3, 4, 5, 6, 7]

    def ffn(self, l):
        scr = self.scr
        self.P.barrier(lambda e: e.memset(scr[0:1, 0:8], 0.0))
        self.cur = self.pers
        ta = self.talloc
        T = T_X if l == 0 else T_OWN
        toks = [(t0, N, 0) for (t0, N) in groups(T)]
        HT = T + (NCTX if l == 0 else 0)
        h2 = ta("h2", [128, 8, HT], BF16)
        keepF = self.cur
        sqbs = [ta("sqb%d" % i, [128, 8, 512], BF16) for i in range(2)]
        rss = [ta("rsn%d" % i, [128, 512], F32) for i in range(2)]
        tf = [ta("tf%d" % i, [128, 512], F32) for i in range(4)]
        tmpn = (sqbs, rss, tf)
        xkey = lambda t0: "XT%d" % (t0 // 512)
        calls = [dict(xT=self.XT, xkey=xkey(t0), col0=t0, N=N, A=(lambda c: self.A2[:, l, 0, c:c + 1]),
                      Bf=(lambda c: self.mod(l, 3, 0, c)), hT=h2, hkey="h2_%d" % (t0 // 512), hcol0=t0, tmp=tmpn) for (t0, N, _) in toks]
        if l == 0:
            calls.append(dict(xT=self.YT, xkey="YT", col0=0, N=NCTX, A=(lambda c: self.A2[:, l, 1, c:c + 1]),
                              Bf=(lambda c: self.mod(l, 3, 1, c)), hT=h2, hkey="h2_c", hcol0=T, tmp=tmpn))
            toks = toks + [(T, NCTX, 1)]
        self.norm_pipeline(calls)
        self.P.barrier(lambda e: e.memset(scr[0:1, 0:8], 0.0))
        self.cur = keepF
        splits = [(0, 4), (4, 4), (8, 4), (12, 4), (16, 3), (19, 3)]
        wgb = [ta("wg%d" % i, [128, 8, 512], BF16) for i in range(2)]
        wub = [ta("wu%d" % i, [128, 8, 512], BF16) for i in range(2)]
        wdb = [ta("wd%d" % i, [128, 4, D], BF16) for i in range(2)]
        actb = [ta("act%d" % i, [128, 4, 512], BF16) for i in range(2)]
        sg = [ta("sg%d" % i, [128, 512], F32) for i in range(3)]
        obs = [ta("ob%d" % i, [128, D], F32) for i in range(2)] if l == 1 else None
        assert self.cur <= self.hi - 8 * T_X * 4, "ffn overflow"
        ai = 0
        si = 0
        for sp_i, (j0, nj) in enumerate(splits):
            wi = sp_i % 2
            wg, wu, wd = wgb[wi], wub[wi], wdb[wi]
            kg, ku, kd = "wg%d" % wi, "wu%d" % wi, "wd%d" % wi
            self.load_w(wg[:, :, 0:nj * 128], kg, self.wg_d.ap()[l].rearrange("(c p) n -> p c n", p=128)[:, :, j0 * 128:(j0 + nj) * 128])
            self.load_w(wu[:, :, 0:nj * 128], ku, self.wu_d.ap()[l].rearrange("(c p) n -> p c n", p=128)[:, :, j0 * 128:(j0 + nj) * 128])
            self.load_w(wd[:, 0:nj, :], kd, self.wd_d.ap()[l].rearrange("(j p) n -> p j n", p=128)[:, j0:j0 + nj, :])
            for (t0, N, s) in toks:
                ab = actb[ai % 2]
                ak = "act%d" % (ai % 2)
                ai += 1
                hk = "h2_c" if s == 1 else "h2_%d" % (t0 // 512)
                for jj in range(nj):
                    bg, bgk = self.bank()
                    for c in range(8):
                        self.mm(bg[:, 0:N], wg[:, c, jj * 128:(jj + 1) * 128], h2[:, c, t0:t0 + N], c == 0, c == 7, [kg, hk], [bgk])
                    bu, buk = self.bank()
                    for c in range(8):
                        self.mm(bu[:, 0:N], wu[:, c, jj * 128:(jj + 1) * 128], h2[:, c, t0:t0 + N], c == 0, c == 7, [ku, hk], [buk])
                    s_ = sg[si % 3]
                    sk = "sg%d" % (si % 3)
                    si += 1
                    self.act(s_[:, 0:N], bg[:, 0:N], AF.Silu, [bgk], [sk])
                    self.tt("dve", ab[:, jj, 0:N], bu[:, 0:N], s_[:, 0:N], ALU.mult, [buk, sk], [ak])
                X, xk, xc = (self.XT, xkey(t0), t0) if s == 0 else (self.YT, "YT", 0)
                for oc in range(8):
                    b, bk = self.bank()
                    for jj in range(nj):
                        self.mm(b[:, 0:N], wd[:, jj, oc * 128:(oc + 1) * 128], ab[:, jj, 0:N], jj == 0, jj == nj - 1, [kd, ak], [bk])
                    xo = X[:, oc, xc:xc + N]
                    self.stt("dve", xo, b[:, 0:N], self.mod(l, 5, s, oc), xo, ALU.mult, ALU.add, [bk, "modT", xk], [xk])
                if l == 1 and sp_i == len(splits) - 1:
                    self.emit_out(range(t0 // 128, (t0 + N) // 128), obs)
        if l == 1:
            self.out_done = True

    def layer1_E(self):
        scr = self.scr
        self.P.barrier(lambda e: e.memset(scr[0:1, 0:8], 0.0))
        self.cur = self.pers
        ta = self.talloc
        l = 1
        NK = T_X + NCTX
        hT = ta("hT1", [128, 8, NK], BF16)
        keepE = self.cur
        sqbs = [ta("sqb%d" % i, [128, 8, 512], BF16) for i in range(2)]
        rss = [ta("rsn%d" % i, [128, 512], F32) for i in range(2)]
        tf = [ta("tf%d" % i, [128, 512], F32) for i in range(4)]
        tmpn = (sqbs, rss, tf)
        calls = [dict(xT=self.XT, xkey="XT%d" % (t0 // 512), col0=t0, N=N, A=(lambda c: self.A1[:, l, 0, c:c + 1]),
                      Bf=(lambda c: self.mod(l, 0, 0, c)), hT=hT, hkey="hT1_%d" % (t0 // 512), hcol0=t0, tmp=tmpn) for (t0, N) in groups(T_X)]
        calls.append(dict(xT=self.YT, xkey="YT", col0=0, N=NCTX, A=(lambda c: self.A1[:, l, 1, c:c + 1]),
                          Bf=(lambda c: self.mod(l, 0, 1, c)), hT=hT, hkey="hT1_%d" % (T_X // 512), hcol0=T_X, tmp=tmpn))
        self.norm_pipeline(calls)
        self.P.barrier(lambda e: e.memset(scr[0:1, 0:8], 0.0))
        self.cur = keepE
        off_rsq = self.cur
        rsq = [ta("rsq%d" % i, [128, 512], F32) for i in range(2)]
        self.uid += 1
        paccs = [(self.nc.alloc_sbuf_tensor_at("pacc%d_%d" % (i, self.uid), [128, 512], BF16, offset=off_rsq + i * 1024), "acc%d" % i) for i in range(4)]
        acck = ["acc%d" % i for i in range(4)]
        sq1s = [ta("sq1_%d" % i, [128, 512], BF16) for i in range(1)]
        tmpq = (sq1s, rsq, None, None)
        namask = ta("namask", [128, 3, 6, 128], BF16)
        self.dma("sp", namask[:], self.cnam_d.ap(), writes=["namask"])
        wq = ta("wq", [128, 8, 256], BF16)
        wk = ta("wk", [128, 8, 256], BF16)
        wv = ta("wv", [128, 8, 256], BF16)
        wo = ta("wo", [128, 2, D], BF16)
        Tt = ta("Tt", [128, 6, 4, 128], BF16)
        BMI = ta("BMI", [128, 5, 4, 128], BF16)
        bmtmp = rsq[0]
        KTh = ta("KTh", [128, 2, NK], BF16)
        Vh = ta("Vh", [128, NK // 128, 256], BF16)
        QTh = ta("QTh", [128, 2, T_OWN], BF16)
        ATTs = [(ta("ATT1_%d" % i, [128, 2, 512], BF16), "ATT1_%d" % i) for i in range(2)]
        qpads = [[(ta("qpad%d_%d" % (e, i), [128, 2, 128], BF16), "qpad%d_%d" % (e, i)) for e in range(2)] for i in range(2)]
        PT = [ta("PT1_%d" % i, [128, 512], BF16) for i in range(5)]
        dtots = [(ta("dtot1_%d" % i, [128, 512], F32), "dtot1_%d" % i) for i in range(1)]
        assert self.cur <= self.hi - 8 * T_X * 4, "layer1 overflow %d" % (self.cur - (self.hi - 8 * T_X * 4))
        for i in range(2):
            for e_ in range(2):
                q_, k_ = qpads[i][e_]
                self.P.add("pool", lambda e, q_=q_: e.memset(q_[:], 0.0), writes=[k_])
        self.rot = [4, 5, 6, 7]
        accs = [((self.pb[0], "pb0"), (self.pb[1], "pb1")), ((self.pb[2], "pb2"), (self.pb[3], "pb3"))]
        z256 = self.cbf[:, 4:6, :].rearrange("p a n -> p (a n)")
        z512 = self.cbf[:, 4:8, :].rearrange("p a n -> p (a n)")
        hkey = lambda t0: "hT1_%d" % (t0 // 512)
        pti = [0]
        tglob = [0]
        src = self.odin_d.ap().rearrange("(c p) n -> p c n", p=128)

        def load_qkv(hg_):
            self.load_w(wq[:], "wq", src[:, :, hg_ * 256:(hg_ + 1) * 256])
            self.load_w(wk[:], "wk", src[:, :, D + hg_ * 256:D + (hg_ + 1) * 256])
            self.load_w(wv[:], "wv", src[:, :, 2 * D + hg_ * 256:2 * D + (hg_ + 1) * 256])

        load_qkv(0)
        for hg in range(4):
            self.load_w(wo[:], "wo", self.odout_d.ap().rearrange("(c p) n -> p c n", p=128)[:, hg * 2:hg * 2 + 2, :])
            for dl in range(6):
                for kr in range(2):
                    for qr in range(2):
                        dr_idx = 2 * (dl - 2) + kr - qr + 7
                        srcb = AP(self.rb_d, hg * 4 * 465 + dr_idx * 31 - 48, [[1, 64], [465, 4], [1, 64]])
                        self.dma("pool", Tt[qr * 64:(qr + 1) * 64, dl, :, kr * 64:(kr + 1) * 64], srcb, writes=["Tt%d" % dl])
            for dl in range(5):
                b, bk = self.bank()
                for h4 in range(4):
                    self.mm(b[:, h4 * 128:(h4 + 1) * 128], Tt[:, dl, h4, :], self.rblk, True, True, ["Tt%d" % dl, "c_bf"], [bk])
                self.tt("dve", BMI[:, dl, :, :], b[:, 0:512].rearrange("p (h n) -> p h n", n=128),
                        namask[:, 2, dl, :].unsqueeze(1).broadcast_to([128, 4, 128]), ALU.add, [bk, "namask"], ["BMI"])
            self.rot = [0, 1, 2, 3, 4, 5, 6, 7]
            self.P.add("dve", lambda e: e.memset(scr[0:1, 8:12], 0.0), reads=acck, writes=["rsq0", "rsq1"])
            for (t0, N) in groups(NK):
                jobs = []
                for ch in range(2):
                    def projk(ch=ch, t0=t0, N=N):
                        b, bk = self.bank()
                        for c in range(8):
                            self.mm(b[:, 0:N], wk[:, c, ch * 128:(ch + 1) * 128], hT[:, c, t0:t0 + N], c == 0, c == 7, ["wk", hkey(t0)], [bk])
                        return b, bk
                    jobs.append(dict(proj=projk, N=N, gcol=self.gq[:, 3:4], out=KTh[:, ch, t0:t0 + N], okey="KTh", rope=None))
                    if t0 + N <= T_OWN:
                        def projq(ch=ch, t0=t0, N=N):
                            b, bk = self.bank()
                            for c in range(8):
                                self.mm(b[:, 0:N], wq[:, c, ch * 128:(ch + 1) * 128], hT[:, c, t0:t0 + N], c == 0, c == 7, ["wq", hkey(t0)], [bk])
                            return b, bk
                        jobs.append(dict(proj=projq, N=N, gcol=self.gq[:, 2:3], out=QTh[:, ch, t0:t0 + N], okey="QTh", rope=None))
                self.qk_pipeline(jobs, tmpq)
                for t in range(N // 128):
                    b, bk = self.bank()
                    for c in range(8):
                        self.mm(b[:, 0:256], hT[:, c, t0 + t * 128:t0 + (t + 1) * 128], wv[:, c, :], c == 0, c == 7, ["wv", hkey(t0)], [bk])
                    self.evac(Vh[:, t0 // 128 + t, :], b[:, 0:256], [bk], ["Vh"])
            self.rot = [4, 5, 6, 7]
            self.P.add("dve", lambda e: e.memset(scr[0:1, 12:16], 0.0), reads=["rsq0", "rsq1"], writes=acck)
            if hg + 1 < 4:
                load_qkv(hg + 1)
            steps = []
            for qt in range(T_OWN // 128):
                kts = [(qt + d_, d_ + 2) for d_ in range(-2, 4 if qt == 0 else 3) if qt + d_ >= 0] + \
                      [(T_X // 128, None), (T_X // 128 + 1, None)]
                for i, (kt, dl) in enumerate(kts):
                    steps.append((qt, i, len(kts), kt, dl))

            def s_stage(st):
                qt, i, n, kt, dl = st
                tq_ = tglob[0] + qt
                qp = qpads[tq_ % 2]
                (pbO, ok_), (pbD, dk_) = accs[tq_ % 2]
                if i == 0:
                    for e_ in range(2):
                        rows = slice(e_ * 64, e_ * 64 + 64)
                        self.copy("pool", qp[e_][0][rows, :, :], QTh[rows, :, qt * 128:(qt + 1) * 128], ["QTh"], [qp[e_][1]])
                    self.mm(pbO[:, 0:256], self.zerob, z256, True, False, ["c_bf"], [ok_])
                b, bk = self.bank()
                first = True
                if dl is not None:
                    if qt >= 2:
                        self.mm(b[:, 0:512], self.identb, BMI[:, dl, :, :].rearrange("p h n -> p (h n)"), True, False, ["BMI", "c_bf"], [bk])
                        first = False
                    else:
                        self.mm(b[:, 0:512], self.identb, namask[:, qt, dl, :].unsqueeze(1).broadcast_to([128, 4, 128]), True, False,
                                ["namask", "c_bf"], [bk])
                        for h4 in range(4):
                            self.mm(b[:, h4 * 128:(h4 + 1) * 128], Tt[:, dl, h4, :], self.rblk, False, False, ["Tt%d" % dl, "c_bf"], [bk])
                        first = False
                for h4 in range(4):
                    ch, e_ = h4 // 2, h4 % 2
                    self.mm(b[:, h4 * 128:(h4 + 1) * 128], KTh[:, ch, kt * 128:(kt + 1) * 128], qp[e_][0][:, ch, :], first, True,
                            ["KTh", qp[e_][1]], [bk])
                pt = PT[pti[0] % 5]
                pk = "PT1_%d" % (pti[0] % 5)
                pti[0] += 1
                self.act(pt[:], b[:, 0:512], AF.Exp, [bk], [pk])
                acc, ak_ = paccs[(tq_ % 2) * 2 + (i % 2)]
                if i < 2:
                    self.copy("dve", acc[:], pt[:], [pk], [ak_])
                else:
                    self.tt("dve", acc[:], acc[:], pt[:], ALU.add, [ak_, pk], [ak_])
                return (pt, pk)

            def p_stage(st, pt, pk):
                qt, i, n, kt, dl = st
                tq_ = tglob[0] + qt
                (pbO, ok_), (pbD, dk_) = accs[tq_ % 2]
                for h4 in range(4):
                    ch, e_ = h4 // 2, h4 % 2
                    rows = slice(e_ * 64, e_ * 64 + 64)
                    self.mm(pbO[rows, ch * 128:(ch + 1) * 128], Vh[:, kt, h4 * 64:(h4 + 1) * 64], pt[:, h4 * 128:(h4 + 1) * 128],
                            False, True, [pk, "Vh"], [ok_])
                if i == n - 1:
                    accA, akA = paccs[(tq_ % 2) * 2]
                    accB, akB = paccs[(tq_ % 2) * 2 + 1]
                    self.mm(pbD[:, 0:512], self.onesb, accA[:], True, False, [akA, "c_bf"], [dk_])
                    self.mm(pbD[:, 0:512], self.onesb, accB[:], False, True, [akB, "c_bf"], [dk_])
                    dtot, dtk = dtots[0]
                    ATT, ak = ATTs[(qt // 4) % 2]
                    tq = qt % 4
                    self.act(dtot[:], pbD[:, 0:512], AF.Ln, [dk_], [dtk])
                    self.act(dtot[:], dtot[:], AF.Exp, [dtk], [dtk], scale=-1.0)
                    dv = dtot[:].rearrange("p (c e n) -> p c e n", e=2, n=128)
                    for e_ in range(2):
                        rows = slice(e_ * 64, e_ * 64 + 64)
                        self.tt("dve", ATT[rows, :, tq * 128:(tq + 1) * 128], pbO[rows, 0:256].rearrange("p (j n) -> p j n", n=128),
                                dv[rows, :, e_, :], ALU.mult, [ok_, dtk], [ak])
                    if tq == 3:
                        t0 = (qt // 4) * 512
                        for oc in range(8):
                            b, bk = self.bank()
                            for ch in range(2):
                                self.mm(b[:, 0:512], wo[:, ch, oc * 128:(oc + 1) * 128], ATT[:, ch, 0:512], ch == 0, ch == 1, ["wo", ak], [bk])
                            xo = self.XT[:, oc, t0:t0 + 512]
                            xk = "XT%d" % (t0 // 512)
                            self.stt("dve", xo, b[:, 0:512], self.mod(l, 2, 0, oc), xo, ALU.mult, ALU.add, [bk, "modT", xk], [xk])

            if "noattn" in self.flags:
                steps = []
            if "few" in self.flags:
                steps = steps[:self.nstep]
            pend = []
            for st in steps:
                cur = s_stage(st)
                pend.append((st, cur[0], cur[1]))
                if len(pend) > 2:
                    p_stage(*pend.pop(0))
            for pp in pend:
                p_stage(*pp)
            tglob[0] += T_OWN // 128
            if "onehg" in self.flags:
                break
        self.rot = [2, 3, 4, 5, 6, 7]

    def finish(self, final=True):
        scr = self.scr
        self.P.barrier(lambda e: e.memset(scr[0:1, 0:8], 0.0))
        self.cur = self.pers
        ta = self.talloc
        if self.debug and hasattr(self, "XT"):
            self.dma("sp", self.dbgx_d.ap(), self.XT[:], reads=["XT%d" % i for i in range(5)])
            self.dma("sp", self.dbgy_d.ap(), self.YT[:], reads=["YT"])
        if not hasattr(self, "XT"):
            return
        if getattr(self, "out_done", False):
            return
        ob = [ta("ob%d" % i, [128, D], F32) for i in range(2)]
        self.emit_out(range(T_OWN // 128), ob)

    def emit_out(self, tiles, ob):
        for t in tiles:
            o_, ok = ob[t % 2], "ob%d" % (t % 2)
            for half in range(2):
                b, bk = self.bank()
                for cc in range(4):
                    c = half * 4 + cc
                    self.tr(b[:, cc * 128:(cc + 1) * 128], self.XT[:, c, t * 128:(t + 1) * 128], ["XT%d" % (t // 4)], [bk])
                self.evac(o_[:, half * 512:(half + 1) * 512], b[:, 0:512], [bk], [ok])
            self.dma("sp", self.out_d.ap()[t * 128:(t + 1) * 128, :], o_[:], reads=[ok])


_CONST_CACHE = {}


def _bf(a):
    return np.ascontiguousarray(a.astype(ml_dtypes.bfloat16))


def host_consts(par):
    if par in _CONST_CACHE:
        return _CONST_CACHE[par]
    c = {}
    c["c_identf"] = np.eye(128, dtype=np.float32)
    psw = np.zeros((128, 128), np.float32)
    for m in range(128):
        i = m % 32
        partner = m + 16 if i < 16 else m - 16
        psw[partner, m] = 1.0
    c["c_pswap"] = psw
    cb = np.zeros((128, 10, 128), np.float32)
    cb[:, 0, :] = 1.0 / 1024.0
    cb[0:64, 1, 0:64] = 1.0 / 64.0
    cb[64:128, 1, 64:128] = 1.0 / 64.0
    cb[:, 2, :] = np.eye(128)
    for blk in range(2):
        for u in range(64):
            cb[blk * 64 + u, 3, blk * 64 + 63 - u] = 1.0
    cb[:, 5, :] = 1.0
    cc = np.arange(128)
    ang = 2.0 * np.pi * ((cc[:, None] * cc[None, :]) % 128) / 128.0
    cb[:, 6, :] = np.cos(ang) / np.sqrt(128.0)
    cb[:, 7, :] = -np.sin(ang) / np.sqrt(128.0)
    j = np.arange(128)[:, None]
    q = np.arange(128)[None, :]
    cb[:, 8, :] = np.where(j >= q, 0.0, NEG)
    cb[:, 9, :] = np.where(j <= q, 0.0, NEG)
    c["c_bf"] = _bf(cb)
    n = np.arange(256)
    a2 = 2.0 * np.pi * ((n[:, None] * n[None, :]) % 256) / 256.0
    C2 = (np.cos(a2) / 16.0).reshape(2, 128, 256)
    S2 = (np.sin(a2) / 16.0).reshape(2, 128, 256)
    cx = np.stack([C2.transpose(1, 0, 2), S2.transpose(1, 0, 2)], axis=1)
    c["c_ctxdft"] = _bf(cx)
    loc = np.arange(T_ALL)
    glob = loc if par == 0 else (T_ALL - 1 - loc)
    inv = (np.float32(10000.0) ** (-np.arange(16, dtype=np.float32) / np.float32(16))).astype(np.float32)
    g_ = glob[:T_A]
    row = (g_ // 64).astype(np.float32)
    col = (g_ % 64).astype(np.float32)
    ar = (row[None, :] * inv[:, None]).astype(np.float32)
    ac = (col[None, :] * inv[:, None]).astype(np.float32)
    cos64 = np.concatenate([np.cos(ar), np.cos(ar), np.cos(ac), np.cos(ac)], axis=0)
    sin64 = np.concatenate([-np.sin(ar), np.sin(ar), -np.sin(ac), np.sin(ac)], axis=0)
    c["c_rope"] = np.ascontiguousarray(np.stack([np.concatenate([cos64, cos64], 0), np.concatenate([sin64, sin64], 0)], 0).astype(np.float32))
    gk = glob[:T_X].astype(np.int64)
    gn = glob.astype(np.int64)
    ph = (gn[:, None] * gk[None, :]) % T_ALL
    angL = (2.0 * np.pi / T_ALL) * ph
    c["c_dftc"] = _bf(np.cos(angL) / 64.0)
    c["c_dfts"] = _bf(np.sin(angL) / 64.0)
    nm = np.zeros((128, 3, 6, 128), np.float32)
    for cls, qt in enumerate((0, 1, 8)):
        for dl in range(6):
            kt = qt + dl - 2
            if kt < 0:
                nm[:, cls, dl, :] = NEG
                continue
            kg = glob[kt * 128 + np.arange(128)]
            qg = glob[qt * 128 + np.arange(128)]
            kr, kc = kg // 64, kg % 64
            qr, qc = qg // 64, qg % 64
            r0 = np.clip(qr - 4, 0, 56)
            c0 = np.clip(qc - 8, 0, 48)
            ok = (kr[:, None] >= r0[None, :]) & (kr[:, None] < r0[None, :] + 8) & (kc[:, None] >= c0[None, :]) & (kc[:, None] < c0[None, :] + 16)
            nm[:, cls, dl, :] = np.where(ok, 0.0, NEG)
    c["c_namask"] = _bf(nm)
    _CONST_CACHE[par] = c
    return c


_NC_CACHE = {}


def get_nc(stop_after=None, debug=False, only=None, flags=()):
    key = (stop_after, debug, only, tuple(flags))
    if key not in _NC_CACHE:
        bld = Builder(stop_after=stop_after, debug=debug, only=only, flags=flags)
        _NC_CACHE[key] = bld.build()
    return _NC_CACHE[key]


def make_in_maps(inputs):
    f32 = lambda a: np.ascontiguousarray(np.asarray(a, dtype=np.float32))
    x = f32(inputs["x"])
    c = f32(inputs["c"])
    ctx = f32(inputs["ctx"])
    c_ctx = f32(inputs["c_ctx"])
    ev_in = f32(inputs["ev_w_in"])[0]
    ev_out = f32(inputs["ev_w_out"])[0]
    hp = [0, 4, 1, 5, 2, 6, 3, 7]
    qcols = np.concatenate([512 + h * 64 + np.arange(64) for h in hp])
    cols = np.concatenate([np.arange(512), qcols, np.arange(1024, 1280)])
    ev_in_p = np.ascontiguousarray(ev_in[:, cols])
    rows = np.concatenate([np.arange(512), qcols])
    ev_out_p = np.ascontiguousarray(ev_out[rows, :])
    rb = f32(inputs["od_rel_bias"])[0]
    shared = {
        "ada_w": f32(inputs["ada_w"]), "ada_b": f32(inputs["ada_b"]),
        "norm1_g": f32(inputs["norm1_g"]), "norm2_g": f32(inputs["norm2_g"]),
        "ffn_w_gate": f32(inputs["ffn_w_gate"]), "ffn_w_up": f32(inputs["ffn_w_up"]), "ffn_w_down": f32(inputs["ffn_w_down"]),
        "ev_w_in": ev_in_p, "ev_w_out": ev_out_p,
        "ev_q_norm": f32(inputs["ev_q_norm"])[0], "ev_k_norm": f32(inputs["ev_k_norm"])[0], "ev_sink": f32(inputs["ev_sink"])[0],
        "od_w_in": f32(inputs["od_w_in"])[0], "od_w_out": f32(inputs["od_w_out"])[0],
        "od_q_norm": f32(inputs["od_q_norm"])[0], "od_k_norm": f32(inputs["od_k_norm"])[0],
    }
    pad = np.zeros(64, np.float32)
    rbs = [np.concatenate([rb.reshape(-1), pad]), np.concatenate([rb[:, ::-1, ::-1].reshape(-1), pad])]
    in_maps = []
    for cid in range(8):
        b, par = cid // 2, cid % 2
        m = dict(shared)
        m["x_loc"] = np.ascontiguousarray(x[b] if par == 0 else x[b][::-1])
        m["ctx_b"] = np.ascontiguousarray(ctx[b])
        m["cvec"] = np.ascontiguousarray(np.stack([c[b], c_ctx], 0))
        m["rel_bias"] = rbs[par]
        m.update(host_consts(par))
        in_maps.append(m)
    return in_maps


def assemble(results):
    out = np.zeros((4, T_ALL, D), np.float32)
    for cid in range(8):
        b, par = cid // 2, cid % 2
        o = np.asarray(results[cid]["out_loc"], dtype=np.float32)
        if par == 0:
            out[b, :T_OWN] = o
        else:
            out[b, T_OWN:] = o[::-1]
    return out


def kernel(**inputs):
    nc = get_nc()
    in_maps = make_in_maps(inputs)
    res = run_bass_kernel_spmd(nc, in_maps, core_ids=list(range(8)))
    return assemble(res.results)
```

```python
import numpy as np
import ml_dtypes
import concourse.bass as bass
import concourse.mybir as mybir
from concourse.bass_utils import run_bass_kernel_spmd
from concourse.ap import AP

F32 = mybir.dt.float32
BF16 = mybir.dt.bfloat16
ALU = mybir.AluOpType
AF = mybir.ActivationFunctionType

NDMA_SEM = 48
D = 1024
DFF = 2816
NEG = -30000.0
EPS = 1e-6
T_ALL = 4096
T_X = 2304
T_A = 2560
T_OWN = 2048
NCTX = 256


class _Op:
    __slots__ = ("id", "eng", "fn", "deps", "is_dma", "sem_i", "sem_val", "signal", "count")


class Prog:
    ENGS = ("pe", "act", "dve", "pool", "sp")

    def __init__(self):
        self.ops = []
        self.lw = {}
        self.rd = {}
        self.dma_rr = 0
        self.dma_sem_uses = [0] * NDMA_SEM
        self.dma_sem_last = [None] * NDMA_SEM
        self.last_eng = {}
        self.bar = None
        self.dopen = set()
        self.saved = {}
        self.strict = True

    def add(self, eng, fn, reads=(), writes=(), dma=False):
        op = _Op()
        op.id = len(self.ops)
        op.eng = eng
        op.fn = fn
        op.is_dma = dma
        op.signal = False
        op.count = 0
        deps = {}
        if self.bar is not None:
            deps[self.bar] = True
        for k in reads:
            for w in self.lw.get(k, ()):
                deps[w] = True
            self.dopen.discard(k)
        cow = set()
        for k in writes:
            if dma and k in self.dopen:
                cow.add(k)
                for r in self.saved.get(k, ()):
                    deps.setdefault(r, False)
                continue
            for w in self.lw.get(k, ()):
                deps.setdefault(w, False)
            for r in self.rd.get(k, ()):
                deps.setdefault(r, False)
        if dma:
            i = self.dma_rr % NDMA_SEM
            self.dma_rr += 1
            prev = self.dma_sem_last[i]
            if prev is not None:
                deps.setdefault(prev, False)
            self.dma_sem_uses[i] += 1
            op.sem_i = i
            op.sem_val = 16 * self.dma_sem_uses[i]
            self.dma_sem_last[i] = op.id
        op.deps = deps
        for k in reads:
            self.rd.setdefault(k, []).append(op.id)
        for k in writes:
            if k in cow:
                self.lw[k].append(op.id)
                continue
            self.saved[k] = list(self.lw.get(k, ())) + list(self.rd.get(k, ()))
            self.lw[k] = [op.id]
            self.rd[k] = []
            if dma:
                self.dopen.add(k)
            else:
                self.dopen.discard(k)
        self.ops.append(op)
        if not dma:
            self.last_eng[eng] = op.id
        return op

    def barrier(self, fn):
        op = self.add("pool", fn)
        for e, i in self.last_eng.items():
            if i != op.id:
                op.deps[i] = True
        for i in self.dma_sem_last:
            if i is not None:
                op.deps[i] = True
        self.bar = op.id
        self.lw = {}
        self.rd = {}
        self.dopen = set()
        self.saved = {}

    def emit(self, block, eng_sems, dma_sems):
        ops = self.ops

        strict = self.strict

        def skip(op, dop, raw):
            if op.is_dma or dop.is_dma or dop.eng != op.eng:
                return False
            return op.eng == "pe" or (not raw and not strict)

        for op in ops:
            for d, raw in op.deps.items():
                dop = ops[d]
                if dop.is_dma or skip(op, dop, raw):
                    continue
                dop.signal = True
        cnt = {e: 0 for e in self.ENGS}
        for op in ops:
            if op.signal and not op.is_dma:
                cnt[op.eng] += 1
                op.count = cnt[op.eng]
        per_eng = {e: [o for o in ops if o.eng == e] for e in self.ENGS}

        def run(e, engine):
            known = {}
            for op in per_eng[e]:
                needs = {}
                for d, raw in op.deps.items():
                    dop = ops[d]
                    if dop.is_dma:
                        key = ("d", dop.sem_i)
                        val = dop.sem_val
                    else:
                        if skip(op, dop, raw):
                            continue
                        key = ("e", dop.eng)
                        val = dop.count
                    if needs.get(key, 0) < val:
                        needs[key] = val
                for key, val in needs.items():
                    if known.get(key, 0) >= val:
                        continue
                    known[key] = val
                    sem = dma_sems[key[1]] if key[0] == "d" else eng_sems[key[1]]
                    engine.wait_ge(sem, val)
                ins = op.fn(engine)
                if op.is_dma:
                    ins.then_inc(dma_sems[op.sem_i], 16)
                elif op.signal:
                    ins.then_inc(eng_sems[e], 1)
            if e == "sp":
                for i in range(NDMA_SEM):
                    if self.dma_sem_uses[i] > 0:
                        engine.wait_ge(dma_sems[i], 16 * self.dma_sem_uses[i])

        block.tensor(lambda eng: run("pe", eng))
        block.scalar(lambda eng: run("act", eng))
        block.vector(lambda eng: run("dve", eng))
        block.gpsimd(lambda eng: run("pool", eng))
        block.sync(lambda eng: run("sp", eng))


def groups(n, g=512):
    out = []
    s = 0
    while s < n:
        out.append((s, min(g, n - s)))
        s += g
    return out


class Builder:
    def __init__(self, stop_after=None, debug=False, only=None, flags=()):
        self.only = only
        self.flags = set(f for f in flags if not f.startswith('n='))
        self.nstep = ([int(f[2:]) for f in flags if f.startswith('n=')] + [8])[0]
        self.nc = bass.Bass("TRN2", target_bir_lowering=False)
        self.P = Prog()
        self.stop_after = stop_after
        self.debug = debug
        nc = self.nc
        self.lo = ((nc.sbuf_base + 63) // 64) * 64
        self.hi = nc.sbuf_top
        self.pers = self.lo
        self.cur = None
        self.uid = 0
        self.rr = 0
        self.evr = 0
        self.rot = [2, 3, 4, 5, 6]
        self.nmi = 0
        self.tfi = 0
        self.qni = 0

    def _alloc(self, ptr, name, shape, dt):
        esz = 4 if dt == F32 else 2
        n = 1
        for s in shape[1:]:
            n *= s
        nbytes = ((n * esz + 63) // 64) * 64
        self.uid += 1
        t = self.nc.alloc_sbuf_tensor_at("%s_%d" % (name, self.uid), list(shape), dt, offset=ptr)
        return t, ptr + nbytes

    def palloc(self, name, shape, dt):
        assert self.cur is None
        t, self.pers = self._alloc(self.pers, name, shape, dt)
        assert self.pers <= self.hi, "persistent overflow"
        return t

    def talloc(self, name, shape, dt):
        t, self.cur = self._alloc(self.cur, name, shape, dt)
        assert self.cur <= self.hi, "phase overflow %s %d" % (name, self.cur - self.hi)
        return t

    def phase(self):
        scr = self.scr
        self.P.barrier(lambda e: e.memset(scr[0:1, 0:8], 0.0))
        self.cur = self.pers

    def dma(self, eng, out, in_, reads=(), writes=()):
        self.P.add(eng, lambda e: e.dma_start(out=out, in_=in_), reads=reads, writes=writes, dma=True)

    def mm(self, out, lhsT, rhs, start, stop, reads, writes):
        self.P.add("pe", lambda e: e.matmul(out, lhsT=lhsT, rhs=rhs, start=start, stop=stop), reads=reads, writes=writes)

    def tr(self, out, in_, reads, writes):
        ident = self.identf
        k = in_.shape[0]
        self.P.add("pe", lambda e: e.transpose(out, in_, ident[0:k, 0:k]), reads=list(reads) + ["c_id"], writes=writes)

    def act(self, out, in_, func, reads, writes, bias=None, scale=None):
        kw = {}
        if bias is not None:
            kw["bias"] = bias
        if scale is not None:
            kw["scale"] = scale
        self.P.add("act", lambda e: e.activation(out=out, in_=in_, func=func, **kw), reads=reads, writes=writes)

    def tt(self, eng, out, in0, in1, op, reads, writes):
        self.P.add(eng, lambda e: e.tensor_tensor(out=out, in0=in0, in1=in1, op=op), reads=reads, writes=writes)

    def stt(self, eng, out, in0, scalar, in1, op0, op1, reads, writes):
        self.P.add(eng, lambda e: e.scalar_tensor_tensor(out=out, in0=in0, scalar=scalar, in1=in1, op0=op0, op1=op1),
                   reads=reads, writes=writes)

    def ts(self, eng, out, in0, s1, s2, op0, op1, reads, writes):
        if s2 is None:
            self.P.add(eng, lambda e: e.tensor_scalar(out=out, in0=in0, scalar1=s1, scalar2=None, op0=op0), reads=reads, writes=writes)
        else:
            self.P.add(eng, lambda e: e.tensor_scalar(out=out, in0=in0, scalar1=s1, scalar2=s2, op0=op0, op1=op1), reads=reads, writes=writes)

    def copy(self, eng, out, in_, reads, writes):
        if eng == "act":
            self.act(out, in_, AF.Copy, reads, writes)
        else:
            self.P.add(eng, lambda e: e.tensor_copy(out=out, in_=in_), reads=reads, writes=writes)

    def evac(self, out, in_, reads, writes):
        self.evr += 1
        self.copy("act" if self.evr % 2 else "dve", out, in_, reads, writes)

    def dump(self, name, t, key):
        if not self.debug:
            return
        shape = list(t.shape)
        d = self.nc.dram_tensor("dbg_" + name, shape, t.dtype, kind="ExternalOutput")
        self.dma("sp", d.ap(), t[:], reads=[key])

    def bank(self):
        self.rr += 1
        i = self.rot[self.rr % len(self.rot)]
        return self.pb[i], "pb%d" % i

    def build(self):
        nc = self.nc
        P = self.P
        dr = lambda name, shape, dt, kind="ExternalInput": nc.dram_tensor(name, list(shape), dt, kind=kind)
        self.x_d = dr("x_loc", [T_ALL, D], F32)
        self.ctx_d = dr("ctx_b", [NCTX, D], F32)
        self.cvec_d = dr("cvec", [2, D], F32)
        self.ada_w_d = dr("ada_w", [2, D, 6 * D], F32)
        self.ada_b_d = dr("ada_b", [2, 6 * D], F32)
        self.n1_d = dr("norm1_g", [2, D], F32)
        self.n2_d = dr("norm2_g", [2, D], F32)
        self.wg_d = dr("ffn_w_gate", [2, D, DFF], F32)
        self.wu_d = dr("ffn_w_up", [2, D, DFF], F32)
        self.wd_d = dr("ffn_w_down", [2, DFF, D], F32)
        self.evin_d = dr("ev_w_in", [D, 1280], F32)
        self.evout_d = dr("ev_w_out", [D, D], F32)
        self.evq_d = dr("ev_q_norm", [64], F32)
        self.evk_d = dr("ev_k_norm", [64], F32)
        self.sink_d = dr("ev_sink", [8], F32)
        self.odin_d = dr("od_w_in", [D, 3 * D], F32)
        self.odout_d = dr("od_w_out", [D, D], F32)
        self.odq_d = dr("od_q_norm", [64], F32)
        self.odk_d = dr("od_k_norm", [64], F32)
        self.rb_d = dr("rel_bias", [16 * 15 * 31 + 64], F32)
        self.cidf_d = dr("c_identf", [128, 128], F32)
        self.cpsw_d = dr("c_pswap", [128, 128], F32)
        self.cbf_d = dr("c_bf", [128, 10, 128], BF16)
        self.cctx_d = dr("c_ctxdft", [128, 2, 2, 256], BF16)
        self.cnam_d = dr("c_namask", [128, 3, 6, 128], BF16)
        self.crope_d = dr("c_rope", [2, 128, T_A], F32)
        self.cdft_d = [dr("c_dftc", [T_ALL, T_X], BF16), dr("c_dfts", [T_ALL, T_X], BF16)]
        self.out_d = dr("out_loc", [T_OWN, D], F32, kind="ExternalOutput")
        if self.debug:
            self.dbgx_d = dr("dbg_x", [128, 8, T_X], F32, kind="ExternalOutput")
            self.dbgy_d = dr("dbg_y", [128, 8, NCTX], F32, kind="ExternalOutput")
            self.dbga_d = dr("dbg_a", [128, 8, T_X], F32, kind="ExternalOutput")

        self.pb = [nc.alloc_psum_tensor("pb%d" % i, [128, 512], F32) for i in range(8)]
        self.eng_sems = {e: nc.alloc_semaphore("s_" + e) for e in Prog.ENGS}
        self.dma_sems = [nc.alloc_semaphore("d%d" % i) for i in range(NDMA_SEM)]

        self.setup()
        done = False
        for name, fn in [("A", self.layer0_A), ("B", self.layer0_B), ("C", self.layer0_C), ("D", lambda: self.ffn(0)),
                         ("E", self.layer1_E), ("F", lambda: self.ffn(1))]:
            if self.only is not None and name not in self.only:
                continue
            if not hasattr(self, "XT") and name in ("D", "E", "F"):
                self.ada_tick(100)
                self.rot = [2, 3, 4, 5, 6, 7]
                self.pers -= 2 * 4096
                self.XT = self.nc.alloc_sbuf_tensor_at("XT_res", [128, 8, T_X], F32, offset=self.hi - 8 * T_X * 4)
            fn()
            if self.stop_after == name:
                done = True
                break
        self.finish(final=not done)
        with nc.Block() as block:
            P.emit(block, self.eng_sems, self.dma_sems)
        return nc

    def setup(self):
        nc = self.nc
        pa = self.palloc
        self.scr = pa("scr", [128, 16], F32)
        self.identf = pa("identf", [128, 128], F32)
        self.pswap = pa("pswap", [128, 128], F32)
        self.cbf = pa("cbf", [128, 10, 128], BF16)
        self.cctx = pa("cctx", [128, 2, 2, 256], BF16)
        self.vecT = pa("vecT", [128, 48], F32)
        self.biasT = pa("biasT", [128, 96], F32)
        self.modT = pa("modT", [128, 2, 48, 2], F32)
        self.A1 = pa("A1", [128, 2, 2, 8], F32)
        self.A2 = pa("A2", [128, 2, 2, 8], F32)
        self.sT = pa("sT", [128, 2, 8], BF16)
        self.gq = pa("gq", [128, 4], F32)
        self.sinkbc = pa("sinkbc", [128, 4, 128], F32)
        self.skt = pa("skt", [128, 4], F32)
        self.zf = pa("zf", [128, 128], F32)
        self.YT = pa("YT", [128, 8, NCTX], F32)
        self.dma("sp", self.identf[:], self.cidf_d.ap(), writes=["c_id"])
        self.dma("sp", self.pswap[:], self.cpsw_d.ap(), writes=["c_ps"])
        self.dma("sp", self.cbf[:], self.cbf_d.ap(), writes=["c_bf"])
        self.dma("sp", self.cctx[:], self.cctx_d.ap(), writes=["c_cx"])
        self.P.add("pool", lambda e: e.memset(self.zf[:], 0.0), writes=["zf"])
        self.P.add("pool", lambda e: e.memset(self.scr[:], 0.0), writes=["scr"])
        self.ones1024 = self.cbf[:, 0, :]
        self.blk64 = self.cbf[:, 1, :]
        self.identb = self.cbf[:, 2, :]
        self.rblk = self.cbf[:, 3, :]
        self.zerob = self.cbf[:, 4, :]
        self.onesb = self.cbf[:, 5, :]
        self.ccos = self.cbf[:, 6, :]
        self.cnsin = self.cbf[:, 7, :]
        self.mprev = self.cbf[:, 8, :]
        self.mnext = self.cbf[:, 9, :]
        self.adabuf = None
        self.cur = self.pers + 2 * 4096
        ta = self.talloc
        rows1 = ta("rows1", [48, 128], F32)
        rows2 = ta("rows2", [96, 128], F32)
        self.dma("sp", rows1[0:16, :], self.n1_d.ap().rearrange("l (c p) -> (l c) p", p=128), writes=["rows1"])
        self.dma("sp", rows1[16:32, :], self.n2_d.ap().rearrange("l (c p) -> (l c) p", p=128), writes=["rows1"])
        self.dma("sp", rows1[32:48, :], self.cvec_d.ap().rearrange("s (c p) -> (s c) p", p=128), writes=["rows1"])
        self.dma("sp", rows2[:, :], self.ada_b_d.ap().rearrange("l (c p) -> (l c) p", p=128), writes=["rows2"])
        b, bk = self.bank()
        self.tr(b[:, 0:48], rows1[:, :], ["rows1"], [bk])
        self.copy("dve", self.vecT[:], b[:, 0:48], [bk], ["vecT"])
        b, bk = self.bank()
        self.tr(b[:, 0:96], rows2[:, :], ["rows2"], [bk])
        self.copy("dve", self.biasT[:], b[:, 0:96], [bk], ["biasT"])
        self.act(self.sT[:].rearrange("p s c -> p (s c)"), self.vecT[:, 32:48], AF.Silu, ["vecT"], ["sT"])
        for i, (d_, sc) in enumerate([(self.evq_d, 0.125), (self.evk_d, 1.0), (self.odq_d, 0.125), (self.odk_d, 1.0)]):
            src = d_.ap().rearrange("(d u) -> d u", u=1)
            self.dma("sp", self.gq[0:64, i:i + 1], src, writes=["gq"])
            self.dma("sp", self.gq[64:128, i:i + 1], src, writes=["gq"])
        self.ts("dve", self.gq[:, 0:1], self.gq[:, 0:1], 0.125, None, ALU.mult, None, ["gq"], ["gq"])
        self.ts("dve", self.gq[:, 2:3], self.gq[:, 2:3], 0.125, None, ALU.mult, None, ["gq"], ["gq"])
        self.dma("sp", self.skt[0:64, :], AP(self.sink_d, 0, [[0, 64], [1, 4]]), writes=["skt"])
        self.dma("sp", self.skt[64:128, :], AP(self.sink_d, 4, [[0, 64], [1, 4]]), writes=["skt"])
        self.act(self.skt[:], self.skt[:], AF.Exp, ["skt"], ["skt"])
        for j in range(4):
            self.ts("dve", self.sinkbc[:, j, :], self.zf[:], self.skt[:, j:j + 1], None, ALU.add, None, ["zf", "skt"], ["sinkbc"])
        self.adabuf = [self.palloc_late("adabuf%d" % i, [128, 8, 256], BF16) for i in range(2)]
        self.ada_list = [(l, sl) for l in range(2) for sl in range(24)]
        self.ada_pos = 0
        self.ada_issued = 0
        self.win = self.nc.alloc_sbuf_tensor_at("win_top", [128, 8, 1280], BF16, offset=self.hi - 20480)
        self.ada_issue()
        self.ada_tick(4)
        self.load_w(self.win[:], "win", self.evin_d.ap().rearrange("(c p) n -> p c n", p=128))
        self.ada_tick(4)

    def palloc_late(self, name, shape, dt):
        t, self.pers = self._alloc(self.pers, name, shape, dt)
        return t

    def ada_issue(self):
        if self.ada_issued >= len(self.ada_list):
            return
        l, sl = self.ada_list[self.ada_issued]
        i = self.ada_issued % 2
        src = self.ada_w_d.ap()[l].rearrange("(c p) n -> p c n", p=128)[:, :, sl * 256:(sl + 1) * 256]
        self.dma("pool", self.adabuf[i][:], src, writes=["adabuf%d" % i])
        self.ada_issued += 1

    def ada_tick(self, n=1):
        for _ in range(n):
            if self.ada_pos >= len(self.ada_list):
                return
            l, sl = self.ada_list[self.ada_pos]
            i = self.ada_pos % 2
            self.ada_pos += 1
            self.ada_issue()
            buf, bk = self.adabuf[i], "adabuf%d" % i
            pm, pmk = self.pb[7], "pb7"
            for o2 in range(2):
                oc = sl * 2 + o2
                for kc in range(8):
                    self.mm(pm[:, oc * 2:oc * 2 + 2], buf[:, kc, o2 * 128:(o2 + 1) * 128], self.sT[:, :, kc],
                            kc == 0, kc == 7, [bk, "sT"], [pmk])
            if sl == 7 or sl == 23:
                lo, hi = (0, 16) if sl == 7 else (16, 48)
                for s_ in range(2):
                    self.tt("dve", self.modT[:, l, lo:hi, s_], pm[:, 0:96].rearrange("p (o s) -> p o s", s=2)[:, lo:hi, s_],
                            self.biasT[:, l * 48 + lo:l * 48 + hi], ALU.add, [pmk, "biasT"], ["modT"])
                    if sl == 7:
                        self.stt("dve", self.A1[:, l, s_, :], self.modT[:, l, 8:16, s_], 1.0, self.vecT[:, l * 8:l * 8 + 8],
                                 ALU.add, ALU.mult, ["modT", "vecT"], ["A1"])
                    else:
                        self.stt("dve", self.A2[:, l, s_, :], self.modT[:, l, 32:40, s_], 1.0, self.vecT[:, 16 + l * 8:16 + l * 8 + 8],
                                 ALU.add, ALU.mult, ["modT", "vecT"], ["A2"])

    def mod(self, l, part, s, c):
        return self.modT[:, l, part * 8 + c, s:s + 1]

    def load_xT(self, src_rows, ntile, dst, dst_key, col0, xtoks):
        for t in range(ntile):
            xt, xk = xtoks[self.xti % len(xtoks)]
            self.xti += 1
            self.dma("sp", xt[:], src_rows[t * 128:(t + 1) * 128, :], writes=[xk])
            for half in range(2):
                b, bk = self.bank()
                for cc in range(4):
                    c = half * 4 + cc
                    self.tr(b[:, cc * 128:(cc + 1) * 128], xt[:, c * 128:(c + 1) * 128], [xk], [bk])
                o = dst[:, half * 4:half * 4 + 4, col0 + t * 128:col0 + (t + 1) * 128]
                self.evac(o, b[:, :].rearrange("p (c n) -> p c n", n=128), [bk], [dst_key])

    def norm_mod(self, xT, xkey, col0, N, A, Bf, hT, hkey, hcol0, tmp, part=None, state=None):
        sqbs, rss, tf = tmp
        if part in (None, 1):
            self.nmi += 1
            sqb, sqk = sqbs[self.nmi % len(sqbs)], "sqb%d" % (self.nmi % len(sqbs))
            rs, rk = rss[self.nmi % len(rss)], "rsn%d" % (self.nmi % len(rss))
            nsq = sqb.shape[1]
            b, bk = self.bank()
            for c0 in range(0, 8, nsq):
                for c in range(c0, c0 + nsq):
                    xi = xT[:, c, col0:col0 + N]
                    if c % 4 == 3:
                        self.tt("dve", sqb[:, c % nsq, 0:N], xi, xi, ALU.mult, [xkey], [sqk])
                    else:
                        self.act(sqb[:, c % nsq, 0:N], xi, AF.Square, [xkey], [sqk])
                for c in range(c0, c0 + nsq):
                    self.mm(b[:, 0:N], self.ones1024, sqb[:, c % nsq, 0:N], c == 0, c == 7, [sqk, "c_bf"], [bk])
            self.act(rs[:, 0:N], b[:, 0:N], AF.Ln, [bk], [rk], bias=EPS)
            self.act(rs[:, 0:N], rs[:, 0:N], AF.Exp, [rk], [rk], scale=-0.5)
            state = (rs, rk)
            if part == 1:
                return state
        rs, rk = state
        for c in range(8):
            self.tfi += 1
            t_ = tf[self.tfi % len(tf)]
            tk = "tf%d" % (self.tfi % len(tf))
            self.stt("dve", t_[:, 0:N], xT[:, c, col0:col0 + N], A(c), rs[:, 0:N], ALU.mult, ALU.mult,
                     [xkey, rk, "A1", "A2"], [tk])
            if part == 2 and c % 2 == 1:
                self.ts("dve", hT[:, c, hcol0:hcol0 + N], t_[:, 0:N], Bf(c), None, ALU.add, None, [tk, "modT"], [hkey])
            else:
                self.act(hT[:, c, hcol0:hcol0 + N], t_[:, 0:N], AF.Identity, [tk, "modT"], [hkey], bias=Bf(c))
        return None

    def norm_pipeline(self, calls):
        st = [None] * len(calls)
        if calls:
            st[0] = self.norm_mod(part=1, **calls[0])
        for i in range(len(calls)):
            if i + 1 < len(calls):
                st[i + 1] = self.norm_mod(part=1, **calls[i + 1])
            self.norm_mod(part=2, state=st[i], **calls[i])

    def qk_norm(self, praw, pk, N, gcol, out, okey, tmp, rope=None):
        sq1s, rss, qns, r1s = tmp
        self.qni += 1
        i = self.qni
        sq1, sk = sq1s[i % len(sq1s)], "sq1_%d" % (i % len(sq1s))
        rs, rk = rss[i % len(rss)], "rsq%d" % (i % len(rss))
        self.act(sq1[:, 0:N], praw[:, 0:N], AF.Square, [pk], [sk])
        b, bk = self.bank()
        self.mm(b[:, 0:N], self.blk64, sq1[:, 0:N], True, True, [sk, "c_bf"], [bk])
        self.act(rs[:, 0:N], b[:, 0:N], AF.Ln, [bk], [rk], bias=EPS)
        self.act(rs[:, 0:N], rs[:, 0:N], AF.Exp, [rk], [rk], scale=-0.5)
        if rope is None:
            self.stt("dve", out, praw[:, 0:N], gcol, rs[:, 0:N], ALU.mult, ALU.mult, [pk, rk, "gq"], [okey])
            return
        qn, qk_ = qns[i % len(qns)], "qn%d" % (i % len(qns))
        r1, r1k = r1s[i % len(r1s)], "r1_%d" % (i % len(r1s))
        cosT, sinT, rpk = rope
        self.stt("dve", qn[:, 0:N], praw[:, 0:N], gcol, rs[:, 0:N], ALU.mult, ALU.mult, [pk, rk, "gq"], [qk_])
        b2, bk2 = self.bank()
        self.mm(b2[:, 0:N], self.pswap[:], qn[:, 0:N], True, True, [qk_, "c_bf"], [bk2])
        self.tt("pool", r1[:, 0:N], qn[:, 0:N], cosT, ALU.mult, [qk_] + rpk, [r1k])
        self.tt("dve", rs[:, 0:N], b2[:, 0:N], sinT, ALU.mult, [bk2] + rpk, [rk])
        self.tt("dve", out, rs[:, 0:N], r1[:, 0:N], ALU.add, [rk, r1k], [okey])

    def qk_pipeline(self, jobs, tmp):
        sq1s, rss, qns, r1s = tmp
        st = []
        for j, jb in enumerate(jobs):
            d = dict(jb)
            d["sq1"], d["sk"] = sq1s[j % len(sq1s)], "sq1_%d" % (j % len(sq1s))
            d["rs"], d["rk"] = rss[j % len(rss)], "rsq%d" % (j % len(rss))
            if jb["rope"] is not None:
                d["qn"], d["qk"] = qns[j % len(qns)], "qn%d" % (j % len(qns))
                d["r1"], d["r1k"] = r1s[j % len(r1s)], "r1_%d" % (j % len(r1s))
            st.append(d)

        def sa(d):
            d["b"], d["bk"] = d["proj"]()

        def sb(d):
            N = d["N"]
            praw, pk, sq1, sk, rs, rk = d["b"], d["bk"], d["sq1"], d["sk"], d["rs"], d["rk"]
            self.act(sq1[:, 0:N], praw[:, 0:N], AF.Square, [pk], [sk])
            b, bk = self.bank()
            self.mm(b[:, 0:N], self.blk64, sq1[:, 0:N], True, True, [sk, "c_bf"], [bk])
            self.act(rs[:, 0:N], b[:, 0:N], AF.Ln, [bk], [rk], bias=EPS)
            self.act(rs[:, 0:N], rs[:, 0:N], AF.Exp, [rk], [rk], scale=-0.5)
            if d["rope"] is None:
                self.stt("dve", d["out"], praw[:, 0:N], d["gcol"], rs[:, 0:N], ALU.mult, ALU.mult, [pk, rk, "gq"], [d["okey"]])
            else:
                self.stt("dve", d["qn"][:, 0:N], praw[:, 0:N], d["gcol"], rs[:, 0:N], ALU.mult, ALU.mult, [pk, rk, "gq"], [d["qk"]])

        def sc(d):
            if d["rope"] is None:
                return
            N = d["N"]
            cosT, sinT, rpk = d["rope"]
            qn, qk_, r1, r1k, rs, rk = d["qn"], d["qk"], d["r1"], d["r1k"], d["rs"], d["rk"]
            b2, bk2 = self.bank()
            self.mm(b2[:, 0:N], self.pswap[:], qn[:, 0:N], True, True, [qk_, "c_bf"], [bk2])
            self.tt("pool", r1[:, 0:N], qn[:, 0:N], cosT, ALU.mult, [qk_] + rpk, [r1k])
            self.tt("dve", rs[:, 0:N], b2[:, 0:N], sinT, ALU.mult, [bk2] + rpk, [rk])
            self.tt("dve", d["out"], rs[:, 0:N], r1[:, 0:N], ALU.add, [rk, r1k], [d["okey"]])

        n = len(st)
        for t in range(n + 2):
            if t < n:
                sa(st[t])
            if 0 <= t - 1 < n:
                sb(st[t - 1])
            if 0 <= t - 2 < n:
                sc(st[t - 2])

    def load_w(self, dst, dkey, src):
        self.dma("pool", dst, src, writes=[dkey])

    def layer0_A(self):
        self.phase()
        ta = self.talloc
        self.rot = [0, 1, 2, 3, 4, 5, 6]
        self.QT = ta("QT", [128, 4, T_A], BF16)
        self.KT = ta("KT", [128, T_A], BF16)
        self.V = ta("V", [128, T_A // 128, 128], BF16)
        self.QcT = ta("QcT", [128, 4, NCTX], BF16)
        self.KcT = ta("KcT", [128, NCTX], BF16)
        self.Vc = ta("Vc", [128, 2, 128], BF16)
        fo_off = self.cur
        self.FO = ta("FO", [128, 4, T_X], BF16)
        self.FOc = ta("FOc", [128, 4, NCTX], BF16)
        self.keepC = self.cur
        self.F_all = ta("F_all", [128, 32, 512], BF16)
        self.Fc = ta("Fc", [128, 2, 512], BF16)
        self.keepA = self.cur
        win = self.win
        xtoks = [(ta("xtok%d" % i, [128, 1024], F32), "xtok%d" % i) for i in range(4)]
        self.xti = 0
        self.uid += 1
        xTb2 = self.nc.alloc_sbuf_tensor_at("xTb2_%d" % self.uid, [128, 8, 512], F32, offset=fo_off)
        xTbs = [(ta("xTb", [128, 8, 512], F32), "xTb0"), (xTb2, "xTb1")]
        hTs = [(ta("hT%d" % i, [128, 8, 512], BF16), "hT%d" % i) for i in range(2)]
        sqbs = [ta("sqb", [128, 4, 512], BF16)]
        rss = [ta("rsn%d" % i, [128, 512], F32) for i in range(2)]
        tf = [ta("tf%d" % i, [128, 512], F32) for i in range(2)]
        sq1s = [ta("sq1_%d" % i, [128, 512], BF16) for i in range(2)]
        rsq = [ta("rsq%d" % i, [128, 512], F32) for i in range(2)]
        self.uid += 1
        qns = [self.nc.alloc_sbuf_tensor_at("qn0_%d" % self.uid, [128, 512], F32, offset=fo_off + 16384), ta("qn1", [128, 512], F32)]
        r1s = [self.nc.alloc_sbuf_tensor_at("r10_%d" % self.uid, [128, 512], F32, offset=fo_off + 16384 + 2048)]
        ropeb = [ta("rope%d" % i, [128, 2, 512], F32) for i in range(1)]
        tmpn = (sqbs, rss, tf)
        tmpq = (sq1s, rsq, qns, r1s)
        assert self.cur <= self.hi - 20480, "phase A overflow into win"
        l = 0

        def projA(hT, hk, N, qkv, s, tok0, tile0, ropek):
            for t in range(N // 128):
                b, bk = self.bank()
                for c in range(8):
                    self.mm(b[:, 0:512], hT[:, c, t * 128:(t + 1) * 128], win[:, c, 0:512], c == 0, c == 7, [hk, "win"], [bk])
                dst = self.F_all[:, tile0 + t, :] if s == 0 else self.Fc[:, t, :]
                self.evac(dst, b[:, 0:512], [bk], ["F_all" if s == 0 else "Fc"])
            if not qkv:
                return
            for t in range(N // 128):
                b, bk = self.bank()
                for c in range(8):
                    self.mm(b[:, 0:128], hT[:, c, t * 128:(t + 1) * 128], win[:, c, 1152:1280], c == 0, c == 7, [hk, "win"], [bk])
                dst = self.V[:, tile0 + t, :] if s == 0 else self.Vc[:, t, :]
                self.evac(dst, b[:, 0:128], [bk], ["V" if s == 0 else "Vc"])

        def projB(hT, hk, N, qkv, s, tok0, tile0, ropek):
            if not qkv:
                return
            jobs = []
            for j in range(5):
                def proj(j=j):
                    b, bk = self.bank()
                    for c in range(8):
                        self.mm(b[:, 0:N], win[:, c, 512 + j * 128:512 + (j + 1) * 128], hT[:, c, 0:N], c == 0, c == 7, [hk, "win"], [bk])
                    return b, bk
                if s == 0:
                    out = self.QT[:, j, tok0:tok0 + N] if j < 4 else self.KT[:, tok0:tok0 + N]
                    okey = "QT" if j < 4 else "KT"
                    rb_ = ropeb[0]
                    rope = (rb_[:, 0, 0:N], rb_[:, 1, 0:N], ["rope0c", "rope0s"])
                else:
                    out = self.QcT[:, j, 0:N] if j < 4 else self.KcT[:, 0:N]
                    okey = "QcT" if j < 4 else "KcT"
                    rope = None
                jobs.append(dict(proj=proj, N=N, gcol=self.gq[:, 0:1] if j < 4 else self.gq[:, 1:2], out=out, okey=okey, rope=rope))
            self.qk_pipeline(jobs, tmpq)

        items = [("ctx", 0)] + [("lat", g) for g in range(8)]

        def stage1(idx):
            kind, g = items[idx]
            hT, hk = hTs[idx % 2]
            if kind == "ctx":
                self.load_xT(self.ctx_d.ap(), 2, self.YT, "YT", 0, xtoks)
                return (hT, hk, NCTX, True, 1, 0, 0, 0, self.YT, "YT")
            tok0 = g * 512
            xTb, xk = xTbs[idx % 2]
            self.load_xT(self.x_d.ap()[tok0:tok0 + 512, :], 4, xTb, xk, 0, xtoks)
            return (hT, hk, 512, g < 5, 0, tok0, g * 4, g, xTb, xk)

        def stage2(st, part, state=None):
            hT, hk, N, qkv, s_, tok0, tile0, rk, xT, xk = st
            return self.norm_mod(xT, xk, 0, N, lambda c: self.A1[:, l, s_, c:c + 1], lambda c: self.mod(l, 0, s_, c),
                                 hT, hk, 0, tmpn, part=part, state=state)

        cur = stage1(0)
        stage2(cur, 2, stage2(cur, 1))
        for idx in range(len(items)):
            nxt = stage1(idx + 1) if idx + 1 < len(items) else None
            projA(*cur[:8])
            self.ada_tick(1)
            if nxt is not None:
                stage2(nxt, 2, stage2(nxt, 1))
            projB(*cur[:8])
            if nxt is not None and nxt[3] and nxt[4] == 0:
                rb_ = ropeb[0]
                tk0 = nxt[5]
                self.dma("sp", rb_[:, 0, :], self.crope_d.ap()[0, :, tk0:tk0 + 512], writes=["rope0c"])
                self.dma("sp", rb_[:, 1, :], self.crope_d.ap()[1, :, tk0:tk0 + 512], writes=["rope0s"])
            self.ada_tick(1)
            cur = nxt
        if self.stop_after == "A":
            for nm_, t_, k_ in [("F_all", self.F_all, "F_all"), ("QT", self.QT, "QT"), ("KT", self.KT, "KT"), ("V", self.V, "V"),
                                ("Fc", self.Fc, "Fc"), ("QcT", self.QcT, "QcT"), ("KcT", self.KcT, "KcT"), ("Vc", self.Vc, "Vc"),
                                ("modT", self.modT, "modT")]:
                self.dump(nm_, t_, k_)

    def dft(self, Fsrc, fkey, ntile, tabs, K, FO, fokey, tabbuf, ABT):
        ti = 0
        for (k0, N) in groups(K):
            for cs in range(2):
                tb, tk = tabbuf[ti % 2], "tab%d" % (ti % 2)
                ti += 1
                src_t = tabs(cs, k0, N)
                for a0 in range(0, ntile, 8):
                    self.dma("sp", tb[:, a0:a0 + 8, 0:N], src_t[:, a0:a0 + 8, :], writes=[tk])
                for g in range(4):
                    b, bk = self.bank()
                    for a in range(ntile):
                        self.mm(b[:, 0:N], Fsrc[:, a, g * 128:(g + 1) * 128], tb[:, a, 0:N], a == 0, a == ntile - 1, [fkey, tk], [bk])
                    self.evac(ABT[:, cs, g, 0:N], b[:, 0:N], [bk], ["ABT"])
                    self.ada_tick(1)
            for g in range(4):
                b, bk = self.bank()
                self.mm(b[:, 0:N], self.ccos, ABT[:, 0, g, 0:N], True, False, ["ABT", "c_bf"], [bk])
                self.mm(b[:, 0:N], self.cnsin, ABT[:, 1, g, 0:N], False, True, ["ABT", "c_bf"], [bk])
                self.evac(FO[:, g, k0:k0 + N], b[:, 0:N], [bk], [fokey])

    def layer0_B(self):
        scr = self.scr
        self.P.barrier(lambda e: e.memset(scr[0:1, 0:8], 0.0))
        self.cur = self.keepA
        ta = self.talloc
        tabbuf = [ta("tab%d" % i, [128, 32, 512], BF16) for i in range(2)]
        ABT = ta("ABT", [128, 2, 4, 512], BF16)
        for cs in range(2):
            for g in range(4):
                b, bk = self.bank()
                for a in range(2):
                    self.mm(b[:, 0:NCTX], self.Fc[:, a, g * 128:(g + 1) * 128], self.cctx[:, cs, a, :], a == 0, a == 1, ["Fc", "c_bf"], [bk])
                self.evac(ABT[:, cs, g, 0:NCTX], b[:, 0:NCTX], [bk], ["ABT"])
        for g in range(4):
            b, bk = self.bank()
            self.mm(b[:, 0:NCTX], self.ccos, ABT[:, 0, g, 0:NCTX], True, False, ["ABT", "c_bf"], [bk])
            self.mm(b[:, 0:NCTX], self.cnsin, ABT[:, 1, g, 0:NCTX], False, True, ["ABT", "c_bf"], [bk])
            self.evac(self.FOc[:, g, :], b[:, 0:NCTX], [bk], ["FOc"])
        tabs = lambda cs, k0, N: self.cdft_d[cs].ap().rearrange("(a p) k -> p a k", p=128)[:, :, k0:k0 + N]
        self.dft(self.F_all, "F_all", 32, tabs, T_X, self.FO, "FO", tabbuf, ABT)
        self.ada_tick(100)
        self.rot = [2, 3, 4, 5, 6, 7]
        self.pers -= 2 * 4096
        if self.stop_after == "B":
            self.dump("FO", self.FO, "FO")
            self.dump("FOc", self.FOc, "FOc")

    def layer0_C(self):
        scr = self.scr
        self.P.barrier(lambda e: e.memset(scr[0:1, 0:8], 0.0))
        self.cur = self.keepC
        ta = self.talloc
        l = 0
        wout = ta("wout", [128, 8, D], BF16)
        self.load_w(wout[:], "wout", self.evout_d.ap().rearrange("(c p) n -> p c n", p=128))
        ATTs = [(ta("ATT%d" % i, [128, 4, 512], BF16), "ATT%d" % i) for i in range(2)]
        qlos = [(ta("qlo%d" % i, [128, 4, 128], BF16), "qlo%d" % i) for i in range(2)]
        qhis = [(ta("qhi%d" % i, [128, 4, 128], BF16), "qhi%d" % i) for i in range(2)]
        PT = [ta("PT%d" % i, [128, 512], BF16) for i in range(4)]
        dtots = [(ta("dtot%d" % i, [128, 512], F32), "dtot%d" % i) for i in range(2)]
        xtoks = [(ta("xtokC%d" % i, [128, 1024], F32), "xtokC%d" % i) for i in range(3)]
        self.xti = 0
        self.XT = self.nc.alloc_sbuf_tensor_at("XT_res", [128, 8, T_X], F32, offset=self.hi - 8 * T_X * 4)
        assert self.cur <= self.hi - 8 * T_X * 4, "phase C overflow"
        for (q_, k_) in qlos + qhis:
            self.P.add("pool", lambda e, q_=q_: e.memset(q_[:], 0.0), writes=[k_])
        self.rot = [4, 5, 6, 7]
        accs = [((self.pb[0], "pb0"), (self.pb[1], "pb1")), ((self.pb[2], "pb2"), (self.pb[3], "pb3"))]
        ctxtiles = [(self.KcT[:, t * 128:(t + 1) * 128], self.Vc[:, t, :], None, ["KcT", "Vc"]) for t in range(2)]

        jobs = []

        def outproj(N, FOsrc, fokey, focol, Xdst, xkey, xcol, s_, ATT, ak):
            for oc in range(8):
                b, bk = self.bank()
                for kc in range(8):
                    rhs = FOsrc[:, kc, focol:focol + N] if kc < 4 else ATT[:, kc - 4, 0:N]
                    self.mm(b[:, 0:N], wout[:, kc, oc * 128:(oc + 1) * 128], rhs, kc == 0, kc == 7, ["wout", fokey, ak], [bk])
                xo = Xdst[:, oc, xcol:xcol + N]
                self.stt("dve", xo, b[:, 0:N], self.mod(l, 2, s_, oc), xo, ALU.mult, ALU.add, [bk, "modT", xkey], [xkey])

        gi = 0
        for qt in range(2):
            post = (lambda ai=gi % 2: outproj(NCTX, self.FOc, "FOc", 0, self.YT, "YT", 0, 1, *ATTs[ai])) if qt == 1 else None
            jobs.append((self.QcT, "QcT", qt * 128, ctxtiles, gi % 2, qt * 128, None, post))
        gi += 1
        for (t0, N) in groups(T_X):
            for tq in range(N // 128):
                qt = t0 // 128 + tq
                kts = []
                for dlt, mask in ((-1, self.mprev), (0, None), (1, self.mnext)):
                    kt = qt + dlt
                    if kt < 0:
                        continue
                    kts.append((self.KT[:, kt * 128:(kt + 1) * 128], self.V[:, kt, :], mask, ["KT", "V"]))
                pre = (lambda t0=t0, N=N: self.load_xT(self.x_d.ap()[t0:t0 + N, :], N // 128, self.XT, "XT%d" % (t0 // 512), t0, xtoks)) if tq == 0 else None
                post = (lambda t0=t0, N=N, ai=gi % 2: outproj(N, self.FO, "FO", t0, self.XT, "XT%d" % (t0 // 512), t0, 0, *ATTs[ai])) \
                    if tq == N // 128 - 1 else None
                jobs.append((self.QT, "QT", qt * 128, kts + ctxtiles, gi % 2, tq * 128, pre, post))
            gi += 1

        steps = []
        for ji, job in enumerate(jobs):
            n = len(job[3])
            for kvh in range(2):
                for i in range(n):
                    steps.append((ji, kvh, i, n))
        pti = [0]

        def s_stage(st):
            ji, kvh, i, n = st
            QTsrc, qkey, qcol, keytiles, ai, dcol, pre, post = jobs[ji]
            (qlo, qlk), (qhi, qhk) = qlos[ji % 2], qhis[ji % 2]
            (pbO, ok_), (pbD, dk_) = accs[ji % 2]
            if kvh == 0 and i == 0:
                if pre is not None:
                    pre()
                self.copy("pool", qlo[0:64, :, :], QTsrc[0:64, :, qcol:qcol + 128], [qkey], [qlk])
                self.copy("pool", qhi[64:128, :, :], QTsrc[64:128, :, qcol:qcol + 128], [qkey], [qhk])
            qp, qk_ = (qlo, qlk) if kvh == 0 else (qhi, qhk)
            kap, vap, mask, kkeys = keytiles[i]
            b, bk = self.bank()
            self.mm(b[:, 0:512], kap, qp[:].rearrange("p j n -> p (j n)"), True, mask is None, [qk_] + kkeys, [bk])
            if mask is not None:
                self.mm(b[:, 0:512], self.identb, mask.unsqueeze(1).broadcast_to([128, 4, 128]), False, True, ["c_bf"], [bk])
            pt = PT[pti[0] % 4]
            pk = "PT%d" % (pti[0] % 4)
            pti[0] += 1
            self.act(pt[:], b[:, 0:512], AF.Exp, [bk], [pk])
            return (pt, pk)

        def p_stage(st, pt, pk):
            ji, kvh, i, n = st
            QTsrc, qkey, qcol, keytiles, ai, dcol, pre, post = jobs[ji]
            (pbO, ok_), (pbD, dk_) = accs[ji % 2]
            kap, vap, mask, kkeys = keytiles[i]
            rows = slice(kvh * 64, kvh * 64 + 64)
            self.mm(pbO[rows, 0:512], vap[:, kvh * 64:kvh * 64 + 64], pt[:], i == 0, i == n - 1, [pk] + kkeys, [ok_])
            self.mm(pbD[rows, 0:512], self.onesb[:, 0:64], pt[:], i == 0, i == n - 1, [pk, "c_bf"], [dk_])
            if kvh == 1 and i == n - 1:
                dtot, dtk = dtots[ji % 2]
                ATT, ak = ATTs[ai]
                self.tt("dve", dtot[:], pbD[:, 0:512], self.sinkbc[:].rearrange("p j n -> p (j n)"), ALU.add, [dk_, "sinkbc"], [dtk])
                self.act(dtot[:], dtot[:], AF.Ln, [dtk], [dtk])
                self.act(dtot[:], dtot[:], AF.Exp, [dtk], [dtk], scale=-1.0)
                self.tt("dve", ATT[:, :, dcol:dcol + 128], pbO[:, 0:512].rearrange("p (j n) -> p j n", n=128),
                        dtot[:].rearrange("p (j n) -> p j n", n=128), ALU.mult, [ok_, dtk], [ak])
                if post is not None:
                    post()

        prev = None
        for st in steps:
            cur = s_stage(st)
            if prev is not None:
                p_stage(*prev)
            prev = (st, cur[0], cur[1])
        p_stage(*prev)
        self.rot = [2, 3, 4, 5, 6, 7]

    def ffn(self, l):
        scr = self.scr
        self.P.barrier(lambda e: e.memset(scr[0:1, 0:8], 0.0))
        self.cur = self.pers
        ta = self.talloc
        T = T_X if l == 0 else T_OWN
        toks = [(t0, N, 0) for (t0, N) in groups(T)]
        HT = T + (NCTX if l == 0 else 0)
        h2 = ta("h2", [128, 8, HT], BF16)
        keepF = self.cur
        sqbs = [ta("sqb%d" % i, [128, 8, 512], BF16) for i in range(2)]
        rss = [ta("rsn%d" % i, [128, 512], F32) for i in range(2)]
        tf = [ta("tf%d" % i, [128, 512], F32) for i in range(4)]
        tmpn = (sqbs, rss, tf)
        xkey = lambda t0: "XT%d" % (t0 // 512)
        calls = [dict(xT=self.XT, xkey=xkey(t0), col0=t0, N=N, A=(lambda c: self.A2[:, l, 0, c:c + 1]),
                      Bf=(lambda c: self.mod(l, 3, 0, c)), hT=h2, hkey="h2_%d" % (t0 // 512), hcol0=t0, tmp=tmpn) for (t0, N, _) in toks]
        if l == 0:
            calls.append(dict(xT=self.YT, xkey="YT", col0=0, N=NCTX, A=(lambda c: self.A2[:, l, 1, c:c + 1]),
                              Bf=(lambda c: self.mod(l, 3, 1, c)), hT=h2, hkey="h2_c", hcol0=T, tmp=tmpn))
            toks = toks + [(T, NCTX, 1)]
        self.norm_pipeline(calls)
        self.P.barrier(lambda e: e.memset(scr[0:1, 0:8], 0.0))
        self.cur = keepF
        splits = [(0, 4), (4, 4), (8, 4), (12, 4), (16, 3), (19, 3)]
        wgb = [ta("wg%d" % i, [128, 8, 512], BF16) for i in range(2)]
        wub = [ta("wu%d" % i, [128, 8, 512], BF16) for i in range(2)]
        wdb = [ta("wd%d" % i, [128, 4, D], BF16) for i in range(2)]
        actb = [ta("act%d" % i, [128, 4, 512], BF16) for i in range(2)]
        sg = [ta("sg%d" % i, [128, 512], F32) for i in range(3)]
        obs = [ta("ob%d" % i, [128, D], F32) for i in range(2)] if l == 1 else None
        assert self.cur <= self.hi - 8 * T_X * 4, "ffn overflow"
        ai = 0
        si = 0
        for sp_i, (j0, nj) in enumerate(splits):
            wi = sp_i % 2
            wg, wu, wd = wgb[wi], wub[wi], wdb[wi]
            kg, ku, kd = "wg%d" % wi, "wu%d" % wi, "wd%d" % wi
            self.load_w(wg[:, :, 0:nj * 128], kg, self.wg_d.ap()[l].rearrange("(c p) n -> p c n", p=128)[:, :, j0 * 128:(j0 + nj) * 128])
            self.load_w(wu[:, :, 0:nj * 128], ku, self.wu_d.ap()[l].rearrange("(c p) n -> p c n", p=128)[:, :, j0 * 128:(j0 + nj) * 128])
            self.load_w(wd[:, 0:nj, :], kd, self.wd_d.ap()[l].rearrange("(j p) n -> p j n", p=128)[:, j0:j0 + nj, :])
            for (t0, N, s) in toks:
                ab = actb[ai % 2]
                ak = "act%d" % (ai % 2)
                ai += 1
                hk = "h2_c" if s == 1 else "h2_%d" % (t0 // 512)
                for jj in range(nj):
                    bg, bgk = self.bank()
                    for c in range(8):
                        self.mm(bg[:, 0:N], wg[:, c, jj * 128:(jj + 1) * 128], h2[:, c, t0:t0 + N], c == 0, c == 7, [kg, hk], [bgk])
                    bu, buk = self.bank()
                    for c in range(8):
                        self.mm(bu[:, 0:N], wu[:, c, jj * 128:(jj + 1) * 128], h2[:, c, t0:t0 + N], c == 0, c == 7, [ku, hk], [buk])
                    s_ = sg[si % 3]
                    sk = "sg%d" % (si % 3)
                    si += 1
                    self.act(s_[:, 0:N], bg[:, 0:N], AF.Silu, [bgk], [sk])
                    self.tt("dve", ab[:, jj, 0:N], bu[:, 0:N], s_[:, 0:N], ALU.mult, [buk, sk], [ak])
                X, xk, xc = (self.XT, xkey(t0), t0) if s == 0 else (self.YT, "YT", 0)
                for oc in range(8):
                    b, bk = self.bank()
                    for jj in range(nj):
                        self.mm(b[:, 0:N], wd[:, jj, oc * 128:(oc + 1) * 128], ab[:, jj, 0:N], jj == 0, jj == nj - 1, [kd, ak], [bk])
                    xo = X[:, oc, xc:xc + N]
                    self.stt("dve", xo, b[:, 0:N], self.mod(l, 5, s, oc), xo, ALU.mult, ALU.add, [bk, "modT", xk], [xk])
                if l == 1 and sp_i == len(splits) - 1:
                    self.emit_out(range(t0 // 128, (t0 + N) // 128), obs)
        if l == 1:
            self.out_done = True

    def layer1_E(self):
        scr = self.scr
        self.P.barrier(lambda e: e.memset(scr[0:1, 0:8], 0.0))
        self.cur = self.pers
        ta = self.talloc
        l = 1
        NK = T_X + NCTX
        hT = ta("hT1", [128, 8, NK], BF16)
        keepE = self.cur
        sqbs = [ta("sqb%d" % i, [128, 8, 512], BF16) for i in range(2)]
        rss = [ta("rsn%d" % i, [128, 512], F32) for i in range(2)]
        tf = [ta("tf%d" % i, [128, 512], F32) for i in range(4)]
        tmpn = (sqbs, rss, tf)
        calls = [dict(xT=self.XT, xkey="XT%d" % (t0 // 512), col0=t0, N=N, A=(lambda c: self.A1[:, l, 0, c:c + 1]),
                      Bf=(lambda c: self.mod(l, 0, 0, c)), hT=hT, hkey="hT1_%d" % (t0 // 512), hcol0=t0, tmp=tmpn) for (t0, N) in groups(T_X)]
        calls.append(dict(xT=self.YT, xkey="YT", col0=0, N=NCTX, A=(lambda c: self.A1[:, l, 1, c:c + 1]),
                          Bf=(lambda c: self.mod(l, 0, 1, c)), hT=hT, hkey="hT1_%d" % (T_X // 512), hcol0=T_X, tmp=tmpn))
        self.norm_pipeline(calls)
        self.P.barrier(lambda e: e.memset(scr[0:1, 0:8], 0.0))
        self.cur = keepE
        off_rsq = self.cur
        rsq = [ta("rsq%d" % i, [128, 512], F32) for i in range(2)]
        self.uid += 1
        paccs = [(self.nc.alloc_sbuf_tensor_at("pacc%d_%d" % (i, self.uid), [128, 512], BF16, offset=off_rsq + i * 1024), "acc%d" % i) for i in range(4)]
        acck = ["acc%d" % i for i in range(4)]
        sq1s = [ta("sq1_%d" % i, [128, 512], BF16) for i in range(1)]
        tmpq = (sq1s, rsq, None, None)
        namask = ta("namask", [128, 3, 6, 128], BF16)
        self.dma("sp", namask[:], self.cnam_d.ap(), writes=["namask"])
        wq = ta("wq", [128, 8, 256], BF16)
        wk = ta("wk", [128, 8, 256], BF16)
        wv = ta("wv", [128, 8, 256], BF16)
        wo = ta("wo", [128, 2, D], BF16)
        Tt = ta("Tt", [128, 6, 4, 128], BF16)
        BMI = ta("BMI", [128, 5, 4, 128], BF16)
        bmtmp = rsq[0]
        KTh = ta("KTh", [128, 2, NK], BF16)
        Vh = ta("Vh", [128, NK // 128, 256], BF16)
        QTh = ta("QTh", [128, 2, T_OWN], BF16)
        ATTs = [(ta("ATT1_%d" % i, [128, 2, 512], BF16), "ATT1_%d" % i) for i in range(2)]
        qpads = [[(ta("qpad%d_%d" % (e, i), [128, 2, 128], BF16), "qpad%d_%d" % (e, i)) for e in range(2)] for i in range(2)]
        PT = [ta("PT1_%d" % i, [128, 512], BF16) for i in range(5)]
        dtots = [(ta("dtot1_%d" % i, [128, 512], F32), "dtot1_%d" % i) for i in range(1)]
        assert self.cur <= self.hi - 8 * T_X * 4, "layer1 overflow %d" % (self.cur - (self.hi - 8 * T_X * 4))
        for i in range(2):
            for e_ in range(2):
                q_, k_ = qpads[i][e_]
                self.P.add("pool", lambda e, q_=q_: e.memset(q_[:], 0.0), writes=[k_])
        self.rot = [4, 5, 6, 7]
        accs = [((self.pb[0], "pb0"), (self.pb[1], "pb1")), ((self.pb[2], "pb2"), (self.pb[3], "pb3"))]
        z256 = self.cbf[:, 4:6, :].rearrange("p a n -> p (a n)")
        z512 = self.cbf[:, 4:8, :].rearrange("p a n -> p (a n)")
        hkey = lambda t0: "hT1_%d" % (t0 // 512)
        pti = [0]
        tglob = [0]
        src = self.odin_d.ap().rearrange("(c p) n -> p c n", p=128)

        def load_qkv(hg_):
            self.load_w(wq[:], "wq", src[:, :, hg_ * 256:(hg_ + 1) * 256])
            self.load_w(wk[:], "wk", src[:, :, D + hg_ * 256:D + (hg_ + 1) * 256])
            self.load_w(wv[:], "wv", src[:, :, 2 * D + hg_ * 256:2 * D + (hg_ + 1) * 256])

        def load_Tt(hg_):
            for dl in range(6):
                for kr in range(2):
                    for qr in range(2):
                        dr_idx = 2 * (dl - 2) + kr - qr + 7
                        srcb = AP(self.rb_d, hg_ * 4 * 465 + dr_idx * 31 - 48, [[1, 64], [465, 4], [1, 64]])
                        self.dma("pool", Tt[qr * 64:(qr + 1) * 64, dl, :, kr * 64:(kr + 1) * 64], srcb, writes=["Tt%d" % dl])

        load_qkv(0)
        for hg in range(4):
            self.load_w(wo[:], "wo", self.odout_d.ap().rearrange("(c p) n -> p c n", p=128)[:, hg * 2:hg * 2 + 2, :])
            if hg == 0:
                load_Tt(0)
            for dl in range(5):
                b, bk = self.bank()
                for h4 in range(4):
                    self.mm(b[:, h4 * 128:(h4 + 1) * 128], Tt[:, dl, h4, :], self.rblk, True, True, ["Tt%d" % dl, "c_bf"], [bk])
                self.tt("dve", BMI[:, dl, :, :], b[:, 0:512].rearrange("p (h n) -> p h n", n=128),
                        namask[:, 2, dl, :].unsqueeze(1).broadcast_to([128, 4, 128]), ALU.add, [bk, "namask"], ["BMI"])
            self.rot = [0, 1, 2, 3, 4, 5, 6, 7]
            self.P.add("dve", lambda e: e.memset(scr[0:1, 8:12], 0.0), reads=acck, writes=["rsq0", "rsq1"])
            for (t0, N) in groups(NK):
                jobs = []
                for ch in range(2):
                    def projk(ch=ch, t0=t0, N=N):
                        b, bk = self.bank()
                        for c in range(8):
                            self.mm(b[:, 0:N], wk[:, c, ch * 128:(ch + 1) * 128], hT[:, c, t0:t0 + N], c == 0, c == 7, ["wk", hkey(t0)], [bk])
                        return b, bk
                    jobs.append(dict(proj=projk, N=N, gcol=self.gq[:, 3:4], out=KTh[:, ch, t0:t0 + N], okey="KTh", rope=None))
                    if t0 + N <= T_OWN:
                        def projq(ch=ch, t0=t0, N=N):
                            b, bk = self.bank()
                            for c in range(8):
                                self.mm(b[:, 0:N], wq[:, c, ch * 128:(ch + 1) * 128], hT[:, c, t0:t0 + N], c == 0, c == 7, ["wq", hkey(t0)], [bk])
                            return b, bk
                        jobs.append(dict(proj=projq, N=N, gcol=self.gq[:, 2:3], out=QTh[:, ch, t0:t0 + N], okey="QTh", rope=None))
                self.qk_pipeline(jobs, tmpq)
                for t in range(N // 128):
                    b, bk = self.bank()
                    for c in range(8):
                        self.mm(b[:, 0:256], hT[:, c, t0 + t * 128:t0 + (t + 1) * 128], wv[:, c, :], c == 0, c == 7, ["wv", hkey(t0)], [bk])
                    self.evac(Vh[:, t0 // 128 + t, :], b[:, 0:256], [bk], ["Vh"])
            self.rot = [4, 5, 6, 7]
            self.P.add("dve", lambda e: e.memset(scr[0:1, 12:16], 0.0), reads=["rsq0", "rsq1"], writes=acck)
            if hg + 1 < 4:
                load_qkv(hg + 1)
            steps = []
            for qt in range(T_OWN // 128):
                kts = [(qt + d_, d_ + 2) for d_ in range(-2, 4 if qt == 0 else 3) if qt + d_ >= 0] + \
                      [(T_X // 128, None), (T_X // 128 + 1, None)]
                for i, (kt, dl) in enumerate(kts):
                    steps.append((qt, i, len(kts), kt, dl))

            def s_stage(st):
                qt, i, n, kt, dl = st
                tq_ = tglob[0] + qt
                qp = qpads[tq_ % 2]
                (pbO, ok_), (pbD, dk_) = accs[tq_ % 2]
                if i == 0:
                    for e_ in range(2):
                        rows = slice(e_ * 64, e_ * 64 + 64)
                        self.copy("pool", qp[e_][0][rows, :, :], QTh[rows, :, qt * 128:(qt + 1) * 128], ["QTh"], [qp[e_][1]])
                    self.mm(pbO[:, 0:256], self.zerob, z256, True, False, ["c_bf"], [ok_])
                b, bk = self.bank()
                first = True
                if dl is not None:
                    if qt >= 2:
                        self.mm(b[:, 0:512], self.identb, BMI[:, dl, :, :].rearrange("p h n -> p (h n)"), True, False, ["BMI", "c_bf"], [bk])
                        first = False
                    else:
                        self.mm(b[:, 0:512], self.identb, namask[:, qt, dl, :].unsqueeze(1).broadcast_to([128, 4, 128]), True, False,
                                ["namask", "c_bf"], [bk])
                        for h4 in range(4):
                            self.mm(b[:, h4 * 128:(h4 + 1) * 128], Tt[:, dl, h4, :], self.rblk, False, False, ["Tt%d" % dl, "c_bf"], [bk])
                        first = False
                for h4 in range(4):
                    ch, e_ = h4 // 2, h4 % 2
                    self.mm(b[:, h4 * 128:(h4 + 1) * 128], KTh[:, ch, kt * 128:(kt + 1) * 128], qp[e_][0][:, ch, :], first, True,
                            ["KTh", qp[e_][1]], [bk])
                pt = PT[pti[0] % 5]
                pk = "PT1_%d" % (pti[0] % 5)
                pti[0] += 1
                self.act(pt[:], b[:, 0:512], AF.Exp, [bk], [pk])
                acc, ak_ = paccs[(tq_ % 2) * 2 + (i % 2)]
                if i < 2:
                    self.copy("dve", acc[:], pt[:], [pk], [ak_])
                else:
                    self.tt("dve", acc[:], acc[:], pt[:], ALU.add, [ak_, pk], [ak_])
                return (pt, pk)

            def p_stage(st, pt, pk):
                qt, i, n, kt, dl = st
                tq_ = tglob[0] + qt
                (pbO, ok_), (pbD, dk_) = accs[tq_ % 2]
                for h4 in range(4):
                    ch, e_ = h4 // 2, h4 % 2
                    rows = slice(e_ * 64, e_ * 64 + 64)
                    self.mm(pbO[rows, ch * 128:(ch + 1) * 128], Vh[:, kt, h4 * 64:(h4 + 1) * 64], pt[:, h4 * 128:(h4 + 1) * 128],
                            False, True, [pk, "Vh"], [ok_])
                if i == n - 1:
                    accA, akA = paccs[(tq_ % 2) * 2]
                    accB, akB = paccs[(tq_ % 2) * 2 + 1]
                    self.mm(pbD[:, 0:512], self.onesb, accA[:], True, False, [akA, "c_bf"], [dk_])
                    self.mm(pbD[:, 0:512], self.onesb, accB[:], False, True, [akB, "c_bf"], [dk_])
                    dtot, dtk = dtots[0]
                    ATT, ak = ATTs[(qt // 4) % 2]
                    tq = qt % 4
                    self.act(dtot[:], pbD[:, 0:512], AF.Ln, [dk_], [dtk])
                    self.act(dtot[:], dtot[:], AF.Exp, [dtk], [dtk], scale=-1.0)
                    dv = dtot[:].rearrange("p (c e n) -> p c e n", e=2, n=128)
                    for e_ in range(2):
                        rows = slice(e_ * 64, e_ * 64 + 64)
                        self.tt("dve", ATT[rows, :, tq * 128:(tq + 1) * 128], pbO[rows, 0:256].rearrange("p (j n) -> p j n", n=128),
                                dv[rows, :, e_, :], ALU.mult, [ok_, dtk], [ak])
                    if tq == 3:
                        t0 = (qt // 4) * 512
                        for oc in range(8):
                            b, bk = self.bank()
                            for ch in range(2):
                                self.mm(b[:, 0:512], wo[:, ch, oc * 128:(oc + 1) * 128], ATT[:, ch, 0:512], ch == 0, ch == 1, ["wo", ak], [bk])
                            xo = self.XT[:, oc, t0:t0 + 512]
                            xk = "XT%d" % (t0 // 512)
                            self.stt("dve", xo, b[:, 0:512], self.mod(l, 2, 0, oc), xo, ALU.mult, ALU.add, [bk, "modT", xk], [xk])

            if "noattn" in self.flags:
                steps = []
            if "few" in self.flags:
                steps = steps[:self.nstep]
            pend = []
            for si_, st in enumerate(steps):
                cur = s_stage(st)
                if st[0] == 1 and si_ + 1 < len(steps) and steps[si_ + 1][0] == 2 and hg + 1 < 4:
                    load_Tt(hg + 1)
                pend.append((st, cur[0], cur[1]))
                if len(pend) > 2:
                    p_stage(*pend.pop(0))
            for pp in pend:
                p_stage(*pp)
            tglob[0] += T_OWN // 128
            if "onehg" in self.flags:
                break
        self.rot = [2, 3, 4, 5, 6, 7]

    def finish(self, final=True):
        scr = self.scr
        self.P.barrier(lambda e: e.memset(scr[0:1, 0:8], 0.0))
        self.cur = self.pers
        ta = self.talloc
        if self.debug and hasattr(self, "XT"):
            self.dma("sp", self.dbgx_d.ap(), self.XT[:], reads=["XT%d" % i for i in range(5)])
            self.dma("sp", self.dbgy_d.ap(), self.YT[:], reads=["YT"])
        if not hasattr(self, "XT"):
            return
        if getattr(self, "out_done", False):
            return
        ob = [ta("ob%d" % i, [128, D], F32) for i in range(2)]
        self.emit_out(range(T_OWN // 128), ob)

    def emit_out(self, tiles, ob):
        for t in tiles:
            o_, ok = ob[t % 2], "ob%d" % (t % 2)
            for half in range(2):
                b, bk = self.bank()
                for cc in range(4):
                    c = half * 4 + cc
                    self.tr(b[:, cc * 128:(cc + 1) * 128], self.XT[:, c, t * 128:(t + 1) * 128], ["XT%d" % (t // 4)], [bk])
                self.evac(o_[:, half * 512:(half + 1) * 512], b[:, 0:512], [bk], [ok])
            self.dma("sp", self.out_d.ap()[t * 128:(t + 1) * 128, :], o_[:], reads=[ok])


_CONST_CACHE = {}


def _bf(a):
    return np.ascontiguousarray(a.astype(ml_dtypes.bfloat16))


def host_consts(par):
    if par in _CONST_CACHE:
        return _CONST_CACHE[par]
    c = {}
    c["c_identf"] = np.eye(128, dtype=np.float32)
    psw = np.zeros((128, 128), np.float32)
    for m in range(128):
        i = m % 32
        partner = m + 16 if i < 16 else m - 16
        psw[partner, m] = 1.0
    c["c_pswap"] = psw
    cb = np.zeros((128, 10, 128), np.float32)
    cb[:, 0, :] = 1.0 / 1024.0
    cb[0:64, 1, 0:64] = 1.0 / 64.0
    cb[64:128, 1, 64:128] = 1.0 / 64.0
    cb[:, 2, :] = np.eye(128)
    for blk in range(2):
        for u in range(64):
            cb[blk * 64 + u, 3, blk * 64 + 63 - u] = 1.0
    cb[:, 5, :] = 1.0
    cc = np.arange(128)
    ang = 2.0 * np.pi * ((cc[:, None] * cc[None, :]) % 128) / 128.0
    cb[:, 6, :] = np.cos(ang) / np.sqrt(128.0)
    cb[:, 7, :] = -np.sin(ang) / np.sqrt(128.0)
    j = np.arange(128)[:, None]
    q = np.arange(128)[None, :]
    cb[:, 8, :] = np.where(j >= q, 0.0, NEG)
    cb[:, 9, :] = np.where(j <= q, 0.0, NEG)
    c["c_bf"] = _bf(cb)
    n = np.arange(256)
    a2 = 2.0 * np.pi * ((n[:, None] * n[None, :]) % 256) / 256.0
    C2 = (np.cos(a2) / 16.0).reshape(2, 128, 256)
    S2 = (np.sin(a2) / 16.0).reshape(2, 128, 256)
    cx = np.stack([C2.transpose(1, 0, 2), S2.transpose(1, 0, 2)], axis=1)
    c["c_ctxdft"] = _bf(cx)
    loc = np.arange(T_ALL)
    glob = loc if par == 0 else (T_ALL - 1 - loc)
    inv = (np.float32(10000.0) ** (-np.arange(16, dtype=np.float32) / np.float32(16))).astype(np.float32)
    g_ = glob[:T_A]
    row = (g_ // 64).astype(np.float32)
    col = (g_ % 64).astype(np.float32)
    ar = (row[None, :] * inv[:, None]).astype(np.float32)
    ac = (col[None, :] * inv[:, None]).astype(np.float32)
    cos64 = np.concatenate([np.cos(ar), np.cos(ar), np.cos(ac), np.cos(ac)], axis=0)
    sin64 = np.concatenate([-np.sin(ar), np.sin(ar), -np.sin(ac), np.sin(ac)], axis=0)
    c["c_rope"] = np.ascontiguousarray(np.stack([np.concatenate([cos64, cos64], 0), np.concatenate([sin64, sin64], 0)], 0).astype(np.float32))
    gk = glob[:T_X].astype(np.int64)
    gn = glob.astype(np.int64)
    ph = (gn[:, None] * gk[None, :]) % T_ALL
    angL = (2.0 * np.pi / T_ALL) * ph
    c["c_dftc"] = _bf(np.cos(angL) / 64.0)
    c["c_dfts"] = _bf(np.sin(angL) / 64.0)
    nm = np.zeros((128, 3, 6, 128), np.float32)
    for cls, qt in enumerate((0, 1, 8)):
        for dl in range(6):
            kt = qt + dl - 2
            if kt < 0:
                nm[:, cls, dl, :] = NEG
                continue
            kg = glob[kt * 128 + np.arange(128)]
            qg = glob[qt * 128 + np.arange(128)]
            kr, kc = kg // 64, kg % 64
            qr, qc = qg // 64, qg % 64
            r0 = np.clip(qr - 4, 0, 56)
            c0 = np.clip(qc - 8, 0, 48)
            ok = (kr[:, None] >= r0[None, :]) & (kr[:, None] < r0[None, :] + 8) & (kc[:, None] >= c0[None, :]) & (kc[:, None] < c0[None, :] + 16)
            nm[:, cls, dl, :] = np.where(ok, 0.0, NEG)
    c["c_namask"] = _bf(nm)
    _CONST_CACHE[par] = c
    return c


_NC_CACHE = {}


def get_nc(stop_after=None, debug=False, only=None, flags=()):
    key = (stop_after, debug, only, tuple(flags))
    if key not in _NC_CACHE:
        bld = Builder(stop_after=stop_after, debug=debug, only=only, flags=flags)
        _NC_CACHE[key] = bld.build()
    return _NC_CACHE[key]


def make_in_maps(inputs):
    f32 = lambda a: np.ascontiguousarray(np.asarray(a, dtype=np.float32))
    x = f32(inputs["x"])
    c = f32(inputs["c"])
    ctx = f32(inputs["ctx"])
    c_ctx = f32(inputs["c_ctx"])
    ev_in = f32(inputs["ev_w_in"])[0]
    ev_out = f32(inputs["ev_w_out"])[0]
    hp = [0, 4, 1, 5, 2, 6, 3, 7]
    qcols = np.concatenate([512 + h * 64 + np.arange(64) for h in hp])
    cols = np.concatenate([np.arange(512), qcols, np.arange(1024, 1280)])
    ev_in_p = np.ascontiguousarray(ev_in[:, cols])
    rows = np.concatenate([np.arange(512), qcols])
    ev_out_p = np.ascontiguousarray(ev_out[rows, :])
    rb = f32(inputs["od_rel_bias"])[0]
    shared = {
        "ada_w": f32(inputs["ada_w"]), "ada_b": f32(inputs["ada_b"]),
        "norm1_g": f32(inputs["norm1_g"]), "norm2_g": f32(inputs["norm2_g"]),
        "ffn_w_gate": f32(inputs["ffn_w_gate"]), "ffn_w_up": f32(inputs["ffn_w_up"]), "ffn_w_down": f32(inputs["ffn_w_down"]),
        "ev_w_in": ev_in_p, "ev_w_out": ev_out_p,
        "ev_q_norm": f32(inputs["ev_q_norm"])[0], "ev_k_norm": f32(inputs["ev_k_norm"])[0], "ev_sink": f32(inputs["ev_sink"])[0],
        "od_w_in": f32(inputs["od_w_in"])[0], "od_w_out": f32(inputs["od_w_out"])[0],
        "od_q_norm": f32(inputs["od_q_norm"])[0], "od_k_norm": f32(inputs["od_k_norm"])[0],
    }
    pad = np.zeros(64, np.float32)
    rbs = [np.concatenate([rb.reshape(-1), pad]), np.concatenate([rb[:, ::-1, ::-1].reshape(-1), pad])]
    in_maps = []
    for cid in range(8):
        b, par = cid // 2, cid % 2
        m = dict(shared)
        m["x_loc"] = np.ascontiguousarray(x[b] if par == 0 else x[b][::-1])
        m["ctx_b"] = np.ascontiguousarray(ctx[b])
        m["cvec"] = np.ascontiguousarray(np.stack([c[b], c_ctx], 0))
        m["rel_bias"] = rbs[par]
        m.update(host_consts(par))
        in_maps.append(m)
    return in_maps


def assemble(results):
    out = np.zeros((4, T_ALL, D), np.float32)
    for cid in range(8):
        b, par = cid // 2, cid % 2
        o = np.asarray(results[cid]["out_loc"], dtype=np.float32)
        if par == 0:
            out[b, :T_OWN] = o
        else:
            out[b, T_OWN:] = o[::-1]
    return out


def kernel(**inputs):
    nc = get_nc()
    in_maps = make_in_maps(inputs)
    res = run_bass_kernel_spmd(nc, in_maps, core_ids=list(range(8)))
    return assemble(res.results)
```

```python
import numpy as np
import ml_dtypes
import concourse.bass as bass
import concourse.mybir as mybir
from concourse.bass_utils import run_bass_kernel_spmd
from concourse.ap import AP

F32 = mybir.dt.float32
BF16 = mybir.dt.bfloat16
ALU = mybir.AluOpType
AF = mybir.ActivationFunctionType

NDMA_SEM = 48
D = 1024
DFF = 2816
NEG = -30000.0
EPS = 1e-6
T_ALL = 4096
T_X = 2304
T_A = 2560
T_OWN = 2048
NCTX = 256


class _Op:
    __slots__ = ("id", "eng", "fn", "deps", "is_dma", "sem_i", "sem_val", "signal", "count")


class Prog:
    ENGS = ("pe", "act", "dve", "pool", "sp")

    def __init__(self):
        self.ops = []
        self.lw = {}
        self.rd = {}
        self.dma_rr = 0
        self.dma_sem_uses = [0] * NDMA_SEM
        self.dma_sem_last = [None] * NDMA_SEM
        self.last_eng = {}
        self.bar = None
        self.dopen = set()
        self.saved = {}
        self.strict = True

    def add(self, eng, fn, reads=(), writes=(), dma=False):
        op = _Op()
        op.id = len(self.ops)
        op.eng = eng
        op.fn = fn
        op.is_dma = dma
        op.signal = False
        op.count = 0
        deps = {}
        if self.bar is not None:
            deps[self.bar] = True
        for k in reads:
            for w in self.lw.get(k, ()):
                deps[w] = True
            self.dopen.discard(k)
        cow = set()
        for k in writes:
            if dma and k in self.dopen:
                cow.add(k)
                for r in self.saved.get(k, ()):
                    deps.setdefault(r, False)
                continue
            for w in self.lw.get(k, ()):
                deps.setdefault(w, False)
            for r in self.rd.get(k, ()):
                deps.setdefault(r, False)
        if dma:
            i = self.dma_rr % NDMA_SEM
            self.dma_rr += 1
            prev = self.dma_sem_last[i]
            if prev is not None:
                deps.setdefault(prev, False)
            self.dma_sem_uses[i] += 1
            op.sem_i = i
            op.sem_val = 16 * self.dma_sem_uses[i]
            self.dma_sem_last[i] = op.id
        op.deps = deps
        for k in reads:
            self.rd.setdefault(k, []).append(op.id)
        for k in writes:
            if k in cow:
                self.lw[k].append(op.id)
                continue
            self.saved[k] = list(self.lw.get(k, ())) + list(self.rd.get(k, ()))
            self.lw[k] = [op.id]
            self.rd[k] = []
            if dma:
                self.dopen.add(k)
            else:
                self.dopen.discard(k)
        self.ops.append(op)
        if not dma:
            self.last_eng[eng] = op.id
        return op

    def barrier(self, fn):
        op = self.add("pool", fn)
        for e, i in self.last_eng.items():
            if i != op.id:
                op.deps[i] = True
        for i in self.dma_sem_last:
            if i is not None:
                op.deps[i] = True
        self.bar = op.id
        self.lw = {}
        self.rd = {}
        self.dopen = set()
        self.saved = {}

    def emit(self, block, eng_sems, dma_sems):
        ops = self.ops

        strict = self.strict

        def skip(op, dop, raw):
            if op.is_dma or dop.is_dma or dop.eng != op.eng:
                return False
            return op.eng == "pe" or (not raw and not strict)

        for op in ops:
            for d, raw in op.deps.items():
                dop = ops[d]
                if dop.is_dma or skip(op, dop, raw):
                    continue
                dop.signal = True
        cnt = {e: 0 for e in self.ENGS}
        for op in ops:
            if op.signal and not op.is_dma:
                cnt[op.eng] += 1
                op.count = cnt[op.eng]
        per_eng = {e: [o for o in ops if o.eng == e] for e in self.ENGS}

        def run(e, engine):
            known = {}
            for op in per_eng[e]:
                needs = {}
                for d, raw in op.deps.items():
                    dop = ops[d]
                    if dop.is_dma:
                        key = ("d", dop.sem_i)
                        val = dop.sem_val
                    else:
                        if skip(op, dop, raw):
                            continue
                        key = ("e", dop.eng)
                        val = dop.count
                    if needs.get(key, 0) < val:
                        needs[key] = val
                for key, val in needs.items():
                    if known.get(key, 0) >= val:
                        continue
                    known[key] = val
                    sem = dma_sems[key[1]] if key[0] == "d" else eng_sems[key[1]]
                    engine.wait_ge(sem, val)
                ins = op.fn(engine)
                if op.is_dma:
                    ins.then_inc(dma_sems[op.sem_i], 16)
                elif op.signal:
                    ins.then_inc(eng_sems[e], 1)
            if e == "sp":
                for i in range(NDMA_SEM):
                    if self.dma_sem_uses[i] > 0:
                        engine.wait_ge(dma_sems[i], 16 * self.dma_sem_uses[i])

        block.tensor(lambda eng: run("pe", eng))
        block.scalar(lambda eng: run("act", eng))
        block.vector(lambda eng: run("dve", eng))
        block.gpsimd(lambda eng: run("pool", eng))
        block.sync(lambda eng: run("sp", eng))


def groups(n, g=512):
    out = []
    s = 0
    while s < n:
        out.append((s, min(g, n - s)))
        s += g
    return out


class Builder:
    def __init__(self, stop_after=None, debug=False, only=None, flags=()):
        self.only = only
        self.flags = set(f for f in flags if not f.startswith('n='))
        self.nstep = ([int(f[2:]) for f in flags if f.startswith('n=')] + [8])[0]
        self.nc = bass.Bass("TRN2", target_bir_lowering=False)
        self.P = Prog()
        self.stop_after = stop_after
        self.debug = debug
        nc = self.nc
        self.lo = ((nc.sbuf_base + 63) // 64) * 64
        self.hi = nc.sbuf_top
        self.pers = self.lo
        self.cur = None
        self.uid = 0
        self.rr = 0
        self.evr = 0
        self.rot = [2, 3, 4, 5, 6]
        self.nmi = 0
        self.tfi = 0
        self.qni = 0

    def _alloc(self, ptr, name, shape, dt):
        esz = 4 if dt == F32 else 2
        n = 1
        for s in shape[1:]:
            n *= s
        nbytes = ((n * esz + 63) // 64) * 64
        self.uid += 1
        t = self.nc.alloc_sbuf_tensor_at("%s_%d" % (name, self.uid), list(shape), dt, offset=ptr)
        return t, ptr + nbytes

    def palloc(self, name, shape, dt):
        assert self.cur is None
        t, self.pers = self._alloc(self.pers, name, shape, dt)
        assert self.pers <= self.hi, "persistent overflow"
        return t

    def talloc(self, name, shape, dt):
        t, self.cur = self._alloc(self.cur, name, shape, dt)
        assert self.cur <= self.hi, "phase overflow %s %d" % (name, self.cur - self.hi)
        return t

    def phase(self):
        scr = self.scr
        self.P.barrier(lambda e: e.memset(scr[0:1, 0:8], 0.0))
        self.cur = self.pers

    def dma(self, eng, out, in_, reads=(), writes=()):
        self.P.add(eng, lambda e: e.dma_start(out=out, in_=in_), reads=reads, writes=writes, dma=True)

    def mm(self, out, lhsT, rhs, start, stop, reads, writes):
        self.P.add("pe", lambda e: e.matmul(out, lhsT=lhsT, rhs=rhs, start=start, stop=stop), reads=reads, writes=writes)

    def tr(self, out, in_, reads, writes):
        ident = self.identf
        k = in_.shape[0]
        self.P.add("pe", lambda e: e.transpose(out, in_, ident[0:k, 0:k]), reads=list(reads) + ["c_id"], writes=writes)

    def act(self, out, in_, func, reads, writes, bias=None, scale=None):
        kw = {}
        if bias is not None:
            kw["bias"] = bias
        if scale is not None:
            kw["scale"] = scale
        self.P.add("act", lambda e: e.activation(out=out, in_=in_, func=func, **kw), reads=reads, writes=writes)

    def tt(self, eng, out, in0, in1, op, reads, writes):
        self.P.add(eng, lambda e: e.tensor_tensor(out=out, in0=in0, in1=in1, op=op), reads=reads, writes=writes)

    def stt(self, eng, out, in0, scalar, in1, op0, op1, reads, writes):
        self.P.add(eng, lambda e: e.scalar_tensor_tensor(out=out, in0=in0, scalar=scalar, in1=in1, op0=op0, op1=op1),
                   reads=reads, writes=writes)

    def ts(self, eng, out, in0, s1, s2, op0, op1, reads, writes):
        if s2 is None:
            self.P.add(eng, lambda e: e.tensor_scalar(out=out, in0=in0, scalar1=s1, scalar2=None, op0=op0), reads=reads, writes=writes)
        else:
            self.P.add(eng, lambda e: e.tensor_scalar(out=out, in0=in0, scalar1=s1, scalar2=s2, op0=op0, op1=op1), reads=reads, writes=writes)

    def copy(self, eng, out, in_, reads, writes):
        if eng == "act":
            self.act(out, in_, AF.Copy, reads, writes)
        else:
            self.P.add(eng, lambda e: e.tensor_copy(out=out, in_=in_), reads=reads, writes=writes)

    def evac(self, out, in_, reads, writes):
        self.evr += 1
        self.copy("act" if self.evr % 2 else "dve", out, in_, reads, writes)

    def dump(self, name, t, key):
        if not self.debug:
            return
        shape = list(t.shape)
        d = self.nc.dram_tensor("dbg_" + name, shape, t.dtype, kind="ExternalOutput")
        self.dma("sp", d.ap(), t[:], reads=[key])

    def bank(self):
        self.rr += 1
        i = self.rot[self.rr % len(self.rot)]
        return self.pb[i], "pb%d" % i

    def build(self):
        nc = self.nc
        P = self.P
        dr = lambda name, shape, dt, kind="ExternalInput": nc.dram_tensor(name, list(shape), dt, kind=kind)
        self.x_d = dr("x_loc", [T_ALL, D], F32)
        self.ctx_d = dr("ctx_b", [NCTX, D], F32)
        self.cvec_d = dr("cvec", [2, D], F32)
        self.ada_w_d = dr("ada_w", [2, D, 6 * D], F32)
        self.ada_b_d = dr("ada_b", [2, 6 * D], F32)
        self.n1_d = dr("norm1_g", [2, D], F32)
        self.n2_d = dr("norm2_g", [2, D], F32)
        self.wg_d = dr("ffn_w_gate", [2, D, DFF], F32)
        self.wu_d = dr("ffn_w_up", [2, D, DFF], F32)
        self.wd_d = dr("ffn_w_down", [2, DFF, D], F32)
        self.evin_d = dr("ev_w_in", [D, 1280], F32)
        self.evout_d = dr("ev_w_out", [D, D], F32)
        self.evq_d = dr("ev_q_norm", [64], F32)
        self.evk_d = dr("ev_k_norm", [64], F32)
        self.sink_d = dr("ev_sink", [8], F32)
        self.odin_d = dr("od_w_in", [D, 3 * D], F32)
        self.odout_d = dr("od_w_out", [D, D], F32)
        self.odq_d = dr("od_q_norm", [64], F32)
        self.odk_d = dr("od_k_norm", [64], F32)
        self.rb_d = dr("rel_bias", [16 * 15 * 31 + 64], F32)
        self.cidf_d = dr("c_identf", [128, 128], F32)
        self.cpsw_d = dr("c_pswap", [128, 128], F32)
        self.cbf_d = dr("c_bf", [128, 10, 128], BF16)
        self.cctx_d = dr("c_ctxdft", [128, 2, 2, 256], BF16)
        self.cnam_d = dr("c_namask", [128, 3, 6, 128], BF16)
        self.crope_d = dr("c_rope", [2, 128, T_A], F32)
        self.cdft_d = [dr("c_dftc", [T_ALL, T_X], BF16), dr("c_dfts", [T_ALL, T_X], BF16)]
        self.out_d = dr("out_loc", [T_OWN, D], F32, kind="ExternalOutput")
        if self.debug:
            self.dbgx_d = dr("dbg_x", [128, 8, T_X], F32, kind="ExternalOutput")
            self.dbgy_d = dr("dbg_y", [128, 8, NCTX], F32, kind="ExternalOutput")
            self.dbga_d = dr("dbg_a", [128, 8, T_X], F32, kind="ExternalOutput")

        self.pb = [nc.alloc_psum_tensor("pb%d" % i, [128, 512], F32) for i in range(8)]
        self.eng_sems = {e: nc.alloc_semaphore("s_" + e) for e in Prog.ENGS}
        self.dma_sems = [nc.alloc_semaphore("d%d" % i) for i in range(NDMA_SEM)]

        self.setup()
        done = False
        for name, fn in [("A", self.layer0_A), ("B", self.layer0_B), ("C", self.layer0_C), ("D", lambda: self.ffn(0)),
                         ("E", self.layer1_E), ("F", lambda: self.ffn(1))]:
            if self.only is not None and name not in self.only:
                continue
            if not hasattr(self, "XT") and name in ("D", "E", "F"):
                self.ada_tick(100)
                self.rot = [2, 3, 4, 5, 6, 7]
                self.pers -= 2 * 4096
                self.XT = self.nc.alloc_sbuf_tensor_at("XT_res", [128, 8, T_X], F32, offset=self.hi - 8 * T_X * 4)
            fn()
            if self.stop_after == name:
                done = True
                break
        self.finish(final=not done)
        with nc.Block() as block:
            P.emit(block, self.eng_sems, self.dma_sems)
        return nc

    def setup(self):
        nc = self.nc
        pa = self.palloc
        self.scr = pa("scr", [128, 16], F32)
        self.identf = pa("identf", [128, 128], F32)
        self.pswap = pa("pswap", [128, 128], F32)
        self.cbf = pa("cbf", [128, 10, 128], BF16)
        self.cctx = pa("cctx", [128, 2, 2, 256], BF16)
        self.vecT = pa("vecT", [128, 48], F32)
        self.biasT = pa("biasT", [128, 96], F32)
        self.modT = pa("modT", [128, 2, 48, 2], F32)
        self.A1 = pa("A1", [128, 2, 2, 8], F32)
        self.A2 = pa("A2", [128, 2, 2, 8], F32)
        self.sT = pa("sT", [128, 2, 8], BF16)
        self.gq = pa("gq", [128, 4], F32)
        self.sinkbc = pa("sinkbc", [128, 4, 128], F32)
        self.skt = pa("skt", [128, 4], F32)
        self.zf = pa("zf", [128, 128], F32)
        self.YT = pa("YT", [128, 8, NCTX], F32)
        self.dma("sp", self.identf[:], self.cidf_d.ap(), writes=["c_id"])
        self.dma("sp", self.pswap[:], self.cpsw_d.ap(), writes=["c_ps"])
        self.dma("sp", self.cbf[:], self.cbf_d.ap(), writes=["c_bf"])
        self.dma("sp", self.cctx[:], self.cctx_d.ap(), writes=["c_cx"])
        self.P.add("pool", lambda e: e.memset(self.zf[:], 0.0), writes=["zf"])
        self.P.add("pool", lambda e: e.memset(self.scr[:], 0.0), writes=["scr"])
        self.ones1024 = self.cbf[:, 0, :]
        self.blk64 = self.cbf[:, 1, :]
        self.identb = self.cbf[:, 2, :]
        self.rblk = self.cbf[:, 3, :]
        self.zerob = self.cbf[:, 4, :]
        self.onesb = self.cbf[:, 5, :]
        self.ccos = self.cbf[:, 6, :]
        self.cnsin = self.cbf[:, 7, :]
        self.mprev = self.cbf[:, 8, :]
        self.mnext = self.cbf[:, 9, :]
        self.adabuf = None
        self.cur = self.pers + 2 * 4096
        ta = self.talloc
        rows1 = ta("rows1", [48, 128], F32)
        rows2 = ta("rows2", [96, 128], F32)
        self.dma("sp", rows1[0:16, :], self.n1_d.ap().rearrange("l (c p) -> (l c) p", p=128), writes=["rows1"])
        self.dma("sp", rows1[16:32, :], self.n2_d.ap().rearrange("l (c p) -> (l c) p", p=128), writes=["rows1"])
        self.dma("sp", rows1[32:48, :], self.cvec_d.ap().rearrange("s (c p) -> (s c) p", p=128), writes=["rows1"])
        self.dma("sp", rows2[:, :], self.ada_b_d.ap().rearrange("l (c p) -> (l c) p", p=128), writes=["rows2"])
        b, bk = self.bank()
        self.tr(b[:, 0:48], rows1[:, :], ["rows1"], [bk])
        self.copy("dve", self.vecT[:], b[:, 0:48], [bk], ["vecT"])
        b, bk = self.bank()
        self.tr(b[:, 0:96], rows2[:, :], ["rows2"], [bk])
        self.copy("dve", self.biasT[:], b[:, 0:96], [bk], ["biasT"])
        self.act(self.sT[:].rearrange("p s c -> p (s c)"), self.vecT[:, 32:48], AF.Silu, ["vecT"], ["sT"])
        for i, (d_, sc) in enumerate([(self.evq_d, 0.125), (self.evk_d, 1.0), (self.odq_d, 0.125), (self.odk_d, 1.0)]):
            src = d_.ap().rearrange("(d u) -> d u", u=1)
            self.dma("sp", self.gq[0:64, i:i + 1], src, writes=["gq"])
            self.dma("sp", self.gq[64:128, i:i + 1], src, writes=["gq"])
        self.ts("dve", self.gq[:, 0:1], self.gq[:, 0:1], 0.125, None, ALU.mult, None, ["gq"], ["gq"])
        self.ts("dve", self.gq[:, 2:3], self.gq[:, 2:3], 0.125, None, ALU.mult, None, ["gq"], ["gq"])
        self.dma("sp", self.skt[0:64, :], AP(self.sink_d, 0, [[0, 64], [1, 4]]), writes=["skt"])
        self.dma("sp", self.skt[64:128, :], AP(self.sink_d, 4, [[0, 64], [1, 4]]), writes=["skt"])
        self.act(self.skt[:], self.skt[:], AF.Exp, ["skt"], ["skt"])
        for j in range(4):
            self.ts("dve", self.sinkbc[:, j, :], self.zf[:], self.skt[:, j:j + 1], None, ALU.add, None, ["zf", "skt"], ["sinkbc"])
        self.adabuf = [self.palloc_late("adabuf%d" % i, [128, 8, 256], BF16) for i in range(2)]
        self.ada_list = [(l, sl) for l in range(2) for sl in range(24)]
        self.ada_pos = 0
        self.ada_issued = 0
        self.win = self.nc.alloc_sbuf_tensor_at("win_top", [128, 8, 1280], BF16, offset=self.hi - 20480)
        self.ada_issue()
        self.ada_tick(4)
        self.load_w(self.win[:], "win", self.evin_d.ap().rearrange("(c p) n -> p c n", p=128))
        self.ada_tick(4)

    def palloc_late(self, name, shape, dt):
        t, self.pers = self._alloc(self.pers, name, shape, dt)
        return t

    def ada_issue(self):
        if self.ada_issued >= len(self.ada_list):
            return
        l, sl = self.ada_list[self.ada_issued]
        i = self.ada_issued % 2
        src = self.ada_w_d.ap()[l].rearrange("(c p) n -> p c n", p=128)[:, :, sl * 256:(sl + 1) * 256]
        self.dma("pool", self.adabuf[i][:], src, writes=["adabuf%d" % i])
        self.ada_issued += 1

    def ada_tick(self, n=1):
        for _ in range(n):
            if self.ada_pos >= len(self.ada_list):
                return
            l, sl = self.ada_list[self.ada_pos]
            i = self.ada_pos % 2
            self.ada_pos += 1
            self.ada_issue()
            buf, bk = self.adabuf[i], "adabuf%d" % i
            pm, pmk = self.pb[7], "pb7"
            for o2 in range(2):
                oc = sl * 2 + o2
                for kc in range(8):
                    self.mm(pm[:, oc * 2:oc * 2 + 2], buf[:, kc, o2 * 128:(o2 + 1) * 128], self.sT[:, :, kc],
                            kc == 0, kc == 7, [bk, "sT"], [pmk])
            if sl == 7 or sl == 23:
                lo, hi = (0, 16) if sl == 7 else (16, 48)
                for s_ in range(2):
                    self.tt("dve", self.modT[:, l, lo:hi, s_], pm[:, 0:96].rearrange("p (o s) -> p o s", s=2)[:, lo:hi, s_],
                            self.biasT[:, l * 48 + lo:l * 48 + hi], ALU.add, [pmk, "biasT"], ["modT"])
                    if sl == 7:
                        self.stt("dve", self.A1[:, l, s_, :], self.modT[:, l, 8:16, s_], 1.0, self.vecT[:, l * 8:l * 8 + 8],
                                 ALU.add, ALU.mult, ["modT", "vecT"], ["A1"])
                    else:
                        self.stt("dve", self.A2[:, l, s_, :], self.modT[:, l, 32:40, s_], 1.0, self.vecT[:, 16 + l * 8:16 + l * 8 + 8],
                                 ALU.add, ALU.mult, ["modT", "vecT"], ["A2"])

    def mod(self, l, part, s, c):
        return self.modT[:, l, part * 8 + c, s:s + 1]

    def load_xT(self, src_rows, ntile, dst, dst_key, col0, xtoks):
        for t in range(ntile):
            xt, xk = xtoks[self.xti % len(xtoks)]
            self.xti += 1
            self.dma("sp", xt[:], src_rows[t * 128:(t + 1) * 128, :], writes=[xk])
            for half in range(2):
                b, bk = self.bank()
                for cc in range(4):
                    c = half * 4 + cc
                    self.tr(b[:, cc * 128:(cc + 1) * 128], xt[:, c * 128:(c + 1) * 128], [xk], [bk])
                o = dst[:, half * 4:half * 4 + 4, col0 + t * 128:col0 + (t + 1) * 128]
                self.evac(o, b[:, :].rearrange("p (c n) -> p c n", n=128), [bk], [dst_key])

    def norm_mod(self, xT, xkey, col0, N, A, Bf, hT, hkey, hcol0, tmp, part=None, state=None):
        sqbs, rss, tf = tmp
        if part in (None, 1):
            self.nmi += 1
            sqb, sqk = sqbs[self.nmi % len(sqbs)], "sqb%d" % (self.nmi % len(sqbs))
            rs, rk = rss[self.nmi % len(rss)], "rsn%d" % (self.nmi % len(rss))
            nsq = sqb.shape[1]
            b, bk = self.bank()
            for c0 in range(0, 8, nsq):
                for c in range(c0, c0 + nsq):
                    xi = xT[:, c, col0:col0 + N]
                    if c % 4 == 3:
                        self.tt("dve", sqb[:, c % nsq, 0:N], xi, xi, ALU.mult, [xkey], [sqk])
                    else:
                        self.act(sqb[:, c % nsq, 0:N], xi, AF.Square, [xkey], [sqk])
                for c in range(c0, c0 + nsq):
                    self.mm(b[:, 0:N], self.ones1024, sqb[:, c % nsq, 0:N], c == 0, c == 7, [sqk, "c_bf"], [bk])
            self.act(rs[:, 0:N], b[:, 0:N], AF.Ln, [bk], [rk], bias=EPS)
            self.act(rs[:, 0:N], rs[:, 0:N], AF.Exp, [rk], [rk], scale=-0.5)
            state = (rs, rk)
            if part == 1:
                return state
        rs, rk = state
        for c in range(8):
            self.tfi += 1
            t_ = tf[self.tfi % len(tf)]
            tk = "tf%d" % (self.tfi % len(tf))
            self.stt("dve", t_[:, 0:N], xT[:, c, col0:col0 + N], A(c), rs[:, 0:N], ALU.mult, ALU.mult,
                     [xkey, rk, "A1", "A2"], [tk])
            if part == 2 and c % 2 == 1:
                self.ts("dve", hT[:, c, hcol0:hcol0 + N], t_[:, 0:N], Bf(c), None, ALU.add, None, [tk, "modT"], [hkey])
            else:
                self.act(hT[:, c, hcol0:hcol0 + N], t_[:, 0:N], AF.Identity, [tk, "modT"], [hkey], bias=Bf(c))
        return None

    def norm_pipeline(self, calls):
        st = [None] * len(calls)
        if calls:
            st[0] = self.norm_mod(part=1, **calls[0])
        for i in range(len(calls)):
            if i + 1 < len(calls):
                st[i + 1] = self.norm_mod(part=1, **calls[i + 1])
            self.norm_mod(part=2, state=st[i], **calls[i])

    def qk_norm(self, praw, pk, N, gcol, out, okey, tmp, rope=None):
        sq1s, rss, qns, r1s = tmp
        self.qni += 1
        i = self.qni
        sq1, sk = sq1s[i % len(sq1s)], "sq1_%d" % (i % len(sq1s))
        rs, rk = rss[i % len(rss)], "rsq%d" % (i % len(rss))
        self.act(sq1[:, 0:N], praw[:, 0:N], AF.Square, [pk], [sk])
        b, bk = self.bank()
        self.mm(b[:, 0:N], self.blk64, sq1[:, 0:N], True, True, [sk, "c_bf"], [bk])
        self.act(rs[:, 0:N], b[:, 0:N], AF.Ln, [bk], [rk], bias=EPS)
        self.act(rs[:, 0:N], rs[:, 0:N], AF.Exp, [rk], [rk], scale=-0.5)
        if rope is None:
            self.stt("dve", out, praw[:, 0:N], gcol, rs[:, 0:N], ALU.mult, ALU.mult, [pk, rk, "gq"], [okey])
            return
        qn, qk_ = qns[i % len(qns)], "qn%d" % (i % len(qns))
        r1, r1k = r1s[i % len(r1s)], "r1_%d" % (i % len(r1s))
        cosT, sinT, rpk = rope
        self.stt("dve", qn[:, 0:N], praw[:, 0:N], gcol, rs[:, 0:N], ALU.mult, ALU.mult, [pk, rk, "gq"], [qk_])
        b2, bk2 = self.bank()
        self.mm(b2[:, 0:N], self.pswap[:], qn[:, 0:N], True, True, [qk_, "c_bf"], [bk2])
        self.tt("pool", r1[:, 0:N], qn[:, 0:N], cosT, ALU.mult, [qk_] + rpk, [r1k])
        self.tt("dve", rs[:, 0:N], b2[:, 0:N], sinT, ALU.mult, [bk2] + rpk, [rk])
        self.tt("dve", out, rs[:, 0:N], r1[:, 0:N], ALU.add, [rk, r1k], [okey])

    def qk_pipeline(self, jobs, tmp):
        sq1s, rss, qns, r1s = tmp
        st = []
        for j, jb in enumerate(jobs):
            d = dict(jb)
            d["sq1"], d["sk"] = sq1s[j % len(sq1s)], "sq1_%d" % (j % len(sq1s))
            d["rs"], d["rk"] = rss[j % len(rss)], "rsq%d" % (j % len(rss))
            if jb["rope"] is not None:
                d["qn"], d["qk"] = qns[j % len(qns)], "qn%d" % (j % len(qns))
                d["r1"], d["r1k"] = r1s[j % len(r1s)], "r1_%d" % (j % len(r1s))
            st.append(d)

        def sa(d):
            d["b"], d["bk"] = d["proj"]()

        def sb(d):
            N = d["N"]
            praw, pk, sq1, sk, rs, rk = d["b"], d["bk"], d["sq1"], d["sk"], d["rs"], d["rk"]
            self.act(sq1[:, 0:N], praw[:, 0:N], AF.Square, [pk], [sk])
            b, bk = self.bank()
            self.mm(b[:, 0:N], self.blk64, sq1[:, 0:N], True, True, [sk, "c_bf"], [bk])
            self.act(rs[:, 0:N], b[:, 0:N], AF.Ln, [bk], [rk], bias=EPS)
            self.act(rs[:, 0:N], rs[:, 0:N], AF.Exp, [rk], [rk], scale=-0.5)
            if d["rope"] is None:
                self.stt("dve", d["out"], praw[:, 0:N], d["gcol"], rs[:, 0:N], ALU.mult, ALU.mult, [pk, rk, "gq"], [d["okey"]])
            else:
                self.stt("dve", d["qn"][:, 0:N], praw[:, 0:N], d["gcol"], rs[:, 0:N], ALU.mult, ALU.mult, [pk, rk, "gq"], [d["qk"]])

        def sc(d):
            if d["rope"] is None:
                return
            N = d["N"]
            cosT, sinT, rpk = d["rope"]
            qn, qk_, r1, r1k, rs, rk = d["qn"], d["qk"], d["r1"], d["r1k"], d["rs"], d["rk"]
            b2, bk2 = self.bank()
            self.mm(b2[:, 0:N], self.pswap[:], qn[:, 0:N], True, True, [qk_, "c_bf"], [bk2])
            self.tt("pool", r1[:, 0:N], qn[:, 0:N], cosT, ALU.mult, [qk_] + rpk, [r1k])
            self.tt("dve", rs[:, 0:N], b2[:, 0:N], sinT, ALU.mult, [bk2] + rpk, [rk])
            self.tt("dve", d["out"], rs[:, 0:N], r1[:, 0:N], ALU.add, [rk, r1k], [d["okey"]])

        n = len(st)
        for t in range(n + 2):
            if t < n:
                sa(st[t])
            if 0 <= t - 1 < n:
                sb(st[t - 1])
            if 0 <= t - 2 < n:
                sc(st[t - 2])

    def load_w(self, dst, dkey, src):
        self.dma("pool", dst, src, writes=[dkey])

    def layer0_A(self):
        self.phase()
        ta = self.talloc
        self.rot = [0, 1, 2, 3, 4, 5, 6]
        self.QT = ta("QT", [128, 4, T_A], BF16)
        self.KT = ta("KT", [128, T_A], BF16)
        self.V = ta("V", [128, T_A // 128, 128], BF16)
        self.QcT = ta("QcT", [128, 4, NCTX], BF16)
        self.KcT = ta("KcT", [128, NCTX], BF16)
        self.Vc = ta("Vc", [128, 2, 128], BF16)
        fo_off = self.cur
        self.FO = ta("FO", [128, 4, T_X], BF16)
        self.FOc = ta("FOc", [128, 4, NCTX], BF16)
        self.keepC = self.cur
        self.F_all = ta("F_all", [128, 32, 512], BF16)
        self.Fc = ta("Fc", [128, 2, 512], BF16)
        self.keepA = self.cur
        win = self.win
        xtoks = [(ta("xtok%d" % i, [128, 1024], F32), "xtok%d" % i) for i in range(4)]
        self.xti = 0
        self.uid += 1
        xTb2 = self.nc.alloc_sbuf_tensor_at("xTb2_%d" % self.uid, [128, 8, 512], F32, offset=fo_off)
        xTbs = [(ta("xTb", [128, 8, 512], F32), "xTb0"), (xTb2, "xTb1")]
        hTs = [(ta("hT%d" % i, [128, 8, 512], BF16), "hT%d" % i) for i in range(2)]
        sqbs = [ta("sqb", [128, 4, 512], BF16)]
        rss = [ta("rsn%d" % i, [128, 512], F32) for i in range(2)]
        tf = [ta("tf%d" % i, [128, 512], F32) for i in range(2)]
        sq1s = [ta("sq1_%d" % i, [128, 512], BF16) for i in range(2)]
        rsq = [ta("rsq%d" % i, [128, 512], F32) for i in range(2)]
        self.uid += 1
        qns = [self.nc.alloc_sbuf_tensor_at("qn0_%d" % self.uid, [128, 512], F32, offset=fo_off + 16384), ta("qn1", [128, 512], F32)]
        r1s = [self.nc.alloc_sbuf_tensor_at("r10_%d" % self.uid, [128, 512], F32, offset=fo_off + 16384 + 2048)]
        ropeb = [ta("rope%d" % i, [128, 2, 512], F32) for i in range(1)]
        tmpn = (sqbs, rss, tf)
        tmpq = (sq1s, rsq, qns, r1s)
        assert self.cur <= self.hi - 20480, "phase A overflow into win"
        l = 0

        def projA(hT, hk, N, qkv, s, tok0, tile0, ropek):
            for t in range(N // 128):
                b, bk = self.bank()
                for c in range(8):
                    self.mm(b[:, 0:512], hT[:, c, t * 128:(t + 1) * 128], win[:, c, 0:512], c == 0, c == 7, [hk, "win"], [bk])
                dst = self.F_all[:, tile0 + t, :] if s == 0 else self.Fc[:, t, :]
                self.evac(dst, b[:, 0:512], [bk], ["F_all" if s == 0 else "Fc"])
            if not qkv:
                return
            for t in range(N // 128):
                b, bk = self.bank()
                for c in range(8):
                    self.mm(b[:, 0:128], hT[:, c, t * 128:(t + 1) * 128], win[:, c, 1152:1280], c == 0, c == 7, [hk, "win"], [bk])
                dst = self.V[:, tile0 + t, :] if s == 0 else self.Vc[:, t, :]
                self.evac(dst, b[:, 0:128], [bk], ["V" if s == 0 else "Vc"])

        def projB(hT, hk, N, qkv, s, tok0, tile0, ropek):
            if not qkv:
                return
            jobs = []
            for j in range(5):
                def proj(j=j):
                    b, bk = self.bank()
                    for c in range(8):
                        self.mm(b[:, 0:N], win[:, c, 512 + j * 128:512 + (j + 1) * 128], hT[:, c, 0:N], c == 0, c == 7, [hk, "win"], [bk])
                    return b, bk
                if s == 0:
                    out = self.QT[:, j, tok0:tok0 + N] if j < 4 else self.KT[:, tok0:tok0 + N]
                    okey = "QT" if j < 4 else "KT"
                    rb_ = ropeb[0]
                    rope = (rb_[:, 0, 0:N], rb_[:, 1, 0:N], ["rope0c", "rope0s"])
                else:
                    out = self.QcT[:, j, 0:N] if j < 4 else self.KcT[:, 0:N]
                    okey = "QcT" if j < 4 else "KcT"
                    rope = None
                jobs.append(dict(proj=proj, N=N, gcol=self.gq[:, 0:1] if j < 4 else self.gq[:, 1:2], out=out, okey=okey, rope=rope))
            self.qk_pipeline(jobs, tmpq)

        items = [("ctx", 0)] + [("lat", g) for g in range(8)]

        def stage1(idx):
            kind, g = items[idx]
            hT, hk = hTs[idx % 2]
            if kind == "ctx":
                self.load_xT(self.ctx_d.ap(), 2, self.YT, "YT", 0, xtoks)
                return (hT, hk, NCTX, True, 1, 0, 0, 0, self.YT, "YT")
            tok0 = g * 512
            xTb, xk = xTbs[idx % 2]
            self.load_xT(self.x_d.ap()[tok0:tok0 + 512, :], 4, xTb, xk, 0, xtoks)
            return (hT, hk, 512, g < 5, 0, tok0, g * 4, g, xTb, xk)

        def stage2(st, part, state=None):
            hT, hk, N, qkv, s_, tok0, tile0, rk, xT, xk = st
            return self.norm_mod(xT, xk, 0, N, lambda c: self.A1[:, l, s_, c:c + 1], lambda c: self.mod(l, 0, s_, c),
                                 hT, hk, 0, tmpn, part=part, state=state)

        cur = stage1(0)
        stage2(cur, 2, stage2(cur, 1))
        for idx in range(len(items)):
            nxt = stage1(idx + 1) if idx + 1 < len(items) else None
            projA(*cur[:8])
            self.ada_tick(1)
            if nxt is not None:
                stage2(nxt, 2, stage2(nxt, 1))
            projB(*cur[:8])
            if nxt is not None and nxt[3] and nxt[4] == 0:
                rb_ = ropeb[0]
                tk0 = nxt[5]
                self.dma("sp", rb_[:, 0, :], self.crope_d.ap()[0, :, tk0:tk0 + 512], writes=["rope0c"])
                self.dma("sp", rb_[:, 1, :], self.crope_d.ap()[1, :, tk0:tk0 + 512], writes=["rope0s"])
            self.ada_tick(1)
            cur = nxt
        if self.stop_after == "A":
            for nm_, t_, k_ in [("F_all", self.F_all, "F_all"), ("QT", self.QT, "QT"), ("KT", self.KT, "KT"), ("V", self.V, "V"),
                                ("Fc", self.Fc, "Fc"), ("QcT", self.QcT, "QcT"), ("KcT", self.KcT, "KcT"), ("Vc", self.Vc, "Vc"),
                                ("modT", self.modT, "modT")]:
                self.dump(nm_, t_, k_)

    def dft(self, Fsrc, fkey, ntile, tabs, K, FO, fokey, tabbuf, ABT):
        ti = 0
        for (k0, N) in groups(K):
            for cs in range(2):
                tb, tk = tabbuf[ti % 2], "tab%d" % (ti % 2)
                ti += 1
                src_t = tabs(cs, k0, N)
                for a0 in range(0, ntile, 8):
                    self.dma("sp", tb[:, a0:a0 + 8, 0:N], src_t[:, a0:a0 + 8, :], writes=[tk])
                for g in range(4):
                    b, bk = self.bank()
                    for a in range(ntile):
                        self.mm(b[:, 0:N], Fsrc[:, a, g * 128:(g + 1) * 128], tb[:, a, 0:N], a == 0, a == ntile - 1, [fkey, tk], [bk])
                    self.evac(ABT[:, cs, g, 0:N], b[:, 0:N], [bk], ["ABT"])
                    self.ada_tick(1)
            for g in range(4):
                b, bk = self.bank()
                self.mm(b[:, 0:N], self.ccos, ABT[:, 0, g, 0:N], True, False, ["ABT", "c_bf"], [bk])
                self.mm(b[:, 0:N], self.cnsin, ABT[:, 1, g, 0:N], False, True, ["ABT", "c_bf"], [bk])
                self.evac(FO[:, g, k0:k0 + N], b[:, 0:N], [bk], [fokey])

    def layer0_B(self):
        scr = self.scr
        self.P.barrier(lambda e: e.memset(scr[0:1, 0:8], 0.0))
        self.cur = self.keepA
        ta = self.talloc
        tabbuf = [ta("tab%d" % i, [128, 32, 512], BF16) for i in range(2)]
        ABT = ta("ABT", [128, 2, 4, 512], BF16)
        for cs in range(2):
            for g in range(4):
                b, bk = self.bank()
                for a in range(2):
                    self.mm(b[:, 0:NCTX], self.Fc[:, a, g * 128:(g + 1) * 128], self.cctx[:, cs, a, :], a == 0, a == 1, ["Fc", "c_bf"], [bk])
                self.evac(ABT[:, cs, g, 0:NCTX], b[:, 0:NCTX], [bk], ["ABT"])
        for g in range(4):
            b, bk = self.bank()
            self.mm(b[:, 0:NCTX], self.ccos, ABT[:, 0, g, 0:NCTX], True, False, ["ABT", "c_bf"], [bk])
            self.mm(b[:, 0:NCTX], self.cnsin, ABT[:, 1, g, 0:NCTX], False, True, ["ABT", "c_bf"], [bk])
            self.evac(self.FOc[:, g, :], b[:, 0:NCTX], [bk], ["FOc"])
        tabs = lambda cs, k0, N: self.cdft_d[cs].ap().rearrange("(a p) k -> p a k", p=128)[:, :, k0:k0 + N]
        self.dft(self.F_all, "F_all", 32, tabs, T_X, self.FO, "FO", tabbuf, ABT)
        self.ada_tick(100)
        self.rot = [2, 3, 4, 5, 6, 7]
        self.pers -= 2 * 4096
        if self.stop_after == "B":
            self.dump("FO", self.FO, "FO")
            self.dump("FOc", self.FOc, "FOc")

    def layer0_C(self):
        scr = self.scr
        self.P.barrier(lambda e: e.memset(scr[0:1, 0:8], 0.0))
        self.cur = self.keepC
        ta = self.talloc
        l = 0
        wout = ta("wout", [128, 8, D], BF16)
        self.load_w(wout[:], "wout", self.evout_d.ap().rearrange("(c p) n -> p c n", p=128))
        ATTs = [(ta("ATT%d" % i, [128, 4, 512], BF16), "ATT%d" % i) for i in range(2)]
        qlos = [(ta("qlo%d" % i, [128, 4, 128], BF16), "qlo%d" % i) for i in range(2)]
        qhis = [(ta("qhi%d" % i, [128, 4, 128], BF16), "qhi%d" % i) for i in range(2)]
        PT = [ta("PT%d" % i, [128, 512], BF16) for i in range(4)]
        dtots = [(ta("dtot%d" % i, [128, 512], F32), "dtot%d" % i) for i in range(2)]
        xtoks = [(ta("xtokC%d" % i, [128, 1024], F32), "xtokC%d" % i) for i in range(3)]
        self.xti = 0
        self.XT = self.nc.alloc_sbuf_tensor_at("XT_res", [128, 8, T_X], F32, offset=self.hi - 8 * T_X * 4)
        assert self.cur <= self.hi - 8 * T_X * 4, "phase C overflow"
        for (q_, k_) in qlos + qhis:
            self.P.add("pool", lambda e, q_=q_: e.memset(q_[:], 0.0), writes=[k_])
        self.rot = [4, 5, 6, 7]
        accs = [((self.pb[0], "pb0"), (self.pb[1], "pb1")), ((self.pb[2], "pb2"), (self.pb[3], "pb3"))]
        ctxtiles = [(self.KcT[:, t * 128:(t + 1) * 128], self.Vc[:, t, :], None, ["KcT", "Vc"]) for t in range(2)]

        jobs = []

        def outproj(N, FOsrc, fokey, focol, Xdst, xkey, xcol, s_, ATT, ak):
            for oc in range(8):
                b, bk = self.bank()
                for kc in range(8):
                    rhs = FOsrc[:, kc, focol:focol + N] if kc < 4 else ATT[:, kc - 4, 0:N]
                    self.mm(b[:, 0:N], wout[:, kc, oc * 128:(oc + 1) * 128], rhs, kc == 0, kc == 7, ["wout", fokey, ak], [bk])
                xo = Xdst[:, oc, xcol:xcol + N]
                self.stt("dve", xo, b[:, 0:N], self.mod(l, 2, s_, oc), xo, ALU.mult, ALU.add, [bk, "modT", xkey], [xkey])

        gi = 0
        for qt in range(2):
            post = (lambda ai=gi % 2: outproj(NCTX, self.FOc, "FOc", 0, self.YT, "YT", 0, 1, *ATTs[ai])) if qt == 1 else None
            jobs.append((self.QcT, "QcT", qt * 128, ctxtiles, gi % 2, qt * 128, None, post))
        gi += 1
        for (t0, N) in groups(T_X):
            for tq in range(N // 128):
                qt = t0 // 128 + tq
                kts = []
                for dlt, mask in ((-1, self.mprev), (0, None), (1, self.mnext)):
                    kt = qt + dlt
                    if kt < 0:
                        continue
                    kts.append((self.KT[:, kt * 128:(kt + 1) * 128], self.V[:, kt, :], mask, ["KT", "V"]))
                pre = (lambda t0=t0, N=N: self.load_xT(self.x_d.ap()[t0:t0 + N, :], N // 128, self.XT, "XT%d" % (t0 // 512), t0, xtoks)) if tq == 0 else None
                post = (lambda t0=t0, N=N, ai=gi % 2: outproj(N, self.FO, "FO", t0, self.XT, "XT%d" % (t0 // 512), t0, 0, *ATTs[ai])) \
                    if tq == N // 128 - 1 else None
                jobs.append((self.QT, "QT", qt * 128, kts + ctxtiles, gi % 2, tq * 128, pre, post))
            gi += 1

        steps = []
        for ji, job in enumerate(jobs):
            n = len(job[3])
            for kvh in range(2):
                for i in range(n):
                    steps.append((ji, kvh, i, n))
        pti = [0]

        def s_stage(st):
            ji, kvh, i, n = st
            QTsrc, qkey, qcol, keytiles, ai, dcol, pre, post = jobs[ji]
            (qlo, qlk), (qhi, qhk) = qlos[ji % 2], qhis[ji % 2]
            (pbO, ok_), (pbD, dk_) = accs[ji % 2]
            if kvh == 0 and i == 0:
                if pre is not None:
                    pre()
                self.copy("pool", qlo[0:64, :, :], QTsrc[0:64, :, qcol:qcol + 128], [qkey], [qlk])
                self.copy("pool", qhi[64:128, :, :], QTsrc[64:128, :, qcol:qcol + 128], [qkey], [qhk])
            qp, qk_ = (qlo, qlk) if kvh == 0 else (qhi, qhk)
            kap, vap, mask, kkeys = keytiles[i]
            b, bk = self.bank()
            self.mm(b[:, 0:512], kap, qp[:].rearrange("p j n -> p (j n)"), True, mask is None, [qk_] + kkeys, [bk])
            if mask is not None:
                self.mm(b[:, 0:512], self.identb, mask.unsqueeze(1).broadcast_to([128, 4, 128]), False, True, ["c_bf"], [bk])
            pt = PT[pti[0] % 4]
            pk = "PT%d" % (pti[0] % 4)
            pti[0] += 1
            self.act(pt[:], b[:, 0:512], AF.Exp, [bk], [pk])
            return (pt, pk)

        def p_stage(st, pt, pk):
            ji, kvh, i, n = st
            QTsrc, qkey, qcol, keytiles, ai, dcol, pre, post = jobs[ji]
            (pbO, ok_), (pbD, dk_) = accs[ji % 2]
            kap, vap, mask, kkeys = keytiles[i]
            rows = slice(kvh * 64, kvh * 64 + 64)
            self.mm(pbO[rows, 0:512], vap[:, kvh * 64:kvh * 64 + 64], pt[:], i == 0, i == n - 1, [pk] + kkeys, [ok_])
            self.mm(pbD[rows, 0:512], self.onesb[:, 0:64], pt[:], i == 0, i == n - 1, [pk, "c_bf"], [dk_])
            if kvh == 1 and i == n - 1:
                dtot, dtk = dtots[ji % 2]
                ATT, ak = ATTs[ai]
                self.tt("dve", dtot[:], pbD[:, 0:512], self.sinkbc[:].rearrange("p j n -> p (j n)"), ALU.add, [dk_, "sinkbc"], [dtk])
                self.act(dtot[:], dtot[:], AF.Ln, [dtk], [dtk])
                self.act(dtot[:], dtot[:], AF.Exp, [dtk], [dtk], scale=-1.0)
                self.tt("dve", ATT[:, :, dcol:dcol + 128], pbO[:, 0:512].rearrange("p (j n) -> p j n", n=128),
                        dtot[:].rearrange("p (j n) -> p j n", n=128), ALU.mult, [ok_, dtk], [ak])
                if post is not None:
                    post()

        prev = None
        for st in steps:
            cur = s_stage(st)
            if prev is not None:
                p_stage(*prev)
            prev = (st, cur[0], cur[1])
        p_stage(*prev)
        self.rot = [2, 3, 4, 5, 6, 7]

    def ffn(self, l):
        scr = self.scr
        self.P.barrier(lambda e: e.memset(scr[0:1, 0:8], 0.0))
        self.cur = self.pers
        ta = self.talloc
        T = T_X if l == 0 else T_OWN
        toks = [(t0, N, 0) for (t0, N) in groups(T)]
        HT = T + (NCTX if l == 0 else 0)
        h2 = ta("h2", [128, 8, HT], BF16)
        keepF = self.cur
        sqbs = [ta("sqb%d" % i, [128, 8, 512], BF16) for i in range(2)]
        rss = [ta("rsn%d" % i, [128, 512], F32) for i in range(2)]
        tf = [ta("tf%d" % i, [128, 512], F32) for i in range(4)]
        tmpn = (sqbs, rss, tf)
        xkey = lambda t0: "XT%d" % (t0 // 512)
        calls = [dict(xT=self.XT, xkey=xkey(t0), col0=t0, N=N, A=(lambda c: self.A2[:, l, 0, c:c + 1]),
                      Bf=(lambda c: self.mod(l, 3, 0, c)), hT=h2, hkey="h2_%d" % (t0 // 512), hcol0=t0, tmp=tmpn) for (t0, N, _) in toks]
        if l == 0:
            calls.append(dict(xT=self.YT, xkey="YT", col0=0, N=NCTX, A=(lambda c: self.A2[:, l, 1, c:c + 1]),
                              Bf=(lambda c: self.mod(l, 3, 1, c)), hT=h2, hkey="h2_c", hcol0=T, tmp=tmpn))
            toks = toks + [(T, NCTX, 1)]
        self.norm_pipeline(calls)
        self.P.barrier(lambda e: e.memset(scr[0:1, 0:8], 0.0))
        self.cur = keepF
        splits = [(0, 4), (4, 4), (8, 4), (12, 4), (16, 3), (19, 3)]
        wgb = [ta("wg%d" % i, [128, 8, 512], BF16) for i in range(2)]
        wub = [ta("wu%d" % i, [128, 8, 512], BF16) for i in range(2)]
        wdb = [ta("wd%d" % i, [128, 4, D], BF16) for i in range(2)]
        actb = [ta("act%d" % i, [128, 4, 512], BF16) for i in range(2)]
        sg = [ta("sg%d" % i, [128, 512], F32) for i in range(3)]
        obs = [ta("ob%d" % i, [128, D], F32) for i in range(2)] if l == 1 else None
        assert self.cur <= self.hi - 8 * T_X * 4, "ffn overflow"
        ai = 0
        si = 0
        for sp_i, (j0, nj) in enumerate(splits):
            wi = sp_i % 2
            wg, wu, wd = wgb[wi], wub[wi], wdb[wi]
            kg, ku, kd = "wg%d" % wi, "wu%d" % wi, "wd%d" % wi
            self.load_w(wg[:, :, 0:nj * 128], kg, self.wg_d.ap()[l].rearrange("(c p) n -> p c n", p=128)[:, :, j0 * 128:(j0 + nj) * 128])
            self.load_w(wu[:, :, 0:nj * 128], ku, self.wu_d.ap()[l].rearrange("(c p) n -> p c n", p=128)[:, :, j0 * 128:(j0 + nj) * 128])
            self.load_w(wd[:, 0:nj, :], kd, self.wd_d.ap()[l].rearrange("(j p) n -> p j n", p=128)[:, j0:j0 + nj, :])
            for (t0, N, s) in toks:
                ab = actb[ai % 2]
                ak = "act%d" % (ai % 2)
                ai += 1
                hk = "h2_c" if s == 1 else "h2_%d" % (t0 // 512)
                for jj in range(nj):
                    bg, bgk = self.bank()
                    for c in range(8):
                        self.mm(bg[:, 0:N], wg[:, c, jj * 128:(jj + 1) * 128], h2[:, c, t0:t0 + N], c == 0, c == 7, [kg, hk], [bgk])
                    bu, buk = self.bank()
                    for c in range(8):
                        self.mm(bu[:, 0:N], wu[:, c, jj * 128:(jj + 1) * 128], h2[:, c, t0:t0 + N], c == 0, c == 7, [ku, hk], [buk])
                    s_ = sg[si % 3]
                    sk = "sg%d" % (si % 3)
                    si += 1
                    self.act(s_[:, 0:N], bg[:, 0:N], AF.Silu, [bgk], [sk])
                    self.tt("dve", ab[:, jj, 0:N], bu[:, 0:N], s_[:, 0:N], ALU.mult, [buk, sk], [ak])
                X, xk, xc = (self.XT, xkey(t0), t0) if s == 0 else (self.YT, "YT", 0)
                for oc in range(8):
                    b, bk = self.bank()
                    for jj in range(nj):
                        self.mm(b[:, 0:N], wd[:, jj, oc * 128:(oc + 1) * 128], ab[:, jj, 0:N], jj == 0, jj == nj - 1, [kd, ak], [bk])
                    xo = X[:, oc, xc:xc + N]
                    self.stt("dve", xo, b[:, 0:N], self.mod(l, 5, s, oc), xo, ALU.mult, ALU.add, [bk, "modT", xk], [xk])
                if l == 1 and sp_i == len(splits) - 1:
                    self.emit_out(range(t0 // 128, (t0 + N) // 128), obs)
        if l == 1:
            self.out_done = True

    def layer1_E(self):
        scr = self.scr
        self.P.barrier(lambda e: e.memset(scr[0:1, 0:8], 0.0))
        self.cur = self.pers
        ta = self.talloc
        l = 1
        NK = T_X + NCTX
        hT = ta("hT1", [128, 8, NK], BF16)
        Tt = ta("Tt", [128, 6, 4, 128], BF16)
        keepE = self.cur

        def load_Tt(hg_):
            for dl in range(6):
                for kr in range(2):
                    for qr in range(2):
                        dr_idx = 2 * (dl - 2) + kr - qr + 7
                        srcb = AP(self.rb_d, hg_ * 4 * 465 + dr_idx * 31 - 48, [[1, 64], [465, 4], [1, 64]])
                        self.dma("pool", Tt[qr * 64:(qr + 1) * 64, dl, :, kr * 64:(kr + 1) * 64], srcb, writes=["Tt%d" % dl])

        load_Tt(0)
        sqbs = [ta("sqb%d" % i, [128, 8, 512], BF16) for i in range(2)]
        rss = [ta("rsn%d" % i, [128, 512], F32) for i in range(2)]
        tf = [ta("tf%d" % i, [128, 512], F32) for i in range(4)]
        tmpn = (sqbs, rss, tf)
        calls = [dict(xT=self.XT, xkey="XT%d" % (t0 // 512), col0=t0, N=N, A=(lambda c: self.A1[:, l, 0, c:c + 1]),
                      Bf=(lambda c: self.mod(l, 0, 0, c)), hT=hT, hkey="hT1_%d" % (t0 // 512), hcol0=t0, tmp=tmpn) for (t0, N) in groups(T_X)]
        calls.append(dict(xT=self.YT, xkey="YT", col0=0, N=NCTX, A=(lambda c: self.A1[:, l, 1, c:c + 1]),
                          Bf=(lambda c: self.mod(l, 0, 1, c)), hT=hT, hkey="hT1_%d" % (T_X // 512), hcol0=T_X, tmp=tmpn))
        self.norm_pipeline(calls)
        self.P.barrier(lambda e: e.memset(scr[0:1, 0:8], 0.0))
        self.cur = keepE
        off_rsq = self.cur
        rsq = [ta("rsq%d" % i, [128, 512], F32) for i in range(2)]
        self.uid += 1
        paccs = [(self.nc.alloc_sbuf_tensor_at("pacc%d_%d" % (i, self.uid), [128, 512], BF16, offset=off_rsq + i * 1024), "acc%d" % i) for i in range(4)]
        acck = ["acc%d" % i for i in range(4)]
        sq1s = [ta("sq1_%d" % i, [128, 512], BF16) for i in range(1)]
        tmpq = (sq1s, rsq, None, None)
        namask = ta("namask", [128, 3, 6, 128], BF16)
        self.dma("sp", namask[:], self.cnam_d.ap(), writes=["namask"])
        wq = ta("wq", [128, 8, 256], BF16)
        wk = ta("wk", [128, 8, 256], BF16)
        wv = ta("wv", [128, 8, 256], BF16)
        wo = ta("wo", [128, 2, D], BF16)
        BMI = ta("BMI", [128, 5, 4, 128], BF16)
        bmtmp = rsq[0]
        KTh = ta("KTh", [128, 2, NK], BF16)
        Vh = ta("Vh", [128, NK // 128, 256], BF16)
        QTh = ta("QTh", [128, 2, T_OWN], BF16)
        ATTs = [(ta("ATT1_%d" % i, [128, 2, 512], BF16), "ATT1_%d" % i) for i in range(2)]
        qpads = [[(ta("qpad%d_%d" % (e, i), [128, 2, 128], BF16), "qpad%d_%d" % (e, i)) for e in range(2)] for i in range(2)]
        PT = [ta("PT1_%d" % i, [128, 512], BF16) for i in range(5)]
        dtots = [(ta("dtot1_%d" % i, [128, 512], F32), "dtot1_%d" % i) for i in range(1)]
        assert self.cur <= self.hi - 8 * T_X * 4, "layer1 overflow %d" % (self.cur - (self.hi - 8 * T_X * 4))
        for i in range(2):
            for e_ in range(2):
                q_, k_ = qpads[i][e_]
                self.P.add("pool", lambda e, q_=q_: e.memset(q_[:], 0.0), writes=[k_])
        self.rot = [4, 5, 6, 7]
        accs = [((self.pb[0], "pb0"), (self.pb[1], "pb1")), ((self.pb[2], "pb2"), (self.pb[3], "pb3"))]
        z256 = self.cbf[:, 4:6, :].rearrange("p a n -> p (a n)")
        z512 = self.cbf[:, 4:8, :].rearrange("p a n -> p (a n)")
        hkey = lambda t0: "hT1_%d" % (t0 // 512)
        pti = [0]
        tglob = [0]
        src = self.odin_d.ap().rearrange("(c p) n -> p c n", p=128)

        def load_qkv(hg_):
            self.load_w(wq[:], "wq", src[:, :, hg_ * 256:(hg_ + 1) * 256])
            self.load_w(wk[:], "wk", src[:, :, D + hg_ * 256:D + (hg_ + 1) * 256])
            self.load_w(wv[:], "wv", src[:, :, 2 * D + hg_ * 256:2 * D + (hg_ + 1) * 256])

        load_qkv(0)
        for hg in range(4):
            self.load_w(wo[:], "wo", self.odout_d.ap().rearrange("(c p) n -> p c n", p=128)[:, hg * 2:hg * 2 + 2, :])
            for dl in range(5):
                b, bk = self.bank()
                for h4 in range(4):
                    self.mm(b[:, h4 * 128:(h4 + 1) * 128], Tt[:, dl, h4, :], self.rblk, True, True, ["Tt%d" % dl, "c_bf"], [bk])
                self.tt("dve", BMI[:, dl, :, :], b[:, 0:512].rearrange("p (h n) -> p h n", n=128),
                        namask[:, 2, dl, :].unsqueeze(1).broadcast_to([128, 4, 128]), ALU.add, [bk, "namask"], ["BMI"])
            self.rot = [0, 1, 2, 3, 4, 5, 6, 7]
            self.P.add("dve", lambda e: e.memset(scr[0:1, 8:12], 0.0), reads=acck, writes=["rsq0", "rsq1"])
            for (t0, N) in groups(NK):
                jobs = []
                for ch in range(2):
                    def projk(ch=ch, t0=t0, N=N):
                        b, bk = self.bank()
                        for c in range(8):
                            self.mm(b[:, 0:N], wk[:, c, ch * 128:(ch + 1) * 128], hT[:, c, t0:t0 + N], c == 0, c == 7, ["wk", hkey(t0)], [bk])
                        return b, bk
                    jobs.append(dict(proj=projk, N=N, gcol=self.gq[:, 3:4], out=KTh[:, ch, t0:t0 + N], okey="KTh", rope=None))
                    if t0 + N <= T_OWN:
                        def projq(ch=ch, t0=t0, N=N):
                            b, bk = self.bank()
                            for c in range(8):
                                self.mm(b[:, 0:N], wq[:, c, ch * 128:(ch + 1) * 128], hT[:, c, t0:t0 + N], c == 0, c == 7, ["wq", hkey(t0)], [bk])
                            return b, bk
                        jobs.append(dict(proj=projq, N=N, gcol=self.gq[:, 2:3], out=QTh[:, ch, t0:t0 + N], okey="QTh", rope=None))
                self.qk_pipeline(jobs, tmpq)
                for t in range(N // 128):
                    b, bk = self.bank()
                    for c in range(8):
                        self.mm(b[:, 0:256], hT[:, c, t0 + t * 128:t0 + (t + 1) * 128], wv[:, c, :], c == 0, c == 7, ["wv", hkey(t0)], [bk])
                    self.evac(Vh[:, t0 // 128 + t, :], b[:, 0:256], [bk], ["Vh"])
            self.rot = [4, 5, 6, 7]
            self.P.add("dve", lambda e: e.memset(scr[0:1, 12:16], 0.0), reads=["rsq0", "rsq1"], writes=acck)
            if hg + 1 < 4:
                load_qkv(hg + 1)
            steps = []
            for qt in range(T_OWN // 128):
                kts = [(qt + d_, d_ + 2) for d_ in range(-2, 4 if qt == 0 else 3) if qt + d_ >= 0] + \
                      [(T_X // 128, None), (T_X // 128 + 1, None)]
                for i, (kt, dl) in enumerate(kts):
                    steps.append((qt, i, len(kts), kt, dl))

            def s_stage(st):
                qt, i, n, kt, dl = st
                tq_ = tglob[0] + qt
                qp = qpads[tq_ % 2]
                (pbO, ok_), (pbD, dk_) = accs[tq_ % 2]
                if i == 0:
                    for e_ in range(2):
                        rows = slice(e_ * 64, e_ * 64 + 64)
                        self.copy("pool", qp[e_][0][rows, :, :], QTh[rows, :, qt * 128:(qt + 1) * 128], ["QTh"], [qp[e_][1]])
                    self.mm(pbO[:, 0:256], self.zerob, z256, True, False, ["c_bf"], [ok_])
                b, bk = self.bank()
                first = True
                if dl is not None:
                    if qt >= 2:
                        self.mm(b[:, 0:512], self.identb, BMI[:, dl, :, :].rearrange("p h n -> p (h n)"), True, False, ["BMI", "c_bf"], [bk])
                        first = False
                    else:
                        self.mm(b[:, 0:512], self.identb, namask[:, qt, dl, :].unsqueeze(1).broadcast_to([128, 4, 128]), True, False,
                                ["namask", "c_bf"], [bk])
                        for h4 in range(4):
                            self.mm(b[:, h4 * 128:(h4 + 1) * 128], Tt[:, dl, h4, :], self.rblk, False, False, ["Tt%d" % dl, "c_bf"], [bk])
                        first = False
                for h4 in range(4):
                    ch, e_ = h4 // 2, h4 % 2
                    self.mm(b[:, h4 * 128:(h4 + 1) * 128], KTh[:, ch, kt * 128:(kt + 1) * 128], qp[e_][0][:, ch, :], first, True,
                            ["KTh", qp[e_][1]], [bk])
                pt = PT[pti[0] % 5]
                pk = "PT1_%d" % (pti[0] % 5)
                pti[0] += 1
                self.act(pt[:], b[:, 0:512], AF.Exp, [bk], [pk])
                acc, ak_ = paccs[(tq_ % 2) * 2 + (i % 2)]
                if i < 2:
                    self.copy("dve", acc[:], pt[:], [pk], [ak_])
                else:
                    self.tt("dve", acc[:], acc[:], pt[:], ALU.add, [ak_, pk], [ak_])
                return (pt, pk)

            def p_stage(st, pt, pk):
                qt, i, n, kt, dl = st
                tq_ = tglob[0] + qt
                (pbO, ok_), (pbD, dk_) = accs[tq_ % 2]
                for h4 in range(4):
                    ch, e_ = h4 // 2, h4 % 2
                    rows = slice(e_ * 64, e_ * 64 + 64)
                    self.mm(pbO[rows, ch * 128:(ch + 1) * 128], Vh[:, kt, h4 * 64:(h4 + 1) * 64], pt[:, h4 * 128:(h4 + 1) * 128],
                            False, True, [pk, "Vh"], [ok_])
                if i == n - 1:
                    accA, akA = paccs[(tq_ % 2) * 2]
                    accB, akB = paccs[(tq_ % 2) * 2 + 1]
                    self.mm(pbD[:, 0:512], self.onesb, accA[:], True, False, [akA, "c_bf"], [dk_])
                    self.mm(pbD[:, 0:512], self.onesb, accB[:], False, True, [akB, "c_bf"], [dk_])
                    dtot, dtk = dtots[0]
                    ATT, ak = ATTs[(qt // 4) % 2]
                    tq = qt % 4
                    self.act(dtot[:], pbD[:, 0:512], AF.Ln, [dk_], [dtk])
                    self.act(dtot[:], dtot[:], AF.Exp, [dtk], [dtk], scale=-1.0)
                    dv = dtot[:].rearrange("p (c e n) -> p c e n", e=2, n=128)
                    for e_ in range(2):
                        rows = slice(e_ * 64, e_ * 64 + 64)
                        self.tt("dve", ATT[rows, :, tq * 128:(tq + 1) * 128], pbO[rows, 0:256].rearrange("p (j n) -> p j n", n=128),
                                dv[rows, :, e_, :], ALU.mult, [ok_, dtk], [ak])
                    if tq == 3:
                        t0 = (qt // 4) * 512
                        for oc in range(8):
                            b, bk = self.bank()
                            for ch in range(2):
                                self.mm(b[:, 0:512], wo[:, ch, oc * 128:(oc + 1) * 128], ATT[:, ch, 0:512], ch == 0, ch == 1, ["wo", ak], [bk])
                            xo = self.XT[:, oc, t0:t0 + 512]
                            xk = "XT%d" % (t0 // 512)
                            self.stt("dve", xo, b[:, 0:512], self.mod(l, 2, 0, oc), xo, ALU.mult, ALU.add, [bk, "modT", xk], [xk])

            if "noattn" in self.flags:
                steps = []
            if "few" in self.flags:
                steps = steps[:self.nstep]
            pend = []
            for si_, st in enumerate(steps):
                cur = s_stage(st)
                if st[0] == 1 and si_ + 1 < len(steps) and steps[si_ + 1][0] == 2 and hg + 1 < 4:
                    load_Tt(hg + 1)
                pend.append((st, cur[0], cur[1]))
                if len(pend) > 2:
                    p_stage(*pend.pop(0))
            for pp in pend:
                p_stage(*pp)
            tglob[0] += T_OWN // 128
            if "onehg" in self.flags:
                break
        self.rot = [2, 3, 4, 5, 6, 7]

    def finish(self, final=True):
        scr = self.scr
        self.P.barrier(lambda e: e.memset(scr[0:1, 0:8], 0.0))
        self.cur = self.pers
        ta = self.talloc
        if self.debug and hasattr(self, "XT"):
            self.dma("sp", self.dbgx_d.ap(), self.XT[:], reads=["XT%d" % i for i in range(5)])
            self.dma("sp", self.dbgy_d.ap(), self.YT[:], reads=["YT"])
        if not hasattr(self, "XT"):
            return
        if getattr(self, "out_done", False):
            return
        ob = [ta("ob%d" % i, [128, D], F32) for i in range(2)]
        self.emit_out(range(T_OWN // 128), ob)

    def emit_out(self, tiles, ob):
        for t in tiles:
            o_, ok = ob[t % 2], "ob%d" % (t % 2)
            for half in range(2):
                b, bk = self.bank()
                for cc in range(4):
                    c = half * 4 + cc
                    self.tr(b[:, cc * 128:(cc + 1) * 128], self.XT[:, c, t * 128:(t + 1) * 128], ["XT%d" % (t // 4)], [bk])
                self.evac(o_[:, half * 512:(half + 1) * 512], b[:, 0:512], [bk], [ok])
            self.dma("sp", self.out_d.ap()[t * 128:(t + 1) * 128, :], o_[:], reads=[ok])


_CONST_CACHE = {}


def _bf(a):
    return np.ascontiguousarray(a.astype(ml_dtypes.bfloat16))


def host_consts(par):
    if par in _CONST_CACHE:
        return _CONST_CACHE[par]
    c = {}
    c["c_identf"] = np.eye(128, dtype=np.float32)
    psw = np.zeros((128, 128), np.float32)
    for m in range(128):
        i = m % 32
        partner = m + 16 if i < 16 else m - 16
        psw[partner, m] = 1.0
    c["c_pswap"] = psw
    cb = np.zeros((128, 10, 128), np.float32)
    cb[:, 0, :] = 1.0 / 1024.0
    cb[0:64, 1, 0:64] = 1.0 / 64.0
    cb[64:128, 1, 64:128] = 1.0 / 64.0
    cb[:, 2, :] = np.eye(128)
    for blk in range(2):
        for u in range(64):
            cb[blk * 64 + u, 3, blk * 64 + 63 - u] = 1.0
    cb[:, 5, :] = 1.0
    cc = np.arange(128)
    ang = 2.0 * np.pi * ((cc[:, None] * cc[None, :]) % 128) / 128.0
    cb[:, 6, :] = np.cos(ang) / np.sqrt(128.0)
    cb[:, 7, :] = -np.sin(ang) / np.sqrt(128.0)
    j = np.arange(128)[:, None]
    q = np.arange(128)[None, :]
    cb[:, 8, :] = np.where(j >= q, 0.0, NEG)
    cb[:, 9, :] = np.where(j <= q, 0.0, NEG)
    c["c_bf"] = _bf(cb)
    n = np.arange(256)
    a2 = 2.0 * np.pi * ((n[:, None] * n[None, :]) % 256) / 256.0
    C2 = (np.cos(a2) / 16.0).reshape(2, 128, 256)
    S2 = (np.sin(a2) / 16.0).reshape(2, 128, 256)
    cx = np.stack([C2.transpose(1, 0, 2), S2.transpose(1, 0, 2)], axis=1)
    c["c_ctxdft"] = _bf(cx)
    loc = np.arange(T_ALL)
    glob = loc if par == 0 else (T_ALL - 1 - loc)
    inv = (np.float32(10000.0) ** (-np.arange(16, dtype=np.float32) / np.float32(16))).astype(np.float32)
    g_ = glob[:T_A]
    row = (g_ // 64).astype(np.float32)
    col = (g_ % 64).astype(np.float32)
    ar = (row[None, :] * inv[:, None]).astype(np.float32)
    ac = (col[None, :] * inv[:, None]).astype(np.float32)
    cos64 = np.concatenate([np.cos(ar), np.cos(ar), np.cos(ac), np.cos(ac)], axis=0)
    sin64 = np.concatenate([-np.sin(ar), np.sin(ar), -np.sin(ac), np.sin(ac)], axis=0)
    c["c_rope"] = np.ascontiguousarray(np.stack([np.concatenate([cos64, cos64], 0), np.concatenate([sin64, sin64], 0)], 0).astype(np.float32))
    gk = glob[:T_X].astype(np.int64)
    gn = glob.astype(np.int64)
    ph = (gn[:, None] * gk[None, :]) % T_ALL
    angL = (2.0 * np.pi / T_ALL) * ph
    c["c_dftc"] = _bf(np.cos(angL) / 64.0)
    c["c_dfts"] = _bf(np.sin(angL) / 64.0)
    nm = np.zeros((128, 3, 6, 128), np.float32)
    for cls, qt in enumerate((0, 1, 8)):
        for dl in range(6):
            kt = qt + dl - 2
            if kt < 0:
                nm[:, cls, dl, :] = NEG
                continue
            kg = glob[kt * 128 + np.arange(128)]
            qg = glob[qt * 128 + np.arange(128)]
            kr, kc = kg // 64, kg % 64
            qr, qc = qg // 64, qg % 64
            r0 = np.clip(qr - 4, 0, 56)
            c0 = np.clip(qc - 8, 0, 48)
            ok = (kr[:, None] >= r0[None, :]) & (kr[:, None] < r0[None, :] + 8) & (kc[:, None] >= c0[None, :]) & (kc[:, None] < c0[None, :] + 16)
            nm[:, cls, dl, :] = np.where(ok, 0.0, NEG)
    c["c_namask"] = _bf(nm)
    _CONST_CACHE[par] = c
    return c


_NC_CACHE = {}


def get_nc(stop_after=None, debug=False, only=None, flags=()):
    key = (stop_after, debug, only, tuple(flags))
    if key not in _NC_CACHE:
        bld = Builder(stop_after=stop_after, debug=debug, only=only, flags=flags)
        _NC_CACHE[key] = bld.build()
    return _NC_CACHE[key]


def make_in_maps(inputs):
    f32 = lambda a: np.ascontiguousarray(np.asarray(a, dtype=np.float32))
    x = f32(inputs["x"])
    c = f32(inputs["c"])
    ctx = f32(inputs["ctx"])
    c_ctx = f32(inputs["c_ctx"])
    ev_in = f32(inputs["ev_w_in"])[0]
    ev_out = f32(inputs["ev_w_out"])[0]
    hp = [0, 4, 1, 5, 2, 6, 3, 7]
    qcols = np.concatenate([512 + h * 64 + np.arange(64) for h in hp])
    cols = np.concatenate([np.arange(512), qcols, np.arange(1024, 1280)])
    ev_in_p = np.ascontiguousarray(ev_in[:, cols])
    rows = np.concatenate([np.arange(512), qcols])
    ev_out_p = np.ascontiguousarray(ev_out[rows, :])
    rb = f32(inputs["od_rel_bias"])[0]
    shared = {
        "ada_w": f32(inputs["ada_w"]), "ada_b": f32(inputs["ada_b"]),
        "norm1_g": f32(inputs["norm1_g"]), "norm2_g": f32(inputs["norm2_g"]),
        "ffn_w_gate": f32(inputs["ffn_w_gate"]), "ffn_w_up": f32(inputs["ffn_w_up"]), "ffn_w_down": f32(inputs["ffn_w_down"]),
        "ev_w_in": ev_in_p, "ev_w_out": ev_out_p,
        "ev_q_norm": f32(inputs["ev_q_norm"])[0], "ev_k_norm": f32(inputs["ev_k_norm"])[0], "ev_sink": f32(inputs["ev_sink"])[0],
        "od_w_in": f32(inputs["od_w_in"])[0], "od_w_out": f32(inputs["od_w_out"])[0],
        "od_q_norm": f32(inputs["od_q_norm"])[0], "od_k_norm": f32(inputs["od_k_norm"])[0],
    }
    pad = np.zeros(64, np.float32)
    rbs = [np.concatenate([rb.reshape(-1), pad]), np.concatenate([rb[:, ::-1, ::-1].reshape(-1), pad])]
    in_maps = []
    for cid in range(8):
        b, par = cid // 2, cid % 2
        m = dict(shared)
        m["x_loc"] = np.ascontiguousarray(x[b] if par == 0 else x[b][::-1])
        m["ctx_b"] = np.ascontiguousarray(ctx[b])
        m["cvec"] = np.ascontiguousarray(np.stack([c[b], c_ctx], 0))
        m["rel_bias"] = rbs[par]
        m.update(host_consts(par))
        in_maps.append(m)
    return in_maps


def assemble(results):
    out = np.zeros((4, T_ALL, D), np.float32)
    for cid in range(8):
        b, par = cid // 2, cid % 2
        o = np.asarray(results[cid]["out_loc"], dtype=np.float32)
        if par == 0:
            out[b, :T_OWN] = o
        else:
            out[b, T_OWN:] = o[::-1]
    return out


def kernel(**inputs):
    nc = get_nc()
    in_maps = make_in_maps(inputs)
    res = run_bass_kernel_spmd(nc, in_maps, core_ids=list(range(8)))
    return assemble(res.results)
```

```python
import numpy as np
import ml_dtypes
import concourse.bass as bass
import concourse.mybir as mybir
from concourse.bass_utils import run_bass_kernel_spmd
from concourse.ap import AP

F32 = mybir.dt.float32
BF16 = mybir.dt.bfloat16
ALU = mybir.AluOpType
AF = mybir.ActivationFunctionType

NDMA_SEM = 48
D = 1024
DFF = 2816
NEG = -30000.0
EPS = 1e-6
T_ALL = 4096
T_X = 2304
T_A = 2560
T_OWN = 2048
NCTX = 256


class _Op:
    __slots__ = ("id", "eng", "fn", "deps", "is_dma", "sem_i", "sem_val", "signal", "count")


class Prog:
    ENGS = ("pe", "act", "dve", "pool", "sp")

    def __init__(self):
        self.ops = []
        self.lw = {}
        self.rd = {}
        self.dma_rr = 0
        self.dma_sem_uses = [0] * NDMA_SEM
        self.dma_sem_last = [None] * NDMA_SEM
        self.last_eng = {}
        self.bar = None
        self.dopen = set()
        self.saved = {}
        self.strict = False

    def add(self, eng, fn, reads=(), writes=(), dma=False):
        op = _Op()
        op.id = len(self.ops)
        op.eng = eng
        op.fn = fn
        op.is_dma = dma
        op.signal = False
        op.count = 0
        deps = {}
        if self.bar is not None:
            deps[self.bar] = True
        for k in reads:
            for w in self.lw.get(k, ()):
                deps[w] = True
            self.dopen.discard(k)
        cow = set()
        for k in writes:
            if dma and k in self.dopen:
                cow.add(k)
                for r in self.saved.get(k, ()):
                    deps.setdefault(r, False)
                continue
            for w in self.lw.get(k, ()):
                deps.setdefault(w, False)
            for r in self.rd.get(k, ()):
                deps.setdefault(r, False)
        if dma:
            i = self.dma_rr % NDMA_SEM
            self.dma_rr += 1
            prev = self.dma_sem_last[i]
            if prev is not None:
                deps.setdefault(prev, False)
            self.dma_sem_uses[i] += 1
            op.sem_i = i
            op.sem_val = 16 * self.dma_sem_uses[i]
            self.dma_sem_last[i] = op.id
        op.deps = deps
        for k in reads:
            self.rd.setdefault(k, []).append(op.id)
        for k in writes:
            if k in cow:
                self.lw[k].append(op.id)
                continue
            self.saved[k] = list(self.lw.get(k, ())) + list(self.rd.get(k, ()))
            self.lw[k] = [op.id]
            self.rd[k] = []
            if dma:
                self.dopen.add(k)
            else:
                self.dopen.discard(k)
        self.ops.append(op)
        if not dma:
            self.last_eng[eng] = op.id
        return op

    def barrier(self, fn):
        op = self.add("pool", fn)
        for e, i in self.last_eng.items():
            if i != op.id:
                op.deps[i] = True
        for i in self.dma_sem_last:
            if i is not None:
                op.deps[i] = True
        self.bar = op.id
        self.lw = {}
        self.rd = {}
        self.dopen = set()
        self.saved = {}

    def emit(self, block, eng_sems, dma_sems):
        ops = self.ops

        strict = self.strict

        def skip(op, dop, raw):
            if op.is_dma or dop.is_dma or dop.eng != op.eng:
                return False
            return op.eng == "pe" or (not raw and not strict)

        for op in ops:
            for d, raw in op.deps.items():
                dop = ops[d]
                if dop.is_dma or skip(op, dop, raw):
                    continue
                dop.signal = True
        cnt = {e: 0 for e in self.ENGS}
        for op in ops:
            if op.signal and not op.is_dma:
                cnt[op.eng] += 1
                op.count = cnt[op.eng]
        per_eng = {e: [o for o in ops if o.eng == e] for e in self.ENGS}

        def run(e, engine):
            known = {}
            for op in per_eng[e]:
                needs = {}
                for d, raw in op.deps.items():
                    dop = ops[d]
                    if dop.is_dma:
                        key = ("d", dop.sem_i)
                        val = dop.sem_val
                    else:
                        if skip(op, dop, raw):
                            continue
                        key = ("e", dop.eng)
                        val = dop.count
                    if needs.get(key, 0) < val:
                        needs[key] = val
                for key, val in needs.items():
                    if known.get(key, 0) >= val:
                        continue
                    known[key] = val
                    sem = dma_sems[key[1]] if key[0] == "d" else eng_sems[key[1]]
                    engine.wait_ge(sem, val)
                ins = op.fn(engine)
                if op.is_dma:
                    ins.then_inc(dma_sems[op.sem_i], 16)
                elif op.signal:
                    ins.then_inc(eng_sems[e], 1)
            if e == "sp":
                for i in range(NDMA_SEM):
                    if self.dma_sem_uses[i] > 0:
                        engine.wait_ge(dma_sems[i], 16 * self.dma_sem_uses[i])

        block.tensor(lambda eng: run("pe", eng))
        block.scalar(lambda eng: run("act", eng))
        block.vector(lambda eng: run("dve", eng))
        block.gpsimd(lambda eng: run("pool", eng))
        block.sync(lambda eng: run("sp", eng))


def groups(n, g=512):
    out = []
    s = 0
    while s < n:
        out.append((s, min(g, n - s)))
        s += g
    return out


class Builder:
    def __init__(self, stop_after=None, debug=False, only=None, flags=()):
        self.only = only
        self.flags = set(f for f in flags if not f.startswith('n='))
        self.nstep = ([int(f[2:]) for f in flags if f.startswith('n=')] + [8])[0]
        self.nc = bass.Bass("TRN2", target_bir_lowering=False)
        self.P = Prog()
        self.stop_after = stop_after
        self.debug = debug
        nc = self.nc
        self.lo = ((nc.sbuf_base + 63) // 64) * 64
        self.hi = nc.sbuf_top
        self.pers = self.lo
        self.cur = None
        self.uid = 0
        self.rr = 0
        self.evr = 0
        self.rot = [2, 3, 4, 5, 6]
        self.nmi = 0
        self.tfi = 0
        self.qni = 0

    def _alloc(self, ptr, name, shape, dt):
        esz = 4 if dt == F32 else 2
        n = 1
        for s in shape[1:]:
            n *= s
        nbytes = ((n * esz + 63) // 64) * 64
        self.uid += 1
        t = self.nc.alloc_sbuf_tensor_at("%s_%d" % (name, self.uid), list(shape), dt, offset=ptr)
        return t, ptr + nbytes

    def palloc(self, name, shape, dt):
        assert self.cur is None
        t, self.pers = self._alloc(self.pers, name, shape, dt)
        assert self.pers <= self.hi, "persistent overflow"
        return t

    def talloc(self, name, shape, dt):
        t, self.cur = self._alloc(self.cur, name, shape, dt)
        assert self.cur <= self.hi, "phase overflow %s %d" % (name, self.cur - self.hi)
        return t

    def phase(self):
        scr = self.scr
        self.P.barrier(lambda e: e.memset(scr[0:1, 0:8], 0.0))
        self.cur = self.pers

    def dma(self, eng, out, in_, reads=(), writes=()):
        self.P.add(eng, lambda e: e.dma_start(out=out, in_=in_), reads=reads, writes=writes, dma=True)

    def mm(self, out, lhsT, rhs, start, stop, reads, writes):
        self.P.add("pe", lambda e: e.matmul(out, lhsT=lhsT, rhs=rhs, start=start, stop=stop), reads=reads, writes=writes)

    def tr(self, out, in_, reads, writes):
        ident = self.identf
        k = in_.shape[0]
        self.P.add("pe", lambda e: e.transpose(out, in_, ident[0:k, 0:k]), reads=list(reads) + ["c_id"], writes=writes)

    def act(self, out, in_, func, reads, writes, bias=None, scale=None):
        kw = {}
        if bias is not None:
            kw["bias"] = bias
        if scale is not None:
            kw["scale"] = scale
        self.P.add("act", lambda e: e.activation(out=out, in_=in_, func=func, **kw), reads=reads, writes=writes)

    def tt(self, eng, out, in0, in1, op, reads, writes):
        self.P.add(eng, lambda e: e.tensor_tensor(out=out, in0=in0, in1=in1, op=op), reads=reads, writes=writes)

    def stt(self, eng, out, in0, scalar, in1, op0, op1, reads, writes):
        self.P.add(eng, lambda e: e.scalar_tensor_tensor(out=out, in0=in0, scalar=scalar, in1=in1, op0=op0, op1=op1),
                   reads=reads, writes=writes)

    def ts(self, eng, out, in0, s1, s2, op0, op1, reads, writes):
        if s2 is None:
            self.P.add(eng, lambda e: e.tensor_scalar(out=out, in0=in0, scalar1=s1, scalar2=None, op0=op0), reads=reads, writes=writes)
        else:
            self.P.add(eng, lambda e: e.tensor_scalar(out=out, in0=in0, scalar1=s1, scalar2=s2, op0=op0, op1=op1), reads=reads, writes=writes)

    def copy(self, eng, out, in_, reads, writes):
        if eng == "act":
            self.act(out, in_, AF.Copy, reads, writes)
        else:
            self.P.add(eng, lambda e: e.tensor_copy(out=out, in_=in_), reads=reads, writes=writes)

    def evac(self, out, in_, reads, writes):
        self.evr += 1
        self.copy("act" if self.evr % 2 else "dve", out, in_, reads, writes)

    def dump(self, name, t, key):
        if not self.debug:
            return
        shape = list(t.shape)
        d = self.nc.dram_tensor("dbg_" + name, shape, t.dtype, kind="ExternalOutput")
        self.dma("sp", d.ap(), t[:], reads=[key])

    def bank(self):
        self.rr += 1
        i = self.rot[self.rr % len(self.rot)]
        return self.pb[i], "pb%d" % i

    def build(self):
        nc = self.nc
        P = self.P
        dr = lambda name, shape, dt, kind="ExternalInput": nc.dram_tensor(name, list(shape), dt, kind=kind)
        self.x_d = dr("x_loc", [T_ALL, D], F32)
        self.ctx_d = dr("ctx_b", [NCTX, D], F32)
        self.cvec_d = dr("cvec", [2, D], F32)
        self.ada_w_d = dr("ada_w", [2, D, 6 * D], F32)
        self.ada_b_d = dr("ada_b", [2, 6 * D], F32)
        self.n1_d = dr("norm1_g", [2, D], F32)
        self.n2_d = dr("norm2_g", [2, D], F32)
        self.wg_d = dr("ffn_w_gate", [2, D, DFF], F32)
        self.wu_d = dr("ffn_w_up", [2, D, DFF], F32)
        self.wd_d = dr("ffn_w_down", [2, DFF, D], F32)
        self.evin_d = dr("ev_w_in", [D, 1280], F32)
        self.evout_d = dr("ev_w_out", [D, D], F32)
        self.evq_d = dr("ev_q_norm", [64], F32)
        self.evk_d = dr("ev_k_norm", [64], F32)
        self.sink_d = dr("ev_sink", [8], F32)
        self.odin_d = dr("od_w_in", [D, 3 * D], F32)
        self.odout_d = dr("od_w_out", [D, D], F32)
        self.odq_d = dr("od_q_norm", [64], F32)
        self.odk_d = dr("od_k_norm", [64], F32)
        self.rb_d = dr("rel_bias", [16 * 15 * 31 + 64], F32)
        self.cidf_d = dr("c_identf", [128, 128], F32)
        self.cpsw_d = dr("c_pswap", [128, 128], F32)
        self.cbf_d = dr("c_bf", [128, 10, 128], BF16)
        self.cctx_d = dr("c_ctxdft", [128, 2, 2, 256], BF16)
        self.cnam_d = dr("c_namask", [128, 3, 6, 128], BF16)
        self.crope_d = dr("c_rope", [2, 128, T_A], F32)
        self.cdft_d = [dr("c_dftc", [T_ALL, T_X], BF16), dr("c_dfts", [T_ALL, T_X], BF16)]
        self.out_d = dr("out_loc", [T_OWN, D], F32, kind="ExternalOutput")
        if self.debug:
            self.dbgx_d = dr("dbg_x", [128, 8, T_X], F32, kind="ExternalOutput")
            self.dbgy_d = dr("dbg_y", [128, 8, NCTX], F32, kind="ExternalOutput")
            self.dbga_d = dr("dbg_a", [128, 8, T_X], F32, kind="ExternalOutput")

        self.pb = [nc.alloc_psum_tensor("pb%d" % i, [128, 512], F32) for i in range(8)]
        self.eng_sems = {e: nc.alloc_semaphore("s_" + e) for e in Prog.ENGS}
        self.dma_sems = [nc.alloc_semaphore("d%d" % i) for i in range(NDMA_SEM)]

        self.setup()
        done = False
        for name, fn in [("A", self.layer0_A), ("B", self.layer0_B), ("C", self.layer0_C), ("D", lambda: self.ffn(0)),
                         ("E", self.layer1_E), ("F", lambda: self.ffn(1))]:
            if self.only is not None and name not in self.only:
                continue
            if not hasattr(self, "XT") and name in ("D", "E", "F"):
                self.ada_tick(100)
                self.rot = [2, 3, 4, 5, 6, 7]
                self.pers -= 2 * 4096
                self.XT = self.nc.alloc_sbuf_tensor_at("XT_res", [128, 8, T_X], F32, offset=self.hi - 8 * T_X * 4)
            fn()
            if self.stop_after == name:
                done = True
                break
        self.finish(final=not done)
        with nc.Block() as block:
            P.emit(block, self.eng_sems, self.dma_sems)
        return nc

    def setup(self):
        nc = self.nc
        pa = self.palloc
        self.scr = pa("scr", [128, 16], F32)
        self.identf = pa("identf", [128, 128], F32)
        self.pswap = pa("pswap", [128, 128], F32)
        self.cbf = pa("cbf", [128, 10, 128], BF16)
        self.cctx = pa("cctx", [128, 2, 2, 256], BF16)
        self.vecT = pa("vecT", [128, 48], F32)
        self.biasT = pa("biasT", [128, 96], F32)
        self.modT = pa("modT", [128, 2, 48, 2], F32)
        self.A1 = pa("A1", [128, 2, 2, 8], F32)
        self.A2 = pa("A2", [128, 2, 2, 8], F32)
        self.sT = pa("sT", [128, 2, 8], BF16)
        self.gq = pa("gq", [128, 4], F32)
        self.sinkbc = pa("sinkbc", [128, 4, 128], F32)
        self.skt = pa("skt", [128, 4], F32)
        self.zf = pa("zf", [128, 128], F32)
        self.YT = pa("YT", [128, 8, NCTX], F32)
        self.dma("sp", self.identf[:], self.cidf_d.ap(), writes=["c_id"])
        self.dma("sp", self.pswap[:], self.cpsw_d.ap(), writes=["c_ps"])
        self.dma("sp", self.cbf[:], self.cbf_d.ap(), writes=["c_bf"])
        self.dma("sp", self.cctx[:], self.cctx_d.ap(), writes=["c_cx"])
        self.P.add("pool", lambda e: e.memset(self.zf[:], 0.0), writes=["zf"])
        self.P.add("pool", lambda e: e.memset(self.scr[:], 0.0), writes=["scr"])
        self.ones1024 = self.cbf[:, 0, :]
        self.blk64 = self.cbf[:, 1, :]
        self.identb = self.cbf[:, 2, :]
        self.rblk = self.cbf[:, 3, :]
        self.zerob = self.cbf[:, 4, :]
        self.onesb = self.cbf[:, 5, :]
        self.ccos = self.cbf[:, 6, :]
        self.cnsin = self.cbf[:, 7, :]
        self.mprev = self.cbf[:, 8, :]
        self.mnext = self.cbf[:, 9, :]
        self.adabuf = None
        self.cur = self.pers + 2 * 4096
        ta = self.talloc
        rows1 = ta("rows1", [48, 128], F32)
        rows2 = ta("rows2", [96, 128], F32)
        self.dma("sp", rows1[0:16, :], self.n1_d.ap().rearrange("l (c p) -> (l c) p", p=128), writes=["rows1"])
        self.dma("sp", rows1[16:32, :], self.n2_d.ap().rearrange("l (c p) -> (l c) p", p=128), writes=["rows1"])
        self.dma("sp", rows1[32:48, :], self.cvec_d.ap().rearrange("s (c p) -> (s c) p", p=128), writes=["rows1"])
        self.dma("sp", rows2[:, :], self.ada_b_d.ap().rearrange("l (c p) -> (l c) p", p=128), writes=["rows2"])
        b, bk = self.bank()
        self.tr(b[:, 0:48], rows1[:, :], ["rows1"], [bk])
        self.copy("dve", self.vecT[:], b[:, 0:48], [bk], ["vecT"])
        b, bk = self.bank()
        self.tr(b[:, 0:96], rows2[:, :], ["rows2"], [bk])
        self.copy("dve", self.biasT[:], b[:, 0:96], [bk], ["biasT"])
        self.act(self.sT[:].rearrange("p s c -> p (s c)"), self.vecT[:, 32:48], AF.Silu, ["vecT"], ["sT"])
        for i, (d_, sc) in enumerate([(self.evq_d, 0.125), (self.evk_d, 1.0), (self.odq_d, 0.125), (self.odk_d, 1.0)]):
            src = d_.ap().rearrange("(d u) -> d u", u=1)
            self.dma("sp", self.gq[0:64, i:i + 1], src, writes=["gq"])
            self.dma("sp", self.gq[64:128, i:i + 1], src, writes=["gq"])
        self.ts("dve", self.gq[:, 0:1], self.gq[:, 0:1], 0.125, None, ALU.mult, None, ["gq"], ["gq"])
        self.ts("dve", self.gq[:, 2:3], self.gq[:, 2:3], 0.125, None, ALU.mult, None, ["gq"], ["gq"])
        self.dma("sp", self.skt[0:64, :], AP(self.sink_d, 0, [[0, 64], [1, 4]]), writes=["skt"])
        self.dma("sp", self.skt[64:128, :], AP(self.sink_d, 4, [[0, 64], [1, 4]]), writes=["skt"])
        self.act(self.skt[:], self.skt[:], AF.Exp, ["skt"], ["skt"])
        for j in range(4):
            self.ts("dve", self.sinkbc[:, j, :], self.zf[:], self.skt[:, j:j + 1], None, ALU.add, None, ["zf", "skt"], ["sinkbc"])
        self.adabuf = [self.palloc_late("adabuf%d" % i, [128, 8, 256], BF16) for i in range(2)]
        self.ada_list = [(l, sl) for l in range(2) for sl in range(24)]
        self.ada_pos = 0
        self.ada_issued = 0
        self.win = self.nc.alloc_sbuf_tensor_at("win_top", [128, 8, 1280], BF16, offset=self.hi - 20480)
        self.ada_issue()
        self.ada_tick(4)
        self.load_w(self.win[:], "win", self.evin_d.ap().rearrange("(c p) n -> p c n", p=128))
        self.ada_tick(4)

    def palloc_late(self, name, shape, dt):
        t, self.pers = self._alloc(self.pers, name, shape, dt)
        return t

    def ada_issue(self):
        if self.ada_issued >= len(self.ada_list):
            return
        l, sl = self.ada_list[self.ada_issued]
        i = self.ada_issued % 2
        src = self.ada_w_d.ap()[l].rearrange("(c p) n -> p c n", p=128)[:, :, sl * 256:(sl + 1) * 256]
        self.dma("pool", self.adabuf[i][:], src, writes=["adabuf%d" % i])
        self.ada_issued += 1

    def ada_tick(self, n=1):
        for _ in range(n):
            if self.ada_pos >= len(self.ada_list):
                return
            l, sl = self.ada_list[self.ada_pos]
            i = self.ada_pos % 2
            self.ada_pos += 1
            self.ada_issue()
            buf, bk = self.adabuf[i], "adabuf%d" % i
            pm, pmk = self.pb[7], "pb7"
            for o2 in range(2):
                oc = sl * 2 + o2
                for kc in range(8):
                    self.mm(pm[:, oc * 2:oc * 2 + 2], buf[:, kc, o2 * 128:(o2 + 1) * 128], self.sT[:, :, kc],
                            kc == 0, kc == 7, [bk, "sT"], [pmk])
            if sl == 7 or sl == 23:
                lo, hi = (0, 16) if sl == 7 else (16, 48)
                for s_ in range(2):
                    self.tt("dve", self.modT[:, l, lo:hi, s_], pm[:, 0:96].rearrange("p (o s) -> p o s", s=2)[:, lo:hi, s_],
                            self.biasT[:, l * 48 + lo:l * 48 + hi], ALU.add, [pmk, "biasT"], ["modT"])
                    if sl == 7:
                        self.stt("dve", self.A1[:, l, s_, :], self.modT[:, l, 8:16, s_], 1.0, self.vecT[:, l * 8:l * 8 + 8],
                                 ALU.add, ALU.mult, ["modT", "vecT"], ["A1"])
                    else:
                        self.stt("dve", self.A2[:, l, s_, :], self.modT[:, l, 32:40, s_], 1.0, self.vecT[:, 16 + l * 8:16 + l * 8 + 8],
                                 ALU.add, ALU.mult, ["modT", "vecT"], ["A2"])

    def mod(self, l, part, s, c):
        return self.modT[:, l, part * 8 + c, s:s + 1]

    def load_xT(self, src_rows, ntile, dst, dst_key, col0, xtoks):
        for t in range(ntile):
            xt, xk = xtoks[self.xti % len(xtoks)]
            self.xti += 1
            self.dma("sp", xt[:], src_rows[t * 128:(t + 1) * 128, :], writes=[xk])
            for half in range(2):
                b, bk = self.bank()
                for cc in range(4):
                    c = half * 4 + cc
                    self.tr(b[:, cc * 128:(cc + 1) * 128], xt[:, c * 128:(c + 1) * 128], [xk], [bk])
                o = dst[:, half * 4:half * 4 + 4, col0 + t * 128:col0 + (t + 1) * 128]
                self.evac(o, b[:, :].rearrange("p (c n) -> p c n", n=128), [bk], [dst_key])

    def norm_mod(self, xT, xkey, col0, N, A, Bf, hT, hkey, hcol0, tmp, part=None, state=None):
        sqbs, rss, tf = tmp
        if part in (None, 1):
            self.nmi += 1
            sqb, sqk = sqbs[self.nmi % len(sqbs)], "sqb%d" % (self.nmi % len(sqbs))
            rs, rk = rss[self.nmi % len(rss)], "rsn%d" % (self.nmi % len(rss))
            nsq = sqb.shape[1]
            b, bk = self.bank()
            for c0 in range(0, 8, nsq):
                for c in range(c0, c0 + nsq):
                    xi = xT[:, c, col0:col0 + N]
                    if c % 4 == 3:
                        self.tt("dve", sqb[:, c % nsq, 0:N], xi, xi, ALU.mult, [xkey], [sqk])
                    else:
                        self.act(sqb[:, c % nsq, 0:N], xi, AF.Square, [xkey], [sqk])
                for c in range(c0, c0 + nsq):
                    self.mm(b[:, 0:N], self.ones1024, sqb[:, c % nsq, 0:N], c == 0, c == 7, [sqk, "c_bf"], [bk])
            self.act(rs[:, 0:N], b[:, 0:N], AF.Ln, [bk], [rk], bias=EPS)
            self.act(rs[:, 0:N], rs[:, 0:N], AF.Exp, [rk], [rk], scale=-0.5)
            state = (rs, rk)
            if part == 1:
                return state
        rs, rk = state
        for c in range(8):
            self.tfi += 1
            t_ = tf[self.tfi % len(tf)]
            tk = "tf%d" % (self.tfi % len(tf))
            self.stt("dve", t_[:, 0:N], xT[:, c, col0:col0 + N], A(c), rs[:, 0:N], ALU.mult, ALU.mult,
                     [xkey, rk, "A1", "A2"], [tk])
            if part == 2 and c % 2 == 1:
                self.ts("dve", hT[:, c, hcol0:hcol0 + N], t_[:, 0:N], Bf(c), None, ALU.add, None, [tk, "modT"], [hkey])
            else:
                self.act(hT[:, c, hcol0:hcol0 + N], t_[:, 0:N], AF.Identity, [tk, "modT"], [hkey], bias=Bf(c))
        return None

    def norm_pipeline(self, calls):
        st = [None] * len(calls)
        if calls:
            st[0] = self.norm_mod(part=1, **calls[0])
        for i in range(len(calls)):
            if i + 1 < len(calls):
                st[i + 1] = self.norm_mod(part=1, **calls[i + 1])
            self.norm_mod(part=2, state=st[i], **calls[i])

    def qk_norm(self, praw, pk, N, gcol, out, okey, tmp, rope=None):
        sq1s, rss, qns, r1s = tmp
        self.qni += 1
        i = self.qni
        sq1, sk = sq1s[i % len(sq1s)], "sq1_%d" % (i % len(sq1s))
        rs, rk = rss[i % len(rss)], "rsq%d" % (i % len(rss))
        self.act(sq1[:, 0:N], praw[:, 0:N], AF.Square, [pk], [sk])
        b, bk = self.bank()
        self.mm(b[:, 0:N], self.blk64, sq1[:, 0:N], True, True, [sk, "c_bf"], [bk])
        self.act(rs[:, 0:N], b[:, 0:N], AF.Ln, [bk], [rk], bias=EPS)
        self.act(rs[:, 0:N], rs[:, 0:N], AF.Exp, [rk], [rk], scale=-0.5)
        if rope is None:
            self.stt("dve", out, praw[:, 0:N], gcol, rs[:, 0:N], ALU.mult, ALU.mult, [pk, rk, "gq"], [okey])
            return
        qn, qk_ = qns[i % len(qns)], "qn%d" % (i % len(qns))
        r1, r1k = r1s[i % len(r1s)], "r1_%d" % (i % len(r1s))
        cosT, sinT, rpk = rope
        self.stt("dve", qn[:, 0:N], praw[:, 0:N], gcol, rs[:, 0:N], ALU.mult, ALU.mult, [pk, rk, "gq"], [qk_])
        b2, bk2 = self.bank()
        self.mm(b2[:, 0:N], self.pswap[:], qn[:, 0:N], True, True, [qk_, "c_bf"], [bk2])
        self.tt("pool", r1[:, 0:N], qn[:, 0:N], cosT, ALU.mult, [qk_] + rpk, [r1k])
        self.tt("dve", rs[:, 0:N], b2[:, 0:N], sinT, ALU.mult, [bk2] + rpk, [rk])
        self.tt("dve", out, rs[:, 0:N], r1[:, 0:N], ALU.add, [rk, r1k], [okey])

    def qk_pipeline(self, jobs, tmp):
        sq1s, rss, qns, r1s = tmp
        st = []
        for j, jb in enumerate(jobs):
            d = dict(jb)
            d["sq1"], d["sk"] = sq1s[j % len(sq1s)], "sq1_%d" % (j % len(sq1s))
            d["rs"], d["rk"] = rss[j % len(rss)], "rsq%d" % (j % len(rss))
            if jb["rope"] is not None:
                d["qn"], d["qk"] = qns[j % len(qns)], "qn%d" % (j % len(qns))
                d["r1"], d["r1k"] = r1s[j % len(r1s)], "r1_%d" % (j % len(r1s))
            st.append(d)

        def sa(d):
            d["b"], d["bk"] = d["proj"]()

        def sb(d):
            N = d["N"]
            praw, pk, sq1, sk, rs, rk = d["b"], d["bk"], d["sq1"], d["sk"], d["rs"], d["rk"]
            self.act(sq1[:, 0:N], praw[:, 0:N], AF.Square, [pk], [sk])
            b, bk = self.bank()
            self.mm(b[:, 0:N], self.blk64, sq1[:, 0:N], True, True, [sk, "c_bf"], [bk])
            self.act(rs[:, 0:N], b[:, 0:N], AF.Ln, [bk], [rk], bias=EPS)
            self.act(rs[:, 0:N], rs[:, 0:N], AF.Exp, [rk], [rk], scale=-0.5)
            if d["rope"] is None:
                self.stt("dve", d["out"], praw[:, 0:N], d["gcol"], rs[:, 0:N], ALU.mult, ALU.mult, [pk, rk, "gq"], [d["okey"]])
            else:
                self.stt("dve", d["qn"][:, 0:N], praw[:, 0:N], d["gcol"], rs[:, 0:N], ALU.mult, ALU.mult, [pk, rk, "gq"], [d["qk"]])

        def sc(d):
            if d["rope"] is None:
                return
            N = d["N"]
            cosT, sinT, rpk = d["rope"]
            qn, qk_, r1, r1k, rs, rk = d["qn"], d["qk"], d["r1"], d["r1k"], d["rs"], d["rk"]
            b2, bk2 = self.bank()
            self.mm(b2[:, 0:N], self.pswap[:], qn[:, 0:N], True, True, [qk_, "c_bf"], [bk2])
            self.tt("pool", r1[:, 0:N], qn[:, 0:N], cosT, ALU.mult, [qk_] + rpk, [r1k])
            self.tt("dve", rs[:, 0:N], b2[:, 0:N], sinT, ALU.mult, [bk2] + rpk, [rk])
            self.tt("dve", d["out"], rs[:, 0:N], r1[:, 0:N], ALU.add, [rk, r1k], [d["okey"]])

        n = len(st)
        for t in range(n + 2):
            if t < n:
                sa(st[t])
            if 0 <= t - 1 < n:
                sb(st[t - 1])
            if 0 <= t - 2 < n:
                sc(st[t - 2])

    def load_w(self, dst, dkey, src):
        self.dma("pool", dst, src, writes=[dkey])

    def layer0_A(self):
        self.phase()
        ta = self.talloc
        self.rot = [0, 1, 2, 3, 4, 5, 6]
        self.QT = ta("QT", [128, 4, T_A], BF16)
        self.KT = ta("KT", [128, T_A], BF16)
        self.V = ta("V", [128, T_A // 128, 128], BF16)
        self.QcT = ta("QcT", [128, 4, NCTX], BF16)
        self.KcT = ta("KcT", [128, NCTX], BF16)
        self.Vc = ta("Vc", [128, 2, 128], BF16)
        fo_off = self.cur
        self.FO = ta("FO", [128, 4, T_X], BF16)
        self.FOc = ta("FOc", [128, 4, NCTX], BF16)
        self.keepC = self.cur
        self.F_all = ta("F_all", [128, 32, 512], BF16)
        self.Fc = ta("Fc", [128, 2, 512], BF16)
        self.keepA = self.cur
        win = self.win
        xtoks = [(ta("xtok%d" % i, [128, 1024], F32), "xtok%d" % i) for i in range(4)]
        self.xti = 0
        self.uid += 1
        xTb2 = self.nc.alloc_sbuf_tensor_at("xTb2_%d" % self.uid, [128, 8, 512], F32, offset=fo_off)
        xTbs = [(ta("xTb", [128, 8, 512], F32), "xTb0"), (xTb2, "xTb1")]
        hTs = [(ta("hT%d" % i, [128, 8, 512], BF16), "hT%d" % i) for i in range(2)]
        sqbs = [ta("sqb", [128, 4, 512], BF16)]
        rss = [ta("rsn%d" % i, [128, 512], F32) for i in range(2)]
        tf = [ta("tf%d" % i, [128, 512], F32) for i in range(2)]
        sq1s = [ta("sq1_%d" % i, [128, 512], BF16) for i in range(2)]
        rsq = [ta("rsq%d" % i, [128, 512], F32) for i in range(2)]
        self.uid += 1
        qns = [self.nc.alloc_sbuf_tensor_at("qn0_%d" % self.uid, [128, 512], F32, offset=fo_off + 16384), ta("qn1", [128, 512], F32)]
        r1s = [self.nc.alloc_sbuf_tensor_at("r10_%d" % self.uid, [128, 512], F32, offset=fo_off + 16384 + 2048)]
        ropeb = [ta("rope%d" % i, [128, 2, 512], F32) for i in range(1)]
        tmpn = (sqbs, rss, tf)
        tmpq = (sq1s, rsq, qns, r1s)
        assert self.cur <= self.hi - 20480, "phase A overflow into win"
        l = 0

        def projA(hT, hk, N, qkv, s, tok0, tile0, ropek):
            for t in range(N // 128):
                b, bk = self.bank()
                for c in range(8):
                    self.mm(b[:, 0:512], hT[:, c, t * 128:(t + 1) * 128], win[:, c, 0:512], c == 0, c == 7, [hk, "win"], [bk])
                dst = self.F_all[:, tile0 + t, :] if s == 0 else self.Fc[:, t, :]
                self.evac(dst, b[:, 0:512], [bk], ["F_all" if s == 0 else "Fc"])
            if not qkv:
                return
            for t in range(N // 128):
                b, bk = self.bank()
                for c in range(8):
                    self.mm(b[:, 0:128], hT[:, c, t * 128:(t + 1) * 128], win[:, c, 1152:1280], c == 0, c == 7, [hk, "win"], [bk])
                dst = self.V[:, tile0 + t, :] if s == 0 else self.Vc[:, t, :]
                self.evac(dst, b[:, 0:128], [bk], ["V" if s == 0 else "Vc"])

        def projB(hT, hk, N, qkv, s, tok0, tile0, ropek):
            if not qkv:
                return
            jobs = []
            for j in range(5):
                def proj(j=j):
                    b, bk = self.bank()
                    for c in range(8):
                        self.mm(b[:, 0:N], win[:, c, 512 + j * 128:512 + (j + 1) * 128], hT[:, c, 0:N], c == 0, c == 7, [hk, "win"], [bk])
                    return b, bk
                if s == 0:
                    out = self.QT[:, j, tok0:tok0 + N] if j < 4 else self.KT[:, tok0:tok0 + N]
                    okey = "QT" if j < 4 else "KT"
                    rb_ = ropeb[0]
                    rope = (rb_[:, 0, 0:N], rb_[:, 1, 0:N], ["rope0c", "rope0s"])
                else:
                    out = self.QcT[:, j, 0:N] if j < 4 else self.KcT[:, 0:N]
                    okey = "QcT" if j < 4 else "KcT"
                    rope = None
                jobs.append(dict(proj=proj, N=N, gcol=self.gq[:, 0:1] if j < 4 else self.gq[:, 1:2], out=out, okey=okey, rope=rope))
            self.qk_pipeline(jobs, tmpq)

        items = [("ctx", 0)] + [("lat", g) for g in range(8)]

        def stage1(idx):
            kind, g = items[idx]
            hT, hk = hTs[idx % 2]
            if kind == "ctx":
                self.load_xT(self.ctx_d.ap(), 2, self.YT, "YT", 0, xtoks)
                return (hT, hk, NCTX, True, 1, 0, 0, 0, self.YT, "YT")
            tok0 = g * 512
            xTb, xk = xTbs[idx % 2]
            self.load_xT(self.x_d.ap()[tok0:tok0 + 512, :], 4, xTb, xk, 0, xtoks)
            return (hT, hk, 512, g < 5, 0, tok0, g * 4, g, xTb, xk)

        def stage2(st, part, state=None):
            hT, hk, N, qkv, s_, tok0, tile0, rk, xT, xk = st
            return self.norm_mod(xT, xk, 0, N, lambda c: self.A1[:, l, s_, c:c + 1], lambda c: self.mod(l, 0, s_, c),
                                 hT, hk, 0, tmpn, part=part, state=state)

        cur = stage1(0)
        stage2(cur, 2, stage2(cur, 1))
        for idx in range(len(items)):
            nxt = stage1(idx + 1) if idx + 1 < len(items) else None
            projA(*cur[:8])
            self.ada_tick(1)
            if nxt is not None:
                stage2(nxt, 2, stage2(nxt, 1))
            projB(*cur[:8])
            if nxt is not None and nxt[3] and nxt[4] == 0:
                rb_ = ropeb[0]
                tk0 = nxt[5]
                self.dma("sp", rb_[:, 0, :], self.crope_d.ap()[0, :, tk0:tk0 + 512], writes=["rope0c"])
                self.dma("sp", rb_[:, 1, :], self.crope_d.ap()[1, :, tk0:tk0 + 512], writes=["rope0s"])
            self.ada_tick(1)
            cur = nxt
        if self.stop_after == "A":
            for nm_, t_, k_ in [("F_all", self.F_all, "F_all"), ("QT", self.QT, "QT"), ("KT", self.KT, "KT"), ("V", self.V, "V"),
                                ("Fc", self.Fc, "Fc"), ("QcT", self.QcT, "QcT"), ("KcT", self.KcT, "KcT"), ("Vc", self.Vc, "Vc"),
                                ("modT", self.modT, "modT")]:
                self.dump(nm_, t_, k_)

    def dft(self, Fsrc, fkey, ntile, tabs, K, FO, fokey, tabbuf, ABT):
        ti = 0
        for (k0, N) in groups(K):
            for cs in range(2):
                tb, tk = tabbuf[ti % 2], "tab%d" % (ti % 2)
                ti += 1
                src_t = tabs(cs, k0, N)
                for a0 in range(0, ntile, 8):
                    self.dma("sp", tb[:, a0:a0 + 8, 0:N], src_t[:, a0:a0 + 8, :], writes=[tk])
                for g in range(4):
                    b, bk = self.bank()
                    for a in range(ntile):
                        self.mm(b[:, 0:N], Fsrc[:, a, g * 128:(g + 1) * 128], tb[:, a, 0:N], a == 0, a == ntile - 1, [fkey, tk], [bk])
                    self.evac(ABT[:, cs, g, 0:N], b[:, 0:N], [bk], ["ABT"])
                    self.ada_tick(1)
            for g in range(4):
                b, bk = self.bank()
                self.mm(b[:, 0:N], self.ccos, ABT[:, 0, g, 0:N], True, False, ["ABT", "c_bf"], [bk])
                self.mm(b[:, 0:N], self.cnsin, ABT[:, 1, g, 0:N], False, True, ["ABT", "c_bf"], [bk])
                self.evac(FO[:, g, k0:k0 + N], b[:, 0:N], [bk], [fokey])

    def layer0_B(self):
        scr = self.scr
        self.P.barrier(lambda e: e.memset(scr[0:1, 0:8], 0.0))
        self.cur = self.keepA
        ta = self.talloc
        tabbuf = [ta("tab%d" % i, [128, 32, 512], BF16) for i in range(2)]
        ABT = ta("ABT", [128, 2, 4, 512], BF16)
        for cs in range(2):
            for g in range(4):
                b, bk = self.bank()
                for a in range(2):
                    self.mm(b[:, 0:NCTX], self.Fc[:, a, g * 128:(g + 1) * 128], self.cctx[:, cs, a, :], a == 0, a == 1, ["Fc", "c_bf"], [bk])
                self.evac(ABT[:, cs, g, 0:NCTX], b[:, 0:NCTX], [bk], ["ABT"])
        for g in range(4):
            b, bk = self.bank()
            self.mm(b[:, 0:NCTX], self.ccos, ABT[:, 0, g, 0:NCTX], True, False, ["ABT", "c_bf"], [bk])
            self.mm(b[:, 0:NCTX], self.cnsin, ABT[:, 1, g, 0:NCTX], False, True, ["ABT", "c_bf"], [bk])
            self.evac(self.FOc[:, g, :], b[:, 0:NCTX], [bk], ["FOc"])
        tabs = lambda cs, k0, N: self.cdft_d[cs].ap().rearrange("(a p) k -> p a k", p=128)[:, :, k0:k0 + N]
        self.dft(self.F_all, "F_all", 32, tabs, T_X, self.FO, "FO", tabbuf, ABT)
        self.ada_tick(100)
        self.rot = [2, 3, 4, 5, 6, 7]
        self.pers -= 2 * 4096
        if self.stop_after == "B":
            self.dump("FO", self.FO, "FO")
            self.dump("FOc", self.FOc, "FOc")

    def layer0_C(self):
        scr = self.scr
        self.P.barrier(lambda e: e.memset(scr[0:1, 0:8], 0.0))
        self.cur = self.keepC
        ta = self.talloc
        l = 0
        wout = ta("wout", [128, 8, D], BF16)
        self.load_w(wout[:], "wout", self.evout_d.ap().rearrange("(c p) n -> p c n", p=128))
        ATTs = [(ta("ATT%d" % i, [128, 4, 512], BF16), "ATT%d" % i) for i in range(2)]
        qlos = [(ta("qlo%d" % i, [128, 4, 128], BF16), "qlo%d" % i) for i in range(2)]
        qhis = [(ta("qhi%d" % i, [128, 4, 128], BF16), "qhi%d" % i) for i in range(2)]
        PT = [ta("PT%d" % i, [128, 512], BF16) for i in range(4)]
        dtots = [(ta("dtot%d" % i, [128, 512], F32), "dtot%d" % i) for i in range(2)]
        xtoks = [(ta("xtokC%d" % i, [128, 1024], F32), "xtokC%d" % i) for i in range(3)]
        self.xti = 0
        self.XT = self.nc.alloc_sbuf_tensor_at("XT_res", [128, 8, T_X], F32, offset=self.hi - 8 * T_X * 4)
        assert self.cur <= self.hi - 8 * T_X * 4, "phase C overflow"
        for (q_, k_) in qlos + qhis:
            self.P.add("pool", lambda e, q_=q_: e.memset(q_[:], 0.0), writes=[k_])
        self.rot = [4, 5, 6, 7]
        accs = [((self.pb[0], "pb0"), (self.pb[1], "pb1")), ((self.pb[2], "pb2"), (self.pb[3], "pb3"))]
        ctxtiles = [(self.KcT[:, t * 128:(t + 1) * 128], self.Vc[:, t, :], None, ["KcT", "Vc"]) for t in range(2)]

        jobs = []

        def outproj(N, FOsrc, fokey, focol, Xdst, xkey, xcol, s_, ATT, ak):
            for oc in range(8):
                b, bk = self.bank()
                for kc in range(8):
                    rhs = FOsrc[:, kc, focol:focol + N] if kc < 4 else ATT[:, kc - 4, 0:N]
                    self.mm(b[:, 0:N], wout[:, kc, oc * 128:(oc + 1) * 128], rhs, kc == 0, kc == 7, ["wout", fokey, ak], [bk])
                xo = Xdst[:, oc, xcol:xcol + N]
                self.stt("dve", xo, b[:, 0:N], self.mod(l, 2, s_, oc), xo, ALU.mult, ALU.add, [bk, "modT", xkey], [xkey])

        gi = 0
        for qt in range(2):
            post = (lambda ai=gi % 2: outproj(NCTX, self.FOc, "FOc", 0, self.YT, "YT", 0, 1, *ATTs[ai])) if qt == 1 else None
            jobs.append((self.QcT, "QcT", qt * 128, ctxtiles, gi % 2, qt * 128, None, post))
        gi += 1
        for (t0, N) in groups(T_X):
            for tq in range(N // 128):
                qt = t0 // 128 + tq
                kts = []
                for dlt, mask in ((-1, self.mprev), (0, None), (1, self.mnext)):
                    kt = qt + dlt
                    if kt < 0:
                        continue
                    kts.append((self.KT[:, kt * 128:(kt + 1) * 128], self.V[:, kt, :], mask, ["KT", "V"]))
                pre = (lambda t0=t0, N=N: self.load_xT(self.x_d.ap()[t0:t0 + N, :], N // 128, self.XT, "XT%d" % (t0 // 512), t0, xtoks)) if tq == 0 else None
                post = (lambda t0=t0, N=N, ai=gi % 2: outproj(N, self.FO, "FO", t0, self.XT, "XT%d" % (t0 // 512), t0, 0, *ATTs[ai])) \
                    if tq == N // 128 - 1 else None
                jobs.append((self.QT, "QT", qt * 128, kts + ctxtiles, gi % 2, tq * 128, pre, post))
            gi += 1

        steps = []
        for ji, job in enumerate(jobs):
            n = len(job[3])
            for kvh in range(2):
                for i in range(n):
                    steps.append((ji, kvh, i, n))
        pti = [0]

        def s_stage(st):
            ji, kvh, i, n = st
            QTsrc, qkey, qcol, keytiles, ai, dcol, pre, post = jobs[ji]
            (qlo, qlk), (qhi, qhk) = qlos[ji % 2], qhis[ji % 2]
            (pbO, ok_), (pbD, dk_) = accs[ji % 2]
            if kvh == 0 and i == 0:
                if pre is not None:
                    pre()
                self.copy("pool", qlo[0:64, :, :], QTsrc[0:64, :, qcol:qcol + 128], [qkey], [qlk])
                self.copy("pool", qhi[64:128, :, :], QTsrc[64:128, :, qcol:qcol + 128], [qkey], [qhk])
            qp, qk_ = (qlo, qlk) if kvh == 0 else (qhi, qhk)
            kap, vap, mask, kkeys = keytiles[i]
            b, bk = self.bank()
            self.mm(b[:, 0:512], kap, qp[:].rearrange("p j n -> p (j n)"), True, mask is None, [qk_] + kkeys, [bk])
            if mask is not None:
                self.mm(b[:, 0:512], self.identb, mask.unsqueeze(1).broadcast_to([128, 4, 128]), False, True, ["c_bf"], [bk])
            pt = PT[pti[0] % 4]
            pk = "PT%d" % (pti[0] % 4)
            pti[0] += 1
            self.act(pt[:], b[:, 0:512], AF.Exp, [bk], [pk])
            return (pt, pk)

        def p_stage(st, pt, pk):
            ji, kvh, i, n = st
            QTsrc, qkey, qcol, keytiles, ai, dcol, pre, post = jobs[ji]
            (pbO, ok_), (pbD, dk_) = accs[ji % 2]
            kap, vap, mask, kkeys = keytiles[i]
            rows = slice(kvh * 64, kvh * 64 + 64)
            self.mm(pbO[rows, 0:512], vap[:, kvh * 64:kvh * 64 + 64], pt[:], i == 0, i == n - 1, [pk] + kkeys, [ok_])
            self.mm(pbD[rows, 0:512], self.onesb[:, 0:64], pt[:], i == 0, i == n - 1, [pk, "c_bf"], [dk_])
            if kvh == 1 and i == n - 1:
                dtot, dtk = dtots[ji % 2]
                ATT, ak = ATTs[ai]
                self.tt("dve", dtot[:], pbD[:, 0:512], self.sinkbc[:].rearrange("p j n -> p (j n)"), ALU.add, [dk_, "sinkbc"], [dtk])
                self.act(dtot[:], dtot[:], AF.Ln, [dtk], [dtk])
                self.act(dtot[:], dtot[:], AF.Exp, [dtk], [dtk], scale=-1.0)
                self.tt("dve", ATT[:, :, dcol:dcol + 128], pbO[:, 0:512].rearrange("p (j n) -> p j n", n=128),
                        dtot[:].rearrange("p (j n) -> p j n", n=128), ALU.mult, [ok_, dtk], [ak])
                if post is not None:
                    post()

        pend = []
        for st in steps:
            cur = s_stage(st)
            pend.append((st, cur[0], cur[1]))
            if len(pend) > 2:
                p_stage(*pend.pop(0))
        for pp in pend:
            p_stage(*pp)
        self.rot = [2, 3, 4, 5, 6, 7]

    def ffn(self, l):
        scr = self.scr
        self.P.barrier(lambda e: e.memset(scr[0:1, 0:8], 0.0))
        self.cur = self.pers
        ta = self.talloc
        T = T_X if l == 0 else T_OWN
        toks = [(t0, N, 0) for (t0, N) in groups(T)]
        HT = T + (NCTX if l == 0 else 0)
        h2 = ta("h2", [128, 8, HT], BF16)
        keepF = self.cur
        sqbs = [ta("sqb%d" % i, [128, 8, 512], BF16) for i in range(2)]
        rss = [ta("rsn%d" % i, [128, 512], F32) for i in range(2)]
        tf = [ta("tf%d" % i, [128, 512], F32) for i in range(4)]
        tmpn = (sqbs, rss, tf)
        xkey = lambda t0: "XT%d" % (t0 // 512)
        calls = [dict(xT=self.XT, xkey=xkey(t0), col0=t0, N=N, A=(lambda c: self.A2[:, l, 0, c:c + 1]),
                      Bf=(lambda c: self.mod(l, 3, 0, c)), hT=h2, hkey="h2_%d" % (t0 // 512), hcol0=t0, tmp=tmpn) for (t0, N, _) in toks]
        if l == 0:
            calls.append(dict(xT=self.YT, xkey="YT", col0=0, N=NCTX, A=(lambda c: self.A2[:, l, 1, c:c + 1]),
                              Bf=(lambda c: self.mod(l, 3, 1, c)), hT=h2, hkey="h2_c", hcol0=T, tmp=tmpn))
            toks = toks + [(T, NCTX, 1)]
        self.norm_pipeline(calls)
        self.P.barrier(lambda e: e.memset(scr[0:1, 0:8], 0.0))
        self.cur = keepF
        splits = [(0, 4), (4, 4), (8, 4), (12, 4), (16, 3), (19, 3)]
        wgb = [ta("wg%d" % i, [128, 8, 512], BF16) for i in range(2)]
        wub = [ta("wu%d" % i, [128, 8, 512], BF16) for i in range(2)]
        wdb = [ta("wd%d" % i, [128, 4, D], BF16) for i in range(2)]
        actb = [ta("act%d" % i, [128, 4, 512], BF16) for i in range(2)]
        sg = [ta("sg%d" % i, [128, 512], F32) for i in range(3)]
        obs = [ta("ob%d" % i, [128, D], F32) for i in range(2)] if l == 1 else None
        assert self.cur <= self.hi - 8 * T_X * 4, "ffn overflow"
        ai = 0
        si = 0
        for sp_i, (j0, nj) in enumerate(splits):
            wi = sp_i % 2
            wg, wu, wd = wgb[wi], wub[wi], wdb[wi]
            kg, ku, kd = "wg%d" % wi, "wu%d" % wi, "wd%d" % wi
            self.load_w(wg[:, :, 0:nj * 128], kg, self.wg_d.ap()[l].rearrange("(c p) n -> p c n", p=128)[:, :, j0 * 128:(j0 + nj) * 128])
            self.load_w(wu[:, :, 0:nj * 128], ku, self.wu_d.ap()[l].rearrange("(c p) n -> p c n", p=128)[:, :, j0 * 128:(j0 + nj) * 128])
            self.load_w(wd[:, 0:nj, :], kd, self.wd_d.ap()[l].rearrange("(j p) n -> p j n", p=128)[:, j0:j0 + nj, :])
            for (t0, N, s) in toks:
                ab = actb[ai % 2]
                ak = "act%d" % (ai % 2)
                ai += 1
                hk = "h2_c" if s == 1 else "h2_%d" % (t0 // 512)
                for jj in range(nj):
                    bg, bgk = self.bank()
                    for c in range(8):
                        self.mm(bg[:, 0:N], wg[:, c, jj * 128:(jj + 1) * 128], h2[:, c, t0:t0 + N], c == 0, c == 7, [kg, hk], [bgk])
                    bu, buk = self.bank()
                    for c in range(8):
                        self.mm(bu[:, 0:N], wu[:, c, jj * 128:(jj + 1) * 128], h2[:, c, t0:t0 + N], c == 0, c == 7, [ku, hk], [buk])
                    s_ = sg[si % 3]
                    sk = "sg%d" % (si % 3)
                    si += 1
                    self.act(s_[:, 0:N], bg[:, 0:N], AF.Silu, [bgk], [sk])
                    self.tt("dve", ab[:, jj, 0:N], bu[:, 0:N], s_[:, 0:N], ALU.mult, [buk, sk], [ak])
                X, xk, xc = (self.XT, xkey(t0), t0) if s == 0 else (self.YT, "YT", 0)
                for oc in range(8):
                    b, bk = self.bank()
                    for jj in range(nj):
                        self.mm(b[:, 0:N], wd[:, jj, oc * 128:(oc + 1) * 128], ab[:, jj, 0:N], jj == 0, jj == nj - 1, [kd, ak], [bk])
                    xo = X[:, oc, xc:xc + N]
                    self.stt("dve", xo, b[:, 0:N], self.mod(l, 5, s, oc), xo, ALU.mult, ALU.add, [bk, "modT", xk], [xk])
                if l == 1 and sp_i == len(splits) - 1:
                    self.emit_out(range(t0 // 128, (t0 + N) // 128), obs)
        if l == 1:
            self.out_done = True

    def layer1_E(self):
        scr = self.scr
        self.P.barrier(lambda e: e.memset(scr[0:1, 0:8], 0.0))
        self.cur = self.pers
        ta = self.talloc
        l = 1
        NK = T_X + NCTX
        hT = ta("hT1", [128, 8, NK], BF16)
        keepE = self.cur
        sqbs = [ta("sqb%d" % i, [128, 8, 512], BF16) for i in range(2)]
        rss = [ta("rsn%d" % i, [128, 512], F32) for i in range(2)]
        tf = [ta("tf%d" % i, [128, 512], F32) for i in range(4)]
        tmpn = (sqbs, rss, tf)
        calls = [dict(xT=self.XT, xkey="XT%d" % (t0 // 512), col0=t0, N=N, A=(lambda c: self.A1[:, l, 0, c:c + 1]),
                      Bf=(lambda c: self.mod(l, 0, 0, c)), hT=hT, hkey="hT1_%d" % (t0 // 512), hcol0=t0, tmp=tmpn) for (t0, N) in groups(T_X)]
        calls.append(dict(xT=self.YT, xkey="YT", col0=0, N=NCTX, A=(lambda c: self.A1[:, l, 1, c:c + 1]),
                          Bf=(lambda c: self.mod(l, 0, 1, c)), hT=hT, hkey="hT1_%d" % (T_X // 512), hcol0=T_X, tmp=tmpn))
        self.norm_pipeline(calls)
        self.P.barrier(lambda e: e.memset(scr[0:1, 0:8], 0.0))
        self.cur = keepE
        off_rsq = self.cur
        rsq = [ta("rsq%d" % i, [128, 512], F32) for i in range(2)]
        self.uid += 1
        paccs = [(self.nc.alloc_sbuf_tensor_at("pacc%d_%d" % (i, self.uid), [128, 512], BF16, offset=off_rsq + i * 1024), "acc%d" % i) for i in range(4)]
        acck = ["acc%d" % i for i in range(4)]
        sq1s = [ta("sq1_%d" % i, [128, 512], BF16) for i in range(1)]
        tmpq = (sq1s, rsq, None, None)
        namask = ta("namask", [128, 3, 6, 128], BF16)
        self.dma("sp", namask[:], self.cnam_d.ap(), writes=["namask"])
        wq = ta("wq", [128, 8, 256], BF16)
        wk = ta("wk", [128, 8, 256], BF16)
        wv = ta("wv", [128, 8, 256], BF16)
        wo = ta("wo", [128, 2, D], BF16)
        Tt = ta("Tt", [128, 6, 4, 128], BF16)
        BMI = ta("BMI", [128, 5, 4, 128], BF16)
        bmtmp = rsq[0]
        KTh = ta("KTh", [128, 2, NK], BF16)
        Vh = ta("Vh", [128, NK // 128, 256], BF16)
        QTh = ta("QTh", [128, 2, T_OWN], BF16)
        ATTs = [(ta("ATT1_%d" % i, [128, 2, 512], BF16), "ATT1_%d" % i) for i in range(2)]
        qpads = [[(ta("qpad%d_%d" % (e, i), [128, 2, 128], BF16), "qpad%d_%d" % (e, i)) for e in range(2)] for i in range(2)]
        PT = [ta("PT1_%d" % i, [128, 512], BF16) for i in range(5)]
        dtots = [(ta("dtot1_%d" % i, [128, 512], F32), "dtot1_%d" % i) for i in range(1)]
        assert self.cur <= self.hi - 8 * T_X * 4, "layer1 overflow %d" % (self.cur - (self.hi - 8 * T_X * 4))
        for i in range(2):
            for e_ in range(2):
                q_, k_ = qpads[i][e_]
                self.P.add("pool", lambda e, q_=q_: e.memset(q_[:], 0.0), writes=[k_])
        self.rot = [4, 5, 6, 7]
        accs = [((self.pb[0], "pb0"), (self.pb[1], "pb1")), ((self.pb[2], "pb2"), (self.pb[3], "pb3"))]
        z256 = self.cbf[:, 4:6, :].rearrange("p a n -> p (a n)")
        z512 = self.cbf[:, 4:8, :].rearrange("p a n -> p (a n)")
        hkey = lambda t0: "hT1_%d" % (t0 // 512)
        pti = [0]
        tglob = [0]
        src = self.odin_d.ap().rearrange("(c p) n -> p c n", p=128)

        def load_qkv(hg_):
            self.load_w(wq[:], "wq", src[:, :, hg_ * 256:(hg_ + 1) * 256])
            self.load_w(wk[:], "wk", src[:, :, D + hg_ * 256:D + (hg_ + 1) * 256])
            self.load_w(wv[:], "wv", src[:, :, 2 * D + hg_ * 256:2 * D + (hg_ + 1) * 256])

        def load_Tt(hg_):
            for dl in range(6):
                for kr in range(2):
                    for qr in range(2):
                        dr_idx = 2 * (dl - 2) + kr - qr + 7
                        srcb = AP(self.rb_d, hg_ * 4 * 465 + dr_idx * 31 - 48, [[1, 64], [465, 4], [1, 64]])
                        self.dma("pool", Tt[qr * 64:(qr + 1) * 64, dl, :, kr * 64:(kr + 1) * 64], srcb, writes=["Tt%d" % dl])

        load_qkv(0)
        for hg in range(4):
            self.load_w(wo[:], "wo", self.odout_d.ap().rearrange("(c p) n -> p c n", p=128)[:, hg * 2:hg * 2 + 2, :])
            if hg == 0:
                load_Tt(0)
            for dl in range(5):
                b, bk = self.bank()
                for h4 in range(4):
                    self.mm(b[:, h4 * 128:(h4 + 1) * 128], Tt[:, dl, h4, :], self.rblk, True, True, ["Tt%d" % dl, "c_bf"], [bk])
                self.tt("dve", BMI[:, dl, :, :], b[:, 0:512].rearrange("p (h n) -> p h n", n=128),
                        namask[:, 2, dl, :].unsqueeze(1).broadcast_to([128, 4, 128]), ALU.add, [bk, "namask"], ["BMI"])
            self.rot = [0, 1, 2, 3, 4, 5, 6, 7]
            self.P.add("dve", lambda e: e.memset(scr[0:1, 8:12], 0.0), reads=acck, writes=["rsq0", "rsq1"])
            for (t0, N) in groups(NK):
                jobs = []
                for ch in range(2):
                    def projk(ch=ch, t0=t0, N=N):
                        b, bk = self.bank()
                        for c in range(8):
                            self.mm(b[:, 0:N], wk[:, c, ch * 128:(ch + 1) * 128], hT[:, c, t0:t0 + N], c == 0, c == 7, ["wk", hkey(t0)], [bk])
                        return b, bk
                    jobs.append(dict(proj=projk, N=N, gcol=self.gq[:, 3:4], out=KTh[:, ch, t0:t0 + N], okey="KTh", rope=None))
                    if t0 + N <= T_OWN:
                        def projq(ch=ch, t0=t0, N=N):
                            b, bk = self.bank()
                            for c in range(8):
                                self.mm(b[:, 0:N], wq[:, c, ch * 128:(ch + 1) * 128], hT[:, c, t0:t0 + N], c == 0, c == 7, ["wq", hkey(t0)], [bk])
                            return b, bk
                        jobs.append(dict(proj=projq, N=N, gcol=self.gq[:, 2:3], out=QTh[:, ch, t0:t0 + N], okey="QTh", rope=None))
                self.qk_pipeline(jobs, tmpq)
                for t in range(N // 128):
                    b, bk = self.bank()
                    for c in range(8):
                        self.mm(b[:, 0:256], hT[:, c, t0 + t * 128:t0 + (t + 1) * 128], wv[:, c, :], c == 0, c == 7, ["wv", hkey(t0)], [bk])
                    self.evac(Vh[:, t0 // 128 + t, :], b[:, 0:256], [bk], ["Vh"])
            self.rot = [4, 5, 6, 7]
            self.P.add("dve", lambda e: e.memset(scr[0:1, 12:16], 0.0), reads=["rsq0", "rsq1"], writes=acck)
            if hg + 1 < 4:
                load_qkv(hg + 1)
            steps = []
            for qt in range(T_OWN // 128):
                kts = [(qt + d_, d_ + 2) for d_ in range(-2, 4 if qt == 0 else 3) if qt + d_ >= 0] + \
                      [(T_X // 128, None), (T_X // 128 + 1, None)]
                for i, (kt, dl) in enumerate(kts):
                    steps.append((qt, i, len(kts), kt, dl))

            def s_stage(st):
                qt, i, n, kt, dl = st
                tq_ = tglob[0] + qt
                qp = qpads[tq_ % 2]
                (pbO, ok_), (pbD, dk_) = accs[tq_ % 2]
                if i == 0:
                    for e_ in range(2):
                        rows = slice(e_ * 64, e_ * 64 + 64)
                        self.copy("pool", qp[e_][0][rows, :, :], QTh[rows, :, qt * 128:(qt + 1) * 128], ["QTh"], [qp[e_][1]])
                    self.mm(pbO[:, 0:256], self.zerob, z256, True, False, ["c_bf"], [ok_])
                b, bk = self.bank()
                first = True
                if dl is not None:
                    if qt >= 2:
                        self.mm(b[:, 0:512], self.identb, BMI[:, dl, :, :].rearrange("p h n -> p (h n)"), True, False, ["BMI", "c_bf"], [bk])
                        first = False
                    else:
                        self.mm(b[:, 0:512], self.identb, namask[:, qt, dl, :].unsqueeze(1).broadcast_to([128, 4, 128]), True, False,
                                ["namask", "c_bf"], [bk])
                        for h4 in range(4):
                            self.mm(b[:, h4 * 128:(h4 + 1) * 128], Tt[:, dl, h4, :], self.rblk, False, False, ["Tt%d" % dl, "c_bf"], [bk])
                        first = False
                for h4 in range(4):
                    ch, e_ = h4 // 2, h4 % 2
                    self.mm(b[:, h4 * 128:(h4 + 1) * 128], KTh[:, ch, kt * 128:(kt + 1) * 128], qp[e_][0][:, ch, :], first, True,
                            ["KTh", qp[e_][1]], [bk])
                pt = PT[pti[0] % 5]
                pk = "PT1_%d" % (pti[0] % 5)
                pti[0] += 1
                self.act(pt[:], b[:, 0:512], AF.Exp, [bk], [pk])
                acc, ak_ = paccs[(tq_ % 2) * 2 + (i % 2)]
                if i < 2:
                    self.copy("dve", acc[:], pt[:], [pk], [ak_])
                else:
                    self.tt("dve", acc[:], acc[:], pt[:], ALU.add, [ak_, pk], [ak_])
                return (pt, pk)

            def p_stage(st, pt, pk):
                qt, i, n, kt, dl = st
                tq_ = tglob[0] + qt
                (pbO, ok_), (pbD, dk_) = accs[tq_ % 2]
                for h4 in range(4):
                    ch, e_ = h4 // 2, h4 % 2
                    rows = slice(e_ * 64, e_ * 64 + 64)
                    self.mm(pbO[rows, ch * 128:(ch + 1) * 128], Vh[:, kt, h4 * 64:(h4 + 1) * 64], pt[:, h4 * 128:(h4 + 1) * 128],
                            False, True, [pk, "Vh"], [ok_])
                if i == n - 1:
                    accA, akA = paccs[(tq_ % 2) * 2]
                    accB, akB = paccs[(tq_ % 2) * 2 + 1]
                    self.mm(pbD[:, 0:512], self.onesb, accA[:], True, False, [akA, "c_bf"], [dk_])
                    self.mm(pbD[:, 0:512], self.onesb, accB[:], False, True, [akB, "c_bf"], [dk_])
                    dtot, dtk = dtots[0]
                    ATT, ak = ATTs[(qt // 4) % 2]
                    tq = qt % 4
                    self.act(dtot[:], pbD[:, 0:512], AF.Ln, [dk_], [dtk])
                    self.act(dtot[:], dtot[:], AF.Exp, [dtk], [dtk], scale=-1.0)
                    dv = dtot[:].rearrange("p (c e n) -> p c e n", e=2, n=128)
                    for e_ in range(2):
                        rows = slice(e_ * 64, e_ * 64 + 64)
                        self.tt("dve", ATT[rows, :, tq * 128:(tq + 1) * 128], pbO[rows, 0:256].rearrange("p (j n) -> p j n", n=128),
                                dv[rows, :, e_, :], ALU.mult, [ok_, dtk], [ak])
                    if tq == 3:
                        t0 = (qt // 4) * 512
                        for oc in range(8):
                            b, bk = self.bank()
                            for ch in range(2):
                                self.mm(b[:, 0:512], wo[:, ch, oc * 128:(oc + 1) * 128], ATT[:, ch, 0:512], ch == 0, ch == 1, ["wo", ak], [bk])
                            xo = self.XT[:, oc, t0:t0 + 512]
                            xk = "XT%d" % (t0 // 512)
                            self.stt("dve", xo, b[:, 0:512], self.mod(l, 2, 0, oc), xo, ALU.mult, ALU.add, [bk, "modT", xk], [xk])

            if "noattn" in self.flags:
                steps = []
            if "few" in self.flags:
                steps = steps[:self.nstep]
            pend = []
            for si_, st in enumerate(steps):
                cur = s_stage(st)
                if st[0] == 1 and si_ + 1 < len(steps) and steps[si_ + 1][0] == 2 and hg + 1 < 4:
                    load_Tt(hg + 1)
                pend.append((st, cur[0], cur[1]))
                if len(pend) > 3:
                    p_stage(*pend.pop(0))
            for pp in pend:
                p_stage(*pp)
            tglob[0] += T_OWN // 128
            if "onehg" in self.flags:
                break
        self.rot = [2, 3, 4, 5, 6, 7]

    def finish(self, final=True):
        scr = self.scr
        self.P.barrier(lambda e: e.memset(scr[0:1, 0:8], 0.0))
        self.cur = self.pers
        ta = self.talloc
        if self.debug and hasattr(self, "XT"):
            self.dma("sp", self.dbgx_d.ap(), self.XT[:], reads=["XT%d" % i for i in range(5)])
            self.dma("sp", self.dbgy_d.ap(), self.YT[:], reads=["YT"])
        if not hasattr(self, "XT"):
            return
        if getattr(self, "out_done", False):
            return
        ob = [ta("ob%d" % i, [128, D], F32) for i in range(2)]
        self.emit_out(range(T_OWN // 128), ob)

    def emit_out(self, tiles, ob):
        for t in tiles:
            o_, ok = ob[t % 2], "ob%d" % (t % 2)
            for half in range(2):
                b, bk = self.bank()
                for cc in range(4):
                    c = half * 4 + cc
                    self.tr(b[:, cc * 128:(cc + 1) * 128], self.XT[:, c, t * 128:(t + 1) * 128], ["XT%d" % (t // 4)], [bk])
                self.evac(o_[:, half * 512:(half + 1) * 512], b[:, 0:512], [bk], [ok])
            self.dma("sp", self.out_d.ap()[t * 128:(t + 1) * 128, :], o_[:], reads=[ok])


_CONST_CACHE = {}


def _bf(a):
    return np.ascontiguousarray(a.astype(ml_dtypes.bfloat16))


def host_consts(par):
    if par in _CONST_CACHE:
        return _CONST_CACHE[par]
    c = {}
    c["c_identf"] = np.eye(128, dtype=np.float32)
    psw = np.zeros((128, 128), np.float32)
    for m in range(128):
        i = m % 32
        partner = m + 16 if i < 16 else m - 16
        psw[partner, m] = 1.0
    c["c_pswap"] = psw
    cb = np.zeros((128, 10, 128), np.float32)
    cb[:, 0, :] = 1.0 / 1024.0
    cb[0:64, 1, 0:64] = 1.0 / 64.0
    cb[64:128, 1, 64:128] = 1.0 / 64.0
    cb[:, 2, :] = np.eye(128)
    for blk in range(2):
        for u in range(64):
            cb[blk * 64 + u, 3, blk * 64 + 63 - u] = 1.0
    cb[:, 5, :] = 1.0
    cc = np.arange(128)
    ang = 2.0 * np.pi * ((cc[:, None] * cc[None, :]) % 128) / 128.0
    cb[:, 6, :] = np.cos(ang) / np.sqrt(128.0)
    cb[:, 7, :] = -np.sin(ang) / np.sqrt(128.0)
    j = np.arange(128)[:, None]
    q = np.arange(128)[None, :]
    cb[:, 8, :] = np.where(j >= q, 0.0, NEG)
    cb[:, 9, :] = np.where(j <= q, 0.0, NEG)
    c["c_bf"] = _bf(cb)
    n = np.arange(256)
    a2 = 2.0 * np.pi * ((n[:, None] * n[None, :]) % 256) / 256.0
    C2 = (np.cos(a2) / 16.0).reshape(2, 128, 256)
    S2 = (np.sin(a2) / 16.0).reshape(2, 128, 256)
    cx = np.stack([C2.transpose(1, 0, 2), S2.transpose(1, 0, 2)], axis=1)
    c["c_ctxdft"] = _bf(cx)
    loc = np.arange(T_ALL)
    glob = loc if par == 0 else (T_ALL - 1 - loc)
    inv = (np.float32(10000.0) ** (-np.arange(16, dtype=np.float32) / np.float32(16))).astype(np.float32)
    g_ = glob[:T_A]
    row = (g_ // 64).astype(np.float32)
    col = (g_ % 64).astype(np.float32)
    ar = (row[None, :] * inv[:, None]).astype(np.float32)
    ac = (col[None, :] * inv[:, None]).astype(np.float32)
    cos64 = np.concatenate([np.cos(ar), np.cos(ar), np.cos(ac), np.cos(ac)], axis=0)
    sin64 = np.concatenate([-np.sin(ar), np.sin(ar), -np.sin(ac), np.sin(ac)], axis=0)
    c["c_rope"] = np.ascontiguousarray(np.stack([np.concatenate([cos64, cos64], 0), np.concatenate([sin64, sin64], 0)], 0).astype(np.float32))
    gk = glob[:T_X].astype(np.int64)
    gn = glob.astype(np.int64)
    ph = (gn[:, None] * gk[None, :]) % T_ALL
    angL = (2.0 * np.pi / T_ALL) * ph
    c["c_dftc"] = _bf(np.cos(angL) / 64.0)
    c["c_dfts"] = _bf(np.sin(angL) / 64.0)
    nm = np.zeros((128, 3, 6, 128), np.float32)
    for cls, qt in enumerate((0, 1, 8)):
        for dl in range(6):
            kt = qt + dl - 2
            if kt < 0:
                nm[:, cls, dl, :] = NEG
                continue
            kg = glob[kt * 128 + np.arange(128)]
            qg = glob[qt * 128 + np.arange(128)]
            kr, kc = kg // 64, kg % 64
            qr, qc = qg // 64, qg % 64
            r0 = np.clip(qr - 4, 0, 56)
            c0 = np.clip(qc - 8, 0, 48)
            ok = (kr[:, None] >= r0[None, :]) & (kr[:, None] < r0[None, :] + 8) & (kc[:, None] >= c0[None, :]) & (kc[:, None] < c0[None, :] + 16)
            nm[:, cls, dl, :] = np.where(ok, 0.0, NEG)
    c["c_namask"] = _bf(nm)
    _CONST_CACHE[par] = c
    return c


_NC_CACHE = {}


def get_nc(stop_after=None, debug=False, only=None, flags=()):
    key = (stop_after, debug, only, tuple(flags))
    if key not in _NC_CACHE:
        bld = Builder(stop_after=stop_after, debug=debug, only=only, flags=flags)
        _NC_CACHE[key] = bld.build()
    return _NC_CACHE[key]


def make_in_maps(inputs):
    f32 = lambda a: np.ascontiguousarray(np.asarray(a, dtype=np.float32))
    x = f32(inputs["x"])
    c = f32(inputs["c"])
    ctx = f32(inputs["ctx"])
    c_ctx = f32(inputs["c_ctx"])
    ev_in = f32(inputs["ev_w_in"])[0]
    ev_out = f32(inputs["ev_w_out"])[0]
    hp = [0, 4, 1, 5, 2, 6, 3, 7]
    qcols = np.concatenate([512 + h * 64 + np.arange(64) for h in hp])
    cols = np.concatenate([np.arange(512), qcols, np.arange(1024, 1280)])
    ev_in_p = np.ascontiguousarray(ev_in[:, cols])
    rows = np.concatenate([np.arange(512), qcols])
    ev_out_p = np.ascontiguousarray(ev_out[rows, :])
    rb = f32(inputs["od_rel_bias"])[0]
    shared = {
        "ada_w": f32(inputs["ada_w"]), "ada_b": f32(inputs["ada_b"]),
        "norm1_g": f32(inputs["norm1_g"]), "norm2_g": f32(inputs["norm2_g"]),
        "ffn_w_gate": f32(inputs["ffn_w_gate"]), "ffn_w_up": f32(inputs["ffn_w_up"]), "ffn_w_down": f32(inputs["ffn_w_down"]),
        "ev_w_in": ev_in_p, "ev_w_out": ev_out_p,
        "ev_q_norm": f32(inputs["ev_q_norm"])[0], "ev_k_norm": f32(inputs["ev_k_norm"])[0], "ev_sink": f32(inputs["ev_sink"])[0],
        "od_w_in": f32(inputs["od_w_in"])[0], "od_w_out": f32(inputs["od_w_out"])[0],
        "od_q_norm": f32(inputs["od_q_norm"])[0], "od_k_norm": f32(inputs["od_k_norm"])[0],
    }
    pad = np.zeros(64, np.float32)
    rbs = [np.concatenate([rb.reshape(-1), pad]), np.concatenate([rb[:, ::-1, ::-1].reshape(-1), pad])]
    in_maps = []
    for cid in range(8):
        b, par = cid // 2, cid % 2
        m = dict(shared)
        m["x_loc"] = np.ascontiguousarray(x[b] if par == 0 else x[b][::-1])
        m["ctx_b"] = np.ascontiguousarray(ctx[b])
        m["cvec"] = np.ascontiguousarray(np.stack([c[b], c_ctx], 0))
        m["rel_bias"] = rbs[par]
        m.update(host_consts(par))
        in_maps.append(m)
    return in_maps


def assemble(results):
    out = np.zeros((4, T_ALL, D), np.float32)
    for cid in range(8):
        b, par = cid // 2, cid % 2
        o = np.asarray(results[cid]["out_loc"], dtype=np.float32)
        if par == 0:
            out[b, :T_OWN] = o
        else:
            out[b, T_OWN:] = o[::-1]
    return out


def kernel(**inputs):
    nc = get_nc()
    in_maps = make_in_maps(inputs)
    res = run_bass_kernel_spmd(nc, in_maps, core_ids=list(range(8)))
    return assemble(res.results)
```

```python
import numpy as np
import ml_dtypes
import concourse.bass as bass
import concourse.mybir as mybir
from concourse.bass_utils import run_bass_kernel_spmd
from concourse.ap import AP

F32 = mybir.dt.float32
BF16 = mybir.dt.bfloat16
ALU = mybir.AluOpType
AF = mybir.ActivationFunctionType

NDMA_SEM = 48
D = 1024
DFF = 2816
NEG = -30000.0
EPS = 1e-6
T_ALL = 4096
T_X = 2304
T_A = 2560
T_OWN = 2048
NCTX = 256


class _Op:
    __slots__ = ("id", "eng", "fn", "deps", "is_dma", "sem_i", "sem_val", "signal", "count")


class Prog:
    ENGS = ("pe", "act", "dve", "pool", "sp")

    def __init__(self):
        self.ops = []
        self.lw = {}
        self.rd = {}
        self.dma_rr = 0
        self.dma_sem_uses = [0] * NDMA_SEM
        self.dma_sem_last = [None] * NDMA_SEM
        self.last_eng = {}
        self.bar = None
        self.dopen = set()
        self.saved = {}
        self.strict = False

    def add(self, eng, fn, reads=(), writes=(), dma=False):
        op = _Op()
        op.id = len(self.ops)
        op.eng = eng
        op.fn = fn
        op.is_dma = dma
        op.signal = False
        op.count = 0
        deps = {}
        if self.bar is not None:
            deps[self.bar] = True
        for k in reads:
            for w in self.lw.get(k, ()):
                deps[w] = True
            self.dopen.discard(k)
        cow = set()
        for k in writes:
            if dma and k in self.dopen:
                cow.add(k)
                for r in self.saved.get(k, ()):
                    deps.setdefault(r, False)
                continue
            for w in self.lw.get(k, ()):
                deps.setdefault(w, False)
            for r in self.rd.get(k, ()):
                deps.setdefault(r, False)
        if dma:
            i = self.dma_rr % NDMA_SEM
            self.dma_rr += 1
            prev = self.dma_sem_last[i]
            if prev is not None:
                deps.setdefault(prev, False)
            self.dma_sem_uses[i] += 1
            op.sem_i = i
            op.sem_val = 16 * self.dma_sem_uses[i]
            self.dma_sem_last[i] = op.id
        op.deps = deps
        for k in reads:
            self.rd.setdefault(k, []).append(op.id)
        for k in writes:
            if k in cow:
                self.lw[k].append(op.id)
                continue
            self.saved[k] = list(self.lw.get(k, ())) + list(self.rd.get(k, ()))
            self.lw[k] = [op.id]
            self.rd[k] = []
            if dma:
                self.dopen.add(k)
            else:
                self.dopen.discard(k)
        self.ops.append(op)
        if not dma:
            self.last_eng[eng] = op.id
        return op

    def barrier(self, fn):
        op = self.add("pool", fn)
        for e, i in self.last_eng.items():
            if i != op.id:
                op.deps[i] = True
        for i in self.dma_sem_last:
            if i is not None:
                op.deps[i] = True
        self.bar = op.id
        self.lw = {}
        self.rd = {}
        self.dopen = set()
        self.saved = {}

    def emit(self, block, eng_sems, dma_sems):
        ops = self.ops

        strict = self.strict

        def skip(op, dop, raw):
            if op.is_dma or dop.is_dma or dop.eng != op.eng:
                return False
            return op.eng == "pe" or (not raw and not strict)

        for op in ops:
            for d, raw in op.deps.items():
                dop = ops[d]
                if dop.is_dma or skip(op, dop, raw):
                    continue
                dop.signal = True
        cnt = {e: 0 for e in self.ENGS}
        for op in ops:
            if op.signal and not op.is_dma:
                cnt[op.eng] += 1
                op.count = cnt[op.eng]
        per_eng = {e: [o for o in ops if o.eng == e] for e in self.ENGS}

        def run(e, engine):
            known = {}
            for op in per_eng[e]:
                needs = {}
                for d, raw in op.deps.items():
                    dop = ops[d]
                    if dop.is_dma:
                        key = ("d", dop.sem_i)
                        val = dop.sem_val
                    else:
                        if skip(op, dop, raw):
                            continue
                        key = ("e", dop.eng)
                        val = dop.count
                    if needs.get(key, 0) < val:
                        needs[key] = val
                for key, val in needs.items():
                    if known.get(key, 0) >= val:
                        continue
                    known[key] = val
                    sem = dma_sems[key[1]] if key[0] == "d" else eng_sems[key[1]]
                    engine.wait_ge(sem, val)
                ins = op.fn(engine)
                if op.is_dma:
                    ins.then_inc(dma_sems[op.sem_i], 16)
                elif op.signal:
                    ins.then_inc(eng_sems[e], 1)
            if e == "sp":
                for i in range(NDMA_SEM):
                    if self.dma_sem_uses[i] > 0:
                        engine.wait_ge(dma_sems[i], 16 * self.dma_sem_uses[i])

        block.tensor(lambda eng: run("pe", eng))
        block.scalar(lambda eng: run("act", eng))
        block.vector(lambda eng: run("dve", eng))
        block.gpsimd(lambda eng: run("pool", eng))
        block.sync(lambda eng: run("sp", eng))


def groups(n, g=512):
    out = []
    s = 0
    while s < n:
        out.append((s, min(g, n - s)))
        s += g
    return out


class Builder:
    def __init__(self, stop_after=None, debug=False, only=None, flags=()):
        self.only = only
        self.flags = set(f for f in flags if not f.startswith('n='))
        self.nstep = ([int(f[2:]) for f in flags if f.startswith('n=')] + [8])[0]
        self.nc = bass.Bass("TRN2", target_bir_lowering=False)
        self.P = Prog()
        self.stop_after = stop_after
        self.debug = debug
        nc = self.nc
        self.lo = ((nc.sbuf_base + 63) // 64) * 64
        self.hi = nc.sbuf_top
        self.pers = self.lo
        self.cur = None
        self.uid = 0
        self.rr = 0
        self.evr = 0
        self.rot = [2, 3, 4, 5, 6]
        self.nmi = 0
        self.tfi = 0
        self.qni = 0

    def _alloc(self, ptr, name, shape, dt):
        esz = 4 if dt == F32 else 2
        n = 1
        for s in shape[1:]:
            n *= s
        nbytes = ((n * esz + 63) // 64) * 64
        self.uid += 1
        t = self.nc.alloc_sbuf_tensor_at("%s_%d" % (name, self.uid), list(shape), dt, offset=ptr)
        return t, ptr + nbytes

    def palloc(self, name, shape, dt):
        assert self.cur is None
        t, self.pers = self._alloc(self.pers, name, shape, dt)
        assert self.pers <= self.hi, "persistent overflow"
        return t

    def talloc(self, name, shape, dt):
        t, self.cur = self._alloc(self.cur, name, shape, dt)
        assert self.cur <= self.hi, "phase overflow %s %d" % (name, self.cur - self.hi)
        return t

    def phase(self):
        scr = self.scr
        self.P.barrier(lambda e: e.memset(scr[0:1, 0:8], 0.0))
        self.cur = self.pers

    def dma(self, eng, out, in_, reads=(), writes=()):
        self.P.add(eng, lambda e: e.dma_start(out=out, in_=in_), reads=reads, writes=writes, dma=True)

    def mm(self, out, lhsT, rhs, start, stop, reads, writes):
        self.P.add("pe", lambda e: e.matmul(out, lhsT=lhsT, rhs=rhs, start=start, stop=stop), reads=reads, writes=writes)

    def tr(self, out, in_, reads, writes):
        ident = self.identf
        k = in_.shape[0]
        self.P.add("pe", lambda e: e.transpose(out, in_, ident[0:k, 0:k]), reads=list(reads) + ["c_id"], writes=writes)

    def act(self, out, in_, func, reads, writes, bias=None, scale=None):
        kw = {}
        if bias is not None:
            kw["bias"] = bias
        if scale is not None:
            kw["scale"] = scale
        self.P.add("act", lambda e: e.activation(out=out, in_=in_, func=func, **kw), reads=reads, writes=writes)

    def tt(self, eng, out, in0, in1, op, reads, writes):
        self.P.add(eng, lambda e: e.tensor_tensor(out=out, in0=in0, in1=in1, op=op), reads=reads, writes=writes)

    def stt(self, eng, out, in0, scalar, in1, op0, op1, reads, writes):
        self.P.add(eng, lambda e: e.scalar_tensor_tensor(out=out, in0=in0, scalar=scalar, in1=in1, op0=op0, op1=op1),
                   reads=reads, writes=writes)

    def ts(self, eng, out, in0, s1, s2, op0, op1, reads, writes):
        if s2 is None:
            self.P.add(eng, lambda e: e.tensor_scalar(out=out, in0=in0, scalar1=s1, scalar2=None, op0=op0), reads=reads, writes=writes)
        else:
            self.P.add(eng, lambda e: e.tensor_scalar(out=out, in0=in0, scalar1=s1, scalar2=s2, op0=op0, op1=op1), reads=reads, writes=writes)

    def copy(self, eng, out, in_, reads, writes):
        if eng == "act":
            self.act(out, in_, AF.Copy, reads, writes)
        else:
            self.P.add(eng, lambda e: e.tensor_copy(out=out, in_=in_), reads=reads, writes=writes)

    def evac(self, out, in_, reads, writes):
        self.evr += 1
        self.copy("act" if self.evr % 2 else "dve", out, in_, reads, writes)

    def dump(self, name, t, key):
        if not self.debug:
            return
        shape = list(t.shape)
        d = self.nc.dram_tensor("dbg_" + name, shape, t.dtype, kind="ExternalOutput")
        self.dma("sp", d.ap(), t[:], reads=[key])

    def bank(self):
        self.rr += 1
        i = self.rot[self.rr % len(self.rot)]
        return self.pb[i], "pb%d" % i

    def build(self):
        nc = self.nc
        P = self.P
        dr = lambda name, shape, dt, kind="ExternalInput": nc.dram_tensor(name, list(shape), dt, kind=kind)
        self.x_d = dr("x_loc", [T_ALL, D], F32)
        self.ctx_d = dr("ctx_b", [NCTX, D], F32)
        self.cvec_d = dr("cvec", [2, D], F32)
        self.ada_w_d = dr("ada_w", [2, D, 6 * D], F32)
        self.ada_b_d = dr("ada_b", [2, 6 * D], F32)
        self.n1_d = dr("norm1_g", [2, D], F32)
        self.n2_d = dr("norm2_g", [2, D], F32)
        self.wg_d = dr("ffn_w_gate", [2, D, DFF], F32)
        self.wu_d = dr("ffn_w_up", [2, D, DFF], F32)
        self.wd_d = dr("ffn_w_down", [2, DFF, D], F32)
        self.evin_d = dr("ev_w_in", [D, 1280], F32)
        self.evout_d = dr("ev_w_out", [D, D], F32)
        self.evq_d = dr("ev_q_norm", [64], F32)
        self.evk_d = dr("ev_k_norm", [64], F32)
        self.sink_d = dr("ev_sink", [8], F32)
        self.odin_d = dr("od_w_in", [D, 3 * D], F32)
        self.odout_d = dr("od_w_out", [D, D], F32)
        self.odq_d = dr("od_q_norm", [64], F32)
        self.odk_d = dr("od_k_norm", [64], F32)
        self.rb_d = dr("rel_bias", [16 * 15 * 31 + 64], F32)
        self.cidf_d = dr("c_identf", [128, 128], F32)
        self.cpsw_d = dr("c_pswap", [128, 128], F32)
        self.cbf_d = dr("c_bf", [128, 10, 128], BF16)
        self.cctx_d = dr("c_ctxdft", [128, 2, 2, 256], BF16)
        self.cnam_d = dr("c_namask", [128, 3, 6, 128], BF16)
        self.crope_d = dr("c_rope", [2, 128, T_A], F32)
        self.cdft_d = [dr("c_dftc", [T_ALL, T_X], BF16), dr("c_dfts", [T_ALL, T_X], BF16)]
        self.out_d = dr("out_loc", [T_OWN, D], F32, kind="ExternalOutput")
        if self.debug:
            self.dbgx_d = dr("dbg_x", [128, 8, T_X], F32, kind="ExternalOutput")
            self.dbgy_d = dr("dbg_y", [128, 8, NCTX], F32, kind="ExternalOutput")
            self.dbga_d = dr("dbg_a", [128, 8, T_X], F32, kind="ExternalOutput")

        self.pb = [nc.alloc_psum_tensor("pb%d" % i, [128, 512], F32) for i in range(8)]
        self.eng_sems = {e: nc.alloc_semaphore("s_" + e) for e in Prog.ENGS}
        self.dma_sems = [nc.alloc_semaphore("d%d" % i) for i in range(NDMA_SEM)]

        self.setup()
        done = False
        for name, fn in [("A", self.layer0_A), ("B", self.layer0_B), ("C", self.layer0_C), ("D", lambda: self.ffn(0)),
                         ("E", self.layer1_E), ("F", lambda: self.ffn(1))]:
            if self.only is not None and name not in self.only:
                continue
            if not hasattr(self, "XT") and name in ("D", "E", "F"):
                self.ada_tick(100)
                self.rot = [2, 3, 4, 5, 6, 7]
                self.pers -= 2 * 4096
                self.XT = self.nc.alloc_sbuf_tensor_at("XT_res", [128, 8, T_X], F32, offset=self.hi - 8 * T_X * 4)
            fn()
            if self.stop_after == name:
                done = True
                break
        self.finish(final=not done)
        with nc.Block() as block:
            P.emit(block, self.eng_sems, self.dma_sems)
        return nc

    def setup(self):
        nc = self.nc
        pa = self.palloc
        self.scr = pa("scr", [128, 16], F32)
        self.identf = pa("identf", [128, 128], F32)
        self.pswap = pa("pswap", [128, 128], F32)
        self.cbf = pa("cbf", [128, 10, 128], BF16)
        self.cctx = pa("cctx", [128, 2, 2, 256], BF16)
        self.vecT = pa("vecT", [128, 48], F32)
        self.biasT = pa("biasT", [128, 96], F32)
        self.modT = pa("modT", [128, 2, 48, 2], F32)
        self.A1 = pa("A1", [128, 2, 2, 8], F32)
        self.A2 = pa("A2", [128, 2, 2, 8], F32)
        self.sT = pa("sT", [128, 2, 8], BF16)
        self.gq = pa("gq", [128, 4], F32)
        self.sinkbc = pa("sinkbc", [128, 4, 128], F32)
        self.skt = pa("skt", [128, 4], F32)
        self.zf = pa("zf", [128, 128], F32)
        self.YT = pa("YT", [128, 8, NCTX], F32)
        self.dma("sp", self.identf[:], self.cidf_d.ap(), writes=["c_id"])
        self.dma("sp", self.pswap[:], self.cpsw_d.ap(), writes=["c_ps"])
        self.dma("sp", self.cbf[:], self.cbf_d.ap(), writes=["c_bf"])
        self.dma("sp", self.cctx[:], self.cctx_d.ap(), writes=["c_cx"])
        self.P.add("pool", lambda e: e.memset(self.zf[:], 0.0), writes=["zf"])
        self.P.add("pool", lambda e: e.memset(self.scr[:], 0.0), writes=["scr"])
        self.ones1024 = self.cbf[:, 0, :]
        self.blk64 = self.cbf[:, 1, :]
        self.identb = self.cbf[:, 2, :]
        self.rblk = self.cbf[:, 3, :]
        self.zerob = self.cbf[:, 4, :]
        self.onesb = self.cbf[:, 5, :]
        self.ccos = self.cbf[:, 6, :]
        self.cnsin = self.cbf[:, 7, :]
        self.mprev = self.cbf[:, 8, :]
        self.mnext = self.cbf[:, 9, :]
        self.adabuf = None
        self.cur = self.pers + 2 * 4096
        ta = self.talloc
        rows1 = ta("rows1", [48, 128], F32)
        rows2 = ta("rows2", [96, 128], F32)
        self.dma("sp", rows1[0:16, :], self.n1_d.ap().rearrange("l (c p) -> (l c) p", p=128), writes=["rows1"])
        self.dma("sp", rows1[16:32, :], self.n2_d.ap().rearrange("l (c p) -> (l c) p", p=128), writes=["rows1"])
        self.dma("sp", rows1[32:48, :], self.cvec_d.ap().rearrange("s (c p) -> (s c) p", p=128), writes=["rows1"])
        self.dma("sp", rows2[:, :], self.ada_b_d.ap().rearrange("l (c p) -> (l c) p", p=128), writes=["rows2"])
        b, bk = self.bank()
        self.tr(b[:, 0:48], rows1[:, :], ["rows1"], [bk])
        self.copy("dve", self.vecT[:], b[:, 0:48], [bk], ["vecT"])
        b, bk = self.bank()
        self.tr(b[:, 0:96], rows2[:, :], ["rows2"], [bk])
        self.copy("dve", self.biasT[:], b[:, 0:96], [bk], ["biasT"])
        self.act(self.sT[:].rearrange("p s c -> p (s c)"), self.vecT[:, 32:48], AF.Silu, ["vecT"], ["sT"])
        for i, (d_, sc) in enumerate([(self.evq_d, 0.125), (self.evk_d, 1.0), (self.odq_d, 0.125), (self.odk_d, 1.0)]):
            src = d_.ap().rearrange("(d u) -> d u", u=1)
            self.dma("sp", self.gq[0:64, i:i + 1], src, writes=["gq"])
            self.dma("sp", self.gq[64:128, i:i + 1], src, writes=["gq"])
        self.ts("dve", self.gq[:, 0:1], self.gq[:, 0:1], 0.125, None, ALU.mult, None, ["gq"], ["gq"])
        self.ts("dve", self.gq[:, 2:3], self.gq[:, 2:3], 0.125, None, ALU.mult, None, ["gq"], ["gq"])
        self.dma("sp", self.skt[0:64, :], AP(self.sink_d, 0, [[0, 64], [1, 4]]), writes=["skt"])
        self.dma("sp", self.skt[64:128, :], AP(self.sink_d, 4, [[0, 64], [1, 4]]), writes=["skt"])
        self.act(self.skt[:], self.skt[:], AF.Exp, ["skt"], ["skt"])
        for j in range(4):
            self.ts("dve", self.sinkbc[:, j, :], self.zf[:], self.skt[:, j:j + 1], None, ALU.add, None, ["zf", "skt"], ["sinkbc"])
        self.adabuf = [self.palloc_late("adabuf%d" % i, [128, 8, 256], BF16) for i in range(2)]
        self.ada_list = [(l, sl) for l in range(2) for sl in range(24)]
        self.ada_pos = 0
        self.ada_issued = 0
        self.win = self.nc.alloc_sbuf_tensor_at("win_top", [128, 8, 1280], BF16, offset=self.hi - 20480)
        self.ada_issue()
        self.ada_tick(4)
        self.load_w(self.win[:], "win", self.evin_d.ap().rearrange("(c p) n -> p c n", p=128))
        self.ada_tick(4)

    def palloc_late(self, name, shape, dt):
        t, self.pers = self._alloc(self.pers, name, shape, dt)
        return t

    def ada_issue(self):
        if self.ada_issued >= len(self.ada_list):
            return
        l, sl = self.ada_list[self.ada_issued]
        i = self.ada_issued % 2
        src = self.ada_w_d.ap()[l].rearrange("(c p) n -> p c n", p=128)[:, :, sl * 256:(sl + 1) * 256]
        self.dma("pool", self.adabuf[i][:], src, writes=["adabuf%d" % i])
        self.ada_issued += 1

    def ada_tick(self, n=1):
        for _ in range(n):
            if self.ada_pos >= len(self.ada_list):
                return
            l, sl = self.ada_list[self.ada_pos]
            i = self.ada_pos % 2
            self.ada_pos += 1
            self.ada_issue()
            buf, bk = self.adabuf[i], "adabuf%d" % i
            pm, pmk = self.pb[7], "pb7"
            for o2 in range(2):
                oc = sl * 2 + o2
                for kc in range(8):
                    self.mm(pm[:, oc * 2:oc * 2 + 2], buf[:, kc, o2 * 128:(o2 + 1) * 128], self.sT[:, :, kc],
                            kc == 0, kc == 7, [bk, "sT"], [pmk])
            if sl == 7 or sl == 23:
                lo, hi = (0, 16) if sl == 7 else (16, 48)
                for s_ in range(2):
                    self.tt("dve", self.modT[:, l, lo:hi, s_], pm[:, 0:96].rearrange("p (o s) -> p o s", s=2)[:, lo:hi, s_],
                            self.biasT[:, l * 48 + lo:l * 48 + hi], ALU.add, [pmk, "biasT"], ["modT"])
                    if sl == 7:
                        self.stt("dve", self.A1[:, l, s_, :], self.modT[:, l, 8:16, s_], 1.0, self.vecT[:, l * 8:l * 8 + 8],
                                 ALU.add, ALU.mult, ["modT", "vecT"], ["A1"])
                    else:
                        self.stt("dve", self.A2[:, l, s_, :], self.modT[:, l, 32:40, s_], 1.0, self.vecT[:, 16 + l * 8:16 + l * 8 + 8],
                                 ALU.add, ALU.mult, ["modT", "vecT"], ["A2"])

    def mod(self, l, part, s, c):
        return self.modT[:, l, part * 8 + c, s:s + 1]

    def load_xT(self, src_rows, ntile, dst, dst_key, col0, xtoks):
        for t in range(ntile):
            xt, xk = xtoks[self.xti % len(xtoks)]
            self.xti += 1
            self.dma("sp", xt[:], src_rows[t * 128:(t + 1) * 128, :], writes=[xk])
            for half in range(2):
                b, bk = self.bank()
                for cc in range(4):
                    c = half * 4 + cc
                    self.tr(b[:, cc * 128:(cc + 1) * 128], xt[:, c * 128:(c + 1) * 128], [xk], [bk])
                o = dst[:, half * 4:half * 4 + 4, col0 + t * 128:col0 + (t + 1) * 128]
                self.evac(o, b[:, :].rearrange("p (c n) -> p c n", n=128), [bk], [dst_key])

    def norm_mod(self, xT, xkey, col0, N, A, Bf, hT, hkey, hcol0, tmp, part=None, state=None):
        sqbs, rss, tf = tmp
        if part in (None, 1):
            self.nmi += 1
            sqb, sqk = sqbs[self.nmi % len(sqbs)], "sqb%d" % (self.nmi % len(sqbs))
            rs, rk = rss[self.nmi % len(rss)], "rsn%d" % (self.nmi % len(rss))
            nsq = sqb.shape[1]
            b, bk = self.bank()
            for c0 in range(0, 8, nsq):
                for c in range(c0, c0 + nsq):
                    xi = xT[:, c, col0:col0 + N]
                    if c % 4 == 3:
                        self.tt("dve", sqb[:, c % nsq, 0:N], xi, xi, ALU.mult, [xkey], [sqk])
                    else:
                        self.act(sqb[:, c % nsq, 0:N], xi, AF.Square, [xkey], [sqk])
                for c in range(c0, c0 + nsq):
                    self.mm(b[:, 0:N], self.ones1024, sqb[:, c % nsq, 0:N], c == 0, c == 7, [sqk, "c_bf"], [bk])
            self.act(rs[:, 0:N], b[:, 0:N], AF.Ln, [bk], [rk], bias=EPS)
            self.act(rs[:, 0:N], rs[:, 0:N], AF.Exp, [rk], [rk], scale=-0.5)
            state = (rs, rk)
            if part == 1:
                return state
        rs, rk = state
        for c in range(8):
            self.tfi += 1
            t_ = tf[self.tfi % len(tf)]
            tk = "tf%d" % (self.tfi % len(tf))
            self.stt("dve", t_[:, 0:N], xT[:, c, col0:col0 + N], A(c), rs[:, 0:N], ALU.mult, ALU.mult,
                     [xkey, rk, "A1", "A2"], [tk])
            if part == 2 and c % 2 == 1:
                self.ts("dve", hT[:, c, hcol0:hcol0 + N], t_[:, 0:N], Bf(c), None, ALU.add, None, [tk, "modT"], [hkey])
            else:
                self.act(hT[:, c, hcol0:hcol0 + N], t_[:, 0:N], AF.Identity, [tk, "modT"], [hkey], bias=Bf(c))
        return None

    def norm_pipeline(self, calls):
        st = [None] * len(calls)
        if calls:
            st[0] = self.norm_mod(part=1, **calls[0])
        for i in range(len(calls)):
            if i + 1 < len(calls):
                st[i + 1] = self.norm_mod(part=1, **calls[i + 1])
            self.norm_mod(part=2, state=st[i], **calls[i])

    def qk_norm(self, praw, pk, N, gcol, out, okey, tmp, rope=None):
        sq1s, rss, qns, r1s = tmp
        self.qni += 1
        i = self.qni
        sq1, sk = sq1s[i % len(sq1s)], "sq1_%d" % (i % len(sq1s))
        rs, rk = rss[i % len(rss)], "rsq%d" % (i % len(rss))
        self.act(sq1[:, 0:N], praw[:, 0:N], AF.Square, [pk], [sk])
        b, bk = self.bank()
        self.mm(b[:, 0:N], self.blk64, sq1[:, 0:N], True, True, [sk, "c_bf"], [bk])
        self.act(rs[:, 0:N], b[:, 0:N], AF.Ln, [bk], [rk], bias=EPS)
        self.act(rs[:, 0:N], rs[:, 0:N], AF.Exp, [rk], [rk], scale=-0.5)
        if rope is None:
            self.stt("dve", out, praw[:, 0:N], gcol, rs[:, 0:N], ALU.mult, ALU.mult, [pk, rk, "gq"], [okey])
            return
        qn, qk_ = qns[i % len(qns)], "qn%d" % (i % len(qns))
        r1, r1k = r1s[i % len(r1s)], "r1_%d" % (i % len(r1s))
        cosT, sinT, rpk = rope
        self.stt("dve", qn[:, 0:N], praw[:, 0:N], gcol, rs[:, 0:N], ALU.mult, ALU.mult, [pk, rk, "gq"], [qk_])
        b2, bk2 = self.bank()
        self.mm(b2[:, 0:N], self.pswap[:], qn[:, 0:N], True, True, [qk_, "c_bf"], [bk2])
        self.tt("pool", r1[:, 0:N], qn[:, 0:N], cosT, ALU.mult, [qk_] + rpk, [r1k])
        self.tt("dve", rs[:, 0:N], b2[:, 0:N], sinT, ALU.mult, [bk2] + rpk, [rk])
        self.tt("dve", out, rs[:, 0:N], r1[:, 0:N], ALU.add, [rk, r1k], [okey])

    def qk_pipeline(self, jobs, tmp):
        sq1s, rss, qns, r1s = tmp
        st = []
        for j, jb in enumerate(jobs):
            d = dict(jb)
            d["sq1"], d["sk"] = sq1s[j % len(sq1s)], "sq1_%d" % (j % len(sq1s))
            d["rs"], d["rk"] = rss[j % len(rss)], "rsq%d" % (j % len(rss))
            if jb["rope"] is not None:
                d["qn"], d["qk"] = qns[j % len(qns)], "qn%d" % (j % len(qns))
                d["r1"], d["r1k"] = r1s[j % len(r1s)], "r1_%d" % (j % len(r1s))
            st.append(d)

        def sa(d):
            d["b"], d["bk"] = d["proj"]()

        def sb(d):
            N = d["N"]
            praw, pk, sq1, sk, rs, rk = d["b"], d["bk"], d["sq1"], d["sk"], d["rs"], d["rk"]
            self.act(sq1[:, 0:N], praw[:, 0:N], AF.Square, [pk], [sk])
            b, bk = self.bank()
            self.mm(b[:, 0:N], self.blk64, sq1[:, 0:N], True, True, [sk, "c_bf"], [bk])
            self.act(rs[:, 0:N], b[:, 0:N], AF.Ln, [bk], [rk], bias=EPS)
            self.act(rs[:, 0:N], rs[:, 0:N], AF.Exp, [rk], [rk], scale=-0.5)
            if d["rope"] is None:
                self.stt("dve", d["out"], praw[:, 0:N], d["gcol"], rs[:, 0:N], ALU.mult, ALU.mult, [pk, rk, "gq"], [d["okey"]])
            else:
                self.stt("dve", d["qn"][:, 0:N], praw[:, 0:N], d["gcol"], rs[:, 0:N], ALU.mult, ALU.mult, [pk, rk, "gq"], [d["qk"]])

        def sc(d):
            if d["rope"] is None:
                return
            N = d["N"]
            cosT, sinT, rpk = d["rope"]
            qn, qk_, r1, r1k, rs, rk = d["qn"], d["qk"], d["r1"], d["r1k"], d["rs"], d["rk"]
            b2, bk2 = self.bank()
            self.mm(b2[:, 0:N], self.pswap[:], qn[:, 0:N], True, True, [qk_, "c_bf"], [bk2])
            self.tt("pool", r1[:, 0:N], qn[:, 0:N], cosT, ALU.mult, [qk_] + rpk, [r1k])
            self.tt("dve", rs[:, 0:N], b2[:, 0:N], sinT, ALU.mult, [bk2] + rpk, [rk])
            self.tt("dve", d["out"], rs[:, 0:N], r1[:, 0:N], ALU.add, [rk, r1k], [d["okey"]])

        n = len(st)
        for t in range(n + 2):
            if t < n:
                sa(st[t])
            if 0 <= t - 1 < n:
                sb(st[t - 1])
            if 0 <= t - 2 < n:
                sc(st[t - 2])

    def load_w(self, dst, dkey, src):
        self.dma("pool", dst, src, writes=[dkey])

    def layer0_A(self):
        self.phase()
        ta = self.talloc
        self.rot = [0, 1, 2, 3, 4, 5, 6]
        self.QT = ta("QT", [128, 4, T_A], BF16)
        self.KT = ta("KT", [128, T_A], BF16)
        self.V = ta("V", [128, T_A // 128, 128], BF16)
        self.QcT = ta("QcT", [128, 4, NCTX], BF16)
        self.KcT = ta("KcT", [128, NCTX], BF16)
        self.Vc = ta("Vc", [128, 2, 128], BF16)
        fo_off = self.cur
        self.FO = ta("FO", [128, 4, T_X], BF16)
        self.FOc = ta("FOc", [128, 4, NCTX], BF16)
        self.keepC = self.cur
        self.F_all = ta("F_all", [128, 32, 512], BF16)
        self.Fc = ta("Fc", [128, 2, 512], BF16)
        self.keepA = self.cur
        win = self.win
        xtoks = [(ta("xtok%d" % i, [128, 1024], F32), "xtok%d" % i) for i in range(4)]
        self.xti = 0
        self.uid += 1
        xTb2 = self.nc.alloc_sbuf_tensor_at("xTb2_%d" % self.uid, [128, 8, 512], F32, offset=fo_off)
        xTbs = [(ta("xTb", [128, 8, 512], F32), "xTb0"), (xTb2, "xTb1")]
        hTs = [(ta("hT%d" % i, [128, 8, 512], BF16), "hT%d" % i) for i in range(2)]
        sqbs = [ta("sqb", [128, 4, 512], BF16)]
        rss = [ta("rsn%d" % i, [128, 512], F32) for i in range(2)]
        tf = [ta("tf%d" % i, [128, 512], F32) for i in range(2)]
        sq1s = [ta("sq1_%d" % i, [128, 512], BF16) for i in range(2)]
        rsq = [ta("rsq%d" % i, [128, 512], F32) for i in range(2)]
        self.uid += 1
        qns = [self.nc.alloc_sbuf_tensor_at("qn0_%d" % self.uid, [128, 512], F32, offset=fo_off + 16384), ta("qn1", [128, 512], F32)]
        r1s = [self.nc.alloc_sbuf_tensor_at("r10_%d" % self.uid, [128, 512], F32, offset=fo_off + 16384 + 2048)]
        ropeb = [ta("rope%d" % i, [128, 2, 512], F32) for i in range(1)]
        tmpn = (sqbs, rss, tf)
        tmpq = (sq1s, rsq, qns, r1s)
        assert self.cur <= self.hi - 20480, "phase A overflow into win"
        l = 0

        def projA(hT, hk, N, qkv, s, tok0, tile0, ropek):
            for t in range(N // 128):
                b, bk = self.bank()
                for c in range(8):
                    self.mm(b[:, 0:512], hT[:, c, t * 128:(t + 1) * 128], win[:, c, 0:512], c == 0, c == 7, [hk, "win"], [bk])
                dst = self.F_all[:, tile0 + t, :] if s == 0 else self.Fc[:, t, :]
                self.evac(dst, b[:, 0:512], [bk], ["F_all" if s == 0 else "Fc"])
            if not qkv:
                return
            for t in range(N // 128):
                b, bk = self.bank()
                for c in range(8):
                    self.mm(b[:, 0:128], hT[:, c, t * 128:(t + 1) * 128], win[:, c, 1152:1280], c == 0, c == 7, [hk, "win"], [bk])
                dst = self.V[:, tile0 + t, :] if s == 0 else self.Vc[:, t, :]
                self.evac(dst, b[:, 0:128], [bk], ["V" if s == 0 else "Vc"])

        def projB(hT, hk, N, qkv, s, tok0, tile0, ropek):
            if not qkv:
                return
            jobs = []
            for j in range(5):
                def proj(j=j):
                    b, bk = self.bank()
                    for c in range(8):
                        self.mm(b[:, 0:N], win[:, c, 512 + j * 128:512 + (j + 1) * 128], hT[:, c, 0:N], c == 0, c == 7, [hk, "win"], [bk])
                    return b, bk
                if s == 0:
                    out = self.QT[:, j, tok0:tok0 + N] if j < 4 else self.KT[:, tok0:tok0 + N]
                    okey = "QT" if j < 4 else "KT"
                    rb_ = ropeb[0]
                    rope = (rb_[:, 0, 0:N], rb_[:, 1, 0:N], ["rope0c", "rope0s"])
                else:
                    out = self.QcT[:, j, 0:N] if j < 4 else self.KcT[:, 0:N]
                    okey = "QcT" if j < 4 else "KcT"
                    rope = None
                jobs.append(dict(proj=proj, N=N, gcol=self.gq[:, 0:1] if j < 4 else self.gq[:, 1:2], out=out, okey=okey, rope=rope))
            self.qk_pipeline(jobs, tmpq)

        items = [("ctx", 0)] + [("lat", g) for g in range(8)]

        def stage1(idx):
            kind, g = items[idx]
            hT, hk = hTs[idx % 2]
            if kind == "ctx":
                self.load_xT(self.ctx_d.ap(), 2, self.YT, "YT", 0, xtoks)
                return (hT, hk, NCTX, True, 1, 0, 0, 0, self.YT, "YT")
            tok0 = g * 512
            xTb, xk = xTbs[idx % 2]
            self.load_xT(self.x_d.ap()[tok0:tok0 + 512, :], 4, xTb, xk, 0, xtoks)
            return (hT, hk, 512, g < 5, 0, tok0, g * 4, g, xTb, xk)

        def stage2(st, part, state=None):
            hT, hk, N, qkv, s_, tok0, tile0, rk, xT, xk = st
            return self.norm_mod(xT, xk, 0, N, lambda c: self.A1[:, l, s_, c:c + 1], lambda c: self.mod(l, 0, s_, c),
                                 hT, hk, 0, tmpn, part=part, state=state)

        cur = stage1(0)
        stage2(cur, 2, stage2(cur, 1))
        for idx in range(len(items)):
            nxt = stage1(idx + 1) if idx + 1 < len(items) else None
            projA(*cur[:8])
            self.ada_tick(1)
            if nxt is not None:
                stage2(nxt, 2, stage2(nxt, 1))
            projB(*cur[:8])
            if nxt is not None and nxt[3] and nxt[4] == 0:
                rb_ = ropeb[0]
                tk0 = nxt[5]
                self.dma("sp", rb_[:, 0, :], self.crope_d.ap()[0, :, tk0:tk0 + 512], writes=["rope0c"])
                self.dma("sp", rb_[:, 1, :], self.crope_d.ap()[1, :, tk0:tk0 + 512], writes=["rope0s"])
            self.ada_tick(1)
            cur = nxt
        if self.stop_after == "A":
            for nm_, t_, k_ in [("F_all", self.F_all, "F_all"), ("QT", self.QT, "QT"), ("KT", self.KT, "KT"), ("V", self.V, "V"),
                                ("Fc", self.Fc, "Fc"), ("QcT", self.QcT, "QcT"), ("KcT", self.KcT, "KcT"), ("Vc", self.Vc, "Vc"),
                                ("modT", self.modT, "modT")]:
                self.dump(nm_, t_, k_)

    def dft(self, Fsrc, fkey, ntile, tabs, K, FO, fokey, tabbuf, ABT):
        ti = 0
        for (k0, N) in groups(K):
            for cs in range(2):
                tb, tk = tabbuf[ti % 2], "tab%d" % (ti % 2)
                ti += 1
                src_t = tabs(cs, k0, N)
                for a0 in range(0, ntile, 8):
                    self.dma("sp", tb[:, a0:a0 + 8, 0:N], src_t[:, a0:a0 + 8, :], writes=[tk])
                for g in range(4):
                    b, bk = self.bank()
                    for a in range(ntile):
                        self.mm(b[:, 0:N], Fsrc[:, a, g * 128:(g + 1) * 128], tb[:, a, 0:N], a == 0, a == ntile - 1, [fkey, tk], [bk])
                    self.evac(ABT[:, cs, g, 0:N], b[:, 0:N], [bk], ["ABT"])
                    self.ada_tick(1)
            for g in range(4):
                b, bk = self.bank()
                self.mm(b[:, 0:N], self.ccos, ABT[:, 0, g, 0:N], True, False, ["ABT", "c_bf"], [bk])
                self.mm(b[:, 0:N], self.cnsin, ABT[:, 1, g, 0:N], False, True, ["ABT", "c_bf"], [bk])
                self.evac(FO[:, g, k0:k0 + N], b[:, 0:N], [bk], [fokey])

    def layer0_B(self):
        scr = self.scr
        self.P.barrier(lambda e: e.memset(scr[0:1, 0:8], 0.0))
        self.cur = self.keepA
        ta = self.talloc
        tabbuf = [ta("tab%d" % i, [128, 32, 512], BF16) for i in range(2)]
        ABT = ta("ABT", [128, 2, 4, 512], BF16)
        for cs in range(2):
            for g in range(4):
                b, bk = self.bank()
                for a in range(2):
                    self.mm(b[:, 0:NCTX], self.Fc[:, a, g * 128:(g + 1) * 128], self.cctx[:, cs, a, :], a == 0, a == 1, ["Fc", "c_bf"], [bk])
                self.evac(ABT[:, cs, g, 0:NCTX], b[:, 0:NCTX], [bk], ["ABT"])
        for g in range(4):
            b, bk = self.bank()
            self.mm(b[:, 0:NCTX], self.ccos, ABT[:, 0, g, 0:NCTX], True, False, ["ABT", "c_bf"], [bk])
            self.mm(b[:, 0:NCTX], self.cnsin, ABT[:, 1, g, 0:NCTX], False, True, ["ABT", "c_bf"], [bk])
            self.evac(self.FOc[:, g, :], b[:, 0:NCTX], [bk], ["FOc"])
        tabs = lambda cs, k0, N: self.cdft_d[cs].ap().rearrange("(a p) k -> p a k", p=128)[:, :, k0:k0 + N]
        self.dft(self.F_all, "F_all", 32, tabs, T_X, self.FO, "FO", tabbuf, ABT)
        self.ada_tick(100)
        self.rot = [2, 3, 4, 5, 6, 7]
        self.pers -= 2 * 4096
        if self.stop_after == "B":
            self.dump("FO", self.FO, "FO")
            self.dump("FOc", self.FOc, "FOc")

    def layer0_C(self):
        scr = self.scr
        self.P.barrier(lambda e: e.memset(scr[0:1, 0:8], 0.0))
        self.cur = self.keepC
        ta = self.talloc
        l = 0
        wout = ta("wout", [128, 8, D], BF16)
        self.load_w(wout[:], "wout", self.evout_d.ap().rearrange("(c p) n -> p c n", p=128))
        ATTs = [(ta("ATT%d" % i, [128, 4, 512], BF16), "ATT%d" % i) for i in range(2)]
        qlos = [(ta("qlo%d" % i, [128, 4, 128], BF16), "qlo%d" % i) for i in range(2)]
        qhis = [(ta("qhi%d" % i, [128, 4, 128], BF16), "qhi%d" % i) for i in range(2)]
        PT = [ta("PT%d" % i, [128, 512], BF16) for i in range(4)]
        dtots = [(ta("dtot%d" % i, [128, 512], F32), "dtot%d" % i) for i in range(2)]
        xtoks = [(ta("xtokC%d" % i, [128, 1024], F32), "xtokC%d" % i) for i in range(3)]
        self.xti = 0
        self.XT = self.nc.alloc_sbuf_tensor_at("XT_res", [128, 8, T_X], F32, offset=self.hi - 8 * T_X * 4)
        assert self.cur <= self.hi - 8 * T_X * 4, "phase C overflow"
        for (q_, k_) in qlos + qhis:
            self.P.add("pool", lambda e, q_=q_: e.memset(q_[:], 0.0), writes=[k_])
        self.rot = [4, 5, 6, 7]
        accs = [((self.pb[0], "pb0"), (self.pb[1], "pb1")), ((self.pb[2], "pb2"), (self.pb[3], "pb3"))]
        ctxtiles = [(self.KcT[:, t * 128:(t + 1) * 128], self.Vc[:, t, :], None, ["KcT", "Vc"]) for t in range(2)]

        jobs = []

        def outproj(N, FOsrc, fokey, focol, Xdst, xkey, xcol, s_, ATT, ak):
            for oc in range(8):
                b, bk = self.bank()
                for kc in range(8):
                    rhs = FOsrc[:, kc, focol:focol + N] if kc < 4 else ATT[:, kc - 4, 0:N]
                    self.mm(b[:, 0:N], wout[:, kc, oc * 128:(oc + 1) * 128], rhs, kc == 0, kc == 7, ["wout", fokey, ak], [bk])
                xo = Xdst[:, oc, xcol:xcol + N]
                self.stt("dve", xo, b[:, 0:N], self.mod(l, 2, s_, oc), xo, ALU.mult, ALU.add, [bk, "modT", xkey], [xkey])

        gi = 0
        for qt in range(2):
            post = (lambda ai=gi % 2: outproj(NCTX, self.FOc, "FOc", 0, self.YT, "YT", 0, 1, *ATTs[ai])) if qt == 1 else None
            jobs.append((self.QcT, "QcT", qt * 128, ctxtiles, gi % 2, qt * 128, None, post))
        gi += 1
        for (t0, N) in groups(T_X):
            for tq in range(N // 128):
                qt = t0 // 128 + tq
                kts = []
                for dlt, mask in ((-1, self.mprev), (0, None), (1, self.mnext)):
                    kt = qt + dlt
                    if kt < 0:
                        continue
                    kts.append((self.KT[:, kt * 128:(kt + 1) * 128], self.V[:, kt, :], mask, ["KT", "V"]))
                pre = (lambda t0=t0, N=N: self.load_xT(self.x_d.ap()[t0:t0 + N, :], N // 128, self.XT, "XT%d" % (t0 // 512), t0, xtoks)) if tq == 0 else None
                post = (lambda t0=t0, N=N, ai=gi % 2: outproj(N, self.FO, "FO", t0, self.XT, "XT%d" % (t0 // 512), t0, 0, *ATTs[ai])) \
                    if tq == N // 128 - 1 else None
                jobs.append((self.QT, "QT", qt * 128, kts + ctxtiles, gi % 2, tq * 128, pre, post))
            gi += 1

        steps = []
        for ji, job in enumerate(jobs):
            n = len(job[3])
            for kvh in range(2):
                for i in range(n):
                    steps.append((ji, kvh, i, n))
        pti = [0]

        def s_stage(st):
            ji, kvh, i, n = st
            QTsrc, qkey, qcol, keytiles, ai, dcol, pre, post = jobs[ji]
            (qlo, qlk), (qhi, qhk) = qlos[ji % 2], qhis[ji % 2]
            (pbO, ok_), (pbD, dk_) = accs[ji % 2]
            if kvh == 0 and i == 0:
                if pre is not None:
                    pre()
                self.copy("pool", qlo[0:64, :, :], QTsrc[0:64, :, qcol:qcol + 128], [qkey], [qlk])
                self.copy("pool", qhi[64:128, :, :], QTsrc[64:128, :, qcol:qcol + 128], [qkey], [qhk])
            qp, qk_ = (qlo, qlk) if kvh == 0 else (qhi, qhk)
            kap, vap, mask, kkeys = keytiles[i]
            b, bk = self.bank()
            self.mm(b[:, 0:512], kap, qp[:].rearrange("p j n -> p (j n)"), True, mask is None, [qk_] + kkeys, [bk])
            if mask is not None:
                self.mm(b[:, 0:512], self.identb, mask.unsqueeze(1).broadcast_to([128, 4, 128]), False, True, ["c_bf"], [bk])
            pt = PT[pti[0] % 4]
            pk = "PT%d" % (pti[0] % 4)
            pti[0] += 1
            self.act(pt[:], b[:, 0:512], AF.Exp, [bk], [pk])
            return (pt, pk)

        def p_stage(st, pt, pk):
            ji, kvh, i, n = st
            QTsrc, qkey, qcol, keytiles, ai, dcol, pre, post = jobs[ji]
            (pbO, ok_), (pbD, dk_) = accs[ji % 2]
            kap, vap, mask, kkeys = keytiles[i]
            rows = slice(kvh * 64, kvh * 64 + 64)
            self.mm(pbO[rows, 0:512], vap[:, kvh * 64:kvh * 64 + 64], pt[:], i == 0, i == n - 1, [pk] + kkeys, [ok_])
            self.mm(pbD[rows, 0:512], self.onesb[:, 0:64], pt[:], i == 0, i == n - 1, [pk, "c_bf"], [dk_])
            if kvh == 1 and i == n - 1:
                dtot, dtk = dtots[ji % 2]
                ATT, ak = ATTs[ai]
                self.tt("dve", dtot[:], pbD[:, 0:512], self.sinkbc[:].rearrange("p j n -> p (j n)"), ALU.add, [dk_, "sinkbc"], [dtk])
                self.act(dtot[:], dtot[:], AF.Ln, [dtk], [dtk])
                self.act(dtot[:], dtot[:], AF.Exp, [dtk], [dtk], scale=-1.0)
                self.tt("dve", ATT[:, :, dcol:dcol + 128], pbO[:, 0:512].rearrange("p (j n) -> p j n", n=128),
                        dtot[:].rearrange("p (j n) -> p j n", n=128), ALU.mult, [ok_, dtk], [ak])
                if post is not None:
                    post()

        prev = None
        for st in steps:
            cur = s_stage(st)
            if prev is not None:
                p_stage(*prev)
            prev = (st, cur[0], cur[1])
        p_stage(*prev)
        self.rot = [2, 3, 4, 5, 6, 7]

    def ffn(self, l):
        scr = self.scr
        self.P.barrier(lambda e: e.memset(scr[0:1, 0:8], 0.0))
        self.cur = self.pers
        self.rot = [0, 1, 2, 3, 4, 5, 6, 7]
        ta = self.talloc
        T = T_X if l == 0 else T_OWN
        toks = [(t0, N, 0) for (t0, N) in groups(T)]
        HT = T + (NCTX if l == 0 else 0)
        h2 = ta("h2", [128, 8, HT], BF16)
        keepF = self.cur
        sqbs = [ta("sqb%d" % i, [128, 8, 512], BF16) for i in range(2)]
        rss = [ta("rsn%d" % i, [128, 512], F32) for i in range(2)]
        tf = [ta("tf%d" % i, [128, 512], F32) for i in range(4)]
        tmpn = (sqbs, rss, tf)
        xkey = lambda t0: "XT%d" % (t0 // 512)
        calls = [dict(xT=self.XT, xkey=xkey(t0), col0=t0, N=N, A=(lambda c: self.A2[:, l, 0, c:c + 1]),
                      Bf=(lambda c: self.mod(l, 3, 0, c)), hT=h2, hkey="h2_%d" % (t0 // 512), hcol0=t0, tmp=tmpn) for (t0, N, _) in toks]
        if l == 0:
            calls.append(dict(xT=self.YT, xkey="YT", col0=0, N=NCTX, A=(lambda c: self.A2[:, l, 1, c:c + 1]),
                              Bf=(lambda c: self.mod(l, 3, 1, c)), hT=h2, hkey="h2_c", hcol0=T, tmp=tmpn))
            toks = toks + [(T, NCTX, 1)]
        self.norm_pipeline(calls)
        self.P.barrier(lambda e: e.memset(scr[0:1, 0:8], 0.0))
        self.cur = keepF
        splits = [(0, 4), (4, 4), (8, 4), (12, 4), (16, 3), (19, 3)]
        wgb = [ta("wg%d" % i, [128, 8, 512], BF16) for i in range(2)]
        wub = [ta("wu%d" % i, [128, 8, 512], BF16) for i in range(2)]
        wdb = [ta("wd%d" % i, [128, 4, D], BF16) for i in range(2)]
        actb = [ta("act%d" % i, [128, 4, 512], BF16) for i in range(2)]
        sg = [ta("sg%d" % i, [128, 512], F32) for i in range(3)]
        obs = [ta("ob%d" % i, [128, D], F32) for i in range(2)] if l == 1 else None
        assert self.cur <= self.hi - 8 * T_X * 4, "ffn overflow"
        ai = 0
        si = 0
        for sp_i, (j0, nj) in enumerate(splits):
            wi = sp_i % 2
            wg, wu, wd = wgb[wi], wub[wi], wdb[wi]
            kg, ku, kd = "wg%d" % wi, "wu%d" % wi, "wd%d" % wi
            self.load_w(wg[:, :, 0:nj * 128], kg, self.wg_d.ap()[l].rearrange("(c p) n -> p c n", p=128)[:, :, j0 * 128:(j0 + nj) * 128])
            self.load_w(wu[:, :, 0:nj * 128], ku, self.wu_d.ap()[l].rearrange("(c p) n -> p c n", p=128)[:, :, j0 * 128:(j0 + nj) * 128])
            self.load_w(wd[:, 0:nj, :], kd, self.wd_d.ap()[l].rearrange("(j p) n -> p j n", p=128)[:, j0:j0 + nj, :])
            for (t0, N, s) in toks:
                ab = actb[ai % 2]
                ak = "act%d" % (ai % 2)
                ai += 1
                hk = "h2_c" if s == 1 else "h2_%d" % (t0 // 512)
                for jj in range(nj):
                    bg, bgk = self.bank()
                    for c in range(8):
                        self.mm(bg[:, 0:N], wg[:, c, jj * 128:(jj + 1) * 128], h2[:, c, t0:t0 + N], c == 0, c == 7, [kg, hk], [bgk])
                    bu, buk = self.bank()
                    for c in range(8):
                        self.mm(bu[:, 0:N], wu[:, c, jj * 128:(jj + 1) * 128], h2[:, c, t0:t0 + N], c == 0, c == 7, [ku, hk], [buk])
                    s_ = sg[si % 3]
                    sk = "sg%d" % (si % 3)
                    si += 1
                    self.act(s_[:, 0:N], bg[:, 0:N], AF.Silu, [bgk], [sk])
                    self.tt("dve", ab[:, jj, 0:N], bu[:, 0:N], s_[:, 0:N], ALU.mult, [buk, sk], [ak])
                X, xk, xc = (self.XT, xkey(t0), t0) if s == 0 else (self.YT, "YT", 0)
                for oc in range(8):
                    b, bk = self.bank()
                    for jj in range(nj):
                        self.mm(b[:, 0:N], wd[:, jj, oc * 128:(oc + 1) * 128], ab[:, jj, 0:N], jj == 0, jj == nj - 1, [kd, ak], [bk])
                    xo = X[:, oc, xc:xc + N]
                    self.stt("dve", xo, b[:, 0:N], self.mod(l, 5, s, oc), xo, ALU.mult, ALU.add, [bk, "modT", xk], [xk])
                if l == 1 and sp_i == len(splits) - 1:
                    self.emit_out(range(t0 // 128, (t0 + N) // 128), obs)
        if l == 1:
            self.out_done = True

    def layer1_E(self):
        scr = self.scr
        self.P.barrier(lambda e: e.memset(scr[0:1, 0:8], 0.0))
        self.cur = self.pers
        ta = self.talloc
        l = 1
        NK = T_X + NCTX
        hT = ta("hT1", [128, 8, NK], BF16)
        keepE = self.cur
        sqbs = [ta("sqb%d" % i, [128, 8, 512], BF16) for i in range(2)]
        rss = [ta("rsn%d" % i, [128, 512], F32) for i in range(2)]
        tf = [ta("tf%d" % i, [128, 512], F32) for i in range(4)]
        tmpn = (sqbs, rss, tf)
        calls = [dict(xT=self.XT, xkey="XT%d" % (t0 // 512), col0=t0, N=N, A=(lambda c: self.A1[:, l, 0, c:c + 1]),
                      Bf=(lambda c: self.mod(l, 0, 0, c)), hT=hT, hkey="hT1_%d" % (t0 // 512), hcol0=t0, tmp=tmpn) for (t0, N) in groups(T_X)]
        calls.append(dict(xT=self.YT, xkey="YT", col0=0, N=NCTX, A=(lambda c: self.A1[:, l, 1, c:c + 1]),
                          Bf=(lambda c: self.mod(l, 0, 1, c)), hT=hT, hkey="hT1_%d" % (T_X // 512), hcol0=T_X, tmp=tmpn))
        self.norm_pipeline(calls)
        self.P.barrier(lambda e: e.memset(scr[0:1, 0:8], 0.0))
        self.cur = keepE
        off_rsq = self.cur
        rsq = [ta("rsq%d" % i, [128, 512], F32) for i in range(2)]
        self.uid += 1
        paccs = [(self.nc.alloc_sbuf_tensor_at("pacc%d_%d" % (i, self.uid), [128, 512], BF16, offset=off_rsq + i * 1024), "acc%d" % i) for i in range(4)]
        acck = ["acc%d" % i for i in range(4)]
        sq1s = [ta("sq1_%d" % i, [128, 512], BF16) for i in range(1)]
        tmpq = (sq1s, rsq, None, None)
        namask = ta("namask", [128, 3, 6, 128], BF16)
        self.dma("sp", namask[:], self.cnam_d.ap(), writes=["namask"])
        wq = ta("wq", [128, 8, 256], BF16)
        wk = ta("wk", [128, 8, 256], BF16)
        wv = ta("wv", [128, 8, 256], BF16)
        wo = ta("wo", [128, 2, D], BF16)
        Tt = ta("Tt", [128, 6, 4, 128], BF16)
        BMI = ta("BMI", [128, 5, 4, 128], BF16)
        bmtmp = rsq[0]
        KTh = ta("KTh", [128, 2, NK], BF16)
        Vh = ta("Vh", [128, NK // 128, 256], BF16)
        QTh = ta("QTh", [128, 2, T_OWN], BF16)
        ATTs = [(ta("ATT1_%d" % i, [128, 2, 512], BF16), "ATT1_%d" % i) for i in range(2)]
        qpads = [[(ta("qpad%d_%d" % (e, i), [128, 2, 128], BF16), "qpad%d_%d" % (e, i)) for e in range(2)] for i in range(2)]
        PT = [ta("PT1_%d" % i, [128, 512], BF16) for i in range(5)]
        dtots = [(ta("dtot1_%d" % i, [128, 512], F32), "dtot1_%d" % i) for i in range(1)]
        assert self.cur <= self.hi - 8 * T_X * 4, "layer1 overflow %d" % (self.cur - (self.hi - 8 * T_X * 4))
        for i in range(2):
            for e_ in range(2):
                q_, k_ = qpads[i][e_]
                self.P.add("pool", lambda e, q_=q_: e.memset(q_[:], 0.0), writes=[k_])
        self.rot = [4, 5, 6, 7]
        accs = [((self.pb[0], "pb0"), (self.pb[1], "pb1")), ((self.pb[2], "pb2"), (self.pb[3], "pb3"))]
        z256 = self.cbf[:, 4:6, :].rearrange("p a n -> p (a n)")
        z512 = self.cbf[:, 4:8, :].rearrange("p a n -> p (a n)")
        hkey = lambda t0: "hT1_%d" % (t0 // 512)
        pti = [0]
        tglob = [0]
        src = self.odin_d.ap().rearrange("(c p) n -> p c n", p=128)

        def load_qkv(hg_):
            self.load_w(wq[:], "wq", src[:, :, hg_ * 256:(hg_ + 1) * 256])
            self.load_w(wk[:], "wk", src[:, :, D + hg_ * 256:D + (hg_ + 1) * 256])
            self.load_w(wv[:], "wv", src[:, :, 2 * D + hg_ * 256:2 * D + (hg_ + 1) * 256])

        def load_Tt(hg_):
            for dl in range(6):
                for kr in range(2):
                    for qr in range(2):
                        dr_idx = 2 * (dl - 2) + kr - qr + 7
                        srcb = AP(self.rb_d, hg_ * 4 * 465 + dr_idx * 31 - 48, [[1, 64], [465, 4], [1, 64]])
                        self.dma("pool", Tt[qr * 64:(qr + 1) * 64, dl, :, kr * 64:(kr + 1) * 64], srcb, writes=["Tt%d" % dl])

        load_qkv(0)
        for hg in range(4):
            self.load_w(wo[:], "wo", self.odout_d.ap().rearrange("(c p) n -> p c n", p=128)[:, hg * 2:hg * 2 + 2, :])
            if hg == 0:
                load_Tt(0)
            for dl in range(5):
                b, bk = self.bank()
                for h4 in range(4):
                    self.mm(b[:, h4 * 128:(h4 + 1) * 128], Tt[:, dl, h4, :], self.rblk, True, True, ["Tt%d" % dl, "c_bf"], [bk])
                self.tt("dve", BMI[:, dl, :, :], b[:, 0:512].rearrange("p (h n) -> p h n", n=128),
                        namask[:, 2, dl, :].unsqueeze(1).broadcast_to([128, 4, 128]), ALU.add, [bk, "namask"], ["BMI"])
            self.rot = [0, 1, 2, 3, 4, 5, 6, 7]
            self.P.add("dve", lambda e: e.memset(scr[0:1, 8:12], 0.0), reads=acck, writes=["rsq0", "rsq1"])
            for (t0, N) in groups(NK):
                jobs = []
                for ch in range(2):
                    def projk(ch=ch, t0=t0, N=N):
                        b, bk = self.bank()
                        for c in range(8):
                            self.mm(b[:, 0:N], wk[:, c, ch * 128:(ch + 1) * 128], hT[:, c, t0:t0 + N], c == 0, c == 7, ["wk", hkey(t0)], [bk])
                        return b, bk
                    jobs.append(dict(proj=projk, N=N, gcol=self.gq[:, 3:4], out=KTh[:, ch, t0:t0 + N], okey="KTh", rope=None))
                    if t0 + N <= T_OWN:
                        def projq(ch=ch, t0=t0, N=N):
                            b, bk = self.bank()
                            for c in range(8):
                                self.mm(b[:, 0:N], wq[:, c, ch * 128:(ch + 1) * 128], hT[:, c, t0:t0 + N], c == 0, c == 7, ["wq", hkey(t0)], [bk])
                            return b, bk
                        jobs.append(dict(proj=projq, N=N, gcol=self.gq[:, 2:3], out=QTh[:, ch, t0:t0 + N], okey="QTh", rope=None))
                self.qk_pipeline(jobs, tmpq)
                for t in range(N // 128):
                    b, bk = self.bank()
                    for c in range(8):
                        self.mm(b[:, 0:256], hT[:, c, t0 + t * 128:t0 + (t + 1) * 128], wv[:, c, :], c == 0, c == 7, ["wv", hkey(t0)], [bk])
                    self.evac(Vh[:, t0 // 128 + t, :], b[:, 0:256], [bk], ["Vh"])
            self.rot = [4, 5, 6, 7]
            self.P.add("dve", lambda e: e.memset(scr[0:1, 12:16], 0.0), reads=["rsq0", "rsq1"], writes=acck)
            if hg + 1 < 4:
                load_qkv(hg + 1)
            steps = []
            for qt in range(T_OWN // 128):
                kts = [(qt + d_, d_ + 2) for d_ in range(-2, 4 if qt == 0 else 3) if qt + d_ >= 0] + \
                      [(T_X // 128, None), (T_X // 128 + 1, None)]
                for i, (kt, dl) in enumerate(kts):
                    steps.append((qt, i, len(kts), kt, dl))

            def s_stage(st):
                qt, i, n, kt, dl = st
                tq_ = tglob[0] + qt
                qp = qpads[tq_ % 2]
                (pbO, ok_), (pbD, dk_) = accs[tq_ % 2]
                if i == 0:
                    for e_ in range(2):
                        rows = slice(e_ * 64, e_ * 64 + 64)
                        self.copy("pool", qp[e_][0][rows, :, :], QTh[rows, :, qt * 128:(qt + 1) * 128], ["QTh"], [qp[e_][1]])
                    self.mm(pbO[:, 0:256], self.zerob, z256, True, False, ["c_bf"], [ok_])
                b, bk = self.bank()
                first = True
                if dl is not None:
                    if qt >= 2:
                        self.mm(b[:, 0:512], self.identb, BMI[:, dl, :, :].rearrange("p h n -> p (h n)"), True, False, ["BMI", "c_bf"], [bk])
                        first = False
                    else:
                        self.mm(b[:, 0:512], self.identb, namask[:, qt, dl, :].unsqueeze(1).broadcast_to([128, 4, 128]), True, False,
                                ["namask", "c_bf"], [bk])
                        for h4 in range(4):
                            self.mm(b[:, h4 * 128:(h4 + 1) * 128], Tt[:, dl, h4, :], self.rblk, False, False, ["Tt%d" % dl, "c_bf"], [bk])
                        first = False
                for h4 in range(4):
                    ch, e_ = h4 // 2, h4 % 2
                    self.mm(b[:, h4 * 128:(h4 + 1) * 128], KTh[:, ch, kt * 128:(kt + 1) * 128], qp[e_][0][:, ch, :], first, True,
                            ["KTh", qp[e_][1]], [bk])
                pt = PT[pti[0] % 5]
                pk = "PT1_%d" % (pti[0] % 5)
                pti[0] += 1
                self.act(pt[:], b[:, 0:512], AF.Exp, [bk], [pk])
                acc, ak_ = paccs[(tq_ % 2) * 2 + (i % 2)]
                if i < 2:
                    self.copy("dve", acc[:], pt[:], [pk], [ak_])
                else:
                    self.tt("dve", acc[:], acc[:], pt[:], ALU.add, [ak_, pk], [ak_])
                return (pt, pk)

            def p_stage(st, pt, pk):
                qt, i, n, kt, dl = st
                tq_ = tglob[0] + qt
                (pbO, ok_), (pbD, dk_) = accs[tq_ % 2]
                for h4 in range(4):
                    ch, e_ = h4 // 2, h4 % 2
                    rows = slice(e_ * 64, e_ * 64 + 64)
                    self.mm(pbO[rows, ch * 128:(ch + 1) * 128], Vh[:, kt, h4 * 64:(h4 + 1) * 64], pt[:, h4 * 128:(h4 + 1) * 128],
                            False, True, [pk, "Vh"], [ok_])
                if i == n - 1:
                    accA, akA = paccs[(tq_ % 2) * 2]
                    accB, akB = paccs[(tq_ % 2) * 2 + 1]
                    self.mm(pbD[:, 0:512], self.onesb, accA[:], True, False, [akA, "c_bf"], [dk_])
                    self.mm(pbD[:, 0:512], self.onesb, accB[:], False, True, [akB, "c_bf"], [dk_])
                    dtot, dtk = dtots[0]
                    ATT, ak = ATTs[(qt // 4) % 2]
                    tq = qt % 4
                    self.act(dtot[:], pbD[:, 0:512], AF.Ln, [dk_], [dtk])
                    self.act(dtot[:], dtot[:], AF.Exp, [dtk], [dtk], scale=-1.0)
                    dv = dtot[:].rearrange("p (c e n) -> p c e n", e=2, n=128)
                    for e_ in range(2):
                        rows = slice(e_ * 64, e_ * 64 + 64)
                        self.tt("dve", ATT[rows, :, tq * 128:(tq + 1) * 128], pbO[rows, 0:256].rearrange("p (j n) -> p j n", n=128),
                                dv[rows, :, e_, :], ALU.mult, [ok_, dtk], [ak])
                    if tq == 3:
                        t0 = (qt // 4) * 512
                        for oc in range(8):
                            b, bk = self.bank()
                            for ch in range(2):
                                self.mm(b[:, 0:512], wo[:, ch, oc * 128:(oc + 1) * 128], ATT[:, ch, 0:512], ch == 0, ch == 1, ["wo", ak], [bk])
                            xo = self.XT[:, oc, t0:t0 + 512]
                            xk = "XT%d" % (t0 // 512)
                            self.stt("dve", xo, b[:, 0:512], self.mod(l, 2, 0, oc), xo, ALU.mult, ALU.add, [bk, "modT", xk], [xk])

            if "noattn" in self.flags:
                steps = []
            if "few" in self.flags:
                steps = steps[:self.nstep]
            pend = []
            for si_, st in enumerate(steps):
                cur = s_stage(st)
                if st[0] == 1 and si_ + 1 < len(steps) and steps[si_ + 1][0] == 2 and hg + 1 < 4:
                    load_Tt(hg + 1)
                pend.append((st, cur[0], cur[1]))
                if len(pend) > 2:
                    p_stage(*pend.pop(0))
            for pp in pend:
                p_stage(*pp)
            tglob[0] += T_OWN // 128
            if "onehg" in self.flags:
                break
        self.rot = [2, 3, 4, 5, 6, 7]

    def finish(self, final=True):
        scr = self.scr
        self.P.barrier(lambda e: e.memset(scr[0:1, 0:8], 0.0))
        self.cur = self.pers
        ta = self.talloc
        if self.debug and hasattr(self, "XT"):
            self.dma("sp", self.dbgx_d.ap(), self.XT[:], reads=["XT%d" % i for i in range(5)])
            self.dma("sp", self.dbgy_d.ap(), self.YT[:], reads=["YT"])
        if not hasattr(self, "XT"):
            return
        if getattr(self, "out_done", False):
            return
        ob = [ta("ob%d" % i, [128, D], F32) for i in range(2)]
        self.emit_out(range(T_OWN // 128), ob)

    def emit_out(self, tiles, ob):
        for t in tiles:
            o_, ok = ob[t % 2], "ob%d" % (t % 2)
            for half in range(2):
                b, bk = self.bank()
                for cc in range(4):
                    c = half * 4 + cc
                    self.tr(b[:, cc * 128:(cc + 1) * 128], self.XT[:, c, t * 128:(t + 1) * 128], ["XT%d" % (t // 4)], [bk])
                self.evac(o_[:, half * 512:(half + 1) * 512], b[:, 0:512], [bk], [ok])
            self.dma("sp", self.out_d.ap()[t * 128:(t + 1) * 128, :], o_[:], reads=[ok])


_CONST_CACHE = {}


def _bf(a):
    return np.ascontiguousarray(a.astype(ml_dtypes.bfloat16))


def host_consts(par):
    if par in _CONST_CACHE:
        return _CONST_CACHE[par]
    c = {}
    c["c_identf"] = np.eye(128, dtype=np.float32)
    psw = np.zeros((128, 128), np.float32)
    for m in range(128):
        i = m % 32
        partner = m + 16 if i < 16 else m - 16
        psw[partner, m] = 1.0
    c["c_pswap"] = psw
    cb = np.zeros((128, 10, 128), np.float32)
    cb[:, 0, :] = 1.0 / 1024.0
    cb[0:64, 1, 0:64] = 1.0 / 64.0
    cb[64:128, 1, 64:128] = 1.0 / 64.0
    cb[:, 2, :] = np.eye(128)
    for blk in range(2):
        for u in range(64):
            cb[blk * 64 + u, 3, blk * 64 + 63 - u] = 1.0
    cb[:, 5, :] = 1.0
    cc = np.arange(128)
    ang = 2.0 * np.pi * ((cc[:, None] * cc[None, :]) % 128) / 128.0
    cb[:, 6, :] = np.cos(ang) / np.sqrt(128.0)
    cb[:, 7, :] = -np.sin(ang) / np.sqrt(128.0)
    j = np.arange(128)[:, None]
    q = np.arange(128)[None, :]
    cb[:, 8, :] = np.where(j >= q, 0.0, NEG)
    cb[:, 9, :] = np.where(j <= q, 0.0, NEG)
    c["c_bf"] = _bf(cb)
    n = np.arange(256)
    a2 = 2.0 * np.pi * ((n[:, None] * n[None, :]) % 256) / 256.0
    C2 = (np.cos(a2) / 16.0).reshape(2, 128, 256)
    S2 = (np.sin(a2) / 16.0).reshape(2, 128, 256)
    cx = np.stack([C2.transpose(1, 0, 2), S2.transpose(1, 0, 2)], axis=1)
    c["c_ctxdft"] = _bf(cx)
    loc = np.arange(T_ALL)
    glob = loc if par == 0 else (T_ALL - 1 - loc)
    inv = (np.float32(10000.0) ** (-np.arange(16, dtype=np.float32) / np.float32(16))).astype(np.float32)
    g_ = glob[:T_A]
    row = (g_ // 64).astype(np.float32)
    col = (g_ % 64).astype(np.float32)
    ar = (row[None, :] * inv[:, None]).astype(np.float32)
    ac = (col[None, :] * inv[:, None]).astype(np.float32)
    cos64 = np.concatenate([np.cos(ar), np.cos(ar), np.cos(ac), np.cos(ac)], axis=0)
    sin64 = np.concatenate([-np.sin(ar), np.sin(ar), -np.sin(ac), np.sin(ac)], axis=0)
    c["c_rope"] = np.ascontiguousarray(np.stack([np.concatenate([cos64, cos64], 0), np.concatenate([sin64, sin64], 0)], 0).astype(np.float32))
    gk = glob[:T_X].astype(np.int64)
    gn = glob.astype(np.int64)
    ph = (gn[:, None] * gk[None, :]) % T_ALL
    angL = (2.0 * np.pi / T_ALL) * ph
    c["c_dftc"] = _bf(np.cos(angL) / 64.0)
    c["c_dfts"] = _bf(np.sin(angL) / 64.0)
    nm = np.zeros((128, 3, 6, 128), np.float32)
    for cls, qt in enumerate((0, 1, 8)):
        for dl in range(6):
            kt = qt + dl - 2
            if kt < 0:
                nm[:, cls, dl, :] = NEG
                continue
            kg = glob[kt * 128 + np.arange(128)]
            qg = glob[qt * 128 + np.arange(128)]
            kr, kc = kg // 64, kg % 64
            qr, qc = qg // 64, qg % 64
            r0 = np.clip(qr - 4, 0, 56)
            c0 = np.clip(qc - 8, 0, 48)
            ok = (kr[:, None] >= r0[None, :]) & (kr[:, None] < r0[None, :] + 8) & (kc[:, None] >= c0[None, :]) & (kc[:, None] < c0[None, :] + 16)
            nm[:, cls, dl, :] = np.where(ok, 0.0, NEG)
    c["c_namask"] = _bf(nm)
    _CONST_CACHE[par] = c
    return c


_NC_CACHE = {}


def get_nc(stop_after=None, debug=False, only=None, flags=()):
    key = (stop_after, debug, only, tuple(flags))
    if key not in _NC_CACHE:
        bld = Builder(stop_after=stop_after, debug=debug, only=only, flags=flags)
        _NC_CACHE[key] = bld.build()
    return _NC_CACHE[key]


def make_in_maps(inputs):
    f32 = lambda a: np.ascontiguousarray(np.asarray(a, dtype=np.float32))
    x = f32(inputs["x"])
    c = f32(inputs["c"])
    ctx = f32(inputs["ctx"])
    c_ctx = f32(inputs["c_ctx"])
    ev_in = f32(inputs["ev_w_in"])[0]
    ev_out = f32(inputs["ev_w_out"])[0]
    hp = [0, 4, 1, 5, 2, 6, 3, 7]
    qcols = np.concatenate([512 + h * 64 + np.arange(64) for h in hp])
    cols = np.concatenate([np.arange(512), qcols, np.arange(1024, 1280)])
    ev_in_p = np.ascontiguousarray(ev_in[:, cols])
    rows = np.concatenate([np.arange(512), qcols])
    ev_out_p = np.ascontiguousarray(ev_out[rows, :])
    rb = f32(inputs["od_rel_bias"])[0]
    shared = {
        "ada_w": f32(inputs["ada_w"]), "ada_b": f32(inputs["ada_b"]),
        "norm1_g": f32(inputs["norm1_g"]), "norm2_g": f32(inputs["norm2_g"]),
        "ffn_w_gate": f32(inputs["ffn_w_gate"]), "ffn_w_up": f32(inputs["ffn_w_up"]), "ffn_w_down": f32(inputs["ffn_w_down"]),
        "ev_w_in": ev_in_p, "ev_w_out": ev_out_p,
        "ev_q_norm": f32(inputs["ev_q_norm"])[0], "ev_k_norm": f32(inputs["ev_k_norm"])[0], "ev_sink": f32(inputs["ev_sink"])[0],
        "od_w_in": f32(inputs["od_w_in"])[0], "od_w_out": f32(inputs["od_w_out"])[0],
        "od_q_norm": f32(inputs["od_q_norm"])[0], "od_k_norm": f32(inputs["od_k_norm"])[0],
    }
    pad = np.zeros(64, np.float32)
    rbs = [np.concatenate([rb.reshape(-1), pad]), np.concatenate([rb[:, ::-1, ::-1].reshape(-1), pad])]
    in_maps = []
    for cid in range(8):
        b, par = cid // 2, cid % 2
        m = dict(shared)
        m["x_loc"] = np.ascontiguousarray(x[b] if par == 0 else x[b][::-1])
        m["ctx_b"] = np.ascontiguousarray(ctx[b])
        m["cvec"] = np.ascontiguousarray(np.stack([c[b], c_ctx], 0))
        m["rel_bias"] = rbs[par]
        m.update(host_consts(par))
        in_maps.append(m)
    return in_maps


def assemble(results):
    out = np.zeros((4, T_ALL, D), np.float32)
    for cid in range(8):
        b, par = cid // 2, cid % 2
        o = np.asarray(results[cid]["out_loc"], dtype=np.float32)
        if par == 0:
            out[b, :T_OWN] = o
        else:
            out[b, T_OWN:] = o[::-1]
    return out


def kernel(**inputs):
    nc = get_nc()
    in_maps = make_in_maps(inputs)
    res = run_bass_kernel_spmd(nc, in_maps, core_ids=list(range(8)))
    return assemble(res.results)
```
